# Optimizing a Trainium2 kernel written in Bass

```python
import math
import jax
import jax.numpy as jnp
from jax import lax
import numpy as np

D_MODEL = 2048
BATCH = 4
SEQ = 2048
DEPTH = 4

GRID_W = 64
CTX_LEN = 256
N_EVEN = (DEPTH + 1) // 2
N_ODD = DEPTH // 2
NORM_EPS = 1e-6
N_MOD = 6

HEAD_DIM = 128
A_Q_HEADS = (D_MODEL // 2) // HEAD_DIM
A_KV_HEADS = max(1, A_Q_HEADS // 4)
A_GROUP = A_Q_HEADS // A_KV_HEADS
A_Q_W = A_Q_HEADS * HEAD_DIM
A_KV_W = A_KV_HEADS * HEAD_DIM
WINDOW = 128
ATTN_BLOCK = 128
ROPE_THETA = 10000.0

CONV_CH = D_MODEL // 2
CONV_GROUPS = CONV_CH // 128
CONV_K = 31

EVEN_SPLITS = (A_Q_W, A_Q_W + A_KV_W, A_Q_W + 2 * A_KV_W)
EVEN_IN_W = A_Q_W + 2 * A_KV_W + 2 * CONV_CH
EVEN_CAT_W = A_Q_W + CONV_CH

DN_DK = 128
DN_DV = 128
DN_K_HEADS = D_MODEL // DN_DK
DN_V_HEADS = 2 * DN_K_HEADS
DN_QK_W = DN_K_HEADS * DN_DK
DN_V_W = DN_V_HEADS * DN_DV
DN_CONV_CH = 2 * DN_QK_W + DN_V_W
DN_BA_W = 4 * DN_V_HEADS
DN_SHORT_K = 5
DN_CHUNK = 64
DT_MIN = 0.001
DT_MAX = 0.1

D_FF = 256 * ((8 * D_MODEL // 3 + 255) // 256)
FFN_CONV_K = 3

kernel_name = 'hybrid_swa_conformer_gdn_convffn_diffusion'


def rms_norm(x, g):
    xf = x.astype(jnp.float32)
    y = xf * lax.rsqrt(jnp.mean(xf * xf, axis=-1, keepdims=True) + NORM_EPS)
    return (y * g.astype(jnp.float32)).astype(x.dtype)


def l2_norm(x):
    xf = x.astype(jnp.float32)
    return xf * lax.rsqrt(jnp.sum(xf * xf, axis=-1, keepdims=True) + NORM_EPS)


def dwconv(x, w, b=None):
    k, ch = w.shape
    left = (k - 1) // 2
    y = lax.conv_general_dilated(x, w[:, None, :].astype(x.dtype), window_strides=(1,),
                                 padding=[(left, k - 1 - left)],
                                 dimension_numbers=('NWC', 'WIO', 'NWC'),
                                 feature_group_count=ch)
    return y if b is None else y + b.astype(x.dtype)


def adaln(cond, w_mod, b_mod):
    return jnp.split(jax.nn.silu(cond) @ w_mod + b_mod, N_MOD, axis=-1)


def axial_rope(n_tokens):
    rows = n_tokens // GRID_W
    row = jnp.repeat(jnp.arange(rows, dtype=jnp.float32), GRID_W)
    col = jnp.tile(jnp.arange(GRID_W, dtype=jnp.float32), rows)
    half = HEAD_DIM // 2
    inv_freq = ROPE_THETA ** (-jnp.arange(0, half, 2, dtype=jnp.float32) / half)
    ang_r = row[:, None] * inv_freq
    ang_c = col[:, None] * inv_freq
    ang = jnp.concatenate([ang_r, ang_r, ang_c, ang_c], axis=-1)
    return jnp.cos(ang), jnp.sin(ang)


def apply_rope(x, cos, sin):
    xf = x.astype(jnp.float32)
    x1, x2, x3, x4 = jnp.split(xf, 4, axis=-1)
    rot = jnp.concatenate([-x2, x1, -x4, x3], axis=-1)
    return (xf * cos[:, None, :] + rot * sin[:, None, :]).astype(x.dtype)


def banded_latent_attention(q, k, v, k_ctx, v_ctx, sink):
    B, T, _, dh = q.shape
    L = k_ctx.shape[1]
    nb = T // ATTN_BLOCK
    scale = dh ** -0.5
    qb = q.reshape(B, nb, ATTN_BLOCK, A_KV_HEADS, A_GROUP, dh)
    pad = ((0, 0), (ATTN_BLOCK, ATTN_BLOCK), (0, 0), (0, 0))

    def band(t):
        tb = jnp.pad(t, pad).reshape(B, nb + 2, ATTN_BLOCK, A_KV_HEADS, dh)
        return jnp.concatenate([tb[:, :-2], tb[:, 1:-1], tb[:, 2:]], axis=2)

    kb, vb = band(k), band(v)
    s_band = jnp.einsum('bnqhgd,bnkhd->bnhgqk', qb, kb).astype(jnp.float32) * scale
    s_ctx = jnp.einsum('bnqhgd,bkhd->bnhgqk', qb, k_ctx).astype(jnp.float32) * scale
    qpos = jnp.arange(nb)[:, None] * ATTN_BLOCK + jnp.arange(ATTN_BLOCK)[None, :]
    kpos = jnp.arange(nb)[:, None] * ATTN_BLOCK - ATTN_BLOCK + jnp.arange(3 * ATTN_BLOCK)[None, :]
    rel = kpos[:, None, :] - qpos[:, :, None]
    valid = (jnp.abs(rel) <= WINDOW) & (kpos[:, None, :] >= 0) & (kpos[:, None, :] < T)
    s_band = jnp.where(valid[None, :, None, None], s_band, -jnp.inf)
    sink_l = jnp.broadcast_to(sink.astype(jnp.float32).reshape(A_KV_HEADS, A_GROUP)[:, :, None, None],
                              (B, nb, A_KV_HEADS, A_GROUP, ATTN_BLOCK, 1))
    p = jax.nn.softmax(jnp.concatenate([sink_l, s_ctx, s_band], axis=-1), axis=-1).astype(v.dtype)
    o = (jnp.einsum('bnhgqk,bkhd->bnqhgd', p[..., 1:1 + L], v_ctx)
         + jnp.einsum('bnhgqk,bnkhd->bnqhgd', p[..., 1 + L:], vb))
    return o.reshape(B, T, A_Q_HEADS * dh)


def context_attention(q, k, v, sink):
    B, L, _, dh = q.shape
    qg = q.reshape(B, L, A_KV_HEADS, A_GROUP, dh)
    s = jnp.einsum('bqhgd,bkhd->bhgqk', qg, k).astype(jnp.float32) * (dh ** -0.5)
    sink_l = jnp.broadcast_to(sink.astype(jnp.float32).reshape(A_KV_HEADS, A_GROUP)[:, :, None, None],
                              (B, A_KV_HEADS, A_GROUP, L, 1))
    p = jax.nn.softmax(jnp.concatenate([sink_l, s], axis=-1), axis=-1).astype(v.dtype)
    o = jnp.einsum('bhgqk,bkhd->bqhgd', p[..., 1:], v)
    return o.reshape(B, L, A_Q_HEADS * dh)


def conformer_conv(u, dw_w, dw_b, ln_g, ln_b):
    a, b = jnp.split(u, 2, axis=-1)
    h = dwconv(a * jax.nn.sigmoid(b), dw_w, dw_b)
    B, T, C = h.shape
    hf = h.astype(jnp.float32).reshape(B, T, CONV_GROUPS, C // CONV_GROUPS)
    mu = jnp.mean(hf, axis=-1, keepdims=True)
    var = jnp.mean(jnp.square(hf - mu), axis=-1, keepdims=True)
    hn = ((hf - mu) * lax.rsqrt(var + NORM_EPS)).reshape(B, T, C) * ln_g + ln_b
    return jax.nn.silu(hn).astype(u.dtype)


def even_mixer(nx, nc, cos, sin, w_in, w_out, q_g, k_g, sink, dw_w, dw_b, ln_g, ln_b, need_ctx):
    B, T, _ = nx.shape
    L = nc.shape[1]
    q, k, v, glu = jnp.split(nx @ w_in, EVEN_SPLITS, axis=-1)
    q = apply_rope(rms_norm(q.reshape(B, T, A_Q_HEADS, HEAD_DIM), q_g), cos, sin)
    k = apply_rope(rms_norm(k.reshape(B, T, A_KV_HEADS, HEAD_DIM), k_g), cos, sin)
    v = v.reshape(B, T, A_KV_HEADS, HEAD_DIM)
    if need_ctx:
        qc, kc, vc, gluc = jnp.split(nc @ w_in, EVEN_SPLITS, axis=-1)
    else:
        kc, vc = jnp.split(nc @ w_in[:, A_Q_W:A_Q_W + 2 * A_KV_W], 2, axis=-1)
    kc = rms_norm(kc.reshape(B, L, A_KV_HEADS, HEAD_DIM), k_g)
    vc = vc.reshape(B, L, A_KV_HEADS, HEAD_DIM)
    ox = jnp.concatenate([banded_latent_attention(q, k, v, kc, vc, sink),
                          conformer_conv(glu, dw_w, dw_b, ln_g, ln_b)], axis=-1) @ w_out
    if not need_ctx:
        return ox, None
    qc = rms_norm(qc.reshape(B, L, A_Q_HEADS, HEAD_DIM), q_g)
    oc = jnp.concatenate([context_attention(qc, kc, vc, sink),
                          conformer_conv(gluc, dw_w, dw_b, ln_g, ln_b)], axis=-1) @ w_out
    return ox, oc


def dn_gates(ba, a_log, dt_bias):
    b_f, b_b, a_f, a_b = jnp.split(ba.astype(jnp.float32), 4, axis=-1)
    beta = jax.nn.sigmoid(jnp.stack([b_f, b_b]))
    a = jnp.stack([a_f, a_b])
    g = -jnp.exp(a_log.astype(jnp.float32))[:, None, None, :] * jax.nn.softplus(
        a + dt_bias.astype(jnp.float32)[:, None, None, :])
    return beta, g


def dn_project(n, w_in, conv_w, a_log, dt_bias, with_qz):
    B, T, _ = n.shape
    rep = DN_V_HEADS // DN_K_HEADS
    if with_qz:
        u = n @ w_in[:, :DN_CONV_CH + DN_V_W]
        qkv = jax.nn.silu(dwconv(u[..., :DN_CONV_CH], conv_w))
        q = jnp.repeat(l2_norm(qkv[..., :DN_QK_W].reshape(B, T, DN_K_HEADS, DN_DK)), rep, axis=2) * (DN_DK ** -0.5)
        kv = qkv[..., DN_QK_W:]
        z = u[..., DN_CONV_CH:]
    else:
        kv = jax.nn.silu(dwconv(n @ w_in[:, DN_QK_W:DN_CONV_CH], conv_w[:, DN_QK_W:]))
        q = None
        z = None
    k = jnp.repeat(l2_norm(kv[..., :DN_QK_W].reshape(B, T, DN_K_HEADS, DN_DK)), rep, axis=2)
    v = kv[..., DN_QK_W:].reshape(B, T, DN_V_HEADS, DN_DV).astype(jnp.float32)
    beta, g = dn_gates(n @ w_in[:, DN_CONV_CH + DN_V_W:], a_log, dt_bias)
    return q, k, v, z, beta, g


def gated_delta_chunked(q, k, v, g, beta, s0):
    B, T, H, _ = k.shape
    n = T // DN_CHUNK

    def to_chunks(t):
        t = t.reshape((B, n, DN_CHUNK, H) + t.shape[3:])
        return jnp.moveaxis(t, (1, 3), (0, 2))

    idx = jnp.arange(DN_CHUNK)
    incl = idx[:, None] >= idx[None, :]
    strict = idx[:, None] > idx[None, :]
    gcum = jnp.cumsum(to_chunks(g), axis=-1)
    with_out = q is not None
    xs = (to_chunks(k), to_chunks(v), gcum, to_chunks(beta))
    if with_out:
        xs = xs + (to_chunks(q),)

    def step(S, xs_i):
        k_i, v_i, g_i, b_i = xs_i[:4]
        decay = jnp.exp(jnp.where(incl, g_i[..., :, None] - g_i[..., None, :], -jnp.inf))
        kb = k_i * b_i[..., None]
        lower = jnp.where(strict, jnp.einsum('bhid,bhjd->bhij', kb, k_i) * decay, 0.0)
        rhs = jnp.concatenate([v_i * b_i[..., None], kb * jnp.exp(g_i)[..., None]], axis=-1)
        sol = lax.linalg.triangular_solve(lower, rhs, left_side=True, lower=True, unit_diagonal=True)
        u_i, w_i = sol[..., :DN_DV], sol[..., DN_DV:]
        v_new = u_i - w_i @ S
        g_last = g_i[..., -1:]
        S_next = S * jnp.exp(g_last)[..., None] + jnp.einsum(
            'bhck,bhcv->bhkv', k_i * jnp.exp(g_last - g_i)[..., None], v_new)
        if not with_out:
            return S_next, None
        q_i = xs_i[4]
        o_i = (q_i * jnp.exp(g_i)[..., None]) @ S + (jnp.einsum('bhid,bhjd->bhij', q_i, k_i) * decay) @ v_new
        return S_next, o_i

    s_final, o = lax.scan(step, s0, xs)
    if not with_out:
        return None, s_final
    o = jnp.moveaxis(o, (0, 2), (1, 3)).reshape(B, T, H, DN_DV)
    return o, s_final


def rev(t):
    return None if t is None else jnp.flip(t, axis=1)


def dn_output(o, z, norm_g, w_out):
    B, T = o.shape[:2]
    o = o * lax.rsqrt(jnp.mean(o * o, axis=-1, keepdims=True) + NORM_EPS) * norm_g.astype(jnp.float32)
    o = o * jax.nn.silu(z.astype(jnp.float32).reshape(B, T, DN_V_HEADS, DN_DV))
    return o.reshape(B, T, DN_V_W).astype(z.dtype) @ w_out


def odd_mixer(nx, nc, w_in, conv_w, a_log, dt_bias, norm_g, w_out, need_ctx):
    B = nx.shape[0]
    s0 = jnp.zeros((B, DN_V_HEADS, DN_DK, DN_DV), jnp.float32)
    qx, kx, vx, zx, bx, gx = dn_project(nx, w_in, conv_w, a_log, dt_bias, True)
    qc, kc, vc, zc, bc, gc = dn_project(nc, w_in, conv_w, a_log, dt_bias, need_ctx)
    oc_f, sc_f = gated_delta_chunked(qc, kc, vc, gc[0], bc[0], s0)
    oc_b, sc_b = gated_delta_chunked(rev(qc), rev(kc), rev(vc), rev(gc[1]), rev(bc[1]), s0)
    ox_f, _ = gated_delta_chunked(qx, kx, vx, gx[0], bx[0], sc_f)
    ox_b, _ = gated_delta_chunked(rev(qx), rev(kx), rev(vx), rev(gx[1]), rev(bx[1]), sc_b)
    out_x = dn_output(ox_f + rev(ox_b), zx, norm_g, w_out)
    if not need_ctx:
        return out_x, None
    return out_x, dn_output(oc_f + rev(oc_b), zc, norm_g, w_out)


def conv_ffn(n, w_up, conv_w, conv_b, w_down):
    gate, val = jnp.split(n @ w_up, 2, axis=-1)
    return (jax.nn.silu(dwconv(gate, conv_w, conv_b)) * val) @ w_down


def setup_inputs(seed: int = 0) -> dict:
    key = jax.random.key(seed)
    ks = iter(jax.random.split(key, 40))
    f32 = jnp.float32
    D = D_MODEL

    def nrm(shape, std):
        return jax.random.normal(next(ks), shape, f32) * std

    def gain(shape):
        return 1.0 + nrm(shape, 0.01)

    dt = jnp.exp(jax.random.uniform(next(ks), (N_ODD, 2, DN_V_HEADS), f32)
                 * (math.log(DT_MAX) - math.log(DT_MIN)) + math.log(DT_MIN))
    return {
        'x': nrm((BATCH, SEQ, D), 1.0),
        'c': nrm((BATCH, D), 1.0),
        'ctx': nrm((BATCH, CTX_LEN, D), 1.0),
        'c_ctx': nrm((D,), 1.0),
        'w_mod': nrm((DEPTH, D, N_MOD * D), 0.5 * D ** -0.5),
        'b_mod': nrm((DEPTH, N_MOD * D), 0.01),
        'norm1_g': gain((DEPTH, D)),
        'norm2_g': gain((DEPTH, D)),
        'ffn_w_up': nrm((DEPTH, D, 2 * D_FF), D ** -0.5),
        'ffn_conv_w': nrm((DEPTH, FFN_CONV_K, D_FF), FFN_CONV_K ** -0.5),
        'ffn_conv_b': nrm((DEPTH, D_FF), 0.01),
        'ffn_w_down': nrm((DEPTH, D_FF, D), D_FF ** -0.5),
        'even_w_in': nrm((N_EVEN, D, EVEN_IN_W), D ** -0.5),
        'even_w_out': nrm((N_EVEN, EVEN_CAT_W, D), EVEN_CAT_W ** -0.5),
        'attn_q_norm_g': gain((N_EVEN, HEAD_DIM)),
        'attn_k_norm_g': gain((N_EVEN, HEAD_DIM)),
        'attn_sink': nrm((N_EVEN, A_Q_HEADS), 1.0),
        'conv_dw_w': nrm((N_EVEN, CONV_K, CONV_CH), CONV_K ** -0.5),
        'conv_dw_b': nrm((N_EVEN, CONV_CH), 0.01),
        'conv_ln_g': gain((N_EVEN, CONV_CH)),
        'conv_ln_b': nrm((N_EVEN, CONV_CH), 0.01),
        'dn_w_in': nrm((N_ODD, D, DN_CONV_CH + DN_V_W + DN_BA_W), D ** -0.5),
        'dn_conv_w': nrm((N_ODD, DN_SHORT_K, DN_CONV_CH), DN_SHORT_K ** -0.5),
        'dn_a_log': jnp.log(jax.random.uniform(next(ks), (N_ODD, 2, DN_V_HEADS), f32, 1.0, 16.0)),
        'dn_dt_bias': dt + jnp.log(-jnp.expm1(-dt)),
        'dn_norm_g': gain((N_ODD, DN_DV)),
        'dn_w_out': nrm((N_ODD, DN_V_W, D), DN_V_W ** -0.5),
    }


def reference(x, c, ctx, c_ctx, w_mod, b_mod, norm1_g, norm2_g, ffn_w_up, ffn_conv_w, ffn_conv_b,
              ffn_w_down, even_w_in, even_w_out, attn_q_norm_g, attn_k_norm_g, attn_sink, conv_dw_w,
              conv_dw_b, conv_ln_g, conv_ln_b, dn_w_in, dn_conv_w, dn_a_log, dn_dt_bias, dn_norm_g,
              dn_w_out):
    cos, sin = axial_rope(x.shape[1])
    hx, hc = x, ctx
    for layer in range(DEPTH):
        last = layer == DEPTH - 1
        i = layer // 2
        sh1, sc1, g1, sh2, sc2, g2 = [m[:, None, :] for m in adaln(c, w_mod[layer], b_mod[layer])]
        csh1, csc1, cg1, csh2, csc2, cg2 = adaln(c_ctx, w_mod[layer], b_mod[layer])
        nx = rms_norm(hx, norm1_g[layer]) * (1 + sc1) + sh1
        nc = rms_norm(hc, norm1_g[layer]) * (1 + csc1) + csh1
        if layer % 2 == 0:
            ox, oc = even_mixer(nx, nc, cos, sin, even_w_in[i], even_w_out[i], attn_q_norm_g[i],
                                attn_k_norm_g[i], attn_sink[i], conv_dw_w[i], conv_dw_b[i],
                                conv_ln_g[i], conv_ln_b[i], not last)
        else:
            ox, oc = odd_mixer(nx, nc, dn_w_in[i], dn_conv_w[i], dn_a_log[i], dn_dt_bias[i],
                               dn_norm_g[i], dn_w_out[i], not last)
        hx = hx + g1 * ox
        hx = hx + g2 * conv_ffn(rms_norm(hx, norm2_g[layer]) * (1 + sc2) + sh2, ffn_w_up[layer],
                                ffn_conv_w[layer], ffn_conv_b[layer], ffn_w_down[layer])
        if not last:
            hc = hc + cg1 * oc
            hc = hc + cg2 * conv_ffn(rms_norm(hc, norm2_g[layer]) * (1 + csc2) + csh2, ffn_w_up[layer],
                                     ffn_conv_w[layer], ffn_conv_b[layer], ffn_w_down[layer])
    return hx
```

```python
import contextlib
import numpy as np
import concourse.bass as bass
import concourse.mybir as mybir
from concourse.bass_utils import run_bass_kernel_spmd

F32 = mybir.dt.float32
BF16 = mybir.dt.bfloat16
AF = mybir.ActivationFunctionType
ALU = mybir.AluOpType
AX = mybir.AxisListType

D = 2048
NCH = 16
LCTX = 256
TLAT = 2048
NT = LCTX + TLAT
DEPTH = 4
DFF = 5632
NFF = DFF // 128
EPS = 1e-6
TILES = [(0, 256), (256, 512), (768, 512), (1280, 512), (1792, 512)]
EVEN_IN_W = 3584
DN_IN_W = 12416

ENGS = ("pe", "act", "dve", "pool", "sp")
NDSEM = 6


class Op:
    __slots__ = ("eng", "fn", "idx", "dma", "slot", "val", "deps", "needed", "cnt", "waits", "clock")


class Prog:
    def __init__(self, nc, same_engine_sync=True):
        self.nc = nc
        self.ops = {e: [] for e in ENGS}
        self.order = []
        self.last_w = {}
        self.readers = {}
        self.same_engine_sync = same_engine_sync
        self.dma_rr = {e: 0 for e in ENGS}
        self.dma_val = {}
        self.stack = contextlib.ExitStack()
        self.sems = {}
        self.dsems = {}
        self._n = 0

    def sb(self, shape, dtype, stack=None):
        self._n += 1
        return (stack or self.stack).enter_context(self.nc.sbuf_tensor(f"sb{self._n}", list(shape), dtype))

    def ps(self, shape, dtype, stack=None):
        self._n += 1
        return (stack or self.stack).enter_context(self.nc.psum_tensor(f"ps{self._n}", list(shape), dtype))

    def add(self, eng, fn, reads=(), writes=(), dma=False):
        op = Op()
        op.eng = eng; op.fn = fn; op.dma = dma; op.needed = False; op.waits = None
        op.idx = len(self.ops[eng]); op.slot = None; op.val = None
        deps = set()
        writes = list(writes)
        if dma:
            s = self.dma_rr[eng] % NDSEM
            self.dma_rr[eng] += 1
            op.slot = (eng, s)
            writes.append(("dsem", eng, s))
            self.dma_val[op.slot] = self.dma_val.get(op.slot, 0) + 16
            op.val = self.dma_val[op.slot]
        for r in reads:
            w = self.last_w.get(r)
            if w is not None:
                deps.add(w)
        for r in writes:
            w = self.last_w.get(r)
            if w is not None:
                deps.add(w)
            for rd in self.readers.get(r, ()):
                deps.add(rd)
        for r in writes:
            self.last_w[r] = op
            self.readers[r] = []
        for r in reads:
            self.readers.setdefault(r, []).append(op)
        deps.discard(op)
        op.deps = deps
        self.ops[eng].append(op)
        self.order.append(op)
        return op

    def barrier(self):
        lasts = []
        for e in ENGS:
            for o in reversed(self.ops[e]):
                if o.fn is not None and not o.dma:
                    lasts.append(o)
                    break
        latest = {}
        for e in ENGS:
            for o in reversed(self.ops[e]):
                if o.dma and o.slot not in latest:
                    latest[o.slot] = o
        for e in ENGS:
            op = Op()
            op.eng = e; op.fn = None; op.dma = False; op.needed = False; op.waits = None
            op.idx = len(self.ops[e]); op.slot = None; op.val = None
            op.deps = set(o for o in lasts if o.eng != e and o.fn is not None and not o.dma) | set(latest.values())
            self.ops[e].append(op)
            self.order.append(op)
        self.last_w.clear(); self.readers.clear()

    def emit(self):
        nc = self.nc
        seen = {e: {} for e in ENGS}
        for op in self.order:
            s = seen[op.eng]
            waits = []
            for d in sorted(op.deps, key=lambda o: (o.eng, o.idx)):
                if d.dma:
                    key = ("d",) + d.slot
                    if s.get(key, 0) >= d.val:
                        continue
                    waits.append(d)
                    s[key] = d.val
                else:
                    if d.fn is None:
                        continue
                    if d.eng == op.eng and (d.eng == "pe" or not self.same_engine_sync or op.fn is None):
                        continue
                    if s.get(d.eng, -1) >= d.idx:
                        continue
                    waits.append(d)
                    s[d.eng] = d.idx
                for k, v in d.clock.items():
                    if k == op.eng:
                        continue
                    if s.get(k, -1) < v:
                        s[k] = v
            for d in waits:
                d.needed = True
            op.waits = waits
            op.clock = dict(s)
        for e in ENGS:
            c = 0
            for op in self.ops[e]:
                if op.needed and not op.dma:
                    c += 1
                op.cnt = c
        st = self.stack
        for e in ENGS:
            self.sems[e] = st.enter_context(nc.semaphore(f"s_{e}"))
        for key in self.dma_val:
            self.dsems[key] = st.enter_context(nc.semaphore(f"d_{key[0]}{key[1]}"))
        block = st.enter_context(nc.Block())
        handles = {"pe": block.tensor, "act": block.scalar, "dve": block.vector, "pool": block.gpsimd, "sp": block.sync}

        def mk(e):
            def body(h):
                for op in self.ops[e]:
                    for d in op.waits:
                        if d.dma:
                            h.wait_ge(self.dsems[d.slot], d.val)
                        else:
                            h.wait_ge(self.sems[d.eng], d.cnt)
                    if op.fn is None:
                        continue
                    ins = op.fn(h)
                    if op.dma:
                        ins.then_inc(self.dsems[op.slot], 16)
                    elif op.needed:
                        ins.then_inc(self.sems[e], 1)
            return body
        for e in ENGS:
            handles[e](mk(e))

    def finish(self):
        self.barrier()
        self.emit()
        self.stack.close()


def _layout(items):
    off = {}
    o = 0
    for name, w in items:
        off[name] = (o, w)
        o += w
    return off, o


VEC_ITEMS = [
    ("bmod", 4 * 96), ("n1g", 64), ("n2g", 64), ("fcw", 4 * 3 * NFF), ("fcb", 4 * NFF),
    ("qg", 2), ("kg", 2), ("sink", 16), ("cdw", 2 * 31 * 8), ("cdb", 16), ("lng", 16), ("lnb", 16),
    ("dcw", 2 * 5 * 64), ("alog", 128), ("dtb", 128), ("dngrep", 256), ("c", 16), ("cctx", 16),
]
VOFF, NV = _layout(VEC_ITEMS)

CONST_ITEMS = [
    ("ident", 128), ("ropeT", 128), ("band", 384),
    ("m_le", 128), ("m_ge", 128), ("m_lt", 128), ("m_gt", 128),
    ("cos", TLAT), ("sin", TLAT), ("mblk", 5 * 128),
]
COFF, NCONST = _layout(CONST_ITEMS)


def fm(v):
    return np.ascontiguousarray(v.reshape(-1, 128).T)


def pack_vecs(inp, b):
    V = np.zeros((128, NV), np.float32)

    def put(name, arr):
        o, w = VOFF[name]
        assert arr.shape == (128, w), (name, arr.shape, w)
        V[:, o:o + w] = arr
    put("bmod", np.concatenate([fm(inp["b_mod"][l]) for l in range(4)], axis=1))
    put("n1g", np.concatenate([fm(inp["norm1_g"][l]) for l in range(4)], axis=1))
    put("n2g", np.concatenate([fm(inp["norm2_g"][l]) for l in range(4)], axis=1))
    put("fcw", np.concatenate([fm(inp["ffn_conv_w"][l, k]) for l in range(4) for k in range(3)], axis=1))
    put("fcb", np.concatenate([fm(inp["ffn_conv_b"][l]) for l in range(4)], axis=1))
    put("qg", inp["attn_q_norm_g"].T.copy())
    put("kg", inp["attn_k_norm_g"].T.copy())
    put("sink", np.broadcast_to(inp["attn_sink"].reshape(1, 16), (128, 16)).copy())
    put("cdw", np.concatenate([fm(inp["conv_dw_w"][i, k]) for i in range(2) for k in range(31)], axis=1))
    put("cdb", np.concatenate([fm(inp["conv_dw_b"][i]) for i in range(2)], axis=1))
    put("lng", np.concatenate([fm(inp["conv_ln_g"][i]) for i in range(2)], axis=1))
    put("lnb", np.concatenate([fm(inp["conv_ln_b"][i]) for i in range(2)], axis=1))
    put("dcw", np.concatenate([fm(inp["dn_conv_w"][i, k]) for i in range(2) for k in range(5)], axis=1))
    put("alog", np.broadcast_to(inp["dn_a_log"].reshape(1, 128), (128, 128)).copy())
    put("dtb", np.broadcast_to(inp["dn_dt_bias"].reshape(1, 128), (128, 128)).copy())
    put("dngrep", np.broadcast_to(inp["dn_norm_g"].reshape(1, 256), (128, 256)).copy())
    put("c", fm(inp["c"][b]))
    put("cctx", fm(inp["c_ctx"]))
    return V


def make_consts():
    C = np.zeros((128, NCONST), np.float32)

    def put(name, arr):
        o, w = COFF[name]
        assert arr.shape == (128, w), (name, arr.shape)
        C[:, o:o + w] = arr
    idx = np.arange(128)
    put("ident", np.eye(128, dtype=np.float32))
    R = np.zeros((128, 128), np.float32)
    for i in range(32):
        R[i, 32 + i] = -1.0
        R[32 + i, i] = 1.0
        R[64 + i, 96 + i] = -1.0
        R[96 + i, 64 + i] = 1.0
    put("ropeT", R.T.copy())
    kk = idx[:, None]; qq = idx[None, :]
    band = np.concatenate([(kk <= qq), np.ones((128, 128), bool), (qq <= kk)], axis=1).astype(np.float32)
    put("band", band)
    put("m_le", (kk <= qq).astype(np.float32))
    put("m_ge", (kk >= qq).astype(np.float32))
    put("m_lt", (kk < qq).astype(np.float32))
    put("m_gt", (kk > qq).astype(np.float32))
    mb = [(kk // 8 == qq // 8)]
    for bsz in (8, 16, 32, 64):
        mb.append((kk // (2 * bsz) == qq // (2 * bsz)) & (kk // bsz != qq // bsz))
    put("mblk", np.concatenate(mb, axis=1).astype(np.float32))
    GRID_W = 64
    rows = TLAT // GRID_W
    row = np.repeat(np.arange(rows, dtype=np.float32), GRID_W)
    col = np.tile(np.arange(GRID_W, dtype=np.float32), rows)
    half = 64
    inv_freq = (10000.0 ** (-np.arange(0, half, 2, dtype=np.float32) / half)).astype(np.float32)
    ang_r = row[:, None] * inv_freq
    ang_c = col[:, None] * inv_freq
    ang = np.concatenate([ang_r, ang_r, ang_c, ang_c], axis=-1)
    put("cos", np.cos(ang).T.astype(np.float32).copy())
    put("sin", np.sin(ang).T.astype(np.float32).copy())
    return C


class K:
    pass


def segs(t0, n):
    out = []
    if t0 < LCTX:
        e = min(t0 + n, LCTX)
        out.append((t0, e - t0, 1))
        if t0 + n > LCTX:
            out.append((LCTX, t0 + n - LCTX, 0))
    else:
        out.append((t0, n, 0))
    return out


def bc(ap, shape):
    return ap.broadcast_to(list(shape))


def stage_adaln(k, l):
    P = k.P
    with contextlib.ExitStack() as st:
        wb = [P.sb([128, 16, 256], BF16, st) for _ in range(2)]
        pm = P.ps([128, 96, 2], F32, st)
        wsrc = k.w_mod[l].rearrange("(kc p) n -> p kc n", p=128)
        for blk in range(48):
            b = blk % 2
            P.add("pool", lambda e, b=b, blk=blk: e.dma_start(out=wb[b][:], in_=wsrc[:, :, blk * 256:(blk + 1) * 256]),
                  writes=[("wb", b)], dma=True)
            for s in range(2):
                j = blk * 2 + s
                for kc in range(16):
                    P.add("pe", lambda e, b=b, s=s, j=j, kc=kc: e.matmul(
                        pm[:, j, :], wb[b][:, kc, s * 128:(s + 1) * 128], k.siluc[:, kc, :],
                        start=(kc == 0), stop=(kc == 15)),
                        reads=[("wb", b), "siluc"], writes=["pm"])
        bo = VOFF["bmod"][0] + l * 96
        P.add("dve", lambda e: e.tensor_tensor(out=k.mod[:], in0=pm[:], in1=bc(k.vecs[:, bo:bo + 96].unsqueeze(2), [128, 96, 2]),
                                               op=ALU.add), reads=["pm"], writes=["mod"])
        for (dst, m, gname) in ((k.A1, 1, "n1g"), (k.A2, 4, "n2g")):
            go = VOFF[gname][0] + l * 16
            P.add("dve", lambda e, dst=dst, m=m: e.tensor_scalar_add(out=dst[:], in0=k.mod[:, m * 16:(m + 1) * 16, :], scalar1=1.0),
                  reads=["mod"], writes=[("A", m)])
            P.add("dve", lambda e, dst=dst, go=go: e.tensor_tensor(
                out=dst[:], in0=dst[:], in1=bc(k.vecs[:, go:go + 16].unsqueeze(2), [128, 16, 2]), op=ALU.mult),
                reads=[("A", m)], writes=[("A", m)])
    P.barrier()


def modv(k, m, c, s):
    return k.mod[:, m * 16 + c, s:s + 1]


def stage_norm(k, which, nx, nbuf=2):
    P = k.P
    A = k.A1 if which == 1 else k.A2
    msh = 0 if which == 1 else 3
    hsrc = k.hT.rearrange("(c p) t -> p c t", p=128)
    with contextlib.ExitStack() as st:
        hs = [P.sb([128, 16, 512], F32, st) for _ in range(nbuf)]
        sq = P.sb([128, 16, 512], BF16, st)
        rstd = P.sb([128, 512], F32, st)
        pss = P.ps([128, 512], F32, st)
        for ti, (t0, n) in enumerate(TILES):
            b = ti % nbuf
            P.add("sp", lambda e, b=b, t0=t0, n=n: e.dma_start(out=hs[b][:, :, :n], in_=hsrc[:, :, t0:t0 + n]),
                  reads=["hT"], writes=[("hs", b)], dma=True)
            P.add("act", lambda e, b=b, n=n: e.activation(out=sq[:, :, :n], in_=hs[b][:, :, :n], func=AF.Square),
                  reads=[("hs", b)], writes=["sq"])
            for c in range(16):
                P.add("pe", lambda e, c=c, n=n: e.matmul(pss[:, :n], k.onesD[:], sq[:, c, :n], start=(c == 0), stop=(c == 15)),
                      reads=["sq"], writes=["pss"])
            P.add("act", lambda e, n=n: e.activation(out=rstd[:, :n], in_=pss[:, :n], func=AF.Sqrt, bias=EPS, scale=1.0),
                  reads=["pss"], writes=["rstd"])
            P.add("dve", lambda e, n=n: e.reciprocal(rstd[:, :n], rstd[:, :n]), reads=["rstd"], writes=["rstd"])
            P.add("dve", lambda e, b=b, n=n: e.tensor_tensor(out=hs[b][:, :, :n], in0=hs[b][:, :, :n],
                                                             in1=bc(rstd[:, :n].unsqueeze(1), [128, 16, n]), op=ALU.mult),
                  reads=["rstd", ("hs", b)], writes=[("hs", b)])
            s = 1 if t0 < LCTX else 0
            for c in range(16):
                P.add("act", lambda e, b=b, c=c, t0=t0, n=n, s=s: e.activation(
                    out=nx[:, c, t0:t0 + n], in_=hs[b][:, c, :n], func=AF.Identity,
                    bias=modv(k, msh, c, s), scale=A[:, c, s:s + 1]),
                    reads=[("hs", b), "mod", ("A", 1), ("A", 4)], writes=[("nx", ti, c)])


def stage_ffn(k, l):
    P = k.P
    with contextlib.ExitStack() as st0:
        nx = P.sb([128, 16, NT], BF16, st0)
        stage_norm(k, 2, nx)
        P.barrier()
        stage_ffn_up(k, l, nx)
        P.barrier()
    stage_ffn_down(k, l)


def stage_ffn_up(k, l, nx):
        P = k.P
        with contextlib.ExitStack() as st:
            wb = [P.sb([128, 16, 256], BF16, st) for _ in range(2)]
            G = [P.sb([128, NT], F32, st) for _ in range(2)]
            Vv = [P.sb([128, NT], BF16, st) for _ in range(2)]
            acc = [P.sb([128, NT], F32, st) for _ in range(2)]
            sg = [P.sb([128, NT], BF16, st) for _ in range(2)]
            hh = [P.sb([128, NT], BF16, st) for _ in range(2)]
            psg = [P.ps([128, 512], F32, st) for _ in range(2)]
            psv = [P.ps([128, 512], F32, st) for _ in range(2)]
            wsrc = k.ffn_w_up[l].rearrange("(kc p) n -> p kc n", p=128)
            cw = VOFF["fcw"][0] + l * 3 * NFF
            cb = VOFF["fcb"][0] + l * NFF
            for j in range(NFF):
                b = j % 2
                P.add("pool", lambda e, b=b, j=j: e.dma_start(out=wb[b][:, :, 0:128], in_=wsrc[:, :, j * 128:(j + 1) * 128]),
                      writes=[("wb", b, 0)], dma=True)
                P.add("pool", lambda e, b=b, j=j: e.dma_start(out=wb[b][:, :, 128:256], in_=wsrc[:, :, DFF + j * 128:DFF + (j + 1) * 128]),
                      writes=[("wb", b, 1)], dma=True)
                for ti, (t0, n) in enumerate(TILES):
                    pb = ti % 2
                    for kc in range(16):
                        P.add("pe", lambda e, b=b, pb=pb, kc=kc, t0=t0, n=n: e.matmul(
                            psg[pb][:, :n], wb[b][:, kc, 0:128], nx[:, kc, t0:t0 + n], start=(kc == 0), stop=(kc == 15)),
                            reads=[("wb", b, 0)], writes=[("psg", pb)])
                    for kc in range(16):
                        P.add("pe", lambda e, b=b, pb=pb, kc=kc, t0=t0, n=n: e.matmul(
                            psv[pb][:, :n], wb[b][:, kc, 128:256], nx[:, kc, t0:t0 + n], start=(kc == 0), stop=(kc == 15)),
                            reads=[("wb", b, 1)], writes=[("psv", pb)])
                    P.add("act", lambda e, b=b, pb=pb, t0=t0, n=n: e.activation(out=G[b][:, t0:t0 + n], in_=psg[pb][:, :n], func=AF.Copy),
                          reads=[("psg", pb)], writes=[("G", b, ti)])
                    P.add("dve", lambda e, b=b, pb=pb, t0=t0, n=n: e.tensor_copy(Vv[b][:, t0:t0 + n], psv[pb][:, :n]),
                          reads=[("psv", pb)], writes=[("V", b, ti)])
                Gk = [("G", b, ti) for ti in range(5)]
                Vk = [("V", b, ti) for ti in range(5)]
                w0 = k.vecs[:, cw + 0 * NFF + j:cw + 0 * NFF + j + 1]
                w1 = k.vecs[:, cw + 1 * NFF + j:cw + 1 * NFF + j + 1]
                w2 = k.vecs[:, cw + 2 * NFF + j:cw + 2 * NFF + j + 1]
                bb = k.vecs[:, cb + j:cb + j + 1]
                P.add("dve", lambda e, b=b, w1=w1, bb=bb: e.tensor_scalar(out=acc[b][:], in0=G[b][:], scalar1=w1, scalar2=bb,
                                                                        op0=ALU.mult, op1=ALU.add), reads=Gk, writes=[("acc", b)])
                for (s0, sn) in ((0, LCTX), (LCTX, TLAT)):
                    P.add("dve", lambda e, b=b, s0=s0, sn=sn, w0=w0: e.scalar_tensor_tensor(
                        out=acc[b][:, s0 + 1:s0 + sn], in0=G[b][:, s0:s0 + sn - 1], scalar=w0, in1=acc[b][:, s0 + 1:s0 + sn],
                        op0=ALU.mult, op1=ALU.add), reads=Gk + [("acc", b)], writes=[("acc", b)])
                    P.add("dve", lambda e, b=b, s0=s0, sn=sn, w2=w2: e.scalar_tensor_tensor(
                        out=acc[b][:, s0:s0 + sn - 1], in0=G[b][:, s0 + 1:s0 + sn], scalar=w2, in1=acc[b][:, s0:s0 + sn - 1],
                        op0=ALU.mult, op1=ALU.add), reads=Gk + [("acc", b)], writes=[("acc", b)])
                P.add("act", lambda e, b=b: e.activation(out=sg[b][:], in_=acc[b][:], func=AF.Silu),
                      reads=[("acc", b)], writes=[("sg", b)])
                P.add("pool", lambda e, b=b: e.tensor_tensor(out=hh[b][:], in0=sg[b][:], in1=Vv[b][:], op=ALU.mult),
                      reads=[("sg", b)] + Vk, writes=[("hh", b)])
                P.add("sp", lambda e, b=b, j=j: e.dma_start(out=k.HF[j * 128:(j + 1) * 128, :], in_=hh[b][:]),
                      reads=[("hh", b)], writes=["HF"], dma=True)


def stage_ffn_down(k, l):
    P = k.P
    HALF = NT // 2
    SUB = 384
    with contextlib.ExitStack() as st:
        hres = P.sb([128, NFF, HALF], BF16, st)
        wb = [P.sb([128, NFF, 256], BF16, st) for _ in range(2)]
        hrow = [P.sb([128, HALF], F32, st) for _ in range(2)]
        pso = [P.ps([128, 512], F32, st) for _ in range(4)]
        hfsrc = k.HF.rearrange("(j p) t -> p j t", p=128)
        wsrc = k.ffn_w_down[l].rearrange("(kc p) n -> p kc n", p=128)
        hsrc = k.hT.rearrange("(c p) t -> p c t", p=128)
        nps = 0
        for half in range(2):
            h0 = half * HALF
            for q in range(4):
                P.add("sp", lambda e, q=q, h0=h0: e.dma_start(out=hres[:, q * 11:(q + 1) * 11, :], in_=hfsrc[:, q * 11:(q + 1) * 11, h0:h0 + HALF]),
                      reads=["HF"], writes=[("hres", q)], dma=True)
            for blk in range(8):
                b = blk % 2
                for q in range(2):
                    P.add("pool", lambda e, b=b, blk=blk, q=q: e.dma_start(
                        out=wb[b][:, q * 22:(q + 1) * 22, :], in_=wsrc[:, q * 22:(q + 1) * 22, blk * 256:(blk + 1) * 256]),
                        writes=[("wb", b, q)], dma=True)
                for s in range(2):
                    c = blk * 2 + s
                    hb = c % 2
                    P.add("sp", lambda e, hb=hb, c=c, h0=h0: e.dma_start(out=hrow[hb][:], in_=hsrc[:, c, h0:h0 + HALF]),
                          reads=[("hT", c)], writes=[("hrow", hb)], dma=True)
                    for sub in range(HALF // SUB):
                        u0 = sub * SUB
                        pb = nps % 4
                        nps += 1
                        for kc in range(NFF):
                            P.add("pe", lambda e, b=b, s=s, pb=pb, kc=kc, u0=u0: e.matmul(
                                pso[pb][:, :SUB], wb[b][:, kc, s * 128:(s + 1) * 128], hres[:, kc, u0:u0 + SUB],
                                start=(kc == 0), stop=(kc == NFF - 1)),
                                reads=[("wb", b, kc // 22), ("hres", kc // 11)], writes=[("pso", pb)])
                        for (g0, gn, sidx) in segs(h0 + u0, SUB):
                            r0 = g0 - h0
                            P.add("dve", lambda e, hb=hb, pb=pb, r0=r0, gn=gn, u0=u0, c=c, sidx=sidx: e.scalar_tensor_tensor(
                                out=hrow[hb][:, r0:r0 + gn], in0=pso[pb][:, r0 - u0:r0 - u0 + gn], scalar=modv(k, 5, c, sidx),
                                in1=hrow[hb][:, r0:r0 + gn], op0=ALU.mult, op1=ALU.add),
                                reads=[("pso", pb), ("hrow", hb), "mod"], writes=[("hrow", hb)])
                    P.add("sp", lambda e, hb=hb, c=c, h0=h0: e.dma_start(out=hsrc[:, c, h0:h0 + HALF], in_=hrow[hb][:]),
                          reads=[("hrow", hb)], writes=[("hT", c)], dma=True)
    P.barrier()


DEBUG_OUT = ()


def build_program(plan, test_out_ctx=False):
    nc = bass.Bass("TRN2", target_bir_lowering=False)
    k = K()
    k.nc = nc
    P = Prog(nc)
    k.P = P
    def dt(name, shape, dtype, kind="Internal"):
        if name in DEBUG_OUT:
            kind = "ExternalOutput"
        return nc.dram_tensor(name, shape, dtype, kind=kind)
    k.h_in = dt("h_in", [D, NT], F32, kind="ExternalInput").ap()
    k.vecs_d = dt("vecs", [128, NV], F32, kind="ExternalInput").ap()
    k.consts_d = dt("consts", [128, NCONST], F32, kind="ExternalInput").ap()
    k.w_mod = dt("w_mod", [DEPTH, D, 6 * D], F32, kind="ExternalInput").ap()
    k.ffn_w_up = dt("ffn_w_up", [DEPTH, D, 2 * DFF], F32, kind="ExternalInput").ap()
    k.ffn_w_down = dt("ffn_w_down", [DEPTH, DFF, D], F32, kind="ExternalInput").ap()
    k.even_w_in = dt("even_w_in", [2, D, EVEN_IN_W], F32, kind="ExternalInput").ap()
    k.even_w_out = dt("even_w_out", [2, D, D], F32, kind="ExternalInput").ap()
    k.dn_w_in = dt("dn_w_in", [2, D, DN_IN_W], F32, kind="ExternalInput").ap()
    k.dn_w_out = dt("dn_w_out", [2, 2 * D, D], F32, kind="ExternalInput").ap()
    k.hT = dt("hT", [D, NT], F32, kind="ExternalOutput").ap()
    k.HF = dt("HF", [DFF, NT], BF16, kind="Internal").ap()
    k.QT = dt("QT", [8, 128, NT], BF16, kind="Internal").ap()
    k.KT = dt("KT", [2, 128, NT], BF16, kind="Internal").ap()
    k.VTM = dt("VTM", [18, 128, 256], BF16, kind="Internal").ap()
    k.CAT = dt("CAT", [16, 128, NT], BF16, kind="Internal").ap()
    k.DQT = dt("DQT", [16, 128, NT], BF16, kind="Internal").ap()
    k.DKT = dt("DKT", [16, 128, NT], BF16, kind="Internal").ap()
    k.DKTM = dt("DKTM", [18, 128, 2048], BF16, kind="Internal").ap()
    k.DVTM = dt("DVTM", [18, 128, 4096], BF16, kind="Internal").ap()
    k.DZ = dt("DZ", [18, 128, 4096], BF16, kind="Internal").ap()
    k.DPH = dt("DPH", [288, 4, 128, 4, 128], BF16, kind="Internal").ap()
    k.DO = dt("DO", [2, 18, 128, 4096], F32, kind="Internal").ap()
    k.DYT = dt("DYT", [32, 128, NT], BF16, kind="Internal").ap()
    k.DBG_G = dt("DBG_G", [5, 128, 18, 64], F32, kind="Internal").ap()

    k.vecs = P.sb([128, NV], F32)
    k.consts = P.sb([128, NCONST], F32)
    k.siluc = P.sb([128, 16, 2], BF16)
    k.mod = P.sb([128, 96, 2], F32)
    k.A1 = P.sb([128, 16, 2], F32)
    k.A2 = P.sb([128, 16, 2], F32)
    k.onesD = P.sb([128, 128], BF16)
    k.ones128 = P.sb([128, 128], BF16)
    k.ones1 = P.sb([128, 128], BF16)
    k.onesf128 = P.sb([128, 128], F32)
    k.bandb = P.sb([128, 384], BF16)
    k.identb = P.sb([128, 128], BF16)
    k.maskb = P.sb([128, 5, 128], BF16)

    P.add("sp", lambda e: e.dma_start(out=k.vecs[:], in_=k.vecs_d[:, :]), writes=["vecs"], dma=True)
    P.add("sp", lambda e: e.dma_start(out=k.consts[:], in_=k.consts_d[:, :]), writes=["consts"], dma=True)
    P.add("pool", lambda e: e.memset(k.onesD[:], 1.0 / D), writes=["onesD"])
    P.add("pool", lambda e: e.memset(k.ones128[:], 1.0 / 128), writes=["ones128"])
    P.add("pool", lambda e: e.memset(k.ones1[:], 1.0), writes=["ones1"])
    P.add("pool", lambda e: e.memset(k.onesf128[:], 1.0 / 128), writes=["onesf128"])
    P.add("dve", lambda e: e.tensor_copy(k.bandb[:], k.consts[:, COFF["band"][0]:COFF["band"][0] + 384]), reads=["consts"], writes=["bandb"])
    P.add("dve", lambda e: e.tensor_copy(k.maskb[:].rearrange("p q j -> p (q j)"), k.consts[:, COFF["mblk"][0]:COFF["mblk"][0] + 640]), reads=["consts"], writes=["maskb"])
    P.add("dve", lambda e: e.tensor_copy(k.identb[:], k.consts[:, COFF["ident"][0]:COFF["ident"][0] + 128]), reads=["consts"], writes=["identb"])
    co = VOFF["c"][0]
    P.add("act", lambda e: e.activation(out=k.siluc[:, :, 0], in_=k.vecs[:, co:co + 16], func=AF.Silu), reads=["vecs"], writes=["siluc"])
    co2 = VOFF["cctx"][0]
    P.add("act", lambda e: e.activation(out=k.siluc[:, :, 1], in_=k.vecs[:, co2:co2 + 16], func=AF.Silu), reads=["vecs", "siluc"], writes=["siluc"])
    with contextlib.ExitStack() as st:
        tmp = [P.sb([128, 4, NT], F32, st) for _ in range(2)]
        src = k.h_in.rearrange("(c p) t -> p c t", p=128)
        dst = k.hT.rearrange("(c p) t -> p c t", p=128)
        for q in range(4):
            b = q % 2
            P.add("sp", lambda e, b=b, q=q: e.dma_start(out=tmp[b][:], in_=src[:, q * 4:(q + 1) * 4, :]), writes=[("tmp", b)], dma=True)
            P.add("sp", lambda e, b=b, q=q: e.dma_start(out=dst[:, q * 4:(q + 1) * 4, :], in_=tmp[b][:]), reads=[("tmp", b)], writes=["hT"], dma=True)
    P.barrier()

    for (l, parts) in plan:
        stage_adaln(k, l)
        if "mix" in parts:
            if l % 2 == 0:
                stage_even(k, l)
            else:
                stage_dn(k, l)
        if "ffn" in parts:
            stage_ffn(k, l)
    P.finish()
    return nc


def stage_even(k, l):
    P = k.P
    with contextlib.ExitStack() as st0:
        nx = P.sb([128, 16, NT], BF16, st0)
        stage_norm(k, 1, nx)
        P.barrier()
        even_qkv(k, l, nx)
        P.barrier()
        even_glu(k, l, nx)
        P.barrier()
    even_attn(k, l)
    P.barrier()
    even_out(k, l)
    P.barrier()


def rsqrt_ops(P, out_ap, in_ap, reads, wkey):
    P.add("act", lambda e: e.activation(out=out_ap, in_=in_ap, func=AF.Sqrt, bias=EPS, scale=1.0), reads=reads, writes=[wkey])
    P.add("dve", lambda e: e.reciprocal(out_ap, out_ap), reads=[wkey], writes=[wkey])


def even_qkv(k, l, nx):
    P = k.P
    i = l // 2
    wsrc = k.even_w_in[i].rearrange("(kc p) n -> p kc n", p=128)
    cos0 = COFF["cos"][0]; sin0 = COFF["sin"][0]; rp0 = COFF["ropeT"][0]
    with contextlib.ExitStack() as st:
        wb = [P.sb([128, 16, 256], BF16, st) for _ in range(2)]
        psa = [P.ps([128, 512], F32, st) for _ in range(2)]
        psb = P.ps([128, 512], F32, st)
        psc = P.ps([128, 512], F32, st)
        x0 = [P.sb([128, 512], F32, st) for _ in range(2)]
        sqb = P.sb([128, 512], BF16, st)
        rr = P.sb([128, 512], F32, st)
        qn = P.sb([128, 512], F32, st)
        t1 = P.sb([128, 512], F32, st)
        t2 = P.sb([128, 512], F32, st)
        qo = [P.sb([128, NT], BF16, st) for _ in range(2)]
        vt = P.sb([128, 18, 256], BF16, st)
        gs = P.sb([128, 2], F32, st)
        qg0 = VOFF["qg"][0] + i; kg0 = VOFF["kg"][0] + i
        P.add("act", lambda e: e.mul(gs[:, 0:1], k.vecs[:, qg0:qg0 + 1], 128 ** -0.5), writes=["gs"])
        P.add("act", lambda e: e.copy(gs[:, 1:2], k.vecs[:, kg0:kg0 + 1]), reads=["gs"], writes=["gs"])
        nblk = 0
        nchunk = 0
        npa = 0
        for blk in range(5):
            b = nblk % 2; nblk += 1
            P.add("pool", lambda e, b=b, blk=blk: e.dma_start(out=wb[b][:], in_=wsrc[:, :, blk * 256:(blk + 1) * 256]),
                  writes=[("wb", b)], dma=True)
            for s in range(2):
                ch = blk * 2 + s
                isq = ch < 8
                qb = nchunk % 2; nchunk += 1
                gcol = gs[:, 0:1] if isq else gs[:, 1:2]
                for ti, (t0, n) in enumerate(TILES):
                    pb = npa % 2; npa += 1
                    for kc in range(16):
                        P.add("pe", lambda e, b=b, s=s, pb=pb, kc=kc, t0=t0, n=n: e.matmul(
                            psa[pb][:, :n], wb[b][:, kc, s * 128:(s + 1) * 128], nx[:, kc, t0:t0 + n], start=(kc == 0), stop=(kc == 15)),
                            reads=[("wb", b)], writes=[("psa", pb)])
                    P.add("act", lambda e, pb=pb, n=n: e.activation(out=x0[pb][:, :n], in_=psa[pb][:, :n], func=AF.Copy),
                          reads=[("psa", pb)], writes=[("x0", pb)])
                    P.add("act", lambda e, pb=pb, n=n: e.activation(out=sqb[:, :n], in_=psa[pb][:, :n], func=AF.Square),
                          reads=[("psa", pb)], writes=["sqb"])
                    P.add("pe", lambda e, n=n: e.matmul(psb[:, :n], k.ones128[:], sqb[:, :n], start=True, stop=True),
                          reads=["sqb"], writes=["psb"])
                    rsqrt_ops(P, rr[:, :n], psb[:, :n], ["psb"], "rr")
                    if t0 < LCTX:
                        P.add("dve", lambda e, pb=pb, qb=qb, n=n, t0=t0, gcol=gcol: e.scalar_tensor_tensor(
                            out=qo[qb][:, t0:t0 + n], in0=x0[pb][:, :n], scalar=gcol, in1=rr[:, :n], op0=ALU.mult, op1=ALU.mult),
                            reads=[("x0", pb), "rr", "gs"], writes=[("qo", qb, ti)])
                    else:
                        P.add("dve", lambda e, pb=pb, n=n, gcol=gcol: e.scalar_tensor_tensor(
                            out=qn[:, :n], in0=x0[pb][:, :n], scalar=gcol, in1=rr[:, :n], op0=ALU.mult, op1=ALU.mult),
                            reads=[("x0", pb), "rr", "gs"], writes=["qn"])
                        P.add("pe", lambda e, n=n: e.matmul(psc[:, :n], k.consts[:, rp0:rp0 + 128], qn[:, :n], start=True, stop=True),
                              reads=["qn"], writes=["psc"])
                        c0 = cos0 + t0 - LCTX; s0 = sin0 + t0 - LCTX
                        P.add("dve", lambda e, n=n, c0=c0: e.tensor_tensor(out=t1[:, :n], in0=qn[:, :n], in1=k.consts[:, c0:c0 + n], op=ALU.mult),
                              reads=["qn"], writes=["t1"])
                        P.add("dve", lambda e, n=n, s0=s0: e.tensor_tensor(out=t2[:, :n], in0=psc[:, :n], in1=k.consts[:, s0:s0 + n], op=ALU.mult),
                              reads=["psc"], writes=["t2"])
                        P.add("pool", lambda e, qb=qb, n=n, t0=t0: e.tensor_tensor(out=qo[qb][:, t0:t0 + n], in0=t1[:, :n], in1=t2[:, :n], op=ALU.add),
                              reads=["t1", "t2"], writes=[("qo", qb, ti)])
                dst = k.QT[ch] if isq else k.KT[ch - 8]
                P.add("sp", lambda e, qb=qb, dst=dst: e.dma_start(out=dst, in_=qo[qb][:]),
                      reads=[("qo", qb, ti) for ti in range(5)], writes=[("qkT", ch)], dma=True)
        b = nblk % 2; nblk += 1
        P.add("pool", lambda e, b=b: e.dma_start(out=wb[b][:], in_=wsrc[:, :, 1280:1536]), writes=[("wb", b)], dma=True)
        for tt in range(18):
            pb = npa % 2; npa += 1
            for kc in range(16):
                P.add("pe", lambda e, b=b, pb=pb, kc=kc, tt=tt: e.matmul(
                    psa[pb][:, :256], nx[:, kc, tt * 128:(tt + 1) * 128], wb[b][:, kc, :], start=(kc == 0), stop=(kc == 15)),
                    reads=[("wb", b)], writes=[("psa", pb)])
            P.add("act", lambda e, pb=pb, tt=tt: e.activation(out=vt[:, tt, :], in_=psa[pb][:, :256], func=AF.Copy),
                  reads=[("psa", pb)], writes=[("vt", tt)])
        P.add("sp", lambda e: e.dma_start(out=k.VTM.rearrange("t p c -> p t c"), in_=vt[:]),
              reads=[("vt", tt) for tt in range(18)], writes=["VTM"], dma=True)


def even_glu(k, l, nx):
    P = k.P
    i = l // 2
    wsrc = k.even_w_in[i].rearrange("(kc p) n -> p kc n", p=128)
    cdw = VOFF["cdw"][0] + i * 31 * 8
    cdb = VOFF["cdb"][0] + i * 8
    lng = VOFF["lng"][0] + i * 8
    lnb = VOFF["lnb"][0] + i * 8
    NDVE = 31
    with contextlib.ExitStack() as st:
        wb = [P.sb([128, 16, 256], BF16, st) for _ in range(2)]
        psa = [P.ps([128, 512], F32, st) for _ in range(2)]
        psb = [P.ps([128, 512], F32, st) for _ in range(2)]
        psm = P.ps([128, 512], F32, st)
        psq = P.ps([128, 512], F32, st)
        sgm = [P.sb([128, 512], F32, st) for _ in range(2)]
        U = [P.sb([128, NT], F32, st) for _ in range(2)]
        accA = P.sb([128, NT], F32, st)
        accB = P.sb([128, NT], F32, st)
        hsq = P.sb([128, NT], F32, st)
        co = [P.sb([128, NT], BF16, st) for _ in range(2)]
        msb = P.sb([128, 512], F32, st)
        m2 = P.sb([128, 512], F32, st)
        var = P.sb([128, 512], F32, st)
        dd = P.sb([128, 512], F32, st)
        npa = 0
        for j in range(8):
            b = j % 2
            P.add("pool", lambda e, b=b, j=j: e.dma_start(out=wb[b][:, :, 0:128], in_=wsrc[:, :, 1536 + j * 128:1536 + (j + 1) * 128]),
                  writes=[("wb", b, 0)], dma=True)
            P.add("pool", lambda e, b=b, j=j: e.dma_start(out=wb[b][:, :, 128:256], in_=wsrc[:, :, 2560 + j * 128:2560 + (j + 1) * 128]),
                  writes=[("wb", b, 1)], dma=True)
            for ti, (t0, n) in enumerate(TILES):
                pb = npa % 2; npa += 1
                for kc in range(16):
                    P.add("pe", lambda e, b=b, pb=pb, kc=kc, t0=t0, n=n: e.matmul(
                        psa[pb][:, :n], wb[b][:, kc, 0:128], nx[:, kc, t0:t0 + n], start=(kc == 0), stop=(kc == 15)),
                        reads=[("wb", b, 0)], writes=[("psa", pb)])
                for kc in range(16):
                    P.add("pe", lambda e, b=b, pb=pb, kc=kc, t0=t0, n=n: e.matmul(
                        psb[pb][:, :n], wb[b][:, kc, 128:256], nx[:, kc, t0:t0 + n], start=(kc == 0), stop=(kc == 15)),
                        reads=[("wb", b, 1)], writes=[("psb", pb)])
                P.add("act", lambda e, pb=pb, n=n: e.activation(out=sgm[pb][:, :n], in_=psb[pb][:, :n], func=AF.Sigmoid),
                      reads=[("psb", pb)], writes=[("sgm", pb)])
                P.add("dve", lambda e, b=b, pb=pb, t0=t0, n=n: e.tensor_tensor(out=U[b][:, t0:t0 + n], in0=psa[pb][:, :n], in1=sgm[pb][:, :n], op=ALU.mult),
                      reads=[("psa", pb), ("sgm", pb)], writes=[("U", b, ti)])
            Uk = [("U", b, ti) for ti in range(5)]
            wc = lambda kk, j=j: k.vecs[:, cdw + kk * 8 + j:cdw + kk * 8 + j + 1]
            bias = k.vecs[:, cdb + j:cdb + j + 1]
            P.add("dve", lambda e, b=b, w=wc(15), bias=bias: e.tensor_scalar(out=accA[:], in0=U[b][:], scalar1=w, scalar2=bias,
                                                                            op0=ALU.mult, op1=ALU.add), reads=Uk, writes=["accA"])
            P.add("pool", lambda e: e.memset(accB[:], 0.0), writes=["accB"])
            taps = [kk for kk in range(31) if kk != 15]
            dve_taps = taps[:NDVE - 1]
            for kk in taps:
                sh = kk - 15
                eng, acc, akey = ("dve", accA, "accA") if kk in dve_taps else ("pool", accB, "accB")
                for (s0, sn) in ((0, LCTX), (LCTX, TLAT)):
                    lo = max(0, -sh); hi = sn - max(0, sh)
                    P.add(eng, lambda e, b=b, acc=acc, s0=s0, lo=lo, hi=hi, sh=sh, w=wc(kk): e.scalar_tensor_tensor(
                        out=acc[:, s0 + lo:s0 + hi], in0=U[b][:, s0 + lo + sh:s0 + hi + sh], scalar=w, in1=acc[:, s0 + lo:s0 + hi],
                        op0=ALU.mult, op1=ALU.add), reads=Uk + [akey], writes=[akey])
            P.add("dve", lambda e: e.tensor_tensor(out=accA[:], in0=accA[:], in1=accB[:], op=ALU.add), reads=["accA", "accB"], writes=["accA"])
            P.add("act", lambda e: e.activation(out=hsq[:], in_=accA[:], func=AF.Square), reads=["accA"], writes=["hsq"])
            for ti, (t0, n) in enumerate(TILES):
                P.add("pe", lambda e, t0=t0, n=n: e.matmul(psm[:, :n], k.onesf128[:], accA[:, t0:t0 + n], start=True, stop=True),
                      reads=["accA"], writes=["psm"])
                P.add("pe", lambda e, t0=t0, n=n: e.matmul(psq[:, :n], k.onesf128[:], hsq[:, t0:t0 + n], start=True, stop=True),
                      reads=["hsq"], writes=["psq"])
                P.add("act", lambda e, n=n: e.activation(out=msb[:, :n], in_=psm[:, :n], func=AF.Copy), reads=["psm"], writes=["msb"])
                P.add("act", lambda e, n=n: e.activation(out=m2[:, :n], in_=psm[:, :n], func=AF.Square), reads=["psm"], writes=["m2"])
                P.add("dve", lambda e, n=n: e.tensor_tensor(out=var[:, :n], in0=psq[:, :n], in1=m2[:, :n], op=ALU.subtract),
                      reads=["psq", "m2"], writes=["var"])
                P.add("dve", lambda e, n=n: e.tensor_scalar_max(out=var[:, :n], in0=var[:, :n], scalar1=0.0), reads=["var"], writes=["var"])
                rsqrt_ops(P, var[:, :n], var[:, :n], ["var"], "var")
                P.add("dve", lambda e, t0=t0, n=n: e.tensor_tensor(out=dd[:, :n], in0=accA[:, t0:t0 + n], in1=msb[:, :n], op=ALU.subtract),
                      reads=["accA", "msb"], writes=["dd"])
                P.add("dve", lambda e, n=n: e.tensor_tensor(out=dd[:, :n], in0=dd[:, :n], in1=var[:, :n], op=ALU.mult),
                      reads=["dd", "var"], writes=["dd"])
                P.add("act", lambda e, b=b, j=j, t0=t0, n=n: e.activation(
                    out=co[b][:, t0:t0 + n], in_=dd[:, :n], func=AF.Silu,
                    bias=k.vecs[:, lnb + j:lnb + j + 1], scale=k.vecs[:, lng + j:lng + j + 1]),
                    reads=["dd"], writes=[("co", b, ti)])
            P.add("sp", lambda e, b=b, j=j: e.dma_start(out=k.CAT[8 + j], in_=co[b][:]),
                  reads=[("co", b, ti) for ti in range(5)], writes=[("CAT", 8 + j)], dma=True)


def even_attn(k, l):
    P = k.P
    i = l // 2
    band0 = COFF["band"][0]
    with contextlib.ExitStack() as st:
        kT = P.sb([128, 2, NT], BF16, st)
        V = P.sb([128, 18, 256], BF16, st)
        qh = [P.sb([128, NT], BF16, st) for _ in range(2)]
        PC = [[P.sb([128, NT], BF16, st) for _ in range(2)] for _ in range(2)]
        PB = [P.sb([128, 16, 384], BF16, st) for _ in range(2)]
        tmpE = [P.sb([128, 384], BF16, st) for _ in range(2)]
        ao = [P.sb([128, NT], BF16, st) for _ in range(2)]
        den = P.sb([128, 512], F32, st)
        esink = P.sb([128, 8], F32, st)
        pss = [P.ps([128, 512], F32, st) for _ in range(2)]
        pso = [P.ps([128, 512], F32, st) for _ in range(2)]
        psd = [P.ps([128, 512], F32, st) for _ in range(2)]
        so = VOFF["sink"][0] + i * 8
        P.add("act", lambda e: e.activation(out=esink[:], in_=k.vecs[:, so:so + 8], func=AF.Exp), writes=["esink"])
        for g in range(2):
            P.add("sp", lambda e, g=g: e.dma_start(out=kT[:, g, :], in_=k.KT[g]), writes=[("kT", g)], dma=True)
        P.add("sp", lambda e: e.dma_start(out=V[:], in_=k.VTM.rearrange("t p c -> p t c")), writes=["V"], dma=True)
        nps = 0
        npo = 0
        for h in range(8):
            g = h // 4
            hb = h % 2
            P.add("sp", lambda e, hb=hb, h=h: e.dma_start(out=qh[hb][:], in_=k.QT[h]), writes=[("qh", hb)], dma=True)
            for cb in range(2):
                for ti, (t0, n) in enumerate(TILES):
                    pb = nps % 2; nps += 1
                    P.add("pe", lambda e, pb=pb, g=g, cb=cb, hb=hb, t0=t0, n=n: e.matmul(
                        pss[pb][:, :n], kT[:, g, cb * 128:(cb + 1) * 128], qh[hb][:, t0:t0 + n], start=True, stop=True),
                        reads=[("kT", g), ("qh", hb)], writes=[("pss", pb)])
                    P.add("act", lambda e, pb=pb, hb=hb, cb=cb, t0=t0, n=n: e.activation(
                        out=PC[hb][cb][:, t0:t0 + n], in_=pss[pb][:, :n], func=AF.Exp),
                        reads=[("pss", pb)], writes=[("PC", hb, cb, ti)])
            for jb in range(16):
                lo = max(jb - 1, 0); hi = min(jb + 1, 15)
                n = (hi - lo + 1) * 128
                off = (lo - (jb - 1)) * 128
                pb = nps % 2; nps += 1
                eb = jb % 2
                P.add("pe", lambda e, pb=pb, g=g, jb=jb, hb=hb, lo=lo, n=n: e.matmul(
                    pss[pb][:, :n], kT[:, g, LCTX + jb * 128:LCTX + (jb + 1) * 128], qh[hb][:, LCTX + lo * 128:LCTX + lo * 128 + n],
                    start=True, stop=True), reads=[("kT", g), ("qh", hb)], writes=[("pss", pb)])
                P.add("act", lambda e, pb=pb, eb=eb, n=n: e.activation(out=tmpE[eb][:, :n], in_=pss[pb][:, :n], func=AF.Exp),
                      reads=[("pss", pb)], writes=[("tmpE", eb)])
                P.add("pool", lambda e, eb=eb, hb=hb, jb=jb, off=off, n=n: e.tensor_tensor(
                    out=PB[hb][:, jb, off:off + n], in0=tmpE[eb][:, :n], in1=k.bandb[:, off:off + n], op=ALU.mult),
                    reads=[("tmpE", eb)], writes=[("PB", hb, jb)])
            for ti, (t0, n) in enumerate(TILES):
                ob = npo % 2; npo += 1
                mms = []
                for cb in range(2):
                    mms.append((V[:, cb, g * 128:(g + 1) * 128], PC[hb][cb][:, t0:t0 + n], 0, n, [("PC", hb, cb, ti)]))
                if t0 >= LCTX:
                    qt = (t0 - LCTX) // 512
                    for nb in range(4 * qt, 4 * qt + 4):
                        for jb in (nb - 1, nb, nb + 1):
                            if 0 <= jb <= 15:
                                c0 = (nb - (jb - 1)) * 128
                                mms.append((V[:, 2 + jb, g * 128:(g + 1) * 128], PB[hb][:, jb, c0:c0 + 128], (nb - 4 * qt) * 128, 128,
                                            [("PB", hb, jb)]))
                for mi, (lhs, rhs, o0, on, rk) in enumerate(mms):
                    P.add("pe", lambda e, ob=ob, lhs=lhs, rhs=rhs, o0=o0, on=on, mi=mi, last=(mi == len(mms) - 1): e.matmul(
                        pso[ob][:, o0:o0 + on], lhs, rhs, start=(mi == 0), stop=last, skip_group_check=True),
                        reads=rk + ["V"], writes=[("pso", ob)])
                for mi, (lhs, rhs, o0, on, rk) in enumerate(mms):
                    P.add("pe", lambda e, ob=ob, rhs=rhs, o0=o0, on=on, mi=mi, last=(mi == len(mms) - 1): e.matmul(
                        psd[ob][:, o0:o0 + on], k.ones1[:], rhs, start=(mi == 0), stop=last, skip_group_check=True),
                        reads=rk, writes=[("psd", ob)])
                P.add("dve", lambda e, ob=ob, n=n, h=h: e.tensor_scalar_add(out=den[:, :n], in0=psd[ob][:, :n], scalar1=esink[:, h:h + 1]),
                      reads=[("psd", ob), "esink"], writes=["den"])
                P.add("dve", lambda e, n=n: e.reciprocal(den[:, :n], den[:, :n]), reads=["den"], writes=["den"])
                P.add("dve", lambda e, ob=ob, hb=hb, t0=t0, n=n: e.tensor_tensor(out=ao[hb][:, t0:t0 + n], in0=pso[ob][:, :n], in1=den[:, :n], op=ALU.mult),
                      reads=[("pso", ob), "den"], writes=[("ao", hb, ti)])
            P.add("sp", lambda e, hb=hb, h=h: e.dma_start(out=k.CAT[h], in_=ao[hb][:]),
                  reads=[("ao", hb, ti) for ti in range(5)], writes=[("CAT", h)], dma=True)


def proj_out_residual(k, wsrc, nkc, cat, gate_m, wkeys_extra=()):
    P = k.P
    hsrc = k.hT.rearrange("(c p) t -> p c t", p=128)
    with contextlib.ExitStack() as st:
        wb = [P.sb([128, nkc, 256], BF16, st) for _ in range(2)]
        hrow = [P.sb([128, NT], F32, st) for _ in range(2)]
        pso = [P.ps([128, 512], F32, st) for _ in range(4)]
        nps = 0
        for blk in range(8):
            b = blk % 2
            P.add("pool", lambda e, b=b, blk=blk: e.dma_start(out=wb[b][:], in_=wsrc[:, :, blk * 256:(blk + 1) * 256]),
                  writes=[("wb", b)], dma=True)
            for s in range(2):
                c = blk * 2 + s
                hb = c % 2
                P.add("sp", lambda e, hb=hb, c=c: e.dma_start(out=hrow[hb][:], in_=hsrc[:, c, :]),
                      reads=[("hT", c)], writes=[("hrow", hb)], dma=True)
                for ti, (t0, n) in enumerate(TILES):
                    pb = nps % 4; nps += 1
                    for kc in range(nkc):
                        P.add("pe", lambda e, b=b, s=s, pb=pb, kc=kc, t0=t0, n=n: e.matmul(
                            pso[pb][:, :n], wb[b][:, kc, s * 128:(s + 1) * 128], cat[:, kc, t0:t0 + n],
                            start=(kc == 0), stop=(kc == nkc - 1)),
                            reads=[("wb", b), ("cat", kc)], writes=[("pso", pb)])
                    sidx = 1 if t0 < LCTX else 0
                    P.add("dve", lambda e, hb=hb, pb=pb, t0=t0, n=n, c=c, sidx=sidx: e.scalar_tensor_tensor(
                        out=hrow[hb][:, t0:t0 + n], in0=pso[pb][:, :n], scalar=modv(k, gate_m, c, sidx),
                        in1=hrow[hb][:, t0:t0 + n], op0=ALU.mult, op1=ALU.add),
                        reads=[("pso", pb), ("hrow", hb), "mod"], writes=[("hrow", hb)])
                P.add("sp", lambda e, hb=hb, c=c: e.dma_start(out=hsrc[:, c, :], in_=hrow[hb][:]),
                      reads=[("hrow", hb)], writes=[("hT", c)], dma=True)


def even_out(k, l):
    P = k.P
    i = l // 2
    wsrc = k.even_w_out[i].rearrange("(kc p) n -> p kc n", p=128)
    with contextlib.ExitStack() as st:
        cat = P.sb([128, 16, NT], BF16, st)
        for c in range(16):
            P.add("sp", lambda e, c=c: e.dma_start(out=cat[:, c, :], in_=k.CAT[c]), writes=[("cat", c)], dma=True)
        proj_out_residual(k, wsrc, 16, cat, 2)


DN_STOP = 99


def stage_dn(k, l):
    P = k.P
    with contextlib.ExitStack() as st1:
        k.BETA = P.sb([128, 18, 64], F32, st1)
        k.GG = P.sb([128, 18, 64], F32, st1)
        k.EG = P.sb([128, 18, 64], F32, st1)
        k.EL = P.sb([128, 18, 64], F32, st1)
        k.BEG = P.sb([128, 18, 64], F32, st1)
        k.ER = P.sb([128, 18, 64], F32, st1)
        with contextlib.ExitStack() as st0:
            nx = P.sb([128, 16, NT], BF16, st0)
            stage_norm(k, 1, nx, nbuf=1)
            P.barrier()
            if DN_STOP >= 1:
                dn_proj_qkv(k, l, nx)
                P.barrier()
            if DN_STOP >= 2:
                dn_proj_z(k, l, nx)
                P.barrier()
            if DN_STOP >= 3:
                dn_proj_ba(k, l, nx)
                P.barrier()
        if DN_STOP >= 4:
            dn_gates(k, l)
            P.barrier()
            if "DBG_G" in DEBUG_OUT:
                for ii, t in enumerate((k.BETA, k.GG, k.EG, k.EL, k.ER)):
                    P.add("sp", lambda e, ii=ii, t=t: e.dma_start(out=k.DBG_G[ii], in_=t[:]), dma=True)
                P.barrier()
        if DN_STOP >= 5:
            dn_prep(k, l)
            P.barrier()
        if DN_STOP >= 6:
            dn_scan(k, l)
            P.barrier()
    if DN_STOP >= 7:
        dn_outnorm(k, l)
        P.barrier()
    if DN_STOP >= 8:
        dn_outproj(k, l)
        P.barrier()


def dn_proj_qkv(k, l, nx):
    P = k.P
    i = l // 2
    wsrc = k.dn_w_in[i].rearrange("(kc p) n -> p kc n", p=128)
    dcw = VOFF["dcw"][0] + i * 5 * 64
    with contextlib.ExitStack() as st:
        wb = [P.sb([128, 16, 256], BF16, st) for _ in range(2)]
        psa = [P.ps([128, 512], F32, st) for _ in range(2)]
        psb = P.ps([128, 512], F32, st)
        pst = [P.ps([128, 4, 128], BF16, st) for _ in range(2)]
        U = [P.sb([128, NT], F32, st) for _ in range(2)]
        acc = P.sb([128, NT], F32, st)
        sq = P.sb([128, NT], BF16, st)
        rr = P.sb([128, 512], F32, st)
        qo = [P.sb([128, NT], BF16, st) for _ in range(2)]
        tm = [P.sb([128, 18, 128], BF16, st) for _ in range(2)]
        npa = 0
        npt = 0
        for blk in range(32):
            b = blk % 2
            P.add("pool", lambda e, b=b, blk=blk: e.dma_start(out=wb[b][:], in_=wsrc[:, :, blk * 256:(blk + 1) * 256]),
                  writes=[("wb", b)], dma=True)
            for s in range(2):
                ch = blk * 2 + s
                ub = ch % 2
                kind = "q" if ch < 16 else ("k" if ch < 32 else "v")
                for ti, (t0, n) in enumerate(TILES):
                    pb = npa % 2; npa += 1
                    for kc in range(16):
                        P.add("pe", lambda e, b=b, s=s, pb=pb, kc=kc, t0=t0, n=n: e.matmul(
                            psa[pb][:, :n], wb[b][:, kc, s * 128:(s + 1) * 128], nx[:, kc, t0:t0 + n], start=(kc == 0), stop=(kc == 15)),
                            reads=[("wb", b)], writes=[("psa", pb)])
                    P.add("act", lambda e, ub=ub, pb=pb, t0=t0, n=n: e.activation(out=U[ub][:, t0:t0 + n], in_=psa[pb][:, :n], func=AF.Copy),
                          reads=[("psa", pb)], writes=[("U", ub, ti)])
                Uk = [("U", ub, ti) for ti in range(5)]
                wc = lambda kk, ch=ch: k.vecs[:, dcw + kk * 64 + ch:dcw + kk * 64 + ch + 1]
                P.add("dve", lambda e, ub=ub, w=wc(2): e.tensor_scalar(out=acc[:], in0=U[ub][:], scalar1=w, scalar2=None, op0=ALU.mult),
                      reads=Uk, writes=["acc"])
                for kk in (0, 1, 3, 4):
                    sh = kk - 2
                    for (s0, sn) in ((0, LCTX), (LCTX, TLAT)):
                        lo = max(0, -sh); hi = sn - max(0, sh)
                        P.add("dve", lambda e, ub=ub, s0=s0, lo=lo, hi=hi, sh=sh, w=wc(kk): e.scalar_tensor_tensor(
                            out=acc[:, s0 + lo:s0 + hi], in0=U[ub][:, s0 + lo + sh:s0 + hi + sh], scalar=w, in1=acc[:, s0 + lo:s0 + hi],
                            op0=ALU.mult, op1=ALU.add), reads=Uk + ["acc"], writes=["acc"])
                qb = ch % 2
                if kind == "v":
                    P.add("act", lambda e, qb=qb: e.activation(out=qo[qb][:], in_=acc[:], func=AF.Silu), reads=["acc"], writes=[("qo", qb)])
                else:
                    P.add("act", lambda e: e.activation(out=acc[:], in_=acc[:], func=AF.Silu), reads=["acc"], writes=["acc"])
                    P.add("act", lambda e: e.activation(out=sq[:], in_=acc[:], func=AF.Square), reads=["acc"], writes=["sq"])
                    for ti, (t0, n) in enumerate(TILES):
                        P.add("pe", lambda e, t0=t0, n=n: e.matmul(psb[:, :n], k.ones1[:], sq[:, t0:t0 + n], start=True, stop=True),
                              reads=["sq"], writes=["psb"])
                        rsqrt_ops(P, rr[:, :n], psb[:, :n], ["psb"], "rr")
                        sc = (128 ** -0.5) if kind == "q" else 1.0
                        P.add("dve", lambda e, qb=qb, t0=t0, n=n, sc=sc: e.scalar_tensor_tensor(
                            out=qo[qb][:, t0:t0 + n], in0=acc[:, t0:t0 + n], scalar=sc, in1=rr[:, :n], op0=ALU.mult, op1=ALU.mult),
                            reads=["acc", "rr"], writes=[("qo", qb)])
                if kind == "q":
                    P.add("sp", lambda e, qb=qb, ch=ch: e.dma_start(out=k.DQT[ch], in_=qo[qb][:]), reads=[("qo", qb)], writes=[("DQT", ch)], dma=True)
                else:
                    if kind == "k":
                        P.add("sp", lambda e, qb=qb, ch=ch: e.dma_start(out=k.DKT[ch - 16], in_=qo[qb][:]), reads=[("qo", qb)],
                              writes=[("DKT", ch)], dma=True)
                    for c4 in range(0, 18, 4):
                        nn = min(4, 18 - c4)
                        tb = npt % 2; npt += 1
                        for cc in range(nn):
                            c = c4 + cc
                            P.add("pe", lambda e, tb=tb, cc=cc, c=c, qb=qb: e.transpose(
                                out=pst[tb][:, cc, :], in_=qo[qb][:, c * 128:(c + 1) * 128], identity=k.identb[:]),
                                reads=[("qo", qb)], writes=[("pst", tb)])
                        P.add("act", lambda e, tb=tb, qb=qb, c4=c4, nn=nn: e.activation(out=tm[qb][:, c4:c4 + nn, :], in_=pst[tb][:, :nn, :], func=AF.Copy),
                              reads=[("pst", tb)], writes=[("tm", qb)])
                    if kind == "k":
                        dst = k.DKTM[:, :, (ch - 16) * 128:(ch - 15) * 128]
                    else:
                        dst = k.DVTM[:, :, (ch - 32) * 128:(ch - 31) * 128]
                    P.add("sp", lambda e, qb=qb, dst=dst: e.dma_start(out=dst.rearrange("c p d -> p c d"), in_=tm[qb][:]),
                          reads=[("tm", qb)], writes=[("DTM", ch)], dma=True)


def dn_proj_z(k, l, nx):
    P = k.P
    i = l // 2
    wsrc = k.dn_w_in[i].rearrange("(kc p) n -> p kc n", p=128)
    with contextlib.ExitStack() as st:
        wb = [P.sb([128, 16, 512], BF16, st) for _ in range(2)]
        psa = [P.ps([128, 512], F32, st) for _ in range(4)]
        zs = [P.sb([128, 512], BF16, st) for _ in range(4)]
        npa = 0
        for blk in range(8):
            b = blk % 2
            for q in range(2):
                P.add("pool", lambda e, b=b, blk=blk, q=q: e.dma_start(
                    out=wb[b][:, :, q * 256:(q + 1) * 256], in_=wsrc[:, :, 8192 + blk * 512 + q * 256:8192 + blk * 512 + (q + 1) * 256]),
                    writes=[("wb", b, q)], dma=True)
            for c in range(18):
                pb = npa % 4; npa += 1
                for kc in range(16):
                    P.add("pe", lambda e, b=b, pb=pb, kc=kc, c=c: e.matmul(
                        psa[pb][:], nx[:, kc, c * 128:(c + 1) * 128], wb[b][:, kc, :], start=(kc == 0), stop=(kc == 15)),
                        reads=[("wb", b, 0), ("wb", b, 1)], writes=[("psa", pb)])
                P.add("act", lambda e, pb=pb: e.activation(out=zs[pb][:], in_=psa[pb][:], func=AF.Silu),
                      reads=[("psa", pb)], writes=[("zs", pb)])
                P.add("sp", lambda e, pb=pb, c=c, blk=blk: e.dma_start(out=k.DZ[c, :, blk * 512:(blk + 1) * 512], in_=zs[pb][:]),
                      reads=[("zs", pb)], writes=[("DZ", c, blk)], dma=True)


def dn_proj_ba(k, l, nx):
    P = k.P
    i = l // 2
    wsrc = k.dn_w_in[i].rearrange("(kc p) n -> p kc n", p=128)
    with contextlib.ExitStack() as st:
        wb = P.sb([128, 16, 128], BF16, st)
        BA = P.sb([128, 18, 128], F32, st)
        psa = [P.ps([128, 512], F32, st) for _ in range(2)]
        P.add("pool", lambda e: e.dma_start(out=wb[:], in_=wsrc[:, :, 12288:12416]), writes=["wb"], dma=True)
        for c in range(18):
            pb = c % 2
            for kc in range(16):
                P.add("pe", lambda e, pb=pb, kc=kc, c=c: e.matmul(
                    psa[pb][:, 0:128], nx[:, kc, c * 128:(c + 1) * 128], wb[:, kc, :], start=(kc == 0), stop=(kc == 15)),
                    reads=["wb"], writes=[("psa", pb)])
            P.add("act", lambda e, pb=pb, c=c: e.activation(out=BA[:, c, :], in_=psa[pb][:, 0:128], func=AF.Copy),
                  reads=[("psa", pb)], writes=[("BA", c)])
        BAk = [("BA", c) for c in range(18)]
        P.add("act", lambda e: e.activation(out=k.BETA[:], in_=BA[:, :, 0:64], func=AF.Sigmoid), reads=BAk, writes=["BETA"])
        P.add("dve", lambda e: e.tensor_copy(k.GG[:], BA[:, :, 64:128]), reads=BAk, writes=["GG"])
        P.barrier()


def dn_gates(k, l):
    P = k.P
    i = l // 2
    al = VOFF["alog"][0] + i * 64
    db = VOFF["dtb"][0] + i * 64
    with contextlib.ExitStack() as st:
        x = P.sb([128, 18, 64], F32, st)
        ax = P.sb([128, 18, 64], F32, st)
        nA = P.sb([128, 64], F32, st)
        onesf = P.sb([128, 128], F32, st)
        ps = [P.ps([128, 3, 64], F32, st) for _ in range(2)]
        P.add("pool", lambda e: e.memset(onesf[:], 1.0), writes=["onesf"])
        P.add("dve", lambda e: e.tensor_tensor(out=x[:], in0=k.GG[:], in1=bc(k.vecs[:, db:db + 64].unsqueeze(1), [128, 18, 64]), op=ALU.add),
              writes=["x"])
        P.add("act", lambda e: e.activation(out=ax[:], in_=x[:], func=AF.Abs), reads=["x"], writes=["ax"])
        P.add("act", lambda e: e.activation(out=ax[:], in_=ax[:], func=AF.Exp, scale=-1.0), reads=["ax"], writes=["ax"])
        P.add("act", lambda e: e.activation(out=ax[:], in_=ax[:], func=AF.Ln, bias=1.0), reads=["ax"], writes=["ax"])
        P.add("dve", lambda e: e.tensor_scalar_max(out=x[:], in0=x[:], scalar1=0.0), reads=["x"], writes=["x"])
        P.add("dve", lambda e: e.tensor_tensor(out=x[:], in0=x[:], in1=ax[:], op=ALU.add), reads=["x", "ax"], writes=["x"])
        P.add("act", lambda e: e.activation(out=nA[:], in_=k.vecs[:, al:al + 64], func=AF.Exp), writes=["nA"])
        P.add("dve", lambda e: e.scalar_tensor_tensor(out=k.GG[:], in0=x[:], scalar=-1.0, in1=bc(nA[:].unsqueeze(1), [128, 18, 64]),
                                                      op0=ALU.mult, op1=ALU.mult), reads=["x", "nA"], writes=["GG"])
        mo = {n: COFF[n][0] for n in ("m_le", "m_ge", "m_lt", "m_gt")}
        for c in range(18):
            pb = c % 2
            for d in range(2):
                m_gc = mo["m_le"] if d == 0 else mo["m_ge"]
                m_rm = mo["m_gt"] if d == 0 else mo["m_lt"]
                rhs = k.GG[:, c, d * 32:(d + 1) * 32]
                P.add("pe", lambda e, pb=pb, d=d, m=m_gc, rhs=rhs: e.matmul(ps[pb][:, 0, d * 32:(d + 1) * 32], k.consts[:, m:m + 128], rhs, start=True, stop=True),
                      reads=["GG"], writes=[("ps", pb)])
                P.add("pe", lambda e, pb=pb, d=d, m=m_rm, rhs=rhs: e.matmul(ps[pb][:, 1, d * 32:(d + 1) * 32], k.consts[:, m:m + 128], rhs, start=True, stop=True),
                      reads=["GG"], writes=[("ps", pb)])
                P.add("pe", lambda e, pb=pb, d=d, rhs=rhs: e.matmul(ps[pb][:, 2, d * 32:(d + 1) * 32], onesf[:], rhs, start=True, stop=True),
                      reads=["GG", "onesf"], writes=[("ps", pb)])
            P.add("act", lambda e, pb=pb, c=c: e.activation(out=k.EG[:, c, :], in_=ps[pb][:, 0, :], func=AF.Exp), reads=[("ps", pb)], writes=[("EG", c)])
            P.add("act", lambda e, pb=pb, c=c: e.activation(out=k.ER[:, c, :], in_=ps[pb][:, 1, :], func=AF.Exp), reads=[("ps", pb)], writes=[("ER", c)])
            P.add("act", lambda e, pb=pb, c=c: e.activation(out=k.EL[:, c, :], in_=ps[pb][:, 2, :], func=AF.Exp), reads=[("ps", pb)], writes=[("EL", c)])
            P.add("dve", lambda e, c=c: e.tensor_tensor(out=k.BEG[:, c, :], in0=k.BETA[:, c, :], in1=k.EG[:, c, :], op=ALU.mult),
                  reads=[("EG", c)], writes=[("BEG", c)])


def dn_prep(k, l):
    P = k.P
    mo = {n: COFF[n][0] for n in ("m_le", "m_ge", "m_lt", "m_gt", "ident")}
    with contextlib.ExitStack() as st:
        kTc = [P.sb([128, 16, 128], BF16, st) for _ in range(2)]
        qTc = [P.sb([128, 16, 128], BF16, st) for _ in range(2)]
        ktm = [P.sb([128, 16, 128], BF16, st) for _ in range(2)]
        vtm = [P.sb([128, 32, 128], BF16, st) for _ in range(2)]
        Gm = P.sb([128, 2, 128], F32, st)
        QKm = P.sb([128, 2, 128], F32, st)
        rhs2 = P.sb([128, 4, 128], F32, st)
        eD = P.sb([128, 4, 128], F32, st)
        L0 = P.sb([128, 4, 128], BF16, st)
        QKD = P.sb([128, 4, 128], BF16, st)
        LtQ = P.sb([128, 8, 128], BF16, st)
        Xb = [P.sb([128, 4, 128], BF16, st) for _ in range(2)]
        Yb = [P.sb([128, 4, 128], BF16, st) for _ in range(2)]
        Zb = [P.sb([128, 4, 128], BF16, st) for _ in range(2)]
        Ab = [P.sb([128, 4, 128], BF16, st) for _ in range(2)]
        Bb = [P.sb([128, 4, 128], BF16, st) for _ in range(2)]
        Lm = [P.sb([128, 4, 128], BF16, st) for _ in range(5)]
        Ltm = [P.sb([128, 4, 128], BF16, st) for _ in range(4)]
        T1a = P.sb([128, 4, 128], BF16, st)
        T1b = P.sb([128, 4, 128], BF16, st)
        vb = P.sb([128, 4, 128], BF16, st)
        kbg = P.sb([128, 4, 128], BF16, st)
        kd = [P.sb([128, 4, 128], BF16, st) for _ in range(2)]
        Uo = [P.sb([128, 4, 128], BF16, st) for _ in range(2)]
        Wo = [P.sb([128, 4, 128], BF16, st) for _ in range(2)]
        Qo = [P.sb([128, 4, 128], BF16, st) for _ in range(2)]
        psA = P.ps([128, 4, 128], F32, st)
        psB = P.ps([128, 4, 128], F32, st)
        psT = P.ps([128, 8, 128], BF16, st)
        psX = P.ps([128, 4, 128], F32, st)
        psY = P.ps([128, 4, 128], F32, st)
        psZ = P.ps([128, 4, 128], F32, st)
        nu = 0
        for c in range(18):
            cb = c % 2
            tsl = slice(c * 128, (c + 1) * 128)
            P.add("sp", lambda e, cb=cb, tsl=tsl: e.dma_start(out=kTc[cb][:], in_=k.DKT.rearrange("h p t -> p h t")[:, :, tsl]),
                  writes=[("kTc", cb)], dma=True)
            P.add("sp", lambda e, cb=cb, tsl=tsl: e.dma_start(out=qTc[cb][:], in_=k.DQT.rearrange("h p t -> p h t")[:, :, tsl]),
                  writes=[("qTc", cb)], dma=True)
            P.add("sp", lambda e, cb=cb, c=c: e.dma_start(out=ktm[cb][:], in_=k.DKTM[c].rearrange("p (h d) -> p h d", d=128)),
                  writes=[("ktm", cb)], dma=True)
            P.add("sp", lambda e, cb=cb, c=c: e.dma_start(out=vtm[cb][:], in_=k.DVTM[c].rearrange("p (h d) -> p h d", d=128)),
                  writes=[("vtm", cb)], dma=True)
            for d in range(2):
                m_strict = mo["m_gt"] if d == 0 else mo["m_lt"]
                m_incl = mo["m_ge"] if d == 0 else mo["m_le"]
                m_l = mo["m_le"] if d == 0 else mo["m_ge"]
                m_r = mo["m_gt"] if d == 0 else mo["m_lt"]
                for hg in range(8):
                    ob = nu % 2; nu += 1
                    h0 = d * 32 + hg * 4
                    kh0 = hg * 2
                    for a in range(2):
                        P.add("pe", lambda e, cb=cb, a=a, kh0=kh0: e.matmul(psA[:, a, :], kTc[cb][:, kh0 + a, :], kTc[cb][:, kh0 + a, :], start=True, stop=True),
                              reads=[("kTc", cb)], writes=["psA"])
                        P.add("pe", lambda e, cb=cb, a=a, kh0=kh0: e.matmul(psA[:, 2 + a, :], qTc[cb][:, kh0 + a, :], kTc[cb][:, kh0 + a, :], start=True, stop=True),
                              reads=[("kTc", cb), ("qTc", cb)], writes=["psA"])
                    P.add("dve", lambda e, m=m_strict: e.tensor_tensor(out=Gm[:], in0=psA[:, 0:2, :], in1=bc(k.consts[:, m:m + 128].unsqueeze(1), [128, 2, 128]), op=ALU.mult),
                          reads=["psA"], writes=["Gm"])
                    P.add("dve", lambda e, m=m_incl: e.tensor_tensor(out=QKm[:], in0=psA[:, 2:4, :], in1=bc(k.consts[:, m:m + 128].unsqueeze(1), [128, 2, 128]), op=ALU.mult),
                          reads=["psA"], writes=["QKm"])
                    P.add("pool", lambda e, m=m_r, c=c, h0=h0: e.tensor_tensor(
                        out=rhs2[:], in0=bc(k.consts[:, m:m + 128].unsqueeze(1), [128, 4, 128]),
                        in1=bc(k.GG[:, c, h0:h0 + 4].unsqueeze(2), [128, 4, 128]), op=ALU.mult), writes=["rhs2"])
                    P.add("pe", lambda e, m=m_l: e.matmul(psB[:], k.consts[:, m:m + 128], rhs2[:], start=True, stop=True),
                          reads=["rhs2"], writes=["psB"])
                    P.add("act", lambda e: e.activation(out=eD[:], in_=psB[:], func=AF.Exp), reads=["psB"], writes=["eD"])
                    for hl in range(4):
                        P.add("dve", lambda e, hl=hl, c=c, h0=h0: e.scalar_tensor_tensor(
                            out=L0[:, hl, :], in0=Gm[:, hl // 2, :], scalar=k.BETA[:, c, h0 + hl:h0 + hl + 1], in1=eD[:, hl, :],
                            op0=ALU.mult, op1=ALU.mult), reads=["Gm", "eD"], writes=[("L0", hl)])
                    P.add("dve", lambda e: e.tensor_tensor(
                        out=QKD[:].rearrange("p (a b) j -> p a b j", b=2), in0=bc(QKm[:].unsqueeze(2), [128, 2, 2, 128]),
                        in1=eD[:].rearrange("p (a b) j -> p a b j", b=2), op=ALU.mult), reads=["QKm", "eD"], writes=["QKD"])
                    L0k = [("L0", hl) for hl in range(4)]
                    for hl in range(4):
                        P.add("pe", lambda e, hl=hl: e.transpose(out=psT[:, hl, :], in_=L0[:, hl, :], identity=k.identb[:]),
                              reads=L0k, writes=["psT"])
                    for hl in range(4):
                        P.add("pe", lambda e, hl=hl: e.transpose(out=psT[:, 4 + hl, :], in_=QKD[:, hl, :], identity=k.identb[:]),
                              reads=["QKD"], writes=["psT"])
                    P.add("act", lambda e: e.activation(out=LtQ[:], in_=psT[:], func=AF.Copy), reads=["psT"], writes=["LtQ"])
                    Lt = LtQ[:, 0:4, :]
                    mk = lambda q: bc(k.maskb[:, q, :].unsqueeze(1), [128, 4, 128])
                    P.add("pool", lambda e: e.tensor_tensor(out=Lm[0][:], in0=L0[:], in1=mk(0), op=ALU.mult), reads=L0k, writes=[("Lm", 0)])
                    P.add("pool", lambda e: e.tensor_tensor(out=Ltm[0][:], in0=Lt, in1=mk(0), op=ALU.mult), reads=["LtQ"], writes=[("Ltm", 0)])
                    for q in range(1, 5):
                        P.add("pool", lambda e, q=q: e.tensor_tensor(out=Lm[q][:], in0=L0[:], in1=mk(q), op=ALU.mult), reads=L0k, writes=[("Lm", q)])
                        if q < 4:
                            P.add("pool", lambda e, q=q: e.tensor_tensor(out=Ltm[q][:], in0=Lt, in1=mk(q), op=ALU.mult), reads=["LtQ"], writes=[("Ltm", q)])
                    for hl in range(4):
                        P.add("pe", lambda e, hl=hl: e.matmul(psX[:, hl, :], Ltm[0][:, hl, :], Lm[0][:, hl, :], start=True, stop=True),
                              reads=[("Lm", 0), ("Ltm", 0)], writes=["psX"])
                    for hl in range(4):
                        P.add("pe", lambda e, hl=hl: e.matmul(psY[:, hl, :], Lm[0][:, hl, :], Ltm[0][:, hl, :], start=True, stop=True),
                              reads=[("Lm", 0), ("Ltm", 0)], writes=["psY"])
                    P.add("act", lambda e: e.activation(out=Xb[0][:], in_=psX[:], func=AF.Copy), reads=["psX"], writes=[("X", 0)])
                    P.add("dve", lambda e: e.tensor_copy(Yb[0][:], psY[:]), reads=["psY"], writes=[("Y", 0)])
                    for hl in range(4):
                        P.add("pe", lambda e, hl=hl: e.matmul(psX[:, hl, :], Yb[0][:, hl, :], Xb[0][:, hl, :], start=True, stop=True),
                              reads=[("X", 0), ("Y", 0)], writes=["psX"])
                    P.add("act", lambda e: e.activation(out=Xb[1][:], in_=psX[:], func=AF.Copy), reads=["psX"], writes=[("X", 1)])
                    P.add("dve", lambda e: e.scalar_tensor_tensor(
                        out=Zb[0][:], in0=Ltm[0][:], scalar=-1.0, in1=bc(k.identb[:].unsqueeze(1), [128, 4, 128]),
                        op0=ALU.mult, op1=ALU.add), reads=[("Ltm", 0)], writes=[("Z", 0)])
                    for m in (1, 2):
                        zi = m % 2; zp = (m - 1) % 2
                        for hl in range(4):
                            P.add("pe", lambda e, hl=hl, m=m, zp=zp: e.matmul(psZ[:, hl, :], Xb[m - 1][:, hl, :], Zb[zp][:, hl, :], start=True, stop=True),
                                  reads=[("X", m - 1), ("Z", zp)], writes=["psZ"])
                        P.add("dve", lambda e, zi=zi, zp=zp: e.tensor_tensor(out=Zb[zi][:], in0=psZ[:], in1=Zb[zp][:], op=ALU.add),
                              reads=["psZ", ("Z", zp)], writes=[("Z", zi)])
                    for hl in range(4):
                        P.add("pe", lambda e, hl=hl: e.transpose(out=psT[:, hl, :], in_=Zb[0][:, hl, :], identity=k.identb[:]),
                              reads=[("Z", 0)], writes=["psT"])
                    P.add("act", lambda e: e.activation(out=Ab[0][:], in_=psT[:, 0:4, :], func=AF.Copy), reads=["psT"], writes=[("A", 0)])
                    Bcur, Bkey = Zb[0], ("Z", 0)
                    Acur, Akey = Ab[0], ("A", 0)
                    for q in range(1, 5):
                        last = (q == 4)
                        nb_ = q % 2
                        if not last:
                            for hl in range(4):
                                P.add("pe", lambda e, hl=hl, q=q, Acur=Acur: e.matmul(psX[:, hl, :], Ltm[q][:, hl, :], Acur[:, hl, :], start=True, stop=True),
                                      reads=[("Ltm", q), Akey], writes=["psX"])
                            P.add("act", lambda e: e.activation(out=T1a[:], in_=psX[:], func=AF.Copy), reads=["psX"], writes=["T1a"])
                        for hl in range(4):
                            P.add("pe", lambda e, hl=hl, q=q, Bcur=Bcur: e.matmul(psY[:, hl, :], Lm[q][:, hl, :], Bcur[:, hl, :], start=True, stop=True),
                                  reads=[("Lm", q), Bkey], writes=["psY"])
                        P.add("act", lambda e: e.activation(out=T1b[:], in_=psY[:], func=AF.Copy), reads=["psY"], writes=["T1b"])
                        if not last:
                            for hl in range(4):
                                P.add("pe", lambda e, hl=hl, Bcur=Bcur: e.matmul(psX[:, hl, :], Bcur[:, hl, :], T1a[:, hl, :], start=True, stop=True),
                                      reads=[Bkey, "T1a"], writes=["psX"])
                        for hl in range(4):
                            P.add("pe", lambda e, hl=hl, Acur=Acur: e.matmul(psZ[:, hl, :], Acur[:, hl, :], T1b[:, hl, :], start=True, stop=True),
                                  reads=[Akey, "T1b"], writes=["psZ"])
                        if not last:
                            P.add("dve", lambda e, nb_=nb_, Acur=Acur: e.tensor_tensor(out=Ab[nb_][:], in0=Acur[:], in1=psX[:], op=ALU.subtract),
                                  reads=["psX", Akey], writes=[("A", nb_)])
                        Bnew = Bb[nb_] if not last else Zb[0]
                        Bnk = ("B", nb_) if not last else ("Z", 0)
                        P.add("dve", lambda e, Bnew=Bnew, Bcur=Bcur: e.tensor_tensor(out=Bnew[:], in0=Bcur[:], in1=psZ[:], op=ALU.subtract),
                              reads=["psZ", Bkey], writes=[Bnk])
                        Bcur, Bkey = Bnew, Bnk
                        if not last:
                            Acur, Akey = Ab[nb_], ("A", nb_)
                    ZF = Zb[0]
                    P.add("pool", lambda e, cb=cb, c=c, h0=h0, hg=hg: e.tensor_tensor(
                        out=vb[:], in0=vtm[cb][:, hg * 4:hg * 4 + 4, :], in1=bc(k.BETA[:, c, h0:h0 + 4].unsqueeze(2), [128, 4, 128]), op=ALU.mult),
                        reads=[("vtm", cb)], writes=["vb"])
                    P.add("pool", lambda e, cb=cb, c=c, h0=h0, kh0=kh0: e.tensor_tensor(
                        out=kbg[:].rearrange("p (a b) j -> p a b j", b=2), in0=bc(ktm[cb][:, kh0:kh0 + 2, :].unsqueeze(2), [128, 2, 2, 128]),
                        in1=bc(k.BEG[:, c, h0:h0 + 4].unsqueeze(2), [128, 4, 128]).rearrange("p (a b) j -> p a b j", b=2), op=ALU.mult),
                        reads=[("ktm", cb)], writes=["kbg"])
                    P.add("pool", lambda e, cb=cb, c=c, h0=h0, kh0=kh0, ob=ob: e.tensor_tensor(
                        out=kd[ob][:].rearrange("p (a b) j -> p a b j", b=2), in0=bc(ktm[cb][:, kh0:kh0 + 2, :].unsqueeze(2), [128, 2, 2, 128]),
                        in1=bc(k.ER[:, c, h0:h0 + 4].unsqueeze(2), [128, 4, 128]).rearrange("p (a b) j -> p a b j", b=2), op=ALU.mult),
                        reads=[("ktm", cb)], writes=[("kd", ob)])
                    for hl in range(4):
                        P.add("pe", lambda e, hl=hl: e.matmul(psA[:, hl, :], ZF[:, hl, :], vb[:, hl, :], start=True, stop=True),
                              reads=[("Z", 0), "vb"], writes=["psA"])
                    for hl in range(4):
                        P.add("pe", lambda e, hl=hl: e.matmul(psB[:, hl, :], kbg[:, hl, :], ZF[:, hl, :], start=True, stop=True),
                              reads=[("Z", 0), "kbg"], writes=["psB"])
                    P.add("act", lambda e, ob=ob: e.activation(out=Uo[ob][:], in_=psA[:], func=AF.Copy), reads=["psA"], writes=[("Uo", ob)])
                    P.add("dve", lambda e, ob=ob: e.tensor_copy(Wo[ob][:], psB[:]), reads=["psB"], writes=[("Wo", ob)])
                    P.add("pool", lambda e, ob=ob: e.tensor_copy(Qo[ob][:], LtQ[:, 4:8, :]), reads=["LtQ"], writes=[("Qo", ob)])
                    u = (c * 2 + d) * 8 + hg
                    P.add("sp", lambda e, ob=ob, u=u: e.dma_start(out=k.DPH[u, 0], in_=Uo[ob][:]), reads=[("Uo", ob)], writes=[("DPH", u, 0)], dma=True)
                    P.add("sp", lambda e, ob=ob, u=u: e.dma_start(out=k.DPH[u, 1], in_=Wo[ob][:]), reads=[("Wo", ob)], writes=[("DPH", u, 1)], dma=True)
                    P.add("sp", lambda e, ob=ob, u=u: e.dma_start(out=k.DPH[u, 2], in_=Qo[ob][:]), reads=[("Qo", ob)], writes=[("DPH", u, 2)], dma=True)
                    P.add("sp", lambda e, ob=ob, u=u: e.dma_start(out=k.DPH[u, 3], in_=kd[ob][:]), reads=[("kd", ob)], writes=[("DPH", u, 3)], dma=True)


def dn_scan(k, l):
    P = k.P
    order = [list(range(18)), [1, 0] + list(range(17, 1, -1))]
    with contextlib.ExitStack() as st:
        Sf = P.sb([128, 64, 128], F32, st)
        Sb = P.sb([128, 64, 128], BF16, st)
        NB = 3
        ph = [P.sb([128, 4, 4, 128], BF16, st) for _ in range(NB)]
        qT = [P.sb([128, 2, 128], BF16, st) for _ in range(NB)]
        vnew = [P.sb([128, 4, 128], BF16, st) for _ in range(2)]
        o2 = [P.sb([128, 4, 128], F32, st) for _ in range(2)]
        ob_ = [P.sb([128, 4, 128], F32, st) for _ in range(2)]
        psP = [P.ps([128, 4, 128], F32, st) for _ in range(2)]
        psO1 = [P.ps([128, 4, 128], F32, st) for _ in range(2)]
        psO2 = [P.ps([128, 4, 128], F32, st) for _ in range(2)]
        psS = [P.ps([128, 4, 128], F32, st) for _ in range(2)]
        P.add("pool", lambda e: e.memset(Sf[:], 0.0), writes=[("Sf", hh) for hh in range(16)])
        P.add("pool", lambda e: e.memset(Sb[:], 0.0), writes=[("Sb", hh) for hh in range(16)])
        P.barrier()
        units = [(s, d, hg) for s in range(18) for d in range(2) for hg in range(8)]

        def stage1(idx):
            s, d, hg = units[idx]
            c = order[d][s]
            u = (c * 2 + d) * 8 + hg
            nb = idx % NB; pb = idx % 2
            g16 = d * 8 + hg
            P.add("sp", lambda e: e.dma_start(out=ph[nb][:], in_=k.DPH[u].rearrange("f p h j -> p f h j")), writes=[("ph", nb)], dma=True)
            P.add("sp", lambda e: e.dma_start(out=qT[nb][:], in_=k.DQT[hg * 2:hg * 2 + 2, :, c * 128:(c + 1) * 128].rearrange("h p t -> p h t")),
                  writes=[("qT", nb)], dma=True)
            for hl in range(4):
                hs = d * 32 + hg * 4 + hl
                P.add("pe", lambda e, hl=hl, hs=hs: e.matmul(psP[pb][:, hl, :], ph[nb][:, 1, hl, :], Sb[:, hs, :], start=True, stop=True),
                      reads=[("ph", nb), ("Sb", g16)], writes=[("psP", pb)])
            for hl in range(4):
                hs = d * 32 + hg * 4 + hl
                P.add("pe", lambda e, hl=hl, hs=hs: e.matmul(psO1[pb][:, hl, :], qT[nb][:, hl // 2, :], Sb[:, hs, :], start=True, stop=True),
                      reads=[("qT", nb), ("Sb", g16)], writes=[("psO1", pb)])
            P.add("dve", lambda e: e.tensor_tensor(out=vnew[pb][:], in0=ph[nb][:, 0, :, :], in1=psP[pb][:], op=ALU.subtract),
                  reads=[("ph", nb), ("psP", pb)], writes=[("vnew", pb)])

        def stage2(idx):
            s, d, hg = units[idx]
            c = order[d][s]
            nb = idx % NB; pb = idx % 2
            g16 = d * 8 + hg
            h0 = d * 32 + hg * 4
            for hl in range(4):
                P.add("pe", lambda e, hl=hl: e.matmul(psO2[pb][:, hl, :], ph[nb][:, 2, hl, :], vnew[pb][:, hl, :], start=True, stop=True),
                      reads=[("ph", nb), ("vnew", pb)], writes=[("psO2", pb)])
            for hl in range(4):
                P.add("pe", lambda e, hl=hl: e.matmul(psS[pb][:, hl, :], ph[nb][:, 3, hl, :], vnew[pb][:, hl, :], start=True, stop=True),
                      reads=[("ph", nb), ("vnew", pb)], writes=[("psS", pb)])
            P.add("act", lambda e: e.activation(out=o2[pb][:], in_=psO2[pb][:], func=AF.Copy), reads=[("psO2", pb)], writes=[("o2", pb)])
            for hl in range(4):
                P.add("dve", lambda e, hl=hl: e.scalar_tensor_tensor(
                    out=ob_[pb][:, hl, :], in0=psO1[pb][:, hl, :], scalar=k.EG[:, c, h0 + hl:h0 + hl + 1], in1=o2[pb][:, hl, :],
                    op0=ALU.mult, op1=ALU.add), reads=[("psO1", pb), ("o2", pb)], writes=[("ob", pb, hl)])
            for hl in range(4):
                P.add("dve", lambda e, hl=hl: e.scalar_tensor_tensor(
                    out=Sf[:, h0 + hl, :], in0=Sf[:, h0 + hl, :], scalar=k.EL[:, c, h0 + hl:h0 + hl + 1], in1=psS[pb][:, hl, :],
                    op0=ALU.mult, op1=ALU.add), reads=[("psS", pb), ("Sf", g16, hl)], writes=[("Sf", g16, hl)])
            P.add("act", lambda e: e.activation(out=Sb[:, h0:h0 + 4, :], in_=Sf[:, h0:h0 + 4, :], func=AF.Copy),
                  reads=[("Sf", g16, hl) for hl in range(4)], writes=[("Sb", g16)])
            dst = k.DO[d, c, :, hg * 512:(hg + 1) * 512]
            P.add("sp", lambda e: e.dma_start(out=dst, in_=ob_[pb][:].rearrange("p h j -> p (h j)")),
                  reads=[("ob", pb, hl) for hl in range(4)], writes=[("DO", d, c, hg)], dma=True)

        for idx in range(len(units) + 1):
            if idx < len(units):
                stage1(idx)
            if idx >= 1:
                stage2(idx - 1)


def dn_outnorm(k, l):
    P = k.P
    i = l // 2
    ng = VOFF["dngrep"][0] + i * 128
    with contextlib.ExitStack() as st:
        of = [P.sb([128, 32, 128], F32, st) for _ in range(2)]
        obk = [P.sb([128, 32, 128], F32, st) for _ in range(2)]
        zz = [P.sb([128, 32, 128], BF16, st) for _ in range(2)]
        sq = P.sb([128, 32, 128], F32, st)
        ss = P.sb([128, 32], F32, st)
        y = [P.sb([128, 32, 128], BF16, st) for _ in range(2)]
        yT = [P.sb([128, 32, 128], BF16, st) for _ in range(2)]
        pst = [P.ps([128, 8, 128], BF16, st) for _ in range(2)]
        npt = 0
        for c in range(18):
            b = c % 2
            P.add("sp", lambda e, b=b, c=c: e.dma_start(out=of[b][:], in_=k.DO[0, c].rearrange("p (h j) -> p h j", j=128)), writes=[("of", b)], dma=True)
            P.add("sp", lambda e, b=b, c=c: e.dma_start(out=obk[b][:], in_=k.DO[1, c].rearrange("p (h j) -> p h j", j=128)), writes=[("obk", b)], dma=True)
            P.add("sp", lambda e, b=b, c=c: e.dma_start(out=zz[b][:], in_=k.DZ[c].rearrange("p (h j) -> p h j", j=128)), writes=[("zz", b)], dma=True)
            P.add("dve", lambda e, b=b: e.tensor_tensor(out=of[b][:], in0=of[b][:], in1=obk[b][:], op=ALU.add),
                  reads=[("of", b), ("obk", b)], writes=[("of", b)])
            P.add("act", lambda e, b=b: e.activation(out=sq[:], in_=of[b][:], func=AF.Square), reads=[("of", b)], writes=["sq"])
            P.add("dve", lambda e: e.tensor_reduce(out=ss[:], in_=sq[:], axis=AX.X, op=ALU.add), reads=["sq"], writes=["ss"])
            P.add("act", lambda e: e.activation(out=ss[:], in_=ss[:], func=AF.Sqrt, bias=EPS, scale=1.0 / 128), reads=["ss"], writes=["ss"])
            P.add("dve", lambda e: e.reciprocal(ss[:], ss[:]), reads=["ss"], writes=["ss"])
            P.add("dve", lambda e, b=b: e.tensor_tensor(out=of[b][:], in0=of[b][:], in1=bc(ss[:].unsqueeze(2), [128, 32, 128]), op=ALU.mult),
                  reads=[("of", b), "ss"], writes=[("of", b)])
            P.add("dve", lambda e, b=b: e.tensor_tensor(out=of[b][:], in0=of[b][:], in1=bc(k.vecs[:, ng:ng + 128].unsqueeze(1), [128, 32, 128]), op=ALU.mult),
                  reads=[("of", b)], writes=[("of", b)])
            P.add("dve", lambda e, b=b: e.tensor_tensor(out=y[b][:], in0=of[b][:], in1=zz[b][:], op=ALU.mult),
                  reads=[("of", b), ("zz", b)], writes=[("y", b)])
            for h8 in range(4):
                tb = npt % 2; npt += 1
                for hh in range(8):
                    P.add("pe", lambda e, tb=tb, hh=hh, h8=h8, b=b: e.transpose(out=pst[tb][:, hh, :], in_=y[b][:, h8 * 8 + hh, :], identity=k.identb[:]),
                          reads=[("y", b)], writes=[("pst", tb)])
                P.add("act", lambda e, tb=tb, h8=h8, b=b: e.activation(out=yT[b][:, h8 * 8:(h8 + 1) * 8, :], in_=pst[tb][:], func=AF.Copy),
                      reads=[("pst", tb)], writes=[("yT", b, h8)])
            P.add("sp", lambda e, b=b, c=c: e.dma_start(out=k.DYT[:, :, c * 128:(c + 1) * 128].rearrange("h p t -> p h t"), in_=yT[b][:]),
                  reads=[("yT", b, h8) for h8 in range(4)], writes=[("DYT", c)], dma=True)


def dn_outproj(k, l):
    P = k.P
    i = l // 2
    wsrc = k.dn_w_out[i].rearrange("(kc p) n -> p kc n", p=128)
    hsrc = k.hT.rearrange("(c p) t -> p c t", p=128)
    HALF = NT // 2
    SUB = 384
    with contextlib.ExitStack() as st:
        cat = P.sb([128, 32, HALF], BF16, st)
        wb = [P.sb([128, 32, 256], BF16, st) for _ in range(2)]
        hrow = [P.sb([128, HALF], F32, st) for _ in range(2)]
        pso = [P.ps([128, 512], F32, st) for _ in range(4)]
        ysrc = k.DYT.rearrange("h p t -> p h t")
        nps = 0
        for half in range(2):
            h0 = half * HALF
            for q in range(4):
                P.add("sp", lambda e, q=q, h0=h0: e.dma_start(out=cat[:, q * 8:(q + 1) * 8, :], in_=ysrc[:, q * 8:(q + 1) * 8, h0:h0 + HALF]),
                      writes=[("cat", q)], dma=True)
            for blk in range(8):
                b = blk % 2
                for q in range(2):
                    P.add("pool", lambda e, b=b, blk=blk, q=q: e.dma_start(
                        out=wb[b][:, q * 16:(q + 1) * 16, :], in_=wsrc[:, q * 16:(q + 1) * 16, blk * 256:(blk + 1) * 256]),
                        writes=[("wb", b, q)], dma=True)
                for s in range(2):
                    c = blk * 2 + s
                    hb = c % 2
                    P.add("sp", lambda e, hb=hb, c=c, h0=h0: e.dma_start(out=hrow[hb][:], in_=hsrc[:, c, h0:h0 + HALF]),
                          reads=[("hT", c)], writes=[("hrow", hb)], dma=True)
                    for sub in range(HALF // SUB):
                        u0 = sub * SUB
                        pb = nps % 4; nps += 1
                        for kc in range(32):
                            P.add("pe", lambda e, b=b, s=s, pb=pb, kc=kc, u0=u0: e.matmul(
                                pso[pb][:, :SUB], wb[b][:, kc, s * 128:(s + 1) * 128], cat[:, kc, u0:u0 + SUB],
                                start=(kc == 0), stop=(kc == 31)),
                                reads=[("wb", b, kc // 16), ("cat", kc // 8)], writes=[("pso", pb)])
                        for (g0, gn, sidx) in segs(h0 + u0, SUB):
                            r0 = g0 - h0
                            P.add("dve", lambda e, hb=hb, pb=pb, r0=r0, gn=gn, u0=u0, c=c, sidx=sidx: e.scalar_tensor_tensor(
                                out=hrow[hb][:, r0:r0 + gn], in0=pso[pb][:, r0 - u0:r0 - u0 + gn], scalar=modv(k, 2, c, sidx),
                                in1=hrow[hb][:, r0:r0 + gn], op0=ALU.mult, op1=ALU.add),
                                reads=[("pso", pb), ("hrow", hb), "mod"], writes=[("hrow", hb)])
                    P.add("sp", lambda e, hb=hb, c=c, h0=h0: e.dma_start(out=hsrc[:, c, h0:h0 + HALF], in_=hrow[hb][:]),
                          reads=[("hrow", hb)], writes=[("hT", c)], dma=True)


FULL_PLAN = [(l, ("mix", "ffn")) for l in range(DEPTH)]


def make_core_inputs(inp, b, consts):
    h_in = np.concatenate([inp["ctx"][b].T, inp["x"][b].T], axis=1)
    return {
        "h_in": np.ascontiguousarray(h_in, dtype=np.float32),
        "vecs": pack_vecs(inp, b),
        "consts": consts,
        "w_mod": inp["w_mod"], "ffn_w_up": inp["ffn_w_up"], "ffn_w_down": inp["ffn_w_down"],
        "even_w_in": inp["even_w_in"], "even_w_out": inp["even_w_out"],
        "dn_w_in": inp["dn_w_in"], "dn_w_out": inp["dn_w_out"],
    }


def kernel(**inputs):
    inp = {k_: np.asarray(v) for k_, v in inputs.items()}
    consts = make_consts()
    nc = build_program(FULL_PLAN)
    in_maps = [make_core_inputs(inp, c % 4, consts) for c in range(8)]
    res = run_bass_kernel_spmd(nc, in_maps, core_ids=list(range(8)))
    out = np.stack([res.results[b]["hT"][:, LCTX:].T for b in range(4)], axis=0)
    return np.ascontiguousarray(out.astype(np.float32))
```

```python
import contextlib
import numpy as np
import concourse.bass as bass
import concourse.mybir as mybir
from concourse.bass_utils import run_bass_kernel_spmd

F32 = mybir.dt.float32
BF16 = mybir.dt.bfloat16
AF = mybir.ActivationFunctionType
ALU = mybir.AluOpType
AX = mybir.AxisListType

D = 2048
NCH = 16
LCTX = 256
TLAT = 2048
NT = LCTX + TLAT
DEPTH = 4
DFF = 5632
NFF = DFF // 128
EPS = 1e-6
TILES = [(0, 256), (256, 512), (768, 512), (1280, 512), (1792, 512)]
EVEN_IN_W = 3584
DN_IN_W = 12416

ENGS = ("pe", "act", "dve", "pool", "sp")
NDSEM = 6


class Op:
    __slots__ = ("eng", "fn", "idx", "dma", "slot", "val", "deps", "needed", "cnt", "waits", "clock")


class Prog:
    def __init__(self, nc, same_engine_sync=True):
        self.nc = nc
        self.ops = {e: [] for e in ENGS}
        self.order = []
        self.last_w = {}
        self.readers = {}
        self.same_engine_sync = same_engine_sync
        self.dma_rr = {e: 0 for e in ENGS}
        self.dma_val = {}
        self.stack = contextlib.ExitStack()
        self.sems = {}
        self.dsems = {}
        self._n = 0

    def sb(self, shape, dtype, stack=None):
        self._n += 1
        return (stack or self.stack).enter_context(self.nc.sbuf_tensor(f"sb{self._n}", list(shape), dtype))

    def ps(self, shape, dtype, stack=None):
        self._n += 1
        return (stack or self.stack).enter_context(self.nc.psum_tensor(f"ps{self._n}", list(shape), dtype))

    def add(self, eng, fn, reads=(), writes=(), dma=False):
        op = Op()
        op.eng = eng; op.fn = fn; op.dma = dma; op.needed = False; op.waits = None
        op.idx = len(self.ops[eng]); op.slot = None; op.val = None
        deps = set()
        writes = list(writes)
        if dma:
            s = self.dma_rr[eng] % NDSEM
            self.dma_rr[eng] += 1
            op.slot = (eng, s)
            writes.append(("dsem", eng, s))
            self.dma_val[op.slot] = self.dma_val.get(op.slot, 0) + 16
            op.val = self.dma_val[op.slot]
        for r in reads:
            w = self.last_w.get(r)
            if w is not None:
                deps.add(w)
        for r in writes:
            w = self.last_w.get(r)
            if w is not None:
                deps.add(w)
            for rd in self.readers.get(r, ()):
                deps.add(rd)
        for r in writes:
            self.last_w[r] = op
            self.readers[r] = []
        for r in reads:
            self.readers.setdefault(r, []).append(op)
        deps.discard(op)
        op.deps = deps
        self.ops[eng].append(op)
        self.order.append(op)
        return op

    def barrier(self):
        lasts = []
        for e in ENGS:
            for o in reversed(self.ops[e]):
                if o.fn is not None and not o.dma:
                    lasts.append(o)
                    break
        latest = {}
        for e in ENGS:
            for o in reversed(self.ops[e]):
                if o.dma and o.slot not in latest:
                    latest[o.slot] = o
        for e in ENGS:
            op = Op()
            op.eng = e; op.fn = None; op.dma = False; op.needed = False; op.waits = None
            op.idx = len(self.ops[e]); op.slot = None; op.val = None
            op.deps = set(o for o in lasts if o.eng != e and o.fn is not None and not o.dma) | set(latest.values())
            self.ops[e].append(op)
            self.order.append(op)
        self.last_w.clear(); self.readers.clear()

    def emit(self):
        nc = self.nc
        seen = {e: {} for e in ENGS}
        for op in self.order:
            s = seen[op.eng]
            waits = []
            for d in sorted(op.deps, key=lambda o: (o.eng, o.idx)):
                if d.dma:
                    key = ("d",) + d.slot
                    if s.get(key, 0) >= d.val:
                        continue
                    waits.append(d)
                    s[key] = d.val
                else:
                    if d.fn is None:
                        continue
                    if d.eng == op.eng and (d.eng == "pe" or not self.same_engine_sync or op.fn is None):
                        continue
                    if s.get(d.eng, -1) >= d.idx:
                        continue
                    waits.append(d)
                    s[d.eng] = d.idx
                for k, v in d.clock.items():
                    if k == op.eng:
                        continue
                    if s.get(k, -1) < v:
                        s[k] = v
            for d in waits:
                d.needed = True
            op.waits = waits
            op.clock = dict(s)
        for e in ENGS:
            c = 0
            for op in self.ops[e]:
                if op.needed and not op.dma:
                    c += 1
                op.cnt = c
        st = self.stack
        for e in ENGS:
            self.sems[e] = st.enter_context(nc.semaphore(f"s_{e}"))
        for key in self.dma_val:
            self.dsems[key] = st.enter_context(nc.semaphore(f"d_{key[0]}{key[1]}"))
        block = st.enter_context(nc.Block())
        handles = {"pe": block.tensor, "act": block.scalar, "dve": block.vector, "pool": block.gpsimd, "sp": block.sync}

        def mk(e):
            def body(h):
                for op in self.ops[e]:
                    for d in op.waits:
                        if d.dma:
                            h.wait_ge(self.dsems[d.slot], d.val)
                        else:
                            h.wait_ge(self.sems[d.eng], d.cnt)
                    if op.fn is None:
                        continue
                    ins = op.fn(h)
                    if op.dma:
                        ins.then_inc(self.dsems[op.slot], 16)
                    elif op.needed:
                        ins.then_inc(self.sems[e], 1)
            return body
        for e in ENGS:
            handles[e](mk(e))

    def finish(self):
        self.barrier()
        self.emit()
        self.stack.close()


def _layout(items):
    off = {}
    o = 0
    for name, w in items:
        off[name] = (o, w)
        o += w
    return off, o


VEC_ITEMS = [
    ("bmod", 4 * 96), ("n1g", 64), ("n2g", 64), ("fcw", 4 * 3 * NFF), ("fcb", 4 * NFF),
    ("qg", 2), ("kg", 2), ("sink", 16), ("cdw", 2 * 31 * 8), ("cdb", 16), ("lng", 16), ("lnb", 16),
    ("dcw", 2 * 5 * 64), ("alog", 128), ("dtb", 128), ("dngrep", 256), ("c", 16), ("cctx", 16),
]
VOFF, NV = _layout(VEC_ITEMS)

CONST_ITEMS = [
    ("ident", 128), ("ropeT", 128), ("band", 384),
    ("m_le", 128), ("m_ge", 128), ("m_lt", 128), ("m_gt", 128),
    ("mblk", 5 * 128),
]
COFF, NCONST = _layout(CONST_ITEMS)


def fm(v):
    return np.ascontiguousarray(v.reshape(-1, 128).T)


def pack_vecs(inp, b):
    V = np.zeros((128, NV), np.float32)

    def put(name, arr):
        o, w = VOFF[name]
        assert arr.shape == (128, w), (name, arr.shape, w)
        V[:, o:o + w] = arr
    put("bmod", np.concatenate([fm(inp["b_mod"][l]) for l in range(4)], axis=1))
    put("n1g", np.concatenate([fm(inp["norm1_g"][l]) for l in range(4)], axis=1))
    put("n2g", np.concatenate([fm(inp["norm2_g"][l]) for l in range(4)], axis=1))
    put("fcw", np.concatenate([fm(inp["ffn_conv_w"][l, k]) for l in range(4) for k in range(3)], axis=1))
    put("fcb", np.concatenate([fm(inp["ffn_conv_b"][l]) for l in range(4)], axis=1))
    put("qg", inp["attn_q_norm_g"].T.copy())
    put("kg", inp["attn_k_norm_g"].T.copy())
    put("sink", np.broadcast_to(inp["attn_sink"].reshape(1, 16), (128, 16)).copy())
    put("cdw", np.concatenate([fm(inp["conv_dw_w"][i, k]) for i in range(2) for k in range(31)], axis=1))
    put("cdb", np.concatenate([fm(inp["conv_dw_b"][i]) for i in range(2)], axis=1))
    put("lng", np.concatenate([fm(inp["conv_ln_g"][i]) for i in range(2)], axis=1))
    put("lnb", np.concatenate([fm(inp["conv_ln_b"][i]) for i in range(2)], axis=1))
    put("dcw", np.concatenate([fm(inp["dn_conv_w"][i, k]) for i in range(2) for k in range(5)], axis=1))
    put("alog", np.broadcast_to(inp["dn_a_log"].reshape(1, 128), (128, 128)).copy())
    put("dtb", np.broadcast_to(inp["dn_dt_bias"].reshape(1, 128), (128, 128)).copy())
    put("dngrep", np.broadcast_to(inp["dn_norm_g"].reshape(1, 256), (128, 256)).copy())
    put("c", fm(inp["c"][b]))
    put("cctx", fm(inp["c_ctx"]))
    return V


def make_consts():
    C = np.zeros((128, NCONST), np.float32)

    def put(name, arr):
        o, w = COFF[name]
        assert arr.shape == (128, w), (name, arr.shape)
        C[:, o:o + w] = arr
    idx = np.arange(128)
    put("ident", np.eye(128, dtype=np.float32))
    R = np.zeros((128, 128), np.float32)
    for i in range(32):
        R[i, 32 + i] = -1.0
        R[32 + i, i] = 1.0
        R[64 + i, 96 + i] = -1.0
        R[96 + i, 64 + i] = 1.0
    put("ropeT", R.T.copy())
    kk = idx[:, None]; qq = idx[None, :]
    band = np.concatenate([(kk <= qq), np.ones((128, 128), bool), (qq <= kk)], axis=1).astype(np.float32)
    put("band", band)
    put("m_le", (kk <= qq).astype(np.float32))
    put("m_ge", (kk >= qq).astype(np.float32))
    put("m_lt", (kk < qq).astype(np.float32))
    put("m_gt", (kk > qq).astype(np.float32))
    mb = [(kk // 8 == qq // 8)]
    for bsz in (8, 16, 32, 64):
        mb.append((kk // (2 * bsz) == qq // (2 * bsz)) & (kk // bsz != qq // bsz))
    put("mblk", np.concatenate(mb, axis=1).astype(np.float32))
    GRID_W = 64
    rows = TLAT // GRID_W
    row = np.repeat(np.arange(rows, dtype=np.float32), GRID_W)
    col = np.tile(np.arange(GRID_W, dtype=np.float32), rows)
    half = 64
    inv_freq = (10000.0 ** (-np.arange(0, half, 2, dtype=np.float32) / half)).astype(np.float32)
    ang_r = row[:, None] * inv_freq
    ang_c = col[:, None] * inv_freq
    ang = np.concatenate([ang_r, ang_r, ang_c, ang_c], axis=-1)
    rope = np.concatenate([np.cos(ang).T, np.sin(ang).T], axis=1).astype(np.float32)
    return C, np.ascontiguousarray(rope)


class K:
    pass


def segs(t0, n):
    out = []
    if t0 < LCTX:
        e = min(t0 + n, LCTX)
        out.append((t0, e - t0, 1))
        if t0 + n > LCTX:
            out.append((LCTX, t0 + n - LCTX, 0))
    else:
        out.append((t0, n, 0))
    return out


def bc(ap, shape):
    return ap.broadcast_to(list(shape))


def stage_adaln(k, l):
    P = k.P
    with contextlib.ExitStack() as st:
        wb = [P.sb([128, 16, 256], BF16, st) for _ in range(2)]
        pm = P.ps([128, 96, 2], F32, st)
        for blk in range(48):
            b = blk % 2
            P.add("pool", lambda e, b=b, blk=blk: e.dma_start(out=wb[b][:], in_=k.w_mod[l, blk]),
                  writes=[("wb", b)], dma=True)
            for s in range(2):
                j = blk * 2 + s
                for kc in range(16):
                    P.add("pe", lambda e, b=b, s=s, j=j, kc=kc: e.matmul(
                        pm[:, j, :], wb[b][:, kc, s * 128:(s + 1) * 128], k.siluc[:, kc, :],
                        start=(kc == 0), stop=(kc == 15)),
                        reads=[("wb", b), "siluc"], writes=["pm"])
        bo = VOFF["bmod"][0] + l * 96
        P.add("dve", lambda e: e.tensor_tensor(out=k.mod[:], in0=pm[:], in1=bc(k.vecs[:, bo:bo + 96].unsqueeze(2), [128, 96, 2]),
                                               op=ALU.add), reads=["pm"], writes=["mod"])
        for (dst, m, gname) in ((k.A1, 1, "n1g"), (k.A2, 4, "n2g")):
            go = VOFF[gname][0] + l * 16
            P.add("dve", lambda e, dst=dst, m=m: e.tensor_scalar_add(out=dst[:], in0=k.mod[:, m * 16:(m + 1) * 16, :], scalar1=1.0),
                  reads=["mod"], writes=[("A", m)])
            P.add("dve", lambda e, dst=dst, go=go: e.tensor_tensor(
                out=dst[:], in0=dst[:], in1=bc(k.vecs[:, go:go + 16].unsqueeze(2), [128, 16, 2]), op=ALU.mult),
                reads=[("A", m)], writes=[("A", m)])
    P.barrier()


def modv(k, m, c, s):
    return k.mod[:, m * 16 + c, s:s + 1]


def stage_norm(k, which, nx, nbuf=2):
    P = k.P
    A = k.A1 if which == 1 else k.A2
    msh = 0 if which == 1 else 3
    hsrc = k.hT.rearrange("(c p) t -> p c t", p=128)
    with contextlib.ExitStack() as st:
        hs = [P.sb([128, 16, 512], F32, st) for _ in range(nbuf)]
        sq = P.sb([128, 16, 512], BF16, st)
        rstd = P.sb([128, 512], F32, st)
        pss = P.ps([128, 512], F32, st)
        for ti, (t0, n) in enumerate(TILES):
            b = ti % nbuf
            P.add("sp", lambda e, b=b, t0=t0, n=n: e.dma_start(out=hs[b][:, :, :n], in_=hsrc[:, :, t0:t0 + n]),
                  reads=["hT"], writes=[("hs", b)], dma=True)
            P.add("act", lambda e, b=b, n=n: e.activation(out=sq[:, :, :n], in_=hs[b][:, :, :n], func=AF.Square),
                  reads=[("hs", b)], writes=["sq"])
            for c in range(16):
                P.add("pe", lambda e, c=c, n=n: e.matmul(pss[:, :n], k.onesD[:], sq[:, c, :n], start=(c == 0), stop=(c == 15)),
                      reads=["sq"], writes=["pss"])
            P.add("act", lambda e, n=n: e.activation(out=rstd[:, :n], in_=pss[:, :n], func=AF.Sqrt, bias=EPS, scale=1.0),
                  reads=["pss"], writes=["rstd"])
            P.add("dve", lambda e, n=n: e.reciprocal(rstd[:, :n], rstd[:, :n]), reads=["rstd"], writes=["rstd"])
            P.add("dve", lambda e, b=b, n=n: e.tensor_tensor(out=hs[b][:, :, :n], in0=hs[b][:, :, :n],
                                                             in1=bc(rstd[:, :n].unsqueeze(1), [128, 16, n]), op=ALU.mult),
                  reads=["rstd", ("hs", b)], writes=[("hs", b)])
            s = 1 if t0 < LCTX else 0
            for c in range(16):
                P.add("act", lambda e, b=b, c=c, t0=t0, n=n, s=s: e.activation(
                    out=nx[:, c, t0:t0 + n], in_=hs[b][:, c, :n], func=AF.Identity,
                    bias=modv(k, msh, c, s), scale=A[:, c, s:s + 1]),
                    reads=[("hs", b), "mod", ("A", 1), ("A", 4)], writes=[("nx", ti, c)])


def stage_ffn(k, l):
    P = k.P
    with contextlib.ExitStack() as st0:
        nx = P.sb([128, 16, NT], BF16, st0)
        stage_norm(k, 2, nx)
        P.barrier()
        stage_ffn_up(k, l, nx)
        P.barrier()
    stage_ffn_down(k, l)


def stage_ffn_up(k, l, nx):
    P = k.P
    with contextlib.ExitStack() as st:
        wb = [P.sb([128, 16, 256], BF16, st) for _ in range(3)]
        G = [P.sb([128, NT], F32, st) for _ in range(2)]
        Vv = [P.sb([128, NT], BF16, st) for _ in range(2)]
        acc = [P.sb([128, NT], F32, st) for _ in range(2)]
        sg = [P.sb([128, NT], BF16, st) for _ in range(2)]
        hh = [P.sb([128, NT], BF16, st) for _ in range(2)]
        psg = [P.ps([128, 512], F32, st) for _ in range(2)]
        psv = [P.ps([128, 512], F32, st) for _ in range(2)]
        cw = VOFF["fcw"][0] + l * 3 * NFF
        cb = VOFF["fcb"][0] + l * NFF

        def make_post(j):
            b = j % 2
            Gk = [("G", b, ti) for ti in range(5)]
            Vk = [("V", b, ti) for ti in range(5)]
            w0 = k.vecs[:, cw + 0 * NFF + j:cw + 0 * NFF + j + 1]
            w1 = k.vecs[:, cw + 1 * NFF + j:cw + 1 * NFF + j + 1]
            w2 = k.vecs[:, cw + 2 * NFF + j:cw + 2 * NFF + j + 1]
            bb = k.vecs[:, cb + j:cb + j + 1]
            conv = []
            conv.append(lambda: P.add("dve", lambda e: e.tensor_scalar(out=acc[b][:], in0=G[b][:], scalar1=w1, scalar2=bb,
                                                                       op0=ALU.mult, op1=ALU.add), reads=Gk, writes=[("acc", b)]))
            for (s0, sn) in ((0, LCTX), (LCTX, TLAT)):
                conv.append(lambda s0=s0, sn=sn: P.add("dve", lambda e: e.scalar_tensor_tensor(
                    out=acc[b][:, s0 + 1:s0 + sn], in0=G[b][:, s0:s0 + sn - 1], scalar=w0, in1=acc[b][:, s0 + 1:s0 + sn],
                    op0=ALU.mult, op1=ALU.add), reads=Gk + [("acc", b)], writes=[("acc", b)]))
                conv.append(lambda s0=s0, sn=sn: P.add("dve", lambda e: e.scalar_tensor_tensor(
                    out=acc[b][:, s0:s0 + sn - 1], in0=G[b][:, s0 + 1:s0 + sn], scalar=w2, in1=acc[b][:, s0:s0 + sn - 1],
                    op0=ALU.mult, op1=ALU.add), reads=Gk + [("acc", b)], writes=[("acc", b)]))

            def tail():
                P.add("act", lambda e: e.activation(out=sg[b][:], in_=acc[b][:], func=AF.Silu), reads=[("acc", b)], writes=[("sg", b)])
                P.add("pool", lambda e: e.tensor_tensor(out=hh[b][:], in0=sg[b][:], in1=Vv[b][:], op=ALU.mult),
                      reads=[("sg", b)] + Vk, writes=[("hh", b)])
                P.add("sp", lambda e: e.dma_start(out=k.HF[j * 128:(j + 1) * 128, :], in_=hh[b][:]),
                      reads=[("hh", b)], writes=["HF"], dma=True)
            return conv, tail

        pending = None
        for j in range(NFF):
            b = j % 2
            wi = j % 3
            P.add("pool", lambda e, wi=wi, j=j: e.dma_start(out=wb[wi][:], in_=k.ffn_w_up[l, j]),
                  writes=[("wb", wi)], dma=True)
            conv_prev = list(pending[0]) if pending else []
            for ti, (t0, n) in enumerate(TILES):
                pb = ti % 2
                for kc in range(16):
                    P.add("pe", lambda e, wi=wi, pb=pb, kc=kc, t0=t0, n=n: e.matmul(
                        psg[pb][:, :n], wb[wi][:, kc, 0:128], nx[:, kc, t0:t0 + n], start=(kc == 0), stop=(kc == 15)),
                        reads=[("wb", wi)], writes=[("psg", pb)])
                for kc in range(16):
                    P.add("pe", lambda e, wi=wi, pb=pb, kc=kc, t0=t0, n=n: e.matmul(
                        psv[pb][:, :n], wb[wi][:, kc, 128:256], nx[:, kc, t0:t0 + n], start=(kc == 0), stop=(kc == 15)),
                        reads=[("wb", wi)], writes=[("psv", pb)])
                P.add("act", lambda e, b=b, pb=pb, t0=t0, n=n: e.activation(out=G[b][:, t0:t0 + n], in_=psg[pb][:, :n], func=AF.Copy),
                      reads=[("psg", pb)], writes=[("G", b, ti)])
                P.add("dve", lambda e, b=b, pb=pb, t0=t0, n=n: e.tensor_copy(Vv[b][:, t0:t0 + n], psv[pb][:, :n]),
                      reads=[("psv", pb)], writes=[("V", b, ti)])
                if conv_prev:
                    conv_prev.pop(0)()
            for c_ in conv_prev:
                c_()
            if pending:
                pending[1]()
            pending = make_post(j)
        for c_ in pending[0]:
            c_()
        pending[1]()


def stage_ffn_down(k, l):
    P = k.P
    HALF = NT // 2
    SUB = 384
    with contextlib.ExitStack() as st:
        hres = P.sb([128, NFF, HALF], BF16, st)
        wb = [P.sb([128, NFF, 256], BF16, st) for _ in range(2)]
        hrow = [P.sb([128, HALF], F32, st) for _ in range(2)]
        pso = [P.ps([128, 512], F32, st) for _ in range(4)]
        hfsrc = k.HF.rearrange("(j p) t -> p j t", p=128)
        hsrc = k.hT.rearrange("(c p) t -> p c t", p=128)
        nps = 0
        for half in range(2):
            h0 = half * HALF
            for q in range(4):
                P.add("sp", lambda e, q=q, h0=h0: e.dma_start(out=hres[:, q * 11:(q + 1) * 11, :], in_=hfsrc[:, q * 11:(q + 1) * 11, h0:h0 + HALF]),
                      reads=["HF"], writes=[("hres", q)], dma=True)
            for blk in range(8):
                b = blk % 2
                for q in range(2):
                    P.add("pool", lambda e, b=b, blk=blk, q=q: e.dma_start(
                        out=wb[b][:, q * 22:(q + 1) * 22, :], in_=k.ffn_w_down[l, blk][:, q * 22:(q + 1) * 22, :]),
                        writes=[("wb", b, q)], dma=True)
                for s in range(2):
                    c = blk * 2 + s
                    hb = c % 2
                    P.add("sp", lambda e, hb=hb, c=c, h0=h0: e.dma_start(out=hrow[hb][:], in_=hsrc[:, c, h0:h0 + HALF]),
                          reads=[("hT", c)], writes=[("hrow", hb)], dma=True)
                    for sub in range(HALF // SUB):
                        u0 = sub * SUB
                        pb = nps % 4
                        nps += 1
                        for kc in range(NFF):
                            P.add("pe", lambda e, b=b, s=s, pb=pb, kc=kc, u0=u0: e.matmul(
                                pso[pb][:, :SUB], wb[b][:, kc, s * 128:(s + 1) * 128], hres[:, kc, u0:u0 + SUB],
                                start=(kc == 0), stop=(kc == NFF - 1)),
                                reads=[("wb", b, kc // 22), ("hres", kc // 11)], writes=[("pso", pb)])
                        for (g0, gn, sidx) in segs(h0 + u0, SUB):
                            r0 = g0 - h0
                            P.add("dve", lambda e, hb=hb, pb=pb, r0=r0, gn=gn, u0=u0, c=c, sidx=sidx: e.scalar_tensor_tensor(
                                out=hrow[hb][:, r0:r0 + gn], in0=pso[pb][:, r0 - u0:r0 - u0 + gn], scalar=modv(k, 5, c, sidx),
                                in1=hrow[hb][:, r0:r0 + gn], op0=ALU.mult, op1=ALU.add),
                                reads=[("pso", pb), ("hrow", hb), "mod"], writes=[("hrow", hb)])
                    P.add("sp", lambda e, hb=hb, c=c, h0=h0: e.dma_start(out=hsrc[:, c, h0:h0 + HALF], in_=hrow[hb][:]),
                          reads=[("hrow", hb)], writes=[("hT", c)], dma=True)
    P.barrier()


DEBUG_OUT = ()


def build_program(plan, test_out_ctx=False):
    nc = bass.Bass("TRN2", target_bir_lowering=False)
    k = K()
    k.nc = nc
    P = Prog(nc)
    k.P = P
    def dt(name, shape, dtype, kind="Internal"):
        if name in DEBUG_OUT:
            kind = "ExternalOutput"
        return nc.dram_tensor(name, shape, dtype, kind=kind)
    k.h_in = dt("h_in", [D, NT], F32, kind="ExternalInput").ap()
    k.vecs_d = dt("vecs", [128, NV], F32, kind="ExternalInput").ap()
    k.consts_d = dt("consts", [128, NCONST], F32, kind="ExternalInput").ap()
    k.rope_d = dt("rope", [128, 2 * TLAT], F32, kind="ExternalInput").ap()
    k.w_mod = dt("w_mod", [DEPTH, 48, 128, 16, 256], F32, kind="ExternalInput").ap()
    k.ffn_w_up = dt("ffn_w_up", [DEPTH, NFF, 128, 16, 256], F32, kind="ExternalInput").ap()
    k.ffn_w_down = dt("ffn_w_down", [DEPTH, 8, 128, NFF, 256], F32, kind="ExternalInput").ap()
    k.even_w_in = dt("even_w_in", [2, 14, 128, 16, 256], F32, kind="ExternalInput").ap()
    k.even_w_out = dt("even_w_out", [2, 8, 128, 16, 256], F32, kind="ExternalInput").ap()
    k.dn_w_in = dt("dn_w_in", [2, 48, 128, 16, 256], F32, kind="ExternalInput").ap()
    k.dn_w_ba = dt("dn_w_ba", [2, 128, 16, 128], F32, kind="ExternalInput").ap()
    k.dn_w_out = dt("dn_w_out", [2, 8, 128, 32, 256], F32, kind="ExternalInput").ap()
    k.hT = dt("hT", [D, NT], F32, kind="ExternalOutput").ap()
    k.HF = dt("HF", [DFF, NT], BF16, kind="Internal").ap()
    k.QT = dt("QT", [8, 128, NT], BF16, kind="Internal").ap()
    k.KT = dt("KT", [2, 128, NT], BF16, kind="Internal").ap()
    k.VTM = dt("VTM", [18, 128, 256], BF16, kind="Internal").ap()
    k.CAT = dt("CAT", [16, 128, NT], BF16, kind="Internal").ap()
    k.DQT = dt("DQT", [16, 128, NT], BF16, kind="Internal").ap()
    k.DKT = dt("DKT", [16, 128, NT], BF16, kind="Internal").ap()
    k.DKTM = dt("DKTM", [18, 128, 2048], BF16, kind="Internal").ap()
    k.DVTM = dt("DVTM", [18, 128, 4096], BF16, kind="Internal").ap()
    k.DZ = dt("DZ", [18, 128, 4096], BF16, kind="Internal").ap()
    k.DPH = dt("DPH", [288, 4, 128, 4, 128], BF16, kind="Internal").ap()
    k.DO = dt("DO", [2, 18, 128, 4096], F32, kind="Internal").ap()
    k.DYT = dt("DYT", [32, 128, NT], BF16, kind="Internal").ap()
    k.DBG_G = dt("DBG_G", [5, 128, 18, 64], F32, kind="Internal").ap()

    k.vecs = P.sb([128, NV], F32)
    k.consts = P.sb([128, NCONST], F32)
    k.siluc = P.sb([128, 16, 2], BF16)
    k.mod = P.sb([128, 96, 2], F32)
    k.A1 = P.sb([128, 16, 2], F32)
    k.A2 = P.sb([128, 16, 2], F32)
    k.onesD = P.sb([128, 128], BF16)
    k.ones128 = P.sb([128, 128], BF16)
    k.ones1 = P.sb([128, 128], BF16)
    k.onesf128 = P.sb([128, 128], F32)
    k.bandb = P.sb([128, 384], BF16)
    k.identb = P.sb([128, 128], BF16)
    k.maskb = P.sb([128, 5, 128], BF16)

    P.add("sp", lambda e: e.dma_start(out=k.vecs[:], in_=k.vecs_d[:, :]), writes=["vecs"], dma=True)
    P.add("sp", lambda e: e.dma_start(out=k.consts[:], in_=k.consts_d[:, :]), writes=["consts"], dma=True)
    P.add("pool", lambda e: e.memset(k.onesD[:], 1.0 / D), writes=["onesD"])
    P.add("pool", lambda e: e.memset(k.ones128[:], 1.0 / 128), writes=["ones128"])
    P.add("pool", lambda e: e.memset(k.ones1[:], 1.0), writes=["ones1"])
    P.add("pool", lambda e: e.memset(k.onesf128[:], 1.0 / 128), writes=["onesf128"])
    P.add("dve", lambda e: e.tensor_copy(k.bandb[:], k.consts[:, COFF["band"][0]:COFF["band"][0] + 384]), reads=["consts"], writes=["bandb"])
    P.add("dve", lambda e: e.tensor_copy(k.maskb[:].rearrange("p q j -> p (q j)"), k.consts[:, COFF["mblk"][0]:COFF["mblk"][0] + 640]), reads=["consts"], writes=["maskb"])
    P.add("dve", lambda e: e.tensor_copy(k.identb[:], k.consts[:, COFF["ident"][0]:COFF["ident"][0] + 128]), reads=["consts"], writes=["identb"])
    co = VOFF["c"][0]
    P.add("act", lambda e: e.activation(out=k.siluc[:, :, 0], in_=k.vecs[:, co:co + 16], func=AF.Silu), reads=["vecs"], writes=["siluc"])
    co2 = VOFF["cctx"][0]
    P.add("act", lambda e: e.activation(out=k.siluc[:, :, 1], in_=k.vecs[:, co2:co2 + 16], func=AF.Silu), reads=["vecs", "siluc"], writes=["siluc"])
    with contextlib.ExitStack() as st:
        tmp = [P.sb([128, 4, NT], F32, st) for _ in range(2)]
        src = k.h_in.rearrange("(c p) t -> p c t", p=128)
        dst = k.hT.rearrange("(c p) t -> p c t", p=128)
        for q in range(4):
            b = q % 2
            P.add("sp", lambda e, b=b, q=q: e.dma_start(out=tmp[b][:], in_=src[:, q * 4:(q + 1) * 4, :]), writes=[("tmp", b)], dma=True)
            P.add("sp", lambda e, b=b, q=q: e.dma_start(out=dst[:, q * 4:(q + 1) * 4, :], in_=tmp[b][:]), reads=[("tmp", b)], writes=["hT"], dma=True)
    P.barrier()

    for (l, parts) in plan:
        stage_adaln(k, l)
        if "mix" in parts:
            if l % 2 == 0:
                stage_even(k, l)
            else:
                stage_dn(k, l)
        if "ffn" in parts:
            stage_ffn(k, l)
    P.finish()
    return nc


def stage_even(k, l):
    P = k.P
    with contextlib.ExitStack() as st0:
        nx = P.sb([128, 16, NT], BF16, st0)
        stage_norm(k, 1, nx)
        P.barrier()
        even_qkv(k, l, nx)
        P.barrier()
        even_glu(k, l, nx)
        P.barrier()
    even_attn(k, l)
    P.barrier()
    even_out(k, l)
    P.barrier()


def rsqrt_ops(P, out_ap, in_ap, reads, wkey):
    P.add("act", lambda e: e.activation(out=out_ap, in_=in_ap, func=AF.Sqrt, bias=EPS, scale=1.0), reads=reads, writes=[wkey])
    P.add("dve", lambda e: e.reciprocal(out_ap, out_ap), reads=[wkey], writes=[wkey])


def even_qkv(k, l, nx):
    P = k.P
    i = l // 2
    cos0 = 0; sin0 = TLAT; rp0 = COFF["ropeT"][0]
    with contextlib.ExitStack() as st:
        rope = P.sb([128, 2 * TLAT], F32, st)
        P.add("sp", lambda e: e.dma_start(out=rope[:], in_=k.rope_d[:, :]), writes=["rope"], dma=True)
        wb = [P.sb([128, 16, 256], BF16, st) for _ in range(2)]
        psa = [P.ps([128, 512], F32, st) for _ in range(2)]
        psb = P.ps([128, 512], F32, st)
        psc = P.ps([128, 512], F32, st)
        x0 = [P.sb([128, 512], F32, st) for _ in range(2)]
        sqb = P.sb([128, 512], BF16, st)
        rr = P.sb([128, 512], F32, st)
        qn = P.sb([128, 512], F32, st)
        t1 = P.sb([128, 512], F32, st)
        t2 = P.sb([128, 512], F32, st)
        qo = [P.sb([128, NT], BF16, st) for _ in range(2)]
        vt = P.sb([128, 18, 256], BF16, st)
        gs = P.sb([128, 2], F32, st)
        qg0 = VOFF["qg"][0] + i; kg0 = VOFF["kg"][0] + i
        P.add("act", lambda e: e.mul(gs[:, 0:1], k.vecs[:, qg0:qg0 + 1], 128 ** -0.5), writes=["gs"])
        P.add("act", lambda e: e.copy(gs[:, 1:2], k.vecs[:, kg0:kg0 + 1]), reads=["gs"], writes=["gs"])
        nblk = 0
        nchunk = 0
        npa = 0
        for blk in range(5):
            b = nblk % 2; nblk += 1
            P.add("pool", lambda e, b=b, blk=blk: e.dma_start(out=wb[b][:], in_=k.even_w_in[i, blk]),
                  writes=[("wb", b)], dma=True)
            for s in range(2):
                ch = blk * 2 + s
                isq = ch < 8
                qb = nchunk % 2; nchunk += 1
                gcol = gs[:, 0:1] if isq else gs[:, 1:2]
                for ti, (t0, n) in enumerate(TILES):
                    pb = npa % 2; npa += 1
                    for kc in range(16):
                        P.add("pe", lambda e, b=b, s=s, pb=pb, kc=kc, t0=t0, n=n: e.matmul(
                            psa[pb][:, :n], wb[b][:, kc, s * 128:(s + 1) * 128], nx[:, kc, t0:t0 + n], start=(kc == 0), stop=(kc == 15)),
                            reads=[("wb", b)], writes=[("psa", pb)])
                    P.add("act", lambda e, pb=pb, n=n: e.activation(out=x0[pb][:, :n], in_=psa[pb][:, :n], func=AF.Copy),
                          reads=[("psa", pb)], writes=[("x0", pb)])
                    P.add("act", lambda e, pb=pb, n=n: e.activation(out=sqb[:, :n], in_=psa[pb][:, :n], func=AF.Square),
                          reads=[("psa", pb)], writes=["sqb"])
                    P.add("pe", lambda e, n=n: e.matmul(psb[:, :n], k.ones128[:], sqb[:, :n], start=True, stop=True),
                          reads=["sqb"], writes=["psb"])
                    rsqrt_ops(P, rr[:, :n], psb[:, :n], ["psb"], "rr")
                    if t0 < LCTX:
                        P.add("dve", lambda e, pb=pb, qb=qb, n=n, t0=t0, gcol=gcol: e.scalar_tensor_tensor(
                            out=qo[qb][:, t0:t0 + n], in0=x0[pb][:, :n], scalar=gcol, in1=rr[:, :n], op0=ALU.mult, op1=ALU.mult),
                            reads=[("x0", pb), "rr", "gs"], writes=[("qo", qb, ti)])
                    else:
                        P.add("dve", lambda e, pb=pb, n=n, gcol=gcol: e.scalar_tensor_tensor(
                            out=qn[:, :n], in0=x0[pb][:, :n], scalar=gcol, in1=rr[:, :n], op0=ALU.mult, op1=ALU.mult),
                            reads=[("x0", pb), "rr", "gs"], writes=["qn"])
                        P.add("pe", lambda e, n=n: e.matmul(psc[:, :n], k.consts[:, rp0:rp0 + 128], qn[:, :n], start=True, stop=True),
                              reads=["qn"], writes=["psc"])
                        c0 = cos0 + t0 - LCTX; s0 = sin0 + t0 - LCTX
                        P.add("dve", lambda e, n=n, c0=c0: e.tensor_tensor(out=t1[:, :n], in0=qn[:, :n], in1=rope[:, c0:c0 + n], op=ALU.mult),
                              reads=["qn", "rope"], writes=["t1"])
                        P.add("dve", lambda e, n=n, s0=s0: e.tensor_tensor(out=t2[:, :n], in0=psc[:, :n], in1=rope[:, s0:s0 + n], op=ALU.mult),
                              reads=["psc", "rope"], writes=["t2"])
                        P.add("pool", lambda e, qb=qb, n=n, t0=t0: e.tensor_tensor(out=qo[qb][:, t0:t0 + n], in0=t1[:, :n], in1=t2[:, :n], op=ALU.add),
                              reads=["t1", "t2"], writes=[("qo", qb, ti)])
                dst = k.QT[ch] if isq else k.KT[ch - 8]
                P.add("sp", lambda e, qb=qb, dst=dst: e.dma_start(out=dst, in_=qo[qb][:]),
                      reads=[("qo", qb, ti) for ti in range(5)], writes=[("qkT", ch)], dma=True)
        b = nblk % 2; nblk += 1
        P.add("pool", lambda e, b=b: e.dma_start(out=wb[b][:], in_=k.even_w_in[i, 5]), writes=[("wb", b)], dma=True)
        for tt in range(18):
            pb = npa % 2; npa += 1
            for kc in range(16):
                P.add("pe", lambda e, b=b, pb=pb, kc=kc, tt=tt: e.matmul(
                    psa[pb][:, :256], nx[:, kc, tt * 128:(tt + 1) * 128], wb[b][:, kc, :], start=(kc == 0), stop=(kc == 15)),
                    reads=[("wb", b)], writes=[("psa", pb)])
            P.add("act", lambda e, pb=pb, tt=tt: e.activation(out=vt[:, tt, :], in_=psa[pb][:, :256], func=AF.Copy),
                  reads=[("psa", pb)], writes=[("vt", tt)])
        P.add("sp", lambda e: e.dma_start(out=k.VTM.rearrange("t p c -> p t c"), in_=vt[:]),
              reads=[("vt", tt) for tt in range(18)], writes=["VTM"], dma=True)


def even_glu(k, l, nx):
    P = k.P
    i = l // 2
    cdw = VOFF["cdw"][0] + i * 31 * 8
    cdb = VOFF["cdb"][0] + i * 8
    lng = VOFF["lng"][0] + i * 8
    lnb = VOFF["lnb"][0] + i * 8
    NDVE = 31
    with contextlib.ExitStack() as st:
        wb = [P.sb([128, 16, 256], BF16, st) for _ in range(2)]
        psa = [P.ps([128, 512], F32, st) for _ in range(2)]
        psb = [P.ps([128, 512], F32, st) for _ in range(2)]
        psm = P.ps([128, 512], F32, st)
        psq = P.ps([128, 512], F32, st)
        sgm = [P.sb([128, 512], F32, st) for _ in range(2)]
        U = [P.sb([128, NT], F32, st) for _ in range(2)]
        accA = P.sb([128, NT], F32, st)
        accB = P.sb([128, NT], F32, st)
        hsq = P.sb([128, NT], F32, st)
        co = [P.sb([128, NT], BF16, st) for _ in range(2)]
        msb = P.sb([128, 512], F32, st)
        m2 = P.sb([128, 512], F32, st)
        var = P.sb([128, 512], F32, st)
        dd = P.sb([128, 512], F32, st)
        npa = 0
        for j in range(8):
            b = j % 2
            P.add("pool", lambda e, b=b, j=j: e.dma_start(out=wb[b][:], in_=k.even_w_in[i, 6 + j]),
                  writes=[("wb", b, 0), ("wb", b, 1)], dma=True)
            for ti, (t0, n) in enumerate(TILES):
                pb = npa % 2; npa += 1
                for kc in range(16):
                    P.add("pe", lambda e, b=b, pb=pb, kc=kc, t0=t0, n=n: e.matmul(
                        psa[pb][:, :n], wb[b][:, kc, 0:128], nx[:, kc, t0:t0 + n], start=(kc == 0), stop=(kc == 15)),
                        reads=[("wb", b, 0)], writes=[("psa", pb)])
                for kc in range(16):
                    P.add("pe", lambda e, b=b, pb=pb, kc=kc, t0=t0, n=n: e.matmul(
                        psb[pb][:, :n], wb[b][:, kc, 128:256], nx[:, kc, t0:t0 + n], start=(kc == 0), stop=(kc == 15)),
                        reads=[("wb", b, 1)], writes=[("psb", pb)])
                P.add("act", lambda e, pb=pb, n=n: e.activation(out=sgm[pb][:, :n], in_=psb[pb][:, :n], func=AF.Sigmoid),
                      reads=[("psb", pb)], writes=[("sgm", pb)])
                P.add("dve", lambda e, b=b, pb=pb, t0=t0, n=n: e.tensor_tensor(out=U[b][:, t0:t0 + n], in0=psa[pb][:, :n], in1=sgm[pb][:, :n], op=ALU.mult),
                      reads=[("psa", pb), ("sgm", pb)], writes=[("U", b, ti)])
            Uk = [("U", b, ti) for ti in range(5)]
            wc = lambda kk, j=j: k.vecs[:, cdw + kk * 8 + j:cdw + kk * 8 + j + 1]
            bias = k.vecs[:, cdb + j:cdb + j + 1]
            P.add("dve", lambda e, b=b, w=wc(15), bias=bias: e.tensor_scalar(out=accA[:], in0=U[b][:], scalar1=w, scalar2=bias,
                                                                            op0=ALU.mult, op1=ALU.add), reads=Uk, writes=["accA"])
            P.add("pool", lambda e: e.memset(accB[:], 0.0), writes=["accB"])
            taps = [kk for kk in range(31) if kk != 15]
            dve_taps = taps[:NDVE - 1]
            for kk in taps:
                sh = kk - 15
                eng, acc, akey = ("dve", accA, "accA") if kk in dve_taps else ("pool", accB, "accB")
                for (s0, sn) in ((0, LCTX), (LCTX, TLAT)):
                    lo = max(0, -sh); hi = sn - max(0, sh)
                    P.add(eng, lambda e, b=b, acc=acc, s0=s0, lo=lo, hi=hi, sh=sh, w=wc(kk): e.scalar_tensor_tensor(
                        out=acc[:, s0 + lo:s0 + hi], in0=U[b][:, s0 + lo + sh:s0 + hi + sh], scalar=w, in1=acc[:, s0 + lo:s0 + hi],
                        op0=ALU.mult, op1=ALU.add), reads=Uk + [akey], writes=[akey])
            P.add("dve", lambda e: e.tensor_tensor(out=accA[:], in0=accA[:], in1=accB[:], op=ALU.add), reads=["accA", "accB"], writes=["accA"])
            P.add("act", lambda e: e.activation(out=hsq[:], in_=accA[:], func=AF.Square), reads=["accA"], writes=["hsq"])
            for ti, (t0, n) in enumerate(TILES):
                P.add("pe", lambda e, t0=t0, n=n: e.matmul(psm[:, :n], k.onesf128[:], accA[:, t0:t0 + n], start=True, stop=True),
                      reads=["accA"], writes=["psm"])
                P.add("pe", lambda e, t0=t0, n=n: e.matmul(psq[:, :n], k.onesf128[:], hsq[:, t0:t0 + n], start=True, stop=True),
                      reads=["hsq"], writes=["psq"])
                P.add("act", lambda e, n=n: e.activation(out=msb[:, :n], in_=psm[:, :n], func=AF.Copy), reads=["psm"], writes=["msb"])
                P.add("act", lambda e, n=n: e.activation(out=m2[:, :n], in_=psm[:, :n], func=AF.Square), reads=["psm"], writes=["m2"])
                P.add("dve", lambda e, n=n: e.tensor_tensor(out=var[:, :n], in0=psq[:, :n], in1=m2[:, :n], op=ALU.subtract),
                      reads=["psq", "m2"], writes=["var"])
                P.add("dve", lambda e, n=n: e.tensor_scalar_max(out=var[:, :n], in0=var[:, :n], scalar1=0.0), reads=["var"], writes=["var"])
                rsqrt_ops(P, var[:, :n], var[:, :n], ["var"], "var")
                P.add("dve", lambda e, t0=t0, n=n: e.tensor_tensor(out=dd[:, :n], in0=accA[:, t0:t0 + n], in1=msb[:, :n], op=ALU.subtract),
                      reads=["accA", "msb"], writes=["dd"])
                P.add("dve", lambda e, n=n: e.tensor_tensor(out=dd[:, :n], in0=dd[:, :n], in1=var[:, :n], op=ALU.mult),
                      reads=["dd", "var"], writes=["dd"])
                P.add("act", lambda e, b=b, j=j, t0=t0, n=n: e.activation(
                    out=co[b][:, t0:t0 + n], in_=dd[:, :n], func=AF.Silu,
                    bias=k.vecs[:, lnb + j:lnb + j + 1], scale=k.vecs[:, lng + j:lng + j + 1]),
                    reads=["dd"], writes=[("co", b, ti)])
            P.add("sp", lambda e, b=b, j=j: e.dma_start(out=k.CAT[8 + j], in_=co[b][:]),
                  reads=[("co", b, ti) for ti in range(5)], writes=[("CAT", 8 + j)], dma=True)


def even_attn(k, l):
    P = k.P
    i = l // 2
    band0 = COFF["band"][0]
    with contextlib.ExitStack() as st:
        kT = P.sb([128, 2, NT], BF16, st)
        V = P.sb([128, 18, 256], BF16, st)
        qh = [P.sb([128, NT], BF16, st) for _ in range(2)]
        PC = [[P.sb([128, NT], BF16, st) for _ in range(2)] for _ in range(2)]
        PB = [P.sb([128, 16, 384], BF16, st) for _ in range(2)]
        tmpE = [P.sb([128, 384], BF16, st) for _ in range(2)]
        ao = [P.sb([128, NT], BF16, st) for _ in range(2)]
        den = P.sb([128, 512], F32, st)
        esink = P.sb([128, 8], F32, st)
        pss = [P.ps([128, 512], F32, st) for _ in range(2)]
        pso = [P.ps([128, 512], F32, st) for _ in range(2)]
        psd = [P.ps([128, 512], F32, st) for _ in range(2)]
        so = VOFF["sink"][0] + i * 8
        P.add("act", lambda e: e.activation(out=esink[:], in_=k.vecs[:, so:so + 8], func=AF.Exp), writes=["esink"])
        for g in range(2):
            P.add("sp", lambda e, g=g: e.dma_start(out=kT[:, g, :], in_=k.KT[g]), writes=[("kT", g)], dma=True)
        P.add("sp", lambda e: e.dma_start(out=V[:], in_=k.VTM.rearrange("t p c -> p t c")), writes=["V"], dma=True)
        nps = 0
        npo = 0
        for h in range(8):
            g = h // 4
            hb = h % 2
            P.add("sp", lambda e, hb=hb, h=h: e.dma_start(out=qh[hb][:], in_=k.QT[h]), writes=[("qh", hb)], dma=True)
            for cb in range(2):
                for ti, (t0, n) in enumerate(TILES):
                    pb = nps % 2; nps += 1
                    P.add("pe", lambda e, pb=pb, g=g, cb=cb, hb=hb, t0=t0, n=n: e.matmul(
                        pss[pb][:, :n], kT[:, g, cb * 128:(cb + 1) * 128], qh[hb][:, t0:t0 + n], start=True, stop=True),
                        reads=[("kT", g), ("qh", hb)], writes=[("pss", pb)])
                    P.add("act", lambda e, pb=pb, hb=hb, cb=cb, t0=t0, n=n: e.activation(
                        out=PC[hb][cb][:, t0:t0 + n], in_=pss[pb][:, :n], func=AF.Exp),
                        reads=[("pss", pb)], writes=[("PC", hb, cb, ti)])
            for jb in range(16):
                lo = max(jb - 1, 0); hi = min(jb + 1, 15)
                n = (hi - lo + 1) * 128
                off = (lo - (jb - 1)) * 128
                pb = nps % 2; nps += 1
                eb = jb % 2
                P.add("pe", lambda e, pb=pb, g=g, jb=jb, hb=hb, lo=lo, n=n: e.matmul(
                    pss[pb][:, :n], kT[:, g, LCTX + jb * 128:LCTX + (jb + 1) * 128], qh[hb][:, LCTX + lo * 128:LCTX + lo * 128 + n],
                    start=True, stop=True), reads=[("kT", g), ("qh", hb)], writes=[("pss", pb)])
                P.add("act", lambda e, pb=pb, eb=eb, n=n: e.activation(out=tmpE[eb][:, :n], in_=pss[pb][:, :n], func=AF.Exp),
                      reads=[("pss", pb)], writes=[("tmpE", eb)])
                P.add("pool", lambda e, eb=eb, hb=hb, jb=jb, off=off, n=n: e.tensor_tensor(
                    out=PB[hb][:, jb, off:off + n], in0=tmpE[eb][:, :n], in1=k.bandb[:, off:off + n], op=ALU.mult),
                    reads=[("tmpE", eb)], writes=[("PB", hb, jb)])
            for ti, (t0, n) in enumerate(TILES):
                ob = npo % 2; npo += 1
                mms = []
                for cb in range(2):
                    mms.append((V[:, cb, g * 128:(g + 1) * 128], PC[hb][cb][:, t0:t0 + n], 0, n, [("PC", hb, cb, ti)]))
                if t0 >= LCTX:
                    qt = (t0 - LCTX) // 512
                    for nb in range(4 * qt, 4 * qt + 4):
                        for jb in (nb - 1, nb, nb + 1):
                            if 0 <= jb <= 15:
                                c0 = (nb - (jb - 1)) * 128
                                mms.append((V[:, 2 + jb, g * 128:(g + 1) * 128], PB[hb][:, jb, c0:c0 + 128], (nb - 4 * qt) * 128, 128,
                                            [("PB", hb, jb)]))
                for mi, (lhs, rhs, o0, on, rk) in enumerate(mms):
                    P.add("pe", lambda e, ob=ob, lhs=lhs, rhs=rhs, o0=o0, on=on, mi=mi, last=(mi == len(mms) - 1): e.matmul(
                        pso[ob][:, o0:o0 + on], lhs, rhs, start=(mi == 0), stop=last, skip_group_check=True),
                        reads=rk + ["V"], writes=[("pso", ob)])
                for mi, (lhs, rhs, o0, on, rk) in enumerate(mms):
                    P.add("pe", lambda e, ob=ob, rhs=rhs, o0=o0, on=on, mi=mi, last=(mi == len(mms) - 1): e.matmul(
                        psd[ob][:, o0:o0 + on], k.ones1[:], rhs, start=(mi == 0), stop=last, skip_group_check=True),
                        reads=rk, writes=[("psd", ob)])
                P.add("dve", lambda e, ob=ob, n=n, h=h: e.tensor_scalar_add(out=den[:, :n], in0=psd[ob][:, :n], scalar1=esink[:, h:h + 1]),
                      reads=[("psd", ob), "esink"], writes=["den"])
                P.add("dve", lambda e, n=n: e.reciprocal(den[:, :n], den[:, :n]), reads=["den"], writes=["den"])
                P.add("dve", lambda e, ob=ob, hb=hb, t0=t0, n=n: e.tensor_tensor(out=ao[hb][:, t0:t0 + n], in0=pso[ob][:, :n], in1=den[:, :n], op=ALU.mult),
                      reads=[("pso", ob), "den"], writes=[("ao", hb, ti)])
            P.add("sp", lambda e, hb=hb, h=h: e.dma_start(out=k.CAT[h], in_=ao[hb][:]),
                  reads=[("ao", hb, ti) for ti in range(5)], writes=[("CAT", h)], dma=True)


def proj_out_residual(k, wsrc, nkc, cat, gate_m, wkeys_extra=()):
    P = k.P
    hsrc = k.hT.rearrange("(c p) t -> p c t", p=128)
    with contextlib.ExitStack() as st:
        wb = [P.sb([128, nkc, 256], BF16, st) for _ in range(2)]
        hrow = [P.sb([128, NT], F32, st) for _ in range(2)]
        pso = [P.ps([128, 512], F32, st) for _ in range(4)]
        nps = 0
        for blk in range(8):
            b = blk % 2
            P.add("pool", lambda e, b=b, blk=blk: e.dma_start(out=wb[b][:], in_=wsrc(blk)),
                  writes=[("wb", b)], dma=True)
            for s in range(2):
                c = blk * 2 + s
                hb = c % 2
                P.add("sp", lambda e, hb=hb, c=c: e.dma_start(out=hrow[hb][:], in_=hsrc[:, c, :]),
                      reads=[("hT", c)], writes=[("hrow", hb)], dma=True)
                for ti, (t0, n) in enumerate(TILES):
                    pb = nps % 4; nps += 1
                    for kc in range(nkc):
                        P.add("pe", lambda e, b=b, s=s, pb=pb, kc=kc, t0=t0, n=n: e.matmul(
                            pso[pb][:, :n], wb[b][:, kc, s * 128:(s + 1) * 128], cat[:, kc, t0:t0 + n],
                            start=(kc == 0), stop=(kc == nkc - 1)),
                            reads=[("wb", b), ("cat", kc)], writes=[("pso", pb)])
                    sidx = 1 if t0 < LCTX else 0
                    P.add("dve", lambda e, hb=hb, pb=pb, t0=t0, n=n, c=c, sidx=sidx: e.scalar_tensor_tensor(
                        out=hrow[hb][:, t0:t0 + n], in0=pso[pb][:, :n], scalar=modv(k, gate_m, c, sidx),
                        in1=hrow[hb][:, t0:t0 + n], op0=ALU.mult, op1=ALU.add),
                        reads=[("pso", pb), ("hrow", hb), "mod"], writes=[("hrow", hb)])
                P.add("sp", lambda e, hb=hb, c=c: e.dma_start(out=hsrc[:, c, :], in_=hrow[hb][:]),
                      reads=[("hrow", hb)], writes=[("hT", c)], dma=True)


def even_out(k, l):
    P = k.P
    i = l // 2
    wsrc = lambda blk: k.even_w_out[i, blk]
    with contextlib.ExitStack() as st:
        cat = P.sb([128, 16, NT], BF16, st)
        for c in range(16):
            P.add("sp", lambda e, c=c: e.dma_start(out=cat[:, c, :], in_=k.CAT[c]), writes=[("cat", c)], dma=True)
        proj_out_residual(k, wsrc, 16, cat, 2)


DN_STOP = 99


def stage_dn(k, l):
    P = k.P
    with contextlib.ExitStack() as st1:
        k.BETA = P.sb([128, 18, 64], F32, st1)
        k.GG = P.sb([128, 18, 64], F32, st1)
        k.EG = P.sb([128, 18, 64], F32, st1)
        k.EL = P.sb([128, 18, 64], F32, st1)
        k.BEG = P.sb([128, 18, 64], F32, st1)
        k.ER = P.sb([128, 18, 64], F32, st1)
        with contextlib.ExitStack() as st0:
            nx = P.sb([128, 16, NT], BF16, st0)
            stage_norm(k, 1, nx, nbuf=1)
            P.barrier()
            if DN_STOP >= 1:
                dn_proj_qkv(k, l, nx)
                P.barrier()
            if DN_STOP >= 2:
                dn_proj_z(k, l, nx)
                P.barrier()
            if DN_STOP >= 3:
                dn_proj_ba(k, l, nx)
                P.barrier()
        if DN_STOP >= 4:
            dn_gates(k, l)
            P.barrier()
            if "DBG_G" in DEBUG_OUT:
                for ii, t in enumerate((k.BETA, k.GG, k.EG, k.EL, k.ER)):
                    P.add("sp", lambda e, ii=ii, t=t: e.dma_start(out=k.DBG_G[ii], in_=t[:]), dma=True)
                P.barrier()
        if DN_STOP >= 5:
            dn_prep(k, l)
            P.barrier()
        if DN_STOP >= 6:
            dn_scan(k, l)
            P.barrier()
    if DN_STOP >= 7:
        dn_outnorm(k, l)
        P.barrier()
    if DN_STOP >= 8:
        dn_outproj(k, l)
        P.barrier()


def dn_proj_qkv(k, l, nx):
    P = k.P
    i = l // 2
    dcw = VOFF["dcw"][0] + i * 5 * 64
    with contextlib.ExitStack() as st:
        wb = [P.sb([128, 16, 256], BF16, st) for _ in range(2)]
        psa = [P.ps([128, 512], F32, st) for _ in range(2)]
        psb = P.ps([128, 512], F32, st)
        pst = [P.ps([128, 4, 128], BF16, st) for _ in range(2)]
        U = [P.sb([128, NT], F32, st) for _ in range(2)]
        accs = [P.sb([128, NT], F32, st) for _ in range(2)]
        sq = P.sb([128, NT], BF16, st)
        rr = P.sb([128, 512], F32, st)
        qo = [P.sb([128, NT], BF16, st) for _ in range(2)]
        tm = [P.sb([128, 18, 128], BF16, st)] * 2
        npa = 0
        npt = 0
        for blk in range(32):
            b = blk % 2
            P.add("pool", lambda e, b=b, blk=blk: e.dma_start(out=wb[b][:], in_=k.dn_w_in[i, blk]),
                  writes=[("wb", b)], dma=True)
            for s in range(2):
                ch = blk * 2 + s
                ub = ch % 2
                acc = accs[ch % 2]
                ak = ("acc", ch % 2)
                kind = "q" if ch < 16 else ("k" if ch < 32 else "v")
                for ti, (t0, n) in enumerate(TILES):
                    pb = npa % 2; npa += 1
                    for kc in range(16):
                        P.add("pe", lambda e, b=b, s=s, pb=pb, kc=kc, t0=t0, n=n: e.matmul(
                            psa[pb][:, :n], wb[b][:, kc, s * 128:(s + 1) * 128], nx[:, kc, t0:t0 + n], start=(kc == 0), stop=(kc == 15)),
                            reads=[("wb", b)], writes=[("psa", pb)])
                    P.add("act", lambda e, ub=ub, pb=pb, t0=t0, n=n: e.activation(out=U[ub][:, t0:t0 + n], in_=psa[pb][:, :n], func=AF.Copy),
                          reads=[("psa", pb)], writes=[("U", ub, ti)])
                Uk = [("U", ub, ti) for ti in range(5)]
                wc = lambda kk, ch=ch: k.vecs[:, dcw + kk * 64 + ch:dcw + kk * 64 + ch + 1]
                P.add("dve", lambda e, acc=acc, ub=ub, w=wc(2): e.tensor_scalar(out=acc[:], in0=U[ub][:], scalar1=w, scalar2=None, op0=ALU.mult),
                      reads=Uk, writes=[ak])
                for kk in (0, 1, 3, 4):
                    sh = kk - 2
                    for (s0, sn) in ((0, LCTX), (LCTX, TLAT)):
                        lo = max(0, -sh); hi = sn - max(0, sh)
                        P.add("dve", lambda e, acc=acc, ub=ub, s0=s0, lo=lo, hi=hi, sh=sh, w=wc(kk): e.scalar_tensor_tensor(
                            out=acc[:, s0 + lo:s0 + hi], in0=U[ub][:, s0 + lo + sh:s0 + hi + sh], scalar=w, in1=acc[:, s0 + lo:s0 + hi],
                            op0=ALU.mult, op1=ALU.add), reads=Uk + [ak], writes=[ak])
                qb = ch % 2
                if kind == "v":
                    P.add("act", lambda e, acc=acc, qb=qb: e.activation(out=qo[qb][:], in_=acc[:], func=AF.Silu), reads=[ak], writes=[("qo", qb)])
                else:
                    P.add("act", lambda e, acc=acc: e.activation(out=acc[:], in_=acc[:], func=AF.Silu), reads=[ak], writes=[ak])
                    P.add("act", lambda e, acc=acc: e.activation(out=sq[:], in_=acc[:], func=AF.Square), reads=[ak], writes=["sq"])
                    for ti, (t0, n) in enumerate(TILES):
                        P.add("pe", lambda e, t0=t0, n=n: e.matmul(psb[:, :n], k.ones1[:], sq[:, t0:t0 + n], start=True, stop=True),
                              reads=["sq"], writes=["psb"])
                        rsqrt_ops(P, rr[:, :n], psb[:, :n], ["psb"], "rr")
                        sc = (128 ** -0.5) if kind == "q" else 1.0
                        P.add("dve", lambda e, acc=acc, qb=qb, t0=t0, n=n, sc=sc: e.scalar_tensor_tensor(
                            out=qo[qb][:, t0:t0 + n], in0=acc[:, t0:t0 + n], scalar=sc, in1=rr[:, :n], op0=ALU.mult, op1=ALU.mult),
                            reads=[ak, "rr"], writes=[("qo", qb)])
                if kind == "q":
                    P.add("sp", lambda e, qb=qb, ch=ch: e.dma_start(out=k.DQT[ch], in_=qo[qb][:]), reads=[("qo", qb)], writes=[("DQT", ch)], dma=True)
                else:
                    if kind == "k":
                        P.add("sp", lambda e, qb=qb, ch=ch: e.dma_start(out=k.DKT[ch - 16], in_=qo[qb][:]), reads=[("qo", qb)],
                              writes=[("DKT", ch)], dma=True)
                    for c4 in range(0, 18, 4):
                        nn = min(4, 18 - c4)
                        tb = npt % 2; npt += 1
                        for cc in range(nn):
                            c = c4 + cc
                            P.add("pe", lambda e, tb=tb, cc=cc, c=c, qb=qb: e.transpose(
                                out=pst[tb][:, cc, :], in_=qo[qb][:, c * 128:(c + 1) * 128], identity=k.identb[:]),
                                reads=[("qo", qb)], writes=[("pst", tb)])
                        P.add("act", lambda e, tb=tb, qb=qb, c4=c4, nn=nn: e.activation(out=tm[qb][:, c4:c4 + nn, :], in_=pst[tb][:, :nn, :], func=AF.Copy),
                              reads=[("pst", tb)], writes=[("tm", 0)])
                    if kind == "k":
                        dst = k.DKTM[:, :, (ch - 16) * 128:(ch - 15) * 128]
                    else:
                        dst = k.DVTM[:, :, (ch - 32) * 128:(ch - 31) * 128]
                    P.add("sp", lambda e, qb=qb, dst=dst: e.dma_start(out=dst.rearrange("c p d -> p c d"), in_=tm[qb][:]),
                          reads=[("tm", 0)], writes=[("DTM", ch)], dma=True)


def dn_proj_z(k, l, nx):
    P = k.P
    i = l // 2
    with contextlib.ExitStack() as st:
        wb = [P.sb([128, 16, 512], BF16, st) for _ in range(2)]
        psa = [P.ps([128, 512], F32, st) for _ in range(4)]
        zs = [P.sb([128, 512], BF16, st) for _ in range(4)]
        npa = 0
        for blk in range(8):
            b = blk % 2
            for q in range(2):
                P.add("pool", lambda e, b=b, blk=blk, q=q: e.dma_start(
                    out=wb[b][:, :, q * 256:(q + 1) * 256], in_=k.dn_w_in[i, 32 + blk * 2 + q]),
                    writes=[("wb", b, q)], dma=True)
            for c in range(18):
                pb = npa % 4; npa += 1
                for kc in range(16):
                    P.add("pe", lambda e, b=b, pb=pb, kc=kc, c=c: e.matmul(
                        psa[pb][:], nx[:, kc, c * 128:(c + 1) * 128], wb[b][:, kc, :], start=(kc == 0), stop=(kc == 15)),
                        reads=[("wb", b, 0), ("wb", b, 1)], writes=[("psa", pb)])
                P.add("act", lambda e, pb=pb: e.activation(out=zs[pb][:], in_=psa[pb][:], func=AF.Silu),
                      reads=[("psa", pb)], writes=[("zs", pb)])
                P.add("sp", lambda e, pb=pb, c=c, blk=blk: e.dma_start(out=k.DZ[c, :, blk * 512:(blk + 1) * 512], in_=zs[pb][:]),
                      reads=[("zs", pb)], writes=[("DZ", c, blk)], dma=True)


def dn_proj_ba(k, l, nx):
    P = k.P
    i = l // 2
    with contextlib.ExitStack() as st:
        wb = P.sb([128, 16, 128], BF16, st)
        BA = P.sb([128, 18, 128], F32, st)
        psa = [P.ps([128, 512], F32, st) for _ in range(2)]
        P.add("pool", lambda e: e.dma_start(out=wb[:], in_=k.dn_w_ba[i]), writes=["wb"], dma=True)
        for c in range(18):
            pb = c % 2
            for kc in range(16):
                P.add("pe", lambda e, pb=pb, kc=kc, c=c: e.matmul(
                    psa[pb][:, 0:128], nx[:, kc, c * 128:(c + 1) * 128], wb[:, kc, :], start=(kc == 0), stop=(kc == 15)),
                    reads=["wb"], writes=[("psa", pb)])
            P.add("act", lambda e, pb=pb, c=c: e.activation(out=BA[:, c, :], in_=psa[pb][:, 0:128], func=AF.Copy),
                  reads=[("psa", pb)], writes=[("BA", c)])
        BAk = [("BA", c) for c in range(18)]
        P.add("act", lambda e: e.activation(out=k.BETA[:], in_=BA[:, :, 0:64], func=AF.Sigmoid), reads=BAk, writes=["BETA"])
        P.add("dve", lambda e: e.tensor_copy(k.GG[:], BA[:, :, 64:128]), reads=BAk, writes=["GG"])
        P.barrier()


def dn_gates(k, l):
    P = k.P
    i = l // 2
    al = VOFF["alog"][0] + i * 64
    db = VOFF["dtb"][0] + i * 64
    with contextlib.ExitStack() as st:
        x = P.sb([128, 18, 64], F32, st)
        ax = P.sb([128, 18, 64], F32, st)
        nA = P.sb([128, 64], F32, st)
        onesf = P.sb([128, 128], F32, st)
        ps = [P.ps([128, 3, 64], F32, st) for _ in range(2)]
        P.add("pool", lambda e: e.memset(onesf[:], 1.0), writes=["onesf"])
        P.add("dve", lambda e: e.tensor_tensor(out=x[:], in0=k.GG[:], in1=bc(k.vecs[:, db:db + 64].unsqueeze(1), [128, 18, 64]), op=ALU.add),
              writes=["x"])
        P.add("act", lambda e: e.activation(out=ax[:], in_=x[:], func=AF.Abs), reads=["x"], writes=["ax"])
        P.add("act", lambda e: e.activation(out=ax[:], in_=ax[:], func=AF.Exp, scale=-1.0), reads=["ax"], writes=["ax"])
        P.add("act", lambda e: e.activation(out=ax[:], in_=ax[:], func=AF.Ln, bias=1.0), reads=["ax"], writes=["ax"])
        P.add("dve", lambda e: e.tensor_scalar_max(out=x[:], in0=x[:], scalar1=0.0), reads=["x"], writes=["x"])
        P.add("dve", lambda e: e.tensor_tensor(out=x[:], in0=x[:], in1=ax[:], op=ALU.add), reads=["x", "ax"], writes=["x"])
        P.add("act", lambda e: e.activation(out=nA[:], in_=k.vecs[:, al:al + 64], func=AF.Exp), writes=["nA"])
        P.add("dve", lambda e: e.scalar_tensor_tensor(out=k.GG[:], in0=x[:], scalar=-1.0, in1=bc(nA[:].unsqueeze(1), [128, 18, 64]),
                                                      op0=ALU.mult, op1=ALU.mult), reads=["x", "nA"], writes=["GG"])
        mo = {n: COFF[n][0] for n in ("m_le", "m_ge", "m_lt", "m_gt")}
        for c in range(18):
            pb = c % 2
            for d in range(2):
                m_gc = mo["m_le"] if d == 0 else mo["m_ge"]
                m_rm = mo["m_gt"] if d == 0 else mo["m_lt"]
                rhs = k.GG[:, c, d * 32:(d + 1) * 32]
                P.add("pe", lambda e, pb=pb, d=d, m=m_gc, rhs=rhs: e.matmul(ps[pb][:, 0, d * 32:(d + 1) * 32], k.consts[:, m:m + 128], rhs, start=True, stop=True),
                      reads=["GG"], writes=[("ps", pb)])
                P.add("pe", lambda e, pb=pb, d=d, m=m_rm, rhs=rhs: e.matmul(ps[pb][:, 1, d * 32:(d + 1) * 32], k.consts[:, m:m + 128], rhs, start=True, stop=True),
                      reads=["GG"], writes=[("ps", pb)])
                P.add("pe", lambda e, pb=pb, d=d, rhs=rhs: e.matmul(ps[pb][:, 2, d * 32:(d + 1) * 32], onesf[:], rhs, start=True, stop=True),
                      reads=["GG", "onesf"], writes=[("ps", pb)])
            P.add("act", lambda e, pb=pb, c=c: e.activation(out=k.EG[:, c, :], in_=ps[pb][:, 0, :], func=AF.Exp), reads=[("ps", pb)], writes=[("EG", c)])
            P.add("act", lambda e, pb=pb, c=c: e.activation(out=k.ER[:, c, :], in_=ps[pb][:, 1, :], func=AF.Exp), reads=[("ps", pb)], writes=[("ER", c)])
            P.add("act", lambda e, pb=pb, c=c: e.activation(out=k.EL[:, c, :], in_=ps[pb][:, 2, :], func=AF.Exp), reads=[("ps", pb)], writes=[("EL", c)])
            P.add("dve", lambda e, c=c: e.tensor_tensor(out=k.BEG[:, c, :], in0=k.BETA[:, c, :], in1=k.EG[:, c, :], op=ALU.mult),
                  reads=[("EG", c)], writes=[("BEG", c)])


def dn_prep(k, l):
    P = k.P
    mo = {n: COFF[n][0] for n in ("m_le", "m_ge", "m_lt", "m_gt", "ident")}
    NL = 2
    with contextlib.ExitStack() as st:
        kTc = [P.sb([128, 16, 128], BF16, st) for _ in range(2)]
        qTc = [P.sb([128, 16, 128], BF16, st) for _ in range(2)]
        ktm = [P.sb([128, 16, 128], BF16, st) for _ in range(2)]
        vtm = [P.sb([128, 32, 128], BF16, st) for _ in range(2)]
        lanes = []
        for ln in range(NL):
            B_ = K()
            B_.Gm = P.sb([128, 2, 128], F32, st); B_.QKm = P.sb([128, 2, 128], F32, st)
            B_.rhs2 = P.sb([128, 4, 128], F32, st); B_.eD = P.sb([128, 4, 128], F32, st)
            B_.L0 = P.sb([128, 4, 128], BF16, st); B_.QKD = P.sb([128, 4, 128], BF16, st)
            B_.LtQ = P.sb([128, 8, 128], BF16, st)
            B_.Xb = [P.sb([128, 4, 128], BF16, st) for _ in range(2)]
            B_.Yb = P.sb([128, 4, 128], BF16, st)
            B_.Zb = [P.sb([128, 4, 128], BF16, st) for _ in range(2)]
            B_.Ab = [P.sb([128, 4, 128], BF16, st) for _ in range(2)]
            B_.Bb = [P.sb([128, 4, 128], BF16, st) for _ in range(2)]
            B_.Lm = [P.sb([128, 4, 128], BF16, st) for _ in range(5)]
            B_.Ltm = [P.sb([128, 4, 128], BF16, st) for _ in range(4)]
            B_.T1a = P.sb([128, 4, 128], BF16, st); B_.T1b = P.sb([128, 4, 128], BF16, st)
            B_.vb = P.sb([128, 4, 128], BF16, st); B_.kbg = P.sb([128, 4, 128], BF16, st)
            B_.kd = P.sb([128, 4, 128], BF16, st); B_.Uo = P.sb([128, 4, 128], BF16, st)
            B_.Wo = P.sb([128, 4, 128], BF16, st); B_.Qo = P.sb([128, 4, 128], BF16, st)
            B_.psT = P.ps([128, 8, 128], BF16, st)
            B_.psX = P.ps([128, 4, 128], F32, st); B_.psY = P.ps([128, 4, 128], F32, st); B_.psZ = P.ps([128, 4, 128], F32, st)
            lanes.append(B_)

        def unit(ln, c, d, hg, cb):
            B_ = lanes[ln]
            K_ = lambda name, *a: (name, ln) + a
            Gm, QKm, rhs2, eD, L0, QKD, LtQ = B_.Gm, B_.QKm, B_.rhs2, B_.eD, B_.L0, B_.QKD, B_.LtQ
            Xb, Yb, Zb, Ab, Bb, Lm, Ltm, T1a, T1b = B_.Xb, B_.Yb, B_.Zb, B_.Ab, B_.Bb, B_.Lm, B_.Ltm, B_.T1a, B_.T1b
            vb, kbg, kd, Uo, Wo, Qo = B_.vb, B_.kbg, B_.kd, B_.Uo, B_.Wo, B_.Qo
            psT, psX, psY, psZ = B_.psT, B_.psX, B_.psY, B_.psZ
            m_strict = mo["m_gt"] if d == 0 else mo["m_lt"]
            m_incl = mo["m_ge"] if d == 0 else mo["m_le"]
            m_l = mo["m_le"] if d == 0 else mo["m_ge"]
            m_r = mo["m_gt"] if d == 0 else mo["m_lt"]
            h0 = d * 32 + hg * 4
            kh0 = hg * 2
            P.add("pool", lambda e: e.tensor_tensor(
                out=rhs2[:], in0=bc(k.consts[:, m_r:m_r + 128].unsqueeze(1), [128, 4, 128]),
                in1=bc(k.GG[:, c, h0:h0 + 4].unsqueeze(2), [128, 4, 128]), op=ALU.mult), writes=[K_("rhs2")])
            for a in range(2):
                P.add("pe", lambda e, a=a: e.matmul(psX[:, a, :], kTc[cb][:, kh0 + a, :], kTc[cb][:, kh0 + a, :], start=True, stop=True),
                      reads=[("kTc", cb)], writes=[K_("psX")])
                P.add("pe", lambda e, a=a: e.matmul(psX[:, 2 + a, :], qTc[cb][:, kh0 + a, :], kTc[cb][:, kh0 + a, :], start=True, stop=True),
                      reads=[("kTc", cb), ("qTc", cb)], writes=[K_("psX")])
            P.add("pe", lambda e: e.matmul(psY[:], k.consts[:, m_l:m_l + 128], rhs2[:], start=True, stop=True),
                  reads=[K_("rhs2")], writes=[K_("psY")])
            P.add("dve", lambda e: e.tensor_tensor(out=Gm[:], in0=psX[:, 0:2, :], in1=bc(k.consts[:, m_strict:m_strict + 128].unsqueeze(1), [128, 2, 128]), op=ALU.mult),
                  reads=[K_("psX")], writes=[K_("Gm")])
            P.add("dve", lambda e: e.tensor_tensor(out=QKm[:], in0=psX[:, 2:4, :], in1=bc(k.consts[:, m_incl:m_incl + 128].unsqueeze(1), [128, 2, 128]), op=ALU.mult),
                  reads=[K_("psX")], writes=[K_("QKm")])
            P.add("act", lambda e: e.activation(out=eD[:], in_=psY[:], func=AF.Exp), reads=[K_("psY")], writes=[K_("eD")])
            yield
            for hl in range(4):
                P.add("dve", lambda e, hl=hl: e.scalar_tensor_tensor(
                    out=L0[:, hl, :], in0=Gm[:, hl // 2, :], scalar=k.BETA[:, c, h0 + hl:h0 + hl + 1], in1=eD[:, hl, :],
                    op0=ALU.mult, op1=ALU.mult), reads=[K_("Gm"), K_("eD")], writes=[K_("L0", hl)])
            P.add("dve", lambda e: e.tensor_tensor(
                out=QKD[:].rearrange("p (a b) j -> p a b j", b=2), in0=bc(QKm[:].unsqueeze(2), [128, 2, 2, 128]),
                in1=eD[:].rearrange("p (a b) j -> p a b j", b=2), op=ALU.mult), reads=[K_("QKm"), K_("eD")], writes=[K_("QKD")])
            L0k = [K_("L0", hl) for hl in range(4)]
            for hl in range(4):
                P.add("pe", lambda e, hl=hl: e.transpose(out=psT[:, hl, :], in_=L0[:, hl, :], identity=k.identb[:]),
                      reads=L0k, writes=[K_("psT")])
            for hl in range(4):
                P.add("pe", lambda e, hl=hl: e.transpose(out=psT[:, 4 + hl, :], in_=QKD[:, hl, :], identity=k.identb[:]),
                      reads=[K_("QKD")], writes=[K_("psT")])
            P.add("act", lambda e: e.activation(out=LtQ[:, 0:4, :], in_=psT[:, 0:4, :], func=AF.Copy), reads=[K_("psT")], writes=[K_("LtQ")])
            P.add("act", lambda e: e.activation(out=Qo[:], in_=psT[:, 4:8, :], func=AF.Copy), reads=[K_("psT")], writes=[K_("Qo")])
            yield
            Lt = LtQ[:, 0:4, :]
            mk = lambda q: bc(k.maskb[:, q, :].unsqueeze(1), [128, 4, 128])
            P.add("dve", lambda e: e.tensor_tensor(out=Lm[0][:], in0=L0[:], in1=mk(0), op=ALU.mult), reads=L0k, writes=[K_("Lm", 0)])
            P.add("dve", lambda e: e.tensor_tensor(out=Ltm[0][:], in0=Lt, in1=mk(0), op=ALU.mult), reads=[K_("LtQ")], writes=[K_("Ltm", 0)])
            for q in range(1, 5):
                eng_m = "pool"
                P.add(eng_m, lambda e, q=q: e.tensor_tensor(out=Lm[q][:], in0=L0[:], in1=mk(q), op=ALU.mult), reads=L0k, writes=[K_("Lm", q)])
                if q < 4:
                    P.add(eng_m, lambda e, q=q: e.tensor_tensor(out=Ltm[q][:], in0=Lt, in1=mk(q), op=ALU.mult), reads=[K_("LtQ")], writes=[K_("Ltm", q)])
            for hl in range(4):
                P.add("act", lambda e, hl=hl: e.activation(out=vb[:, hl, :], in_=vtm[cb][:, hg * 4 + hl, :], func=AF.Identity,
                                                           scale=k.BETA[:, c, h0 + hl:h0 + hl + 1]), reads=[("vtm", cb)], writes=[K_("vb", hl)])
                P.add("act", lambda e, hl=hl: e.activation(out=kbg[:, hl, :], in_=ktm[cb][:, kh0 + hl // 2, :], func=AF.Identity,
                                                           scale=k.BEG[:, c, h0 + hl:h0 + hl + 1]), reads=[("ktm", cb)], writes=[K_("kbg", hl)])
                P.add("act", lambda e, hl=hl: e.activation(out=kd[:, hl, :], in_=ktm[cb][:, kh0 + hl // 2, :], func=AF.Identity,
                                                           scale=k.ER[:, c, h0 + hl:h0 + hl + 1]), reads=[("ktm", cb)], writes=[K_("kd", hl)])
            for hl in range(4):
                P.add("pe", lambda e, hl=hl: e.matmul(psX[:, hl, :], Ltm[0][:, hl, :], Lm[0][:, hl, :], start=True, stop=True),
                      reads=[K_("Lm", 0), K_("Ltm", 0)], writes=[K_("psX")])
            for hl in range(4):
                P.add("pe", lambda e, hl=hl: e.matmul(psY[:, hl, :], Lm[0][:, hl, :], Ltm[0][:, hl, :], start=True, stop=True),
                      reads=[K_("Lm", 0), K_("Ltm", 0)], writes=[K_("psY")])
            P.add("act", lambda e: e.activation(out=Xb[0][:], in_=psX[:], func=AF.Copy), reads=[K_("psX")], writes=[K_("X", 0)])
            P.add("act", lambda e: e.activation(out=Yb[:], in_=psY[:], func=AF.Copy), reads=[K_("psY")], writes=[K_("Y")])
            P.add("dve", lambda e: e.scalar_tensor_tensor(
                out=Zb[0][:], in0=Ltm[0][:], scalar=-1.0, in1=bc(k.identb[:].unsqueeze(1), [128, 4, 128]),
                op0=ALU.mult, op1=ALU.add), reads=[K_("Ltm", 0)], writes=[K_("Z", 0)])
            yield
            for hl in range(4):
                P.add("pe", lambda e, hl=hl: e.matmul(psX[:, hl, :], Yb[:, hl, :], Xb[0][:, hl, :], start=True, stop=True),
                      reads=[K_("X", 0), K_("Y")], writes=[K_("psX")])
            for hl in range(4):
                P.add("pe", lambda e, hl=hl: e.matmul(psZ[:, hl, :], Xb[0][:, hl, :], Zb[0][:, hl, :], start=True, stop=True),
                      reads=[K_("X", 0), K_("Z", 0)], writes=[K_("psZ")])
            P.add("act", lambda e: e.activation(out=Xb[1][:], in_=psX[:], func=AF.Copy), reads=[K_("psX")], writes=[K_("X", 1)])
            P.add("dve", lambda e: e.tensor_tensor(out=Zb[1][:], in0=psZ[:], in1=Zb[0][:], op=ALU.add),
                  reads=[K_("psZ"), K_("Z", 0)], writes=[K_("Z", 1)])
            yield
            for hl in range(4):
                P.add("pe", lambda e, hl=hl: e.matmul(psZ[:, hl, :], Xb[1][:, hl, :], Zb[1][:, hl, :], start=True, stop=True),
                      reads=[K_("X", 1), K_("Z", 1)], writes=[K_("psZ")])
            P.add("dve", lambda e: e.tensor_tensor(out=Zb[0][:], in0=psZ[:], in1=Zb[1][:], op=ALU.add),
                  reads=[K_("psZ"), K_("Z", 1)], writes=[K_("Z", 0)])
            yield
            for hl in range(4):
                P.add("pe", lambda e, hl=hl: e.transpose(out=psT[:, hl, :], in_=Zb[0][:, hl, :], identity=k.identb[:]),
                      reads=[K_("Z", 0)], writes=[K_("psT")])
            P.add("act", lambda e: e.activation(out=Ab[0][:], in_=psT[:, 0:4, :], func=AF.Copy), reads=[K_("psT")], writes=[K_("A", 0)])
            yield
            Bcur, Bkey = Zb[0], K_("Z", 0)
            Acur, Akey = Ab[0], K_("A", 0)
            for q in range(1, 5):
                last = (q == 4)
                nb_ = q % 2
                if not last:
                    for hl in range(4):
                        P.add("pe", lambda e, hl=hl, q=q, Acur=Acur: e.matmul(psX[:, hl, :], Ltm[q][:, hl, :], Acur[:, hl, :], start=True, stop=True),
                              reads=[K_("Ltm", q), Akey], writes=[K_("psX")])
                for hl in range(4):
                    P.add("pe", lambda e, hl=hl, q=q, Bcur=Bcur: e.matmul(psY[:, hl, :], Lm[q][:, hl, :], Bcur[:, hl, :], start=True, stop=True),
                          reads=[K_("Lm", q), Bkey], writes=[K_("psY")])
                if not last:
                    P.add("act", lambda e: e.activation(out=T1a[:], in_=psX[:], func=AF.Copy), reads=[K_("psX")], writes=[K_("T1a")])
                P.add("act", lambda e: e.activation(out=T1b[:], in_=psY[:], func=AF.Copy), reads=[K_("psY")], writes=[K_("T1b")])
                yield
                if not last:
                    for hl in range(4):
                        P.add("pe", lambda e, hl=hl, Bcur=Bcur: e.matmul(psX[:, hl, :], Bcur[:, hl, :], T1a[:, hl, :], start=True, stop=True),
                              reads=[Bkey, K_("T1a")], writes=[K_("psX")])
                for hl in range(4):
                    P.add("pe", lambda e, hl=hl, Acur=Acur: e.matmul(psZ[:, hl, :], Acur[:, hl, :], T1b[:, hl, :], start=True, stop=True),
                          reads=[Akey, K_("T1b")], writes=[K_("psZ")])
                if not last:
                    P.add("dve", lambda e, nb_=nb_, Acur=Acur: e.tensor_tensor(out=Ab[nb_][:], in0=Acur[:], in1=psX[:], op=ALU.subtract),
                          reads=[K_("psX"), Akey], writes=[K_("A", nb_)])
                Bnew = Bb[nb_] if not last else Zb[0]
                Bnk = K_("B", nb_) if not last else K_("Z", 0)
                P.add("dve", lambda e, Bnew=Bnew, Bcur=Bcur: e.tensor_tensor(out=Bnew[:], in0=Bcur[:], in1=psZ[:], op=ALU.subtract),
                      reads=[K_("psZ"), Bkey], writes=[Bnk])
                Bcur, Bkey = Bnew, Bnk
                if not last:
                    Acur, Akey = Ab[nb_], K_("A", nb_)
                yield
            ZF = Zb[0]
            for hl in range(4):
                P.add("pe", lambda e, hl=hl: e.matmul(psX[:, hl, :], ZF[:, hl, :], vb[:, hl, :], start=True, stop=True),
                      reads=[K_("Z", 0)] + [K_("vb", q_) for q_ in range(4)], writes=[K_("psX")])
            for hl in range(4):
                P.add("pe", lambda e, hl=hl: e.matmul(psY[:, hl, :], kbg[:, hl, :], ZF[:, hl, :], start=True, stop=True),
                      reads=[K_("Z", 0)] + [K_("kbg", q_) for q_ in range(4)], writes=[K_("psY")])
            P.add("act", lambda e: e.activation(out=Uo[:], in_=psX[:], func=AF.Copy), reads=[K_("psX")], writes=[K_("Uo")])
            P.add("act", lambda e: e.activation(out=Wo[:], in_=psY[:], func=AF.Copy), reads=[K_("psY")], writes=[K_("Wo")])
            u = (c * 2 + d) * 8 + hg
            P.add("sp", lambda e: e.dma_start(out=k.DPH[u, 0], in_=Uo[:]), reads=[K_("Uo")], writes=[("DPH", u, 0)], dma=True)
            P.add("sp", lambda e: e.dma_start(out=k.DPH[u, 1], in_=Wo[:]), reads=[K_("Wo")], writes=[("DPH", u, 1)], dma=True)
            P.add("sp", lambda e: e.dma_start(out=k.DPH[u, 2], in_=Qo[:]), reads=[K_("Qo")], writes=[("DPH", u, 2)], dma=True)
            P.add("sp", lambda e: e.dma_start(out=k.DPH[u, 3], in_=kd[:]), reads=[K_("kd", q_) for q_ in range(4)], writes=[("DPH", u, 3)], dma=True)
            yield

        def load_chunk(c):
            cb = c % 2
            tsl = slice(c * 128, (c + 1) * 128)
            P.add("sp", lambda e: e.dma_start(out=kTc[cb][:], in_=k.DKT.rearrange("h p t -> p h t")[:, :, tsl]), writes=[("kTc", cb)], dma=True)
            P.add("sp", lambda e: e.dma_start(out=qTc[cb][:], in_=k.DQT.rearrange("h p t -> p h t")[:, :, tsl]), writes=[("qTc", cb)], dma=True)
            P.add("sp", lambda e: e.dma_start(out=ktm[cb][:], in_=k.DKTM[c].rearrange("p (h d) -> p h d", d=128)), writes=[("ktm", cb)], dma=True)
            P.add("sp", lambda e: e.dma_start(out=vtm[cb][:], in_=k.DVTM[c].rearrange("p (h d) -> p h d", d=128)), writes=[("vtm", cb)], dma=True)

        for c in range(18):
            load_chunk(c)
            units = [(d, hg) for d in range(2) for hg in range(8)]
            for p0 in range(0, len(units), NL):
                gens = [unit(ln, c, units[p0 + ln][0], units[p0 + ln][1], c % 2) for ln in range(NL)]
                alive = list(gens)
                while alive:
                    nxt = []
                    for g in alive:
                        try:
                            next(g)
                            nxt.append(g)
                        except StopIteration:
                            pass
                    alive = nxt


def dn_scan(k, l):
    P = k.P
    order = [list(range(18)), [1, 0] + list(range(17, 1, -1))]
    with contextlib.ExitStack() as st:
        Sf = P.sb([128, 64, 128], F32, st)
        Sb = P.sb([128, 64, 128], BF16, st)
        NB = 3
        ph = [P.sb([128, 4, 4, 128], BF16, st) for _ in range(NB)]
        qT = [P.sb([128, 2, 128], BF16, st) for _ in range(NB)]
        vnew = [P.sb([128, 4, 128], BF16, st) for _ in range(2)]
        o2 = [P.sb([128, 4, 128], F32, st) for _ in range(2)]
        ob_ = [P.sb([128, 4, 128], F32, st) for _ in range(2)]
        psP = [P.ps([128, 4, 128], F32, st) for _ in range(2)]
        psO1 = [P.ps([128, 4, 128], F32, st) for _ in range(2)]
        psO2 = [P.ps([128, 4, 128], F32, st) for _ in range(2)]
        psS = [P.ps([128, 4, 128], F32, st) for _ in range(2)]
        P.add("pool", lambda e: e.memset(Sf[:], 0.0), writes=[("Sf", hh) for hh in range(16)])
        P.add("pool", lambda e: e.memset(Sb[:], 0.0), writes=[("Sb", hh) for hh in range(16)])
        P.barrier()
        units = [(s, d, hg) for s in range(18) for d in range(2) for hg in range(8)]

        def stage1(idx):
            s, d, hg = units[idx]
            c = order[d][s]
            u = (c * 2 + d) * 8 + hg
            nb = idx % NB; pb = idx % 2
            g16 = d * 8 + hg
            P.add("sp", lambda e: e.dma_start(out=ph[nb][:], in_=k.DPH[u].rearrange("f p h j -> p f h j")), writes=[("ph", nb)], dma=True)
            P.add("sp", lambda e: e.dma_start(out=qT[nb][:], in_=k.DQT[hg * 2:hg * 2 + 2, :, c * 128:(c + 1) * 128].rearrange("h p t -> p h t")),
                  writes=[("qT", nb)], dma=True)
            for hl in range(4):
                hs = d * 32 + hg * 4 + hl
                P.add("pe", lambda e, hl=hl, hs=hs: e.matmul(psP[pb][:, hl, :], ph[nb][:, 1, hl, :], Sb[:, hs, :], start=True, stop=True),
                      reads=[("ph", nb), ("Sb", g16)], writes=[("psP", pb)])
            for hl in range(4):
                hs = d * 32 + hg * 4 + hl
                P.add("pe", lambda e, hl=hl, hs=hs: e.matmul(psO1[pb][:, hl, :], qT[nb][:, hl // 2, :], Sb[:, hs, :], start=True, stop=True),
                      reads=[("qT", nb), ("Sb", g16)], writes=[("psO1", pb)])
            P.add("dve", lambda e: e.tensor_tensor(out=vnew[pb][:], in0=ph[nb][:, 0, :, :], in1=psP[pb][:], op=ALU.subtract),
                  reads=[("ph", nb), ("psP", pb)], writes=[("vnew", pb)])

        def stage2(idx):
            s, d, hg = units[idx]
            c = order[d][s]
            nb = idx % NB; pb = idx % 2
            g16 = d * 8 + hg
            h0 = d * 32 + hg * 4
            for hl in range(4):
                P.add("pe", lambda e, hl=hl: e.matmul(psO2[pb][:, hl, :], ph[nb][:, 2, hl, :], vnew[pb][:, hl, :], start=True, stop=True),
                      reads=[("ph", nb), ("vnew", pb)], writes=[("psO2", pb)])
            for hl in range(4):
                P.add("pe", lambda e, hl=hl: e.matmul(psS[pb][:, hl, :], ph[nb][:, 3, hl, :], vnew[pb][:, hl, :], start=True, stop=True),
                      reads=[("ph", nb), ("vnew", pb)], writes=[("psS", pb)])
            P.add("act", lambda e: e.activation(out=o2[pb][:], in_=psO2[pb][:], func=AF.Copy), reads=[("psO2", pb)], writes=[("o2", pb)])
            for hl in range(4):
                P.add("dve", lambda e, hl=hl: e.scalar_tensor_tensor(
                    out=ob_[pb][:, hl, :], in0=psO1[pb][:, hl, :], scalar=k.EG[:, c, h0 + hl:h0 + hl + 1], in1=o2[pb][:, hl, :],
                    op0=ALU.mult, op1=ALU.add), reads=[("psO1", pb), ("o2", pb)], writes=[("ob", pb, hl)])
            for hl in range(4):
                P.add("dve", lambda e, hl=hl: e.scalar_tensor_tensor(
                    out=Sf[:, h0 + hl, :], in0=Sf[:, h0 + hl, :], scalar=k.EL[:, c, h0 + hl:h0 + hl + 1], in1=psS[pb][:, hl, :],
                    op0=ALU.mult, op1=ALU.add), reads=[("psS", pb), ("Sf", g16, hl)], writes=[("Sf", g16, hl)])
            P.add("act", lambda e: e.activation(out=Sb[:, h0:h0 + 4, :], in_=Sf[:, h0:h0 + 4, :], func=AF.Copy),
                  reads=[("Sf", g16, hl) for hl in range(4)], writes=[("Sb", g16)])
            dst = k.DO[d, c, :, hg * 512:(hg + 1) * 512]
            P.add("sp", lambda e: e.dma_start(out=dst, in_=ob_[pb][:].rearrange("p h j -> p (h j)")),
                  reads=[("ob", pb, hl) for hl in range(4)], writes=[("DO", d, c, hg)], dma=True)

        for idx in range(len(units) + 1):
            if idx < len(units):
                stage1(idx)
            if idx >= 1:
                stage2(idx - 1)


def dn_outnorm(k, l):
    P = k.P
    i = l // 2
    ng = VOFF["dngrep"][0] + i * 128
    with contextlib.ExitStack() as st:
        of = [P.sb([128, 32, 128], F32, st) for _ in range(2)]
        obk = [P.sb([128, 32, 128], F32, st) for _ in range(2)]
        zz = [P.sb([128, 32, 128], BF16, st) for _ in range(2)]
        sq = P.sb([128, 32, 128], F32, st)
        ss = P.sb([128, 32], F32, st)
        y = [P.sb([128, 32, 128], BF16, st) for _ in range(2)]
        yT = [P.sb([128, 32, 128], BF16, st) for _ in range(2)]
        pst = [P.ps([128, 8, 128], BF16, st) for _ in range(2)]
        npt = 0
        for c in range(18):
            b = c % 2
            P.add("sp", lambda e, b=b, c=c: e.dma_start(out=of[b][:], in_=k.DO[0, c].rearrange("p (h j) -> p h j", j=128)), writes=[("of", b)], dma=True)
            P.add("sp", lambda e, b=b, c=c: e.dma_start(out=obk[b][:], in_=k.DO[1, c].rearrange("p (h j) -> p h j", j=128)), writes=[("obk", b)], dma=True)
            P.add("sp", lambda e, b=b, c=c: e.dma_start(out=zz[b][:], in_=k.DZ[c].rearrange("p (h j) -> p h j", j=128)), writes=[("zz", b)], dma=True)
            P.add("dve", lambda e, b=b: e.tensor_tensor(out=of[b][:], in0=of[b][:], in1=obk[b][:], op=ALU.add),
                  reads=[("of", b), ("obk", b)], writes=[("of", b)])
            P.add("act", lambda e, b=b: e.activation(out=sq[:], in_=of[b][:], func=AF.Square), reads=[("of", b)], writes=["sq"])
            P.add("dve", lambda e: e.tensor_reduce(out=ss[:], in_=sq[:], axis=AX.X, op=ALU.add), reads=["sq"], writes=["ss"])
            P.add("act", lambda e: e.activation(out=ss[:], in_=ss[:], func=AF.Sqrt, bias=EPS, scale=1.0 / 128), reads=["ss"], writes=["ss"])
            P.add("dve", lambda e: e.reciprocal(ss[:], ss[:]), reads=["ss"], writes=["ss"])
            P.add("dve", lambda e, b=b: e.tensor_tensor(out=of[b][:], in0=of[b][:], in1=bc(ss[:].unsqueeze(2), [128, 32, 128]), op=ALU.mult),
                  reads=[("of", b), "ss"], writes=[("of", b)])
            P.add("dve", lambda e, b=b: e.tensor_tensor(out=of[b][:], in0=of[b][:], in1=bc(k.vecs[:, ng:ng + 128].unsqueeze(1), [128, 32, 128]), op=ALU.mult),
                  reads=[("of", b)], writes=[("of", b)])
            P.add("dve", lambda e, b=b: e.tensor_tensor(out=y[b][:], in0=of[b][:], in1=zz[b][:], op=ALU.mult),
                  reads=[("of", b), ("zz", b)], writes=[("y", b)])
            for h8 in range(4):
                tb = npt % 2; npt += 1
                for hh in range(8):
                    P.add("pe", lambda e, tb=tb, hh=hh, h8=h8, b=b: e.transpose(out=pst[tb][:, hh, :], in_=y[b][:, h8 * 8 + hh, :], identity=k.identb[:]),
                          reads=[("y", b)], writes=[("pst", tb)])
                P.add("act", lambda e, tb=tb, h8=h8, b=b: e.activation(out=yT[b][:, h8 * 8:(h8 + 1) * 8, :], in_=pst[tb][:], func=AF.Copy),
                      reads=[("pst", tb)], writes=[("yT", b, h8)])
            P.add("sp", lambda e, b=b, c=c: e.dma_start(out=k.DYT[:, :, c * 128:(c + 1) * 128].rearrange("h p t -> p h t"), in_=yT[b][:]),
                  reads=[("yT", b, h8) for h8 in range(4)], writes=[("DYT", c)], dma=True)


def dn_outproj(k, l):
    P = k.P
    i = l // 2
    hsrc = k.hT.rearrange("(c p) t -> p c t", p=128)
    HALF = NT // 2
    SUB = 384
    with contextlib.ExitStack() as st:
        cat = P.sb([128, 32, HALF], BF16, st)
        wb = [P.sb([128, 32, 256], BF16, st) for _ in range(2)]
        hrow = [P.sb([128, HALF], F32, st) for _ in range(2)]
        pso = [P.ps([128, 512], F32, st) for _ in range(4)]
        ysrc = k.DYT.rearrange("h p t -> p h t")
        nps = 0
        for half in range(2):
            h0 = half * HALF
            for q in range(4):
                P.add("sp", lambda e, q=q, h0=h0: e.dma_start(out=cat[:, q * 8:(q + 1) * 8, :], in_=ysrc[:, q * 8:(q + 1) * 8, h0:h0 + HALF]),
                      writes=[("cat", q)], dma=True)
            for blk in range(8):
                b = blk % 2
                for q in range(2):
                    P.add("pool", lambda e, b=b, blk=blk, q=q: e.dma_start(
                        out=wb[b][:, q * 16:(q + 1) * 16, :], in_=k.dn_w_out[i, blk][:, q * 16:(q + 1) * 16, :]),
                        writes=[("wb", b, q)], dma=True)
                for s in range(2):
                    c = blk * 2 + s
                    hb = c % 2
                    P.add("sp", lambda e, hb=hb, c=c, h0=h0: e.dma_start(out=hrow[hb][:], in_=hsrc[:, c, h0:h0 + HALF]),
                          reads=[("hT", c)], writes=[("hrow", hb)], dma=True)
                    for sub in range(HALF // SUB):
                        u0 = sub * SUB
                        pb = nps % 4; nps += 1
                        for kc in range(32):
                            P.add("pe", lambda e, b=b, s=s, pb=pb, kc=kc, u0=u0: e.matmul(
                                pso[pb][:, :SUB], wb[b][:, kc, s * 128:(s + 1) * 128], cat[:, kc, u0:u0 + SUB],
                                start=(kc == 0), stop=(kc == 31)),
                                reads=[("wb", b, kc // 16), ("cat", kc // 8)], writes=[("pso", pb)])
                        for (g0, gn, sidx) in segs(h0 + u0, SUB):
                            r0 = g0 - h0
                            P.add("dve", lambda e, hb=hb, pb=pb, r0=r0, gn=gn, u0=u0, c=c, sidx=sidx: e.scalar_tensor_tensor(
                                out=hrow[hb][:, r0:r0 + gn], in0=pso[pb][:, r0 - u0:r0 - u0 + gn], scalar=modv(k, 2, c, sidx),
                                in1=hrow[hb][:, r0:r0 + gn], op0=ALU.mult, op1=ALU.add),
                                reads=[("pso", pb), ("hrow", hb), "mod"], writes=[("hrow", hb)])
                    P.add("sp", lambda e, hb=hb, c=c, h0=h0: e.dma_start(out=hsrc[:, c, h0:h0 + HALF], in_=hrow[hb][:]),
                          reads=[("hrow", hb)], writes=[("hT", c)], dma=True)


FULL_PLAN = [(l, ("mix", "ffn")) for l in range(DEPTH)]


def tile_w(W, cols=None, width=256):
    if cols is not None:
        W = W[:, cols]
    K_, N_ = W.shape
    return np.ascontiguousarray(W.reshape(K_ // 128, 128, N_ // width, width).transpose(2, 1, 0, 3))


def tile_weights(inp):
    out = {}
    out["w_mod"] = np.stack([tile_w(inp["w_mod"][l]) for l in range(DEPTH)])
    up_cols = np.concatenate([np.concatenate([np.arange(j * 128, (j + 1) * 128), DFF + np.arange(j * 128, (j + 1) * 128)]) for j in range(NFF)])
    out["ffn_w_up"] = np.stack([tile_w(inp["ffn_w_up"][l], up_cols) for l in range(DEPTH)])
    out["ffn_w_down"] = np.stack([tile_w(inp["ffn_w_down"][l]) for l in range(DEPTH)])
    ev_cols = np.concatenate([np.arange(0, 1536)] + [np.concatenate([1536 + np.arange(j * 128, (j + 1) * 128), 2560 + np.arange(j * 128, (j + 1) * 128)])
                                                      for j in range(8)])
    out["even_w_in"] = np.stack([tile_w(inp["even_w_in"][i], ev_cols) for i in range(2)])
    out["even_w_out"] = np.stack([tile_w(inp["even_w_out"][i]) for i in range(2)])
    out["dn_w_in"] = np.stack([tile_w(inp["dn_w_in"][i][:, :12288]) for i in range(2)])
    out["dn_w_ba"] = np.stack([tile_w(inp["dn_w_in"][i][:, 12288:12416], width=128)[0] for i in range(2)])
    out["dn_w_out"] = np.stack([tile_w(inp["dn_w_out"][i]) for i in range(2)])
    return out


def make_core_inputs(inp, b, consts, tiled=None):
    h_in = np.concatenate([inp["ctx"][b].T, inp["x"][b].T], axis=1)
    return {
        "h_in": np.ascontiguousarray(h_in, dtype=np.float32),
        "vecs": pack_vecs(inp, b),
        "consts": consts[0], "rope": consts[1],
        **(tiled if tiled is not None else {}),
    }


def kernel(**inputs):
    inp = {k_: np.asarray(v) for k_, v in inputs.items()}
    consts = make_consts()
    nc = build_program(FULL_PLAN)
    tiled = tile_weights(inp)
    in_maps = [make_core_inputs(inp, c % 4, consts, tiled) for c in range(8)]
    res = run_bass_kernel_spmd(nc, in_maps, core_ids=list(range(8)))
    out = np.stack([res.results[b]["hT"][:, LCTX:].T for b in range(4)], axis=0)
    return np.ascontiguousarray(out.astype(np.float32))
```

```python
import contextlib
import numpy as np
import concourse.bass as bass
import concourse.mybir as mybir
from concourse.bass_utils import run_bass_kernel_spmd

F32 = mybir.dt.float32
BF16 = mybir.dt.bfloat16
AF = mybir.ActivationFunctionType
ALU = mybir.AluOpType
AX = mybir.AxisListType

D = 2048
NCH = 16
LCTX = 256
TLAT = 2048
NT = LCTX + TLAT
DEPTH = 4
DFF = 5632
NFF = DFF // 128
EPS = 1e-6
TILES = [(0, 256), (256, 512), (768, 512), (1280, 512), (1792, 512)]
EVEN_IN_W = 3584
DN_IN_W = 12416

ENGS = ("pe", "act", "dve", "pool", "sp")
NDSEM = 6


class Op:
    __slots__ = ("eng", "fn", "idx", "dma", "slot", "val", "deps", "needed", "cnt", "waits", "clock")


class Prog:
    def __init__(self, nc, same_engine_sync=True):
        self.nc = nc
        self.ops = {e: [] for e in ENGS}
        self.order = []
        self.last_w = {}
        self.readers = {}
        self.same_engine_sync = same_engine_sync
        self.dma_rr = {e: 0 for e in ENGS}
        self.dma_val = {}
        self.stack = contextlib.ExitStack()
        self.sems = {}
        self.dsems = {}
        self._n = 0

    def sb(self, shape, dtype, stack=None):
        self._n += 1
        return (stack or self.stack).enter_context(self.nc.sbuf_tensor(f"sb{self._n}", list(shape), dtype))

    def ps(self, shape, dtype, stack=None):
        self._n += 1
        return (stack or self.stack).enter_context(self.nc.psum_tensor(f"ps{self._n}", list(shape), dtype))

    def add(self, eng, fn, reads=(), writes=(), dma=False):
        op = Op()
        op.eng = eng; op.fn = fn; op.dma = dma; op.needed = False; op.waits = None
        op.idx = len(self.ops[eng]); op.slot = None; op.val = None
        deps = set()
        writes = list(writes)
        if dma:
            s = self.dma_rr[eng] % NDSEM
            self.dma_rr[eng] += 1
            op.slot = (eng, s)
            writes.append(("dsem", eng, s))
            self.dma_val[op.slot] = self.dma_val.get(op.slot, 0) + 16
            op.val = self.dma_val[op.slot]
        for r in reads:
            w = self.last_w.get(r)
            if w is not None:
                deps.add(w)
        for r in writes:
            w = self.last_w.get(r)
            if w is not None:
                deps.add(w)
            for rd in self.readers.get(r, ()):
                deps.add(rd)
        for r in writes:
            self.last_w[r] = op
            self.readers[r] = []
        for r in reads:
            self.readers.setdefault(r, []).append(op)
        deps.discard(op)
        op.deps = deps
        self.ops[eng].append(op)
        self.order.append(op)
        return op

    def barrier(self):
        lasts = []
        for e in ENGS:
            for o in reversed(self.ops[e]):
                if o.fn is not None and not o.dma:
                    lasts.append(o)
                    break
        latest = {}
        for e in ENGS:
            for o in reversed(self.ops[e]):
                if o.dma and o.slot not in latest:
                    latest[o.slot] = o
        for e in ENGS:
            op = Op()
            op.eng = e; op.fn = None; op.dma = False; op.needed = False; op.waits = None
            op.idx = len(self.ops[e]); op.slot = None; op.val = None
            op.deps = set(o for o in lasts if o.eng != e and o.fn is not None and not o.dma) | set(latest.values())
            self.ops[e].append(op)
            self.order.append(op)
        self.last_w.clear(); self.readers.clear()

    def emit(self):
        nc = self.nc
        seen = {e: {} for e in ENGS}
        for op in self.order:
            s = seen[op.eng]
            waits = []
            for d in sorted(op.deps, key=lambda o: (o.eng, o.idx)):
                if d.dma:
                    key = ("d",) + d.slot
                    if s.get(key, 0) >= d.val:
                        continue
                    waits.append(d)
                    s[key] = d.val
                else:
                    if d.fn is None:
                        continue
                    if d.eng == op.eng and (d.eng == "pe" or not self.same_engine_sync or op.fn is None):
                        continue
                    if s.get(d.eng, -1) >= d.idx:
                        continue
                    waits.append(d)
                    s[d.eng] = d.idx
                for k, v in d.clock.items():
                    if k == op.eng:
                        continue
                    if s.get(k, -1) < v:
                        s[k] = v
            for d in waits:
                d.needed = True
            op.waits = waits
            op.clock = dict(s)
        for e in ENGS:
            c = 0
            for op in self.ops[e]:
                if op.needed and not op.dma:
                    c += 1
                op.cnt = c
        st = self.stack
        for e in ENGS:
            self.sems[e] = st.enter_context(nc.semaphore(f"s_{e}"))
        for key in self.dma_val:
            self.dsems[key] = st.enter_context(nc.semaphore(f"d_{key[0]}{key[1]}"))
        block = st.enter_context(nc.Block())
        handles = {"pe": block.tensor, "act": block.scalar, "dve": block.vector, "pool": block.gpsimd, "sp": block.sync}

        def mk(e):
            def body(h):
                for op in self.ops[e]:
                    for d in op.waits:
                        if d.dma:
                            h.wait_ge(self.dsems[d.slot], d.val)
                        else:
                            h.wait_ge(self.sems[d.eng], d.cnt)
                    if op.fn is None:
                        continue
                    ins = op.fn(h)
                    if op.dma:
                        ins.then_inc(self.dsems[op.slot], 16)
                    elif op.needed:
                        ins.then_inc(self.sems[e], 1)
            return body
        for e in ENGS:
            handles[e](mk(e))

    def finish(self):
        self.barrier()
        self.emit()
        self.stack.close()


def _layout(items):
    off = {}
    o = 0
    for name, w in items:
        off[name] = (o, w)
        o += w
    return off, o


VEC_ITEMS = [
    ("bmod", 4 * 96), ("n1g", 64), ("n2g", 64), ("fcw", 4 * 3 * NFF), ("fcb", 4 * NFF),
    ("qg", 2), ("kg", 2), ("sink", 16), ("cdw", 2 * 31 * 8), ("cdb", 16), ("lng", 16), ("lnb", 16),
    ("dcw", 2 * 5 * 64), ("alog", 128), ("dtb", 128), ("dngrep", 256), ("c", 16), ("cctx", 16),
]
VOFF, NV = _layout(VEC_ITEMS)

CONST_ITEMS = [
    ("ident", 128), ("ropeT", 128), ("band", 384),
    ("m_le", 128), ("m_ge", 128), ("m_lt", 128), ("m_gt", 128),
    ("mblk", 5 * 128),
]
COFF, NCONST = _layout(CONST_ITEMS)


def fm(v):
    return np.ascontiguousarray(v.reshape(-1, 128).T)


def pack_vecs(inp, b):
    V = np.zeros((128, NV), np.float32)

    def put(name, arr):
        o, w = VOFF[name]
        assert arr.shape == (128, w), (name, arr.shape, w)
        V[:, o:o + w] = arr
    put("bmod", np.concatenate([fm(inp["b_mod"][l]) for l in range(4)], axis=1))
    put("n1g", np.concatenate([fm(inp["norm1_g"][l]) for l in range(4)], axis=1))
    put("n2g", np.concatenate([fm(inp["norm2_g"][l]) for l in range(4)], axis=1))
    put("fcw", np.concatenate([fm(inp["ffn_conv_w"][l, k]) for l in range(4) for k in range(3)], axis=1))
    put("fcb", np.concatenate([fm(inp["ffn_conv_b"][l]) for l in range(4)], axis=1))
    put("qg", inp["attn_q_norm_g"].T.copy())
    put("kg", inp["attn_k_norm_g"].T.copy())
    put("sink", np.broadcast_to(inp["attn_sink"].reshape(1, 16), (128, 16)).copy())
    put("cdw", np.concatenate([fm(inp["conv_dw_w"][i, k]) for i in range(2) for k in range(31)], axis=1))
    put("cdb", np.concatenate([fm(inp["conv_dw_b"][i]) for i in range(2)], axis=1))
    put("lng", np.concatenate([fm(inp["conv_ln_g"][i]) for i in range(2)], axis=1))
    put("lnb", np.concatenate([fm(inp["conv_ln_b"][i]) for i in range(2)], axis=1))
    put("dcw", np.concatenate([fm(inp["dn_conv_w"][i, k]) for i in range(2) for k in range(5)], axis=1))
    put("alog", np.broadcast_to(inp["dn_a_log"].reshape(1, 128), (128, 128)).copy())
    put("dtb", np.broadcast_to(inp["dn_dt_bias"].reshape(1, 128), (128, 128)).copy())
    put("dngrep", np.broadcast_to(inp["dn_norm_g"].reshape(1, 256), (128, 256)).copy())
    put("c", fm(inp["c"][b]))
    put("cctx", fm(inp["c_ctx"]))
    return V


def make_consts():
    C = np.zeros((128, NCONST), np.float32)

    def put(name, arr):
        o, w = COFF[name]
        assert arr.shape == (128, w), (name, arr.shape)
        C[:, o:o + w] = arr
    idx = np.arange(128)
    put("ident", np.eye(128, dtype=np.float32))
    R = np.zeros((128, 128), np.float32)
    for i in range(32):
        R[i, 32 + i] = -1.0
        R[32 + i, i] = 1.0
        R[64 + i, 96 + i] = -1.0
        R[96 + i, 64 + i] = 1.0
    put("ropeT", R.T.copy())
    kk = idx[:, None]; qq = idx[None, :]
    band = np.concatenate([(kk <= qq), np.ones((128, 128), bool), (qq <= kk)], axis=1).astype(np.float32)
    put("band", band)
    put("m_le", (kk <= qq).astype(np.float32))
    put("m_ge", (kk >= qq).astype(np.float32))
    put("m_lt", (kk < qq).astype(np.float32))
    put("m_gt", (kk > qq).astype(np.float32))
    mb = [(kk // 8 == qq // 8)]
    for bsz in (8, 16, 32, 64):
        mb.append((kk // (2 * bsz) == qq // (2 * bsz)) & (kk // bsz != qq // bsz))
    put("mblk", np.concatenate(mb, axis=1).astype(np.float32))
    GRID_W = 64
    rows = TLAT // GRID_W
    row = np.repeat(np.arange(rows, dtype=np.float32), GRID_W)
    col = np.tile(np.arange(GRID_W, dtype=np.float32), rows)
    half = 64
    inv_freq = (10000.0 ** (-np.arange(0, half, 2, dtype=np.float32) / half)).astype(np.float32)
    ang_r = row[:, None] * inv_freq
    ang_c = col[:, None] * inv_freq
    ang = np.concatenate([ang_r, ang_r, ang_c, ang_c], axis=-1)
    rope = np.concatenate([np.cos(ang).T, np.sin(ang).T], axis=1).astype(np.float32)
    return C, np.ascontiguousarray(rope)


class K:
    pass


def segs(t0, n):
    out = []
    if t0 < LCTX:
        e = min(t0 + n, LCTX)
        out.append((t0, e - t0, 1))
        if t0 + n > LCTX:
            out.append((LCTX, t0 + n - LCTX, 0))
    else:
        out.append((t0, n, 0))
    return out


def bc(ap, shape):
    return ap.broadcast_to(list(shape))


def stage_adaln(k, l):
    P = k.P
    with contextlib.ExitStack() as st:
        wb = [P.sb([128, 16, 256], BF16, st) for _ in range(2)]
        pm = P.ps([128, 96, 2], F32, st)
        for blk in range(48):
            b = blk % 2
            P.add("pool", lambda e, b=b, blk=blk: e.dma_start(out=wb[b][:], in_=k.w_mod[l, blk]),
                  writes=[("wb", b)], dma=True)
            for s in range(2):
                j = blk * 2 + s
                for kc in range(16):
                    P.add("pe", lambda e, b=b, s=s, j=j, kc=kc: e.matmul(
                        pm[:, j, :], wb[b][:, kc, s * 128:(s + 1) * 128], k.siluc[:, kc, :],
                        start=(kc == 0), stop=(kc == 15)),
                        reads=[("wb", b), "siluc"], writes=["pm"])
        bo = VOFF["bmod"][0] + l * 96
        P.add("dve", lambda e: e.tensor_tensor(out=k.mod[:], in0=pm[:], in1=bc(k.vecs[:, bo:bo + 96].unsqueeze(2), [128, 96, 2]),
                                               op=ALU.add), reads=["pm"], writes=["mod"])
        for (dst, m, gname) in ((k.A1, 1, "n1g"), (k.A2, 4, "n2g")):
            go = VOFF[gname][0] + l * 16
            P.add("dve", lambda e, dst=dst, m=m: e.tensor_scalar_add(out=dst[:], in0=k.mod[:, m * 16:(m + 1) * 16, :], scalar1=1.0),
                  reads=["mod"], writes=[("A", m)])
            P.add("dve", lambda e, dst=dst, go=go: e.tensor_tensor(
                out=dst[:], in0=dst[:], in1=bc(k.vecs[:, go:go + 16].unsqueeze(2), [128, 16, 2]), op=ALU.mult),
                reads=[("A", m)], writes=[("A", m)])
    P.barrier()


def modv(k, m, c, s):
    return k.mod[:, m * 16 + c, s:s + 1]


def stage_norm(k, which, nx, nbuf=2):
    P = k.P
    A = k.A1 if which == 1 else k.A2
    msh = 0 if which == 1 else 3
    hsrc = k.hT.rearrange("(c p) t -> p c t", p=128)
    with contextlib.ExitStack() as st:
        hs = [P.sb([128, 16, 512], F32, st) for _ in range(nbuf)]
        sq = P.sb([128, 16, 512], BF16, st)
        rstd = P.sb([128, 512], F32, st)
        pss = P.ps([128, 512], F32, st)
        for ti, (t0, n) in enumerate(TILES):
            b = ti % nbuf
            P.add("sp", lambda e, b=b, t0=t0, n=n: e.dma_start(out=hs[b][:, :, :n], in_=hsrc[:, :, t0:t0 + n]),
                  reads=["hT"], writes=[("hs", b)], dma=True)
            P.add("act", lambda e, b=b, n=n: e.activation(out=sq[:, :, :n], in_=hs[b][:, :, :n], func=AF.Square),
                  reads=[("hs", b)], writes=["sq"])
            for c in range(16):
                P.add("pe", lambda e, c=c, n=n: e.matmul(pss[:, :n], k.onesD[:], sq[:, c, :n], start=(c == 0), stop=(c == 15)),
                      reads=["sq"], writes=["pss"])
            P.add("act", lambda e, n=n: e.activation(out=rstd[:, :n], in_=pss[:, :n], func=AF.Sqrt, bias=EPS, scale=1.0),
                  reads=["pss"], writes=["rstd"])
            P.add("dve", lambda e, n=n: e.reciprocal(rstd[:, :n], rstd[:, :n]), reads=["rstd"], writes=["rstd"])
            P.add("dve", lambda e, b=b, n=n: e.tensor_tensor(out=hs[b][:, :, :n], in0=hs[b][:, :, :n],
                                                             in1=bc(rstd[:, :n].unsqueeze(1), [128, 16, n]), op=ALU.mult),
                  reads=["rstd", ("hs", b)], writes=[("hs", b)])
            s = 1 if t0 < LCTX else 0
            for c in range(16):
                P.add("act", lambda e, b=b, c=c, t0=t0, n=n, s=s: e.activation(
                    out=nx[:, c, t0:t0 + n], in_=hs[b][:, c, :n], func=AF.Identity,
                    bias=modv(k, msh, c, s), scale=A[:, c, s:s + 1]),
                    reads=[("hs", b), "mod", ("A", 1), ("A", 4)], writes=[("nx", ti, c)])


def stage_ffn(k, l):
    P = k.P
    with contextlib.ExitStack() as st0:
        nx = P.sb([128, 16, NT], BF16, st0)
        stage_norm(k, 2, nx)
        P.barrier()
        stage_ffn_up(k, l, nx)
        P.barrier()
    stage_ffn_down(k, l)


def stage_ffn_up(k, l, nx):
    P = k.P
    with contextlib.ExitStack() as st:
        wb = [P.sb([128, 16, 256], BF16, st) for _ in range(3)]
        G = [P.sb([128, NT], F32, st) for _ in range(2)]
        Vv = [P.sb([128, NT], BF16, st) for _ in range(2)]
        acc = [P.sb([128, NT], F32, st) for _ in range(2)]
        sg = [P.sb([128, NT], BF16, st) for _ in range(2)]
        hh = [P.sb([128, NT], BF16, st) for _ in range(2)]
        psg = [P.ps([128, 512], F32, st) for _ in range(2)]
        psv = [P.ps([128, 512], F32, st) for _ in range(2)]
        cw = VOFF["fcw"][0] + l * 3 * NFF
        cb = VOFF["fcb"][0] + l * NFF

        def make_post(j):
            b = j % 2
            Gk = [("G", b, ti) for ti in range(5)]
            Vk = [("V", b, ti) for ti in range(5)]
            w0 = k.vecs[:, cw + 0 * NFF + j:cw + 0 * NFF + j + 1]
            w1 = k.vecs[:, cw + 1 * NFF + j:cw + 1 * NFF + j + 1]
            w2 = k.vecs[:, cw + 2 * NFF + j:cw + 2 * NFF + j + 1]
            bb = k.vecs[:, cb + j:cb + j + 1]
            conv = []
            conv.append(lambda: P.add("dve", lambda e: e.tensor_scalar(out=acc[b][:], in0=G[b][:], scalar1=w1, scalar2=bb,
                                                                       op0=ALU.mult, op1=ALU.add), reads=Gk, writes=[("acc", b)]))
            for (s0, sn) in ((0, LCTX), (LCTX, TLAT)):
                conv.append(lambda s0=s0, sn=sn: P.add("dve", lambda e: e.scalar_tensor_tensor(
                    out=acc[b][:, s0 + 1:s0 + sn], in0=G[b][:, s0:s0 + sn - 1], scalar=w0, in1=acc[b][:, s0 + 1:s0 + sn],
                    op0=ALU.mult, op1=ALU.add), reads=Gk + [("acc", b)], writes=[("acc", b)]))
                conv.append(lambda s0=s0, sn=sn: P.add("dve", lambda e: e.scalar_tensor_tensor(
                    out=acc[b][:, s0:s0 + sn - 1], in0=G[b][:, s0 + 1:s0 + sn], scalar=w2, in1=acc[b][:, s0:s0 + sn - 1],
                    op0=ALU.mult, op1=ALU.add), reads=Gk + [("acc", b)], writes=[("acc", b)]))

            def tail():
                P.add("act", lambda e: e.activation(out=sg[b][:], in_=acc[b][:], func=AF.Silu), reads=[("acc", b)], writes=[("sg", b)])
                P.add("pool", lambda e: e.tensor_tensor(out=hh[b][:], in0=sg[b][:], in1=Vv[b][:], op=ALU.mult),
                      reads=[("sg", b)] + Vk, writes=[("hh", b)])
                P.add("sp", lambda e: e.dma_start(out=k.HF[j * 128:(j + 1) * 128, :], in_=hh[b][:]),
                      reads=[("hh", b)], writes=["HF"], dma=True)
            return conv, tail

        pending = None
        for j in range(NFF):
            b = j % 2
            wi = j % 3
            P.add("pool", lambda e, wi=wi, j=j: e.dma_start(out=wb[wi][:], in_=k.ffn_w_up[l, j]),
                  writes=[("wb", wi)], dma=True)
            conv_prev = list(pending[0]) if pending else []
            for ti, (t0, n) in enumerate(TILES):
                pb = ti % 2
                for kc in range(16):
                    P.add("pe", lambda e, wi=wi, pb=pb, kc=kc, t0=t0, n=n: e.matmul(
                        psg[pb][:, :n], wb[wi][:, kc, 0:128], nx[:, kc, t0:t0 + n], start=(kc == 0), stop=(kc == 15)),
                        reads=[("wb", wi)], writes=[("psg", pb)])
                for kc in range(16):
                    P.add("pe", lambda e, wi=wi, pb=pb, kc=kc, t0=t0, n=n: e.matmul(
                        psv[pb][:, :n], wb[wi][:, kc, 128:256], nx[:, kc, t0:t0 + n], start=(kc == 0), stop=(kc == 15)),
                        reads=[("wb", wi)], writes=[("psv", pb)])
                P.add("act", lambda e, b=b, pb=pb, t0=t0, n=n: e.activation(out=G[b][:, t0:t0 + n], in_=psg[pb][:, :n], func=AF.Copy),
                      reads=[("psg", pb)], writes=[("G", b, ti)])
                P.add("dve", lambda e, b=b, pb=pb, t0=t0, n=n: e.tensor_copy(Vv[b][:, t0:t0 + n], psv[pb][:, :n]),
                      reads=[("psv", pb)], writes=[("V", b, ti)])
                if conv_prev:
                    conv_prev.pop(0)()
            for c_ in conv_prev:
                c_()
            if pending:
                pending[1]()
            pending = make_post(j)
        for c_ in pending[0]:
            c_()
        pending[1]()


def stage_ffn_down(k, l):
    P = k.P
    HALF = NT // 2
    SUB = 384
    with contextlib.ExitStack() as st:
        hres = P.sb([128, NFF, HALF], BF16, st)
        wb = [P.sb([128, NFF, 256], BF16, st) for _ in range(2)]
        hrow = [P.sb([128, HALF], F32, st) for _ in range(2)]
        pso = [P.ps([128, 512], F32, st) for _ in range(4)]
        hfsrc = k.HF.rearrange("(j p) t -> p j t", p=128)
        hsrc = k.hT.rearrange("(c p) t -> p c t", p=128)
        nps = 0
        for half in range(2):
            h0 = half * HALF
            for q in range(4):
                P.add("sp", lambda e, q=q, h0=h0: e.dma_start(out=hres[:, q * 11:(q + 1) * 11, :], in_=hfsrc[:, q * 11:(q + 1) * 11, h0:h0 + HALF]),
                      reads=["HF"], writes=[("hres", q)], dma=True)
            for blk in range(8):
                b = blk % 2
                for q in range(2):
                    P.add("pool", lambda e, b=b, blk=blk, q=q: e.dma_start(
                        out=wb[b][:, q * 22:(q + 1) * 22, :], in_=k.ffn_w_down[l, blk][:, q * 22:(q + 1) * 22, :]),
                        writes=[("wb", b, q)], dma=True)
                for s in range(2):
                    c = blk * 2 + s
                    hb = c % 2
                    P.add("sp", lambda e, hb=hb, c=c, h0=h0: e.dma_start(out=hrow[hb][:], in_=hsrc[:, c, h0:h0 + HALF]),
                          reads=[("hT", c)], writes=[("hrow", hb)], dma=True)
                    for sub in range(HALF // SUB):
                        u0 = sub * SUB
                        pb = nps % 4
                        nps += 1
                        for kc in range(NFF):
                            P.add("pe", lambda e, b=b, s=s, pb=pb, kc=kc, u0=u0: e.matmul(
                                pso[pb][:, :SUB], wb[b][:, kc, s * 128:(s + 1) * 128], hres[:, kc, u0:u0 + SUB],
                                start=(kc == 0), stop=(kc == NFF - 1)),
                                reads=[("wb", b, kc // 22), ("hres", kc // 11)], writes=[("pso", pb)])
                        for (g0, gn, sidx) in segs(h0 + u0, SUB):
                            r0 = g0 - h0
                            P.add("dve", lambda e, hb=hb, pb=pb, r0=r0, gn=gn, u0=u0, c=c, sidx=sidx: e.scalar_tensor_tensor(
                                out=hrow[hb][:, r0:r0 + gn], in0=pso[pb][:, r0 - u0:r0 - u0 + gn], scalar=modv(k, 5, c, sidx),
                                in1=hrow[hb][:, r0:r0 + gn], op0=ALU.mult, op1=ALU.add),
                                reads=[("pso", pb), ("hrow", hb), "mod"], writes=[("hrow", hb)])
                    P.add("sp", lambda e, hb=hb, c=c, h0=h0: e.dma_start(out=hsrc[:, c, h0:h0 + HALF], in_=hrow[hb][:]),
                          reads=[("hrow", hb)], writes=[("hT", c)], dma=True)
    P.barrier()


DEBUG_OUT = ()


def build_program(plan, test_out_ctx=False):
    nc = bass.Bass("TRN2", target_bir_lowering=False)
    k = K()
    k.nc = nc
    P = Prog(nc)
    k.P = P
    def dt(name, shape, dtype, kind="Internal"):
        if name in DEBUG_OUT:
            kind = "ExternalOutput"
        return nc.dram_tensor(name, shape, dtype, kind=kind)
    k.h_in = dt("h_in", [D, NT], F32, kind="ExternalInput").ap()
    k.vecs_d = dt("vecs", [128, NV], F32, kind="ExternalInput").ap()
    k.consts_d = dt("consts", [128, NCONST], F32, kind="ExternalInput").ap()
    k.rope_d = dt("rope", [128, 2 * TLAT], F32, kind="ExternalInput").ap()
    k.w_mod = dt("w_mod", [DEPTH, 48, 128, 16, 256], F32, kind="ExternalInput").ap()
    k.ffn_w_up = dt("ffn_w_up", [DEPTH, NFF, 128, 16, 256], F32, kind="ExternalInput").ap()
    k.ffn_w_down = dt("ffn_w_down", [DEPTH, 8, 128, NFF, 256], F32, kind="ExternalInput").ap()
    k.even_w_in = dt("even_w_in", [2, 14, 128, 16, 256], F32, kind="ExternalInput").ap()
    k.even_w_out = dt("even_w_out", [2, 8, 128, 16, 256], F32, kind="ExternalInput").ap()
    k.dn_w_in = dt("dn_w_in", [2, 48, 128, 16, 256], F32, kind="ExternalInput").ap()
    k.dn_w_ba = dt("dn_w_ba", [2, 128, 16, 128], F32, kind="ExternalInput").ap()
    k.dn_w_out = dt("dn_w_out", [2, 8, 128, 32, 256], F32, kind="ExternalInput").ap()
    k.hT = dt("hT", [D, NT], F32, kind="ExternalOutput").ap()
    k.HF = dt("HF", [DFF, NT], BF16, kind="Internal").ap()
    k.QT = dt("QT", [8, 128, NT], BF16, kind="Internal").ap()
    k.KT = dt("KT", [2, 128, NT], BF16, kind="Internal").ap()
    k.VTM = dt("VTM", [18, 128, 256], BF16, kind="Internal").ap()
    k.CAT = dt("CAT", [16, 128, NT], BF16, kind="Internal").ap()
    k.DQT = dt("DQT", [16, 128, NT], BF16, kind="Internal").ap()
    k.DKT = dt("DKT", [16, 128, NT], BF16, kind="Internal").ap()
    k.DKTM = dt("DKTM", [18, 128, 2048], BF16, kind="Internal").ap()
    k.DVTM = dt("DVTM", [18, 128, 4096], BF16, kind="Internal").ap()
    k.DZ = dt("DZ", [18, 128, 4096], BF16, kind="Internal").ap()
    k.DPH = dt("DPH", [288, 128, 4, 4, 128], BF16, kind="Internal").ap()
    k.DO = dt("DO", [2, 18, 128, 4096], F32, kind="Internal").ap()
    k.DYT = dt("DYT", [32, 128, NT], BF16, kind="Internal").ap()
    k.DBG_G = dt("DBG_G", [5, 128, 18, 64], F32, kind="Internal").ap()

    k.vecs = P.sb([128, NV], F32)
    k.consts = P.sb([128, NCONST], F32)
    k.siluc = P.sb([128, 16, 2], BF16)
    k.mod = P.sb([128, 96, 2], F32)
    k.A1 = P.sb([128, 16, 2], F32)
    k.A2 = P.sb([128, 16, 2], F32)
    k.onesD = P.sb([128, 128], BF16)
    k.ones128 = P.sb([128, 128], BF16)
    k.ones1 = P.sb([128, 128], BF16)
    k.onesf128 = P.sb([128, 128], F32)
    k.bandb = P.sb([128, 384], BF16)
    k.identb = P.sb([128, 128], BF16)
    k.maskb = P.sb([128, 5, 128], BF16)

    P.add("sp", lambda e: e.dma_start(out=k.vecs[:], in_=k.vecs_d[:, :]), writes=["vecs"], dma=True)
    P.add("sp", lambda e: e.dma_start(out=k.consts[:], in_=k.consts_d[:, :]), writes=["consts"], dma=True)
    P.add("pool", lambda e: e.memset(k.onesD[:], 1.0 / D), writes=["onesD"])
    P.add("pool", lambda e: e.memset(k.ones128[:], 1.0 / 128), writes=["ones128"])
    P.add("pool", lambda e: e.memset(k.ones1[:], 1.0), writes=["ones1"])
    P.add("pool", lambda e: e.memset(k.onesf128[:], 1.0 / 128), writes=["onesf128"])
    P.add("dve", lambda e: e.tensor_copy(k.bandb[:], k.consts[:, COFF["band"][0]:COFF["band"][0] + 384]), reads=["consts"], writes=["bandb"])
    P.add("dve", lambda e: e.tensor_copy(k.maskb[:].rearrange("p q j -> p (q j)"), k.consts[:, COFF["mblk"][0]:COFF["mblk"][0] + 640]), reads=["consts"], writes=["maskb"])
    P.add("dve", lambda e: e.tensor_copy(k.identb[:], k.consts[:, COFF["ident"][0]:COFF["ident"][0] + 128]), reads=["consts"], writes=["identb"])
    co = VOFF["c"][0]
    P.add("act", lambda e: e.activation(out=k.siluc[:, :, 0], in_=k.vecs[:, co:co + 16], func=AF.Silu), reads=["vecs"], writes=["siluc"])
    co2 = VOFF["cctx"][0]
    P.add("act", lambda e: e.activation(out=k.siluc[:, :, 1], in_=k.vecs[:, co2:co2 + 16], func=AF.Silu), reads=["vecs", "siluc"], writes=["siluc"])
    with contextlib.ExitStack() as st:
        tmp = [P.sb([128, 4, NT], F32, st) for _ in range(2)]
        src = k.h_in.rearrange("(c p) t -> p c t", p=128)
        dst = k.hT.rearrange("(c p) t -> p c t", p=128)
        for q in range(4):
            b = q % 2
            P.add("sp", lambda e, b=b, q=q: e.dma_start(out=tmp[b][:], in_=src[:, q * 4:(q + 1) * 4, :]), writes=[("tmp", b)], dma=True)
            P.add("sp", lambda e, b=b, q=q: e.dma_start(out=dst[:, q * 4:(q + 1) * 4, :], in_=tmp[b][:]), reads=[("tmp", b)], writes=["hT"], dma=True)
    P.barrier()

    for (l, parts) in plan:
        stage_adaln(k, l)
        if "mix" in parts:
            if l % 2 == 0:
                stage_even(k, l)
            else:
                stage_dn(k, l)
        if "ffn" in parts:
            stage_ffn(k, l)
    P.finish()
    return nc


def stage_even(k, l):
    P = k.P
    with contextlib.ExitStack() as st0:
        nx = P.sb([128, 16, NT], BF16, st0)
        stage_norm(k, 1, nx)
        P.barrier()
        even_qkv(k, l, nx)
        P.barrier()
        even_glu(k, l, nx)
        P.barrier()
    even_attn(k, l)
    P.barrier()
    even_out(k, l)
    P.barrier()


def rsqrt_ops(P, out_ap, in_ap, reads, wkey):
    P.add("act", lambda e: e.activation(out=out_ap, in_=in_ap, func=AF.Sqrt, bias=EPS, scale=1.0), reads=reads, writes=[wkey])
    P.add("dve", lambda e: e.reciprocal(out_ap, out_ap), reads=[wkey], writes=[wkey])


def even_qkv(k, l, nx):
    P = k.P
    i = l // 2
    cos0 = 0; sin0 = TLAT; rp0 = COFF["ropeT"][0]
    with contextlib.ExitStack() as st:
        rope = P.sb([128, 2 * TLAT], F32, st)
        P.add("sp", lambda e: e.dma_start(out=rope[:], in_=k.rope_d[:, :]), writes=["rope"], dma=True)
        wb = [P.sb([128, 16, 256], BF16, st) for _ in range(2)]
        psa = [P.ps([128, 512], F32, st) for _ in range(2)]
        psb = P.ps([128, 512], F32, st)
        psc = P.ps([128, 512], F32, st)
        x0 = [P.sb([128, 512], F32, st) for _ in range(2)]
        sqb = P.sb([128, 512], BF16, st)
        rr = P.sb([128, 512], F32, st)
        qn = P.sb([128, 512], F32, st)
        t1 = P.sb([128, 512], F32, st)
        t2 = P.sb([128, 512], F32, st)
        qo = [P.sb([128, NT], BF16, st) for _ in range(2)]
        vt = P.sb([128, 18, 256], BF16, st)
        gs = P.sb([128, 2], F32, st)
        qg0 = VOFF["qg"][0] + i; kg0 = VOFF["kg"][0] + i
        P.add("act", lambda e: e.mul(gs[:, 0:1], k.vecs[:, qg0:qg0 + 1], 128 ** -0.5), writes=["gs"])
        P.add("act", lambda e: e.copy(gs[:, 1:2], k.vecs[:, kg0:kg0 + 1]), reads=["gs"], writes=["gs"])
        nblk = 0
        nchunk = 0
        npa = 0
        for blk in range(5):
            b = nblk % 2; nblk += 1
            P.add("pool", lambda e, b=b, blk=blk: e.dma_start(out=wb[b][:], in_=k.even_w_in[i, blk]),
                  writes=[("wb", b)], dma=True)
            for s in range(2):
                ch = blk * 2 + s
                isq = ch < 8
                qb = nchunk % 2; nchunk += 1
                gcol = gs[:, 0:1] if isq else gs[:, 1:2]
                for ti, (t0, n) in enumerate(TILES):
                    pb = npa % 2; npa += 1
                    for kc in range(16):
                        P.add("pe", lambda e, b=b, s=s, pb=pb, kc=kc, t0=t0, n=n: e.matmul(
                            psa[pb][:, :n], wb[b][:, kc, s * 128:(s + 1) * 128], nx[:, kc, t0:t0 + n], start=(kc == 0), stop=(kc == 15)),
                            reads=[("wb", b)], writes=[("psa", pb)])
                    P.add("act", lambda e, pb=pb, n=n: e.activation(out=x0[pb][:, :n], in_=psa[pb][:, :n], func=AF.Copy),
                          reads=[("psa", pb)], writes=[("x0", pb)])
                    P.add("act", lambda e, pb=pb, n=n: e.activation(out=sqb[:, :n], in_=psa[pb][:, :n], func=AF.Square),
                          reads=[("psa", pb)], writes=["sqb"])
                    P.add("pe", lambda e, n=n: e.matmul(psb[:, :n], k.ones128[:], sqb[:, :n], start=True, stop=True),
                          reads=["sqb"], writes=["psb"])
                    rsqrt_ops(P, rr[:, :n], psb[:, :n], ["psb"], "rr")
                    if t0 < LCTX:
                        P.add("dve", lambda e, pb=pb, qb=qb, n=n, t0=t0, gcol=gcol: e.scalar_tensor_tensor(
                            out=qo[qb][:, t0:t0 + n], in0=x0[pb][:, :n], scalar=gcol, in1=rr[:, :n], op0=ALU.mult, op1=ALU.mult),
                            reads=[("x0", pb), "rr", "gs"], writes=[("qo", qb, ti)])
                    else:
                        P.add("dve", lambda e, pb=pb, n=n, gcol=gcol: e.scalar_tensor_tensor(
                            out=qn[:, :n], in0=x0[pb][:, :n], scalar=gcol, in1=rr[:, :n], op0=ALU.mult, op1=ALU.mult),
                            reads=[("x0", pb), "rr", "gs"], writes=["qn"])
                        P.add("pe", lambda e, n=n: e.matmul(psc[:, :n], k.consts[:, rp0:rp0 + 128], qn[:, :n], start=True, stop=True),
                              reads=["qn"], writes=["psc"])
                        c0 = cos0 + t0 - LCTX; s0 = sin0 + t0 - LCTX
                        P.add("dve", lambda e, n=n, c0=c0: e.tensor_tensor(out=t1[:, :n], in0=qn[:, :n], in1=rope[:, c0:c0 + n], op=ALU.mult),
                              reads=["qn", "rope"], writes=["t1"])
                        P.add("dve", lambda e, n=n, s0=s0: e.tensor_tensor(out=t2[:, :n], in0=psc[:, :n], in1=rope[:, s0:s0 + n], op=ALU.mult),
                              reads=["psc", "rope"], writes=["t2"])
                        P.add("pool", lambda e, qb=qb, n=n, t0=t0: e.tensor_tensor(out=qo[qb][:, t0:t0 + n], in0=t1[:, :n], in1=t2[:, :n], op=ALU.add),
                              reads=["t1", "t2"], writes=[("qo", qb, ti)])
                dst = k.QT[ch] if isq else k.KT[ch - 8]
                P.add("sp", lambda e, qb=qb, dst=dst: e.dma_start(out=dst, in_=qo[qb][:]),
                      reads=[("qo", qb, ti) for ti in range(5)], writes=[("qkT", ch)], dma=True)
        b = nblk % 2; nblk += 1
        P.add("pool", lambda e, b=b: e.dma_start(out=wb[b][:], in_=k.even_w_in[i, 5]), writes=[("wb", b)], dma=True)
        for tt in range(18):
            pb = npa % 2; npa += 1
            for kc in range(16):
                P.add("pe", lambda e, b=b, pb=pb, kc=kc, tt=tt: e.matmul(
                    psa[pb][:, :256], nx[:, kc, tt * 128:(tt + 1) * 128], wb[b][:, kc, :], start=(kc == 0), stop=(kc == 15)),
                    reads=[("wb", b)], writes=[("psa", pb)])
            P.add("act", lambda e, pb=pb, tt=tt: e.activation(out=vt[:, tt, :], in_=psa[pb][:, :256], func=AF.Copy),
                  reads=[("psa", pb)], writes=[("vt", tt)])
        P.add("sp", lambda e: e.dma_start(out=k.VTM.rearrange("t p c -> p t c"), in_=vt[:]),
              reads=[("vt", tt) for tt in range(18)], writes=["VTM"], dma=True)


def even_glu(k, l, nx):
    P = k.P
    i = l // 2
    cdw = VOFF["cdw"][0] + i * 31 * 8
    cdb = VOFF["cdb"][0] + i * 8
    lng = VOFF["lng"][0] + i * 8
    lnb = VOFF["lnb"][0] + i * 8
    NDVE = 31
    with contextlib.ExitStack() as st:
        wb = [P.sb([128, 16, 256], BF16, st) for _ in range(2)]
        psa = [P.ps([128, 512], F32, st) for _ in range(2)]
        psb = [P.ps([128, 512], F32, st) for _ in range(2)]
        psm = P.ps([128, 512], F32, st)
        psq = P.ps([128, 512], F32, st)
        sgm = [P.sb([128, 512], F32, st) for _ in range(2)]
        U = [P.sb([128, NT], F32, st) for _ in range(2)]
        accA = P.sb([128, NT], F32, st)
        accB = P.sb([128, NT], F32, st)
        hsq = P.sb([128, NT], F32, st)
        co = [P.sb([128, NT], BF16, st) for _ in range(2)]
        msb = P.sb([128, 512], F32, st)
        m2 = P.sb([128, 512], F32, st)
        var = P.sb([128, 512], F32, st)
        dd = P.sb([128, 512], F32, st)
        npa = 0
        for j in range(8):
            b = j % 2
            P.add("pool", lambda e, b=b, j=j: e.dma_start(out=wb[b][:], in_=k.even_w_in[i, 6 + j]),
                  writes=[("wb", b, 0), ("wb", b, 1)], dma=True)
            for ti, (t0, n) in enumerate(TILES):
                pb = npa % 2; npa += 1
                for kc in range(16):
                    P.add("pe", lambda e, b=b, pb=pb, kc=kc, t0=t0, n=n: e.matmul(
                        psa[pb][:, :n], wb[b][:, kc, 0:128], nx[:, kc, t0:t0 + n], start=(kc == 0), stop=(kc == 15)),
                        reads=[("wb", b, 0)], writes=[("psa", pb)])
                for kc in range(16):
                    P.add("pe", lambda e, b=b, pb=pb, kc=kc, t0=t0, n=n: e.matmul(
                        psb[pb][:, :n], wb[b][:, kc, 128:256], nx[:, kc, t0:t0 + n], start=(kc == 0), stop=(kc == 15)),
                        reads=[("wb", b, 1)], writes=[("psb", pb)])
                P.add("act", lambda e, pb=pb, n=n: e.activation(out=sgm[pb][:, :n], in_=psb[pb][:, :n], func=AF.Sigmoid),
                      reads=[("psb", pb)], writes=[("sgm", pb)])
                P.add("dve", lambda e, b=b, pb=pb, t0=t0, n=n: e.tensor_tensor(out=U[b][:, t0:t0 + n], in0=psa[pb][:, :n], in1=sgm[pb][:, :n], op=ALU.mult),
                      reads=[("psa", pb), ("sgm", pb)], writes=[("U", b, ti)])
            Uk = [("U", b, ti) for ti in range(5)]
            wc = lambda kk, j=j: k.vecs[:, cdw + kk * 8 + j:cdw + kk * 8 + j + 1]
            bias = k.vecs[:, cdb + j:cdb + j + 1]
            P.add("dve", lambda e, b=b, w=wc(15), bias=bias: e.tensor_scalar(out=accA[:], in0=U[b][:], scalar1=w, scalar2=bias,
                                                                            op0=ALU.mult, op1=ALU.add), reads=Uk, writes=["accA"])
            P.add("pool", lambda e: e.memset(accB[:], 0.0), writes=["accB"])
            taps = [kk for kk in range(31) if kk != 15]
            dve_taps = taps[:NDVE - 1]
            for kk in taps:
                sh = kk - 15
                eng, acc, akey = ("dve", accA, "accA") if kk in dve_taps else ("pool", accB, "accB")
                for (s0, sn) in ((0, LCTX), (LCTX, TLAT)):
                    lo = max(0, -sh); hi = sn - max(0, sh)
                    P.add(eng, lambda e, b=b, acc=acc, s0=s0, lo=lo, hi=hi, sh=sh, w=wc(kk): e.scalar_tensor_tensor(
                        out=acc[:, s0 + lo:s0 + hi], in0=U[b][:, s0 + lo + sh:s0 + hi + sh], scalar=w, in1=acc[:, s0 + lo:s0 + hi],
                        op0=ALU.mult, op1=ALU.add), reads=Uk + [akey], writes=[akey])
            P.add("dve", lambda e: e.tensor_tensor(out=accA[:], in0=accA[:], in1=accB[:], op=ALU.add), reads=["accA", "accB"], writes=["accA"])
            P.add("act", lambda e: e.activation(out=hsq[:], in_=accA[:], func=AF.Square), reads=["accA"], writes=["hsq"])
            for ti, (t0, n) in enumerate(TILES):
                P.add("pe", lambda e, t0=t0, n=n: e.matmul(psm[:, :n], k.onesf128[:], accA[:, t0:t0 + n], start=True, stop=True),
                      reads=["accA"], writes=["psm"])
                P.add("pe", lambda e, t0=t0, n=n: e.matmul(psq[:, :n], k.onesf128[:], hsq[:, t0:t0 + n], start=True, stop=True),
                      reads=["hsq"], writes=["psq"])
                P.add("act", lambda e, n=n: e.activation(out=msb[:, :n], in_=psm[:, :n], func=AF.Copy), reads=["psm"], writes=["msb"])
                P.add("act", lambda e, n=n: e.activation(out=m2[:, :n], in_=psm[:, :n], func=AF.Square), reads=["psm"], writes=["m2"])
                P.add("dve", lambda e, n=n: e.tensor_tensor(out=var[:, :n], in0=psq[:, :n], in1=m2[:, :n], op=ALU.subtract),
                      reads=["psq", "m2"], writes=["var"])
                P.add("dve", lambda e, n=n: e.tensor_scalar_max(out=var[:, :n], in0=var[:, :n], scalar1=0.0), reads=["var"], writes=["var"])
                rsqrt_ops(P, var[:, :n], var[:, :n], ["var"], "var")
                P.add("dve", lambda e, t0=t0, n=n: e.tensor_tensor(out=dd[:, :n], in0=accA[:, t0:t0 + n], in1=msb[:, :n], op=ALU.subtract),
                      reads=["accA", "msb"], writes=["dd"])
                P.add("dve", lambda e, n=n: e.tensor_tensor(out=dd[:, :n], in0=dd[:, :n], in1=var[:, :n], op=ALU.mult),
                      reads=["dd", "var"], writes=["dd"])
                P.add("act", lambda e, b=b, j=j, t0=t0, n=n: e.activation(
                    out=co[b][:, t0:t0 + n], in_=dd[:, :n], func=AF.Silu,
                    bias=k.vecs[:, lnb + j:lnb + j + 1], scale=k.vecs[:, lng + j:lng + j + 1]),
                    reads=["dd"], writes=[("co", b, ti)])
            P.add("sp", lambda e, b=b, j=j: e.dma_start(out=k.CAT[8 + j], in_=co[b][:]),
                  reads=[("co", b, ti) for ti in range(5)], writes=[("CAT", 8 + j)], dma=True)


def even_attn(k, l):
    P = k.P
    i = l // 2
    band0 = COFF["band"][0]
    with contextlib.ExitStack() as st:
        kT = P.sb([128, 2, NT], BF16, st)
        V = P.sb([128, 18, 256], BF16, st)
        qh = [P.sb([128, NT], BF16, st) for _ in range(2)]
        PC = [[P.sb([128, NT], BF16, st) for _ in range(2)] for _ in range(2)]
        PB = [P.sb([128, 16, 384], BF16, st) for _ in range(2)]
        tmpE = [P.sb([128, 384], BF16, st) for _ in range(2)]
        ao = [P.sb([128, NT], BF16, st) for _ in range(2)]
        den = P.sb([128, 512], F32, st)
        esink = P.sb([128, 8], F32, st)
        pss = [P.ps([128, 512], F32, st) for _ in range(2)]
        pso = [P.ps([128, 512], F32, st) for _ in range(2)]
        psd = [P.ps([128, 512], F32, st) for _ in range(2)]
        so = VOFF["sink"][0] + i * 8
        P.add("act", lambda e: e.activation(out=esink[:], in_=k.vecs[:, so:so + 8], func=AF.Exp), writes=["esink"])
        for g in range(2):
            P.add("sp", lambda e, g=g: e.dma_start(out=kT[:, g, :], in_=k.KT[g]), writes=[("kT", g)], dma=True)
        P.add("sp", lambda e: e.dma_start(out=V[:], in_=k.VTM.rearrange("t p c -> p t c")), writes=["V"], dma=True)
        nps = 0
        npo = 0
        for h in range(8):
            g = h // 4
            hb = h % 2
            P.add("sp", lambda e, hb=hb, h=h: e.dma_start(out=qh[hb][:], in_=k.QT[h]), writes=[("qh", hb)], dma=True)
            for cb in range(2):
                for ti, (t0, n) in enumerate(TILES):
                    pb = nps % 2; nps += 1
                    P.add("pe", lambda e, pb=pb, g=g, cb=cb, hb=hb, t0=t0, n=n: e.matmul(
                        pss[pb][:, :n], kT[:, g, cb * 128:(cb + 1) * 128], qh[hb][:, t0:t0 + n], start=True, stop=True),
                        reads=[("kT", g), ("qh", hb)], writes=[("pss", pb)])
                    P.add("act", lambda e, pb=pb, hb=hb, cb=cb, t0=t0, n=n: e.activation(
                        out=PC[hb][cb][:, t0:t0 + n], in_=pss[pb][:, :n], func=AF.Exp),
                        reads=[("pss", pb)], writes=[("PC", hb, cb, ti)])
            for jb in range(16):
                lo = max(jb - 1, 0); hi = min(jb + 1, 15)
                n = (hi - lo + 1) * 128
                off = (lo - (jb - 1)) * 128
                pb = nps % 2; nps += 1
                eb = jb % 2
                P.add("pe", lambda e, pb=pb, g=g, jb=jb, hb=hb, lo=lo, n=n: e.matmul(
                    pss[pb][:, :n], kT[:, g, LCTX + jb * 128:LCTX + (jb + 1) * 128], qh[hb][:, LCTX + lo * 128:LCTX + lo * 128 + n],
                    start=True, stop=True), reads=[("kT", g), ("qh", hb)], writes=[("pss", pb)])
                P.add("act", lambda e, pb=pb, eb=eb, n=n: e.activation(out=tmpE[eb][:, :n], in_=pss[pb][:, :n], func=AF.Exp),
                      reads=[("pss", pb)], writes=[("tmpE", eb)])
                P.add("pool", lambda e, eb=eb, hb=hb, jb=jb, off=off, n=n: e.tensor_tensor(
                    out=PB[hb][:, jb, off:off + n], in0=tmpE[eb][:, :n], in1=k.bandb[:, off:off + n], op=ALU.mult),
                    reads=[("tmpE", eb)], writes=[("PB", hb, jb)])
            for ti, (t0, n) in enumerate(TILES):
                ob = npo % 2; npo += 1
                mms = []
                for cb in range(2):
                    mms.append((V[:, cb, g * 128:(g + 1) * 128], PC[hb][cb][:, t0:t0 + n], 0, n, [("PC", hb, cb, ti)]))
                if t0 >= LCTX:
                    qt = (t0 - LCTX) // 512
                    for nb in range(4 * qt, 4 * qt + 4):
                        for jb in (nb - 1, nb, nb + 1):
                            if 0 <= jb <= 15:
                                c0 = (nb - (jb - 1)) * 128
                                mms.append((V[:, 2 + jb, g * 128:(g + 1) * 128], PB[hb][:, jb, c0:c0 + 128], (nb - 4 * qt) * 128, 128,
                                            [("PB", hb, jb)]))
                for mi, (lhs, rhs, o0, on, rk) in enumerate(mms):
                    P.add("pe", lambda e, ob=ob, lhs=lhs, rhs=rhs, o0=o0, on=on, mi=mi, last=(mi == len(mms) - 1): e.matmul(
                        pso[ob][:, o0:o0 + on], lhs, rhs, start=(mi == 0), stop=last, skip_group_check=True),
                        reads=rk + ["V"], writes=[("pso", ob)])
                for mi, (lhs, rhs, o0, on, rk) in enumerate(mms):
                    P.add("pe", lambda e, ob=ob, rhs=rhs, o0=o0, on=on, mi=mi, last=(mi == len(mms) - 1): e.matmul(
                        psd[ob][:, o0:o0 + on], k.ones1[:], rhs, start=(mi == 0), stop=last, skip_group_check=True),
                        reads=rk, writes=[("psd", ob)])
                P.add("dve", lambda e, ob=ob, n=n, h=h: e.tensor_scalar_add(out=den[:, :n], in0=psd[ob][:, :n], scalar1=esink[:, h:h + 1]),
                      reads=[("psd", ob), "esink"], writes=["den"])
                P.add("dve", lambda e, n=n: e.reciprocal(den[:, :n], den[:, :n]), reads=["den"], writes=["den"])
                P.add("dve", lambda e, ob=ob, hb=hb, t0=t0, n=n: e.tensor_tensor(out=ao[hb][:, t0:t0 + n], in0=pso[ob][:, :n], in1=den[:, :n], op=ALU.mult),
                      reads=[("pso", ob), "den"], writes=[("ao", hb, ti)])
            P.add("sp", lambda e, hb=hb, h=h: e.dma_start(out=k.CAT[h], in_=ao[hb][:]),
                  reads=[("ao", hb, ti) for ti in range(5)], writes=[("CAT", h)], dma=True)


def proj_out_residual(k, wsrc, nkc, cat, gate_m, wkeys_extra=()):
    P = k.P
    hsrc = k.hT.rearrange("(c p) t -> p c t", p=128)
    with contextlib.ExitStack() as st:
        wb = [P.sb([128, nkc, 256], BF16, st) for _ in range(2)]
        hrow = [P.sb([128, NT], F32, st) for _ in range(2)]
        pso = [P.ps([128, 512], F32, st) for _ in range(4)]
        nps = 0
        for blk in range(8):
            b = blk % 2
            P.add("pool", lambda e, b=b, blk=blk: e.dma_start(out=wb[b][:], in_=wsrc(blk)),
                  writes=[("wb", b)], dma=True)
            for s in range(2):
                c = blk * 2 + s
                hb = c % 2
                P.add("sp", lambda e, hb=hb, c=c: e.dma_start(out=hrow[hb][:], in_=hsrc[:, c, :]),
                      reads=[("hT", c)], writes=[("hrow", hb)], dma=True)
                for ti, (t0, n) in enumerate(TILES):
                    pb = nps % 4; nps += 1
                    for kc in range(nkc):
                        P.add("pe", lambda e, b=b, s=s, pb=pb, kc=kc, t0=t0, n=n: e.matmul(
                            pso[pb][:, :n], wb[b][:, kc, s * 128:(s + 1) * 128], cat[:, kc, t0:t0 + n],
                            start=(kc == 0), stop=(kc == nkc - 1)),
                            reads=[("wb", b), ("cat", kc)], writes=[("pso", pb)])
                    sidx = 1 if t0 < LCTX else 0
                    P.add("dve", lambda e, hb=hb, pb=pb, t0=t0, n=n, c=c, sidx=sidx: e.scalar_tensor_tensor(
                        out=hrow[hb][:, t0:t0 + n], in0=pso[pb][:, :n], scalar=modv(k, gate_m, c, sidx),
                        in1=hrow[hb][:, t0:t0 + n], op0=ALU.mult, op1=ALU.add),
                        reads=[("pso", pb), ("hrow", hb), "mod"], writes=[("hrow", hb)])
                P.add("sp", lambda e, hb=hb, c=c: e.dma_start(out=hsrc[:, c, :], in_=hrow[hb][:]),
                      reads=[("hrow", hb)], writes=[("hT", c)], dma=True)


def even_out(k, l):
    P = k.P
    i = l // 2
    wsrc = lambda blk: k.even_w_out[i, blk]
    with contextlib.ExitStack() as st:
        cat = P.sb([128, 16, NT], BF16, st)
        for c in range(16):
            P.add("sp", lambda e, c=c: e.dma_start(out=cat[:, c, :], in_=k.CAT[c]), writes=[("cat", c)], dma=True)
        proj_out_residual(k, wsrc, 16, cat, 2)


DN_STOP = 99


def stage_dn(k, l):
    P = k.P
    with contextlib.ExitStack() as st1:
        k.BETA = P.sb([128, 18, 64], F32, st1)
        k.GG = P.sb([128, 18, 64], F32, st1)
        k.EG = P.sb([128, 18, 64], F32, st1)
        k.EL = P.sb([128, 18, 64], F32, st1)
        k.BEG = P.sb([128, 18, 64], F32, st1)
        k.ER = P.sb([128, 18, 64], F32, st1)
        with contextlib.ExitStack() as st0:
            nx = P.sb([128, 16, NT], BF16, st0)
            stage_norm(k, 1, nx, nbuf=1)
            P.barrier()
            if DN_STOP >= 1:
                dn_proj_qkv(k, l, nx)
                P.barrier()
            if DN_STOP >= 2:
                dn_proj_z(k, l, nx)
                P.barrier()
            if DN_STOP >= 3:
                dn_proj_ba(k, l, nx)
                P.barrier()
        if DN_STOP >= 4:
            dn_gates(k, l)
            P.barrier()
            if "DBG_G" in DEBUG_OUT:
                for ii, t in enumerate((k.BETA, k.GG, k.EG, k.EL, k.ER)):
                    P.add("sp", lambda e, ii=ii, t=t: e.dma_start(out=k.DBG_G[ii], in_=t[:]), dma=True)
                P.barrier()
        if DN_STOP >= 5:
            dn_prep(k, l)
            P.barrier()
        if DN_STOP >= 6:
            dn_scan(k, l)
            P.barrier()
    if DN_STOP >= 7:
        dn_outnorm(k, l)
        P.barrier()
    if DN_STOP >= 8:
        dn_outproj(k, l)
        P.barrier()


def dn_proj_qkv(k, l, nx):
    P = k.P
    i = l // 2
    dcw = VOFF["dcw"][0] + i * 5 * 64
    with contextlib.ExitStack() as st:
        wb = [P.sb([128, 16, 256], BF16, st) for _ in range(2)]
        psa = [P.ps([128, 512], F32, st) for _ in range(2)]
        psb = P.ps([128, 512], F32, st)
        pst = [P.ps([128, 4, 128], BF16, st) for _ in range(2)]
        U = [P.sb([128, NT], F32, st) for _ in range(2)]
        accs = [P.sb([128, NT], F32, st) for _ in range(2)]
        sq = P.sb([128, NT], BF16, st)
        rr = P.sb([128, 512], F32, st)
        qo = [P.sb([128, NT], BF16, st) for _ in range(2)]
        tm = [P.sb([128, 18, 128], BF16, st)] * 2
        npa = 0
        npt = 0
        for blk in range(32):
            b = blk % 2
            P.add("pool", lambda e, b=b, blk=blk: e.dma_start(out=wb[b][:], in_=k.dn_w_in[i, blk]),
                  writes=[("wb", b)], dma=True)
            for s in range(2):
                ch = blk * 2 + s
                ub = ch % 2
                acc = accs[ch % 2]
                ak = ("acc", ch % 2)
                kind = "q" if ch < 16 else ("k" if ch < 32 else "v")
                for ti, (t0, n) in enumerate(TILES):
                    pb = npa % 2; npa += 1
                    for kc in range(16):
                        P.add("pe", lambda e, b=b, s=s, pb=pb, kc=kc, t0=t0, n=n: e.matmul(
                            psa[pb][:, :n], wb[b][:, kc, s * 128:(s + 1) * 128], nx[:, kc, t0:t0 + n], start=(kc == 0), stop=(kc == 15)),
                            reads=[("wb", b)], writes=[("psa", pb)])
                    P.add("act", lambda e, ub=ub, pb=pb, t0=t0, n=n: e.activation(out=U[ub][:, t0:t0 + n], in_=psa[pb][:, :n], func=AF.Copy),
                          reads=[("psa", pb)], writes=[("U", ub, ti)])
                Uk = [("U", ub, ti) for ti in range(5)]
                wc = lambda kk, ch=ch: k.vecs[:, dcw + kk * 64 + ch:dcw + kk * 64 + ch + 1]
                P.add("dve", lambda e, acc=acc, ub=ub, w=wc(2): e.tensor_scalar(out=acc[:], in0=U[ub][:], scalar1=w, scalar2=None, op0=ALU.mult),
                      reads=Uk, writes=[ak])
                for kk in (0, 1, 3, 4):
                    sh = kk - 2
                    for (s0, sn) in ((0, LCTX), (LCTX, TLAT)):
                        lo = max(0, -sh); hi = sn - max(0, sh)
                        P.add("dve", lambda e, acc=acc, ub=ub, s0=s0, lo=lo, hi=hi, sh=sh, w=wc(kk): e.scalar_tensor_tensor(
                            out=acc[:, s0 + lo:s0 + hi], in0=U[ub][:, s0 + lo + sh:s0 + hi + sh], scalar=w, in1=acc[:, s0 + lo:s0 + hi],
                            op0=ALU.mult, op1=ALU.add), reads=Uk + [ak], writes=[ak])
                qb = ch % 2
                if kind == "v":
                    P.add("act", lambda e, acc=acc, qb=qb: e.activation(out=qo[qb][:], in_=acc[:], func=AF.Silu), reads=[ak], writes=[("qo", qb)])
                else:
                    P.add("act", lambda e, acc=acc: e.activation(out=acc[:], in_=acc[:], func=AF.Silu), reads=[ak], writes=[ak])
                    P.add("act", lambda e, acc=acc: e.activation(out=sq[:], in_=acc[:], func=AF.Square), reads=[ak], writes=["sq"])
                    for ti, (t0, n) in enumerate(TILES):
                        P.add("pe", lambda e, t0=t0, n=n: e.matmul(psb[:, :n], k.ones1[:], sq[:, t0:t0 + n], start=True, stop=True),
                              reads=["sq"], writes=["psb"])
                        rsqrt_ops(P, rr[:, :n], psb[:, :n], ["psb"], "rr")
                        sc = (128 ** -0.5) if kind == "q" else 1.0
                        P.add("dve", lambda e, acc=acc, qb=qb, t0=t0, n=n, sc=sc: e.scalar_tensor_tensor(
                            out=qo[qb][:, t0:t0 + n], in0=acc[:, t0:t0 + n], scalar=sc, in1=rr[:, :n], op0=ALU.mult, op1=ALU.mult),
                            reads=[ak, "rr"], writes=[("qo", qb)])
                if kind == "q":
                    P.add("sp", lambda e, qb=qb, ch=ch: e.dma_start(out=k.DQT[ch], in_=qo[qb][:]), reads=[("qo", qb)], writes=[("DQT", ch)], dma=True)
                else:
                    if kind == "k":
                        P.add("sp", lambda e, qb=qb, ch=ch: e.dma_start(out=k.DKT[ch - 16], in_=qo[qb][:]), reads=[("qo", qb)],
                              writes=[("DKT", ch)], dma=True)
                    for c4 in range(0, 18, 4):
                        nn = min(4, 18 - c4)
                        tb = npt % 2; npt += 1
                        for cc in range(nn):
                            c = c4 + cc
                            P.add("pe", lambda e, tb=tb, cc=cc, c=c, qb=qb: e.transpose(
                                out=pst[tb][:, cc, :], in_=qo[qb][:, c * 128:(c + 1) * 128], identity=k.identb[:]),
                                reads=[("qo", qb)], writes=[("pst", tb)])
                        P.add("act", lambda e, tb=tb, qb=qb, c4=c4, nn=nn: e.activation(out=tm[qb][:, c4:c4 + nn, :], in_=pst[tb][:, :nn, :], func=AF.Copy),
                              reads=[("pst", tb)], writes=[("tm", 0)])
                    if kind == "k":
                        dst = k.DKTM[:, :, (ch - 16) * 128:(ch - 15) * 128]
                    else:
                        dst = k.DVTM[:, :, (ch - 32) * 128:(ch - 31) * 128]
                    P.add("sp", lambda e, qb=qb, dst=dst: e.dma_start(out=dst.rearrange("c p d -> p c d"), in_=tm[qb][:]),
                          reads=[("tm", 0)], writes=[("DTM", ch)], dma=True)


def dn_proj_z(k, l, nx):
    P = k.P
    i = l // 2
    with contextlib.ExitStack() as st:
        wb = [P.sb([128, 16, 512], BF16, st) for _ in range(2)]
        psa = [P.ps([128, 512], F32, st) for _ in range(4)]
        zs = [P.sb([128, 512], BF16, st) for _ in range(4)]
        npa = 0
        for blk in range(8):
            b = blk % 2
            for q in range(2):
                P.add("pool", lambda e, b=b, blk=blk, q=q: e.dma_start(
                    out=wb[b][:, :, q * 256:(q + 1) * 256], in_=k.dn_w_in[i, 32 + blk * 2 + q]),
                    writes=[("wb", b, q)], dma=True)
            for c in range(18):
                pb = npa % 4; npa += 1
                for kc in range(16):
                    P.add("pe", lambda e, b=b, pb=pb, kc=kc, c=c: e.matmul(
                        psa[pb][:], nx[:, kc, c * 128:(c + 1) * 128], wb[b][:, kc, :], start=(kc == 0), stop=(kc == 15)),
                        reads=[("wb", b, 0), ("wb", b, 1)], writes=[("psa", pb)])
                P.add("act", lambda e, pb=pb: e.activation(out=zs[pb][:], in_=psa[pb][:], func=AF.Silu),
                      reads=[("psa", pb)], writes=[("zs", pb)])
                P.add("sp", lambda e, pb=pb, c=c, blk=blk: e.dma_start(out=k.DZ[c, :, blk * 512:(blk + 1) * 512], in_=zs[pb][:]),
                      reads=[("zs", pb)], writes=[("DZ", c, blk)], dma=True)


def dn_proj_ba(k, l, nx):
    P = k.P
    i = l // 2
    with contextlib.ExitStack() as st:
        wb = P.sb([128, 16, 128], BF16, st)
        BA = P.sb([128, 18, 128], F32, st)
        psa = [P.ps([128, 512], F32, st) for _ in range(2)]
        P.add("pool", lambda e: e.dma_start(out=wb[:], in_=k.dn_w_ba[i]), writes=["wb"], dma=True)
        for c in range(18):
            pb = c % 2
            for kc in range(16):
                P.add("pe", lambda e, pb=pb, kc=kc, c=c: e.matmul(
                    psa[pb][:, 0:128], nx[:, kc, c * 128:(c + 1) * 128], wb[:, kc, :], start=(kc == 0), stop=(kc == 15)),
                    reads=["wb"], writes=[("psa", pb)])
            P.add("act", lambda e, pb=pb, c=c: e.activation(out=BA[:, c, :], in_=psa[pb][:, 0:128], func=AF.Copy),
                  reads=[("psa", pb)], writes=[("BA", c)])
        BAk = [("BA", c) for c in range(18)]
        P.add("act", lambda e: e.activation(out=k.BETA[:], in_=BA[:, :, 0:64], func=AF.Sigmoid), reads=BAk, writes=["BETA"])
        P.add("dve", lambda e: e.tensor_copy(k.GG[:], BA[:, :, 64:128]), reads=BAk, writes=["GG"])
        P.barrier()


def dn_gates(k, l):
    P = k.P
    i = l // 2
    al = VOFF["alog"][0] + i * 64
    db = VOFF["dtb"][0] + i * 64
    with contextlib.ExitStack() as st:
        x = P.sb([128, 18, 64], F32, st)
        ax = P.sb([128, 18, 64], F32, st)
        nA = P.sb([128, 64], F32, st)
        onesf = P.sb([128, 128], F32, st)
        ps = [P.ps([128, 3, 64], F32, st) for _ in range(2)]
        P.add("pool", lambda e: e.memset(onesf[:], 1.0), writes=["onesf"])
        P.add("dve", lambda e: e.tensor_tensor(out=x[:], in0=k.GG[:], in1=bc(k.vecs[:, db:db + 64].unsqueeze(1), [128, 18, 64]), op=ALU.add),
              writes=["x"])
        P.add("act", lambda e: e.activation(out=ax[:], in_=x[:], func=AF.Abs), reads=["x"], writes=["ax"])
        P.add("act", lambda e: e.activation(out=ax[:], in_=ax[:], func=AF.Exp, scale=-1.0), reads=["ax"], writes=["ax"])
        P.add("act", lambda e: e.activation(out=ax[:], in_=ax[:], func=AF.Ln, bias=1.0), reads=["ax"], writes=["ax"])
        P.add("dve", lambda e: e.tensor_scalar_max(out=x[:], in0=x[:], scalar1=0.0), reads=["x"], writes=["x"])
        P.add("dve", lambda e: e.tensor_tensor(out=x[:], in0=x[:], in1=ax[:], op=ALU.add), reads=["x", "ax"], writes=["x"])
        P.add("act", lambda e: e.activation(out=nA[:], in_=k.vecs[:, al:al + 64], func=AF.Exp), writes=["nA"])
        P.add("dve", lambda e: e.scalar_tensor_tensor(out=k.GG[:], in0=x[:], scalar=-1.0, in1=bc(nA[:].unsqueeze(1), [128, 18, 64]),
                                                      op0=ALU.mult, op1=ALU.mult), reads=["x", "nA"], writes=["GG"])
        mo = {n: COFF[n][0] for n in ("m_le", "m_ge", "m_lt", "m_gt")}
        for c in range(18):
            pb = c % 2
            for d in range(2):
                m_gc = mo["m_le"] if d == 0 else mo["m_ge"]
                m_rm = mo["m_gt"] if d == 0 else mo["m_lt"]
                rhs = k.GG[:, c, d * 32:(d + 1) * 32]
                P.add("pe", lambda e, pb=pb, d=d, m=m_gc, rhs=rhs: e.matmul(ps[pb][:, 0, d * 32:(d + 1) * 32], k.consts[:, m:m + 128], rhs, start=True, stop=True),
                      reads=["GG"], writes=[("ps", pb)])
                P.add("pe", lambda e, pb=pb, d=d, m=m_rm, rhs=rhs: e.matmul(ps[pb][:, 1, d * 32:(d + 1) * 32], k.consts[:, m:m + 128], rhs, start=True, stop=True),
                      reads=["GG"], writes=[("ps", pb)])
                P.add("pe", lambda e, pb=pb, d=d, rhs=rhs: e.matmul(ps[pb][:, 2, d * 32:(d + 1) * 32], onesf[:], rhs, start=True, stop=True),
                      reads=["GG", "onesf"], writes=[("ps", pb)])
            P.add("act", lambda e, pb=pb, c=c: e.activation(out=k.EG[:, c, :], in_=ps[pb][:, 0, :], func=AF.Exp), reads=[("ps", pb)], writes=[("EG", c)])
            P.add("act", lambda e, pb=pb, c=c: e.activation(out=k.ER[:, c, :], in_=ps[pb][:, 1, :], func=AF.Exp), reads=[("ps", pb)], writes=[("ER", c)])
            P.add("act", lambda e, pb=pb, c=c: e.activation(out=k.EL[:, c, :], in_=ps[pb][:, 2, :], func=AF.Exp), reads=[("ps", pb)], writes=[("EL", c)])
            P.add("dve", lambda e, c=c: e.tensor_tensor(out=k.BEG[:, c, :], in0=k.BETA[:, c, :], in1=k.EG[:, c, :], op=ALU.mult),
                  reads=[("EG", c)], writes=[("BEG", c)])


def dn_prep(k, l):
    P = k.P
    mo = {n: COFF[n][0] for n in ("m_le", "m_ge", "m_lt", "m_gt", "ident")}
    NL = 2
    with contextlib.ExitStack() as st:
        kTc = [P.sb([128, 16, 128], BF16, st) for _ in range(2)]
        qTc = [P.sb([128, 16, 128], BF16, st) for _ in range(2)]
        ktm = [P.sb([128, 16, 128], BF16, st) for _ in range(2)]
        vtm = [P.sb([128, 32, 128], BF16, st) for _ in range(2)]
        lanes = []
        for ln in range(NL):
            B_ = K()
            B_.Gm = P.sb([128, 2, 128], F32, st); B_.QKm = P.sb([128, 2, 128], F32, st)
            B_.rhs2 = P.sb([128, 4, 128], F32, st); B_.eD = P.sb([128, 4, 128], F32, st)
            B_.L0 = P.sb([128, 4, 128], BF16, st); B_.QKD = P.sb([128, 4, 128], BF16, st)
            B_.LtQ = P.sb([128, 8, 128], BF16, st)
            B_.Xb = [P.sb([128, 4, 128], BF16, st) for _ in range(2)]
            B_.Yb = P.sb([128, 4, 128], BF16, st)
            B_.Zb = [P.sb([128, 4, 128], BF16, st) for _ in range(2)]
            B_.Ab = [P.sb([128, 4, 128], BF16, st) for _ in range(2)]
            B_.Bb = [P.sb([128, 4, 128], BF16, st) for _ in range(2)]
            B_.Lm = [P.sb([128, 4, 128], BF16, st) for _ in range(5)]
            B_.Ltm = [P.sb([128, 4, 128], BF16, st) for _ in range(4)]
            B_.T1a = P.sb([128, 4, 128], BF16, st); B_.T1b = P.sb([128, 4, 128], BF16, st)
            B_.vb = P.sb([128, 4, 128], BF16, st); B_.kbg = P.sb([128, 4, 128], BF16, st)
            B_.PH = [P.sb([128, 4, 4, 128], BF16, st) for _ in range(2)]
            B_.cnt = 0
            B_.psT = P.ps([128, 8, 128], BF16, st)
            B_.psX = P.ps([128, 4, 128], F32, st); B_.psY = P.ps([128, 4, 128], F32, st); B_.psZ = P.ps([128, 4, 128], F32, st)
            lanes.append(B_)

        def unit(ln, c, d, hg, cb):
            B_ = lanes[ln]
            K_ = lambda name, *a: (name, ln) + a
            Gm, QKm, rhs2, eD, L0, QKD, LtQ = B_.Gm, B_.QKm, B_.rhs2, B_.eD, B_.L0, B_.QKD, B_.LtQ
            Xb, Yb, Zb, Ab, Bb, Lm, Ltm, T1a, T1b = B_.Xb, B_.Yb, B_.Zb, B_.Ab, B_.Bb, B_.Lm, B_.Ltm, B_.T1a, B_.T1b
            vb, kbg = B_.vb, B_.kbg
            pi = B_.cnt % 2
            B_.cnt += 1
            PH = B_.PH[pi]
            Uo, Wo, Qo, kd = PH[:, 0], PH[:, 1], PH[:, 2], PH[:, 3]
            K2 = lambda name, *a: (name, ln, pi) + a
            psT, psX, psY, psZ = B_.psT, B_.psX, B_.psY, B_.psZ
            m_strict = mo["m_gt"] if d == 0 else mo["m_lt"]
            m_incl = mo["m_ge"] if d == 0 else mo["m_le"]
            m_l = mo["m_le"] if d == 0 else mo["m_ge"]
            m_r = mo["m_gt"] if d == 0 else mo["m_lt"]
            h0 = d * 32 + hg * 4
            kh0 = hg * 2
            P.add("pool", lambda e: e.tensor_tensor(
                out=rhs2[:], in0=bc(k.consts[:, m_r:m_r + 128].unsqueeze(1), [128, 4, 128]),
                in1=bc(k.GG[:, c, h0:h0 + 4].unsqueeze(2), [128, 4, 128]), op=ALU.mult), writes=[K_("rhs2")])
            for a in range(2):
                P.add("pe", lambda e, a=a: e.matmul(psX[:, a, :], kTc[cb][:, kh0 + a, :], kTc[cb][:, kh0 + a, :], start=True, stop=True),
                      reads=[("kTc", cb)], writes=[K_("psX")])
                P.add("pe", lambda e, a=a: e.matmul(psX[:, 2 + a, :], qTc[cb][:, kh0 + a, :], kTc[cb][:, kh0 + a, :], start=True, stop=True),
                      reads=[("kTc", cb), ("qTc", cb)], writes=[K_("psX")])
            P.add("pe", lambda e: e.matmul(psY[:], k.consts[:, m_l:m_l + 128], rhs2[:], start=True, stop=True),
                  reads=[K_("rhs2")], writes=[K_("psY")])
            P.add("dve", lambda e: e.tensor_tensor(out=Gm[:], in0=psX[:, 0:2, :], in1=bc(k.consts[:, m_strict:m_strict + 128].unsqueeze(1), [128, 2, 128]), op=ALU.mult),
                  reads=[K_("psX")], writes=[K_("Gm")])
            P.add("dve", lambda e: e.tensor_tensor(out=QKm[:], in0=psX[:, 2:4, :], in1=bc(k.consts[:, m_incl:m_incl + 128].unsqueeze(1), [128, 2, 128]), op=ALU.mult),
                  reads=[K_("psX")], writes=[K_("QKm")])
            P.add("act", lambda e: e.activation(out=eD[:], in_=psY[:], func=AF.Exp), reads=[K_("psY")], writes=[K_("eD")])
            yield
            for hl in range(4):
                P.add("dve", lambda e, hl=hl: e.scalar_tensor_tensor(
                    out=L0[:, hl, :], in0=Gm[:, hl // 2, :], scalar=k.BETA[:, c, h0 + hl:h0 + hl + 1], in1=eD[:, hl, :],
                    op0=ALU.mult, op1=ALU.mult), reads=[K_("Gm"), K_("eD")], writes=[K_("L0", hl)])
            P.add("dve", lambda e: e.tensor_tensor(
                out=QKD[:].rearrange("p (a b) j -> p a b j", b=2), in0=bc(QKm[:].unsqueeze(2), [128, 2, 2, 128]),
                in1=eD[:].rearrange("p (a b) j -> p a b j", b=2), op=ALU.mult), reads=[K_("QKm"), K_("eD")], writes=[K_("QKD")])
            L0k = [K_("L0", hl) for hl in range(4)]
            for hl in range(4):
                P.add("pe", lambda e, hl=hl: e.transpose(out=psT[:, hl, :], in_=L0[:, hl, :], identity=k.identb[:]),
                      reads=L0k, writes=[K_("psT")])
            for hl in range(4):
                P.add("pe", lambda e, hl=hl: e.transpose(out=psT[:, 4 + hl, :], in_=QKD[:, hl, :], identity=k.identb[:]),
                      reads=[K_("QKD")], writes=[K_("psT")])
            P.add("act", lambda e: e.activation(out=LtQ[:, 0:4, :], in_=psT[:, 0:4, :], func=AF.Copy), reads=[K_("psT")], writes=[K_("LtQ")])
            P.add("act", lambda e: e.activation(out=Qo, in_=psT[:, 4:8, :], func=AF.Copy), reads=[K_("psT")], writes=[K2("Qo")])
            yield
            Lt = LtQ[:, 0:4, :]
            mk = lambda q: bc(k.maskb[:, q, :].unsqueeze(1), [128, 4, 128])
            P.add("dve", lambda e: e.tensor_tensor(out=Lm[0][:], in0=L0[:], in1=mk(0), op=ALU.mult), reads=L0k, writes=[K_("Lm", 0)])
            P.add("dve", lambda e: e.tensor_tensor(out=Ltm[0][:], in0=Lt, in1=mk(0), op=ALU.mult), reads=[K_("LtQ")], writes=[K_("Ltm", 0)])
            for q in range(1, 5):
                eng_m = "pool"
                P.add(eng_m, lambda e, q=q: e.tensor_tensor(out=Lm[q][:], in0=L0[:], in1=mk(q), op=ALU.mult), reads=L0k, writes=[K_("Lm", q)])
                if q < 4:
                    P.add(eng_m, lambda e, q=q: e.tensor_tensor(out=Ltm[q][:], in0=Lt, in1=mk(q), op=ALU.mult), reads=[K_("LtQ")], writes=[K_("Ltm", q)])
            for hl in range(4):
                P.add("act", lambda e, hl=hl: e.activation(out=vb[:, hl, :], in_=vtm[cb][:, hg * 4 + hl, :], func=AF.Identity,
                                                           scale=k.BETA[:, c, h0 + hl:h0 + hl + 1]), reads=[("vtm", cb)], writes=[K_("vb", hl)])
                P.add("act", lambda e, hl=hl: e.activation(out=kbg[:, hl, :], in_=ktm[cb][:, kh0 + hl // 2, :], func=AF.Identity,
                                                           scale=k.BEG[:, c, h0 + hl:h0 + hl + 1]), reads=[("ktm", cb)], writes=[K_("kbg", hl)])
                P.add("act", lambda e, hl=hl: e.activation(out=kd[:, hl, :], in_=ktm[cb][:, kh0 + hl // 2, :], func=AF.Identity,
                                                           scale=k.ER[:, c, h0 + hl:h0 + hl + 1]), reads=[("ktm", cb)], writes=[K2("kd", hl)])
            for hl in range(4):
                P.add("pe", lambda e, hl=hl: e.matmul(psX[:, hl, :], Ltm[0][:, hl, :], Lm[0][:, hl, :], start=True, stop=True),
                      reads=[K_("Lm", 0), K_("Ltm", 0)], writes=[K_("psX")])
            for hl in range(4):
                P.add("pe", lambda e, hl=hl: e.matmul(psY[:, hl, :], Lm[0][:, hl, :], Ltm[0][:, hl, :], start=True, stop=True),
                      reads=[K_("Lm", 0), K_("Ltm", 0)], writes=[K_("psY")])
            P.add("act", lambda e: e.activation(out=Xb[0][:], in_=psX[:], func=AF.Copy), reads=[K_("psX")], writes=[K_("X", 0)])
            P.add("act", lambda e: e.activation(out=Yb[:], in_=psY[:], func=AF.Copy), reads=[K_("psY")], writes=[K_("Y")])
            P.add("dve", lambda e: e.scalar_tensor_tensor(
                out=Zb[0][:], in0=Ltm[0][:], scalar=-1.0, in1=bc(k.identb[:].unsqueeze(1), [128, 4, 128]),
                op0=ALU.mult, op1=ALU.add), reads=[K_("Ltm", 0)], writes=[K_("Z", 0)])
            yield
            for hl in range(4):
                P.add("pe", lambda e, hl=hl: e.matmul(psX[:, hl, :], Yb[:, hl, :], Xb[0][:, hl, :], start=True, stop=True),
                      reads=[K_("X", 0), K_("Y")], writes=[K_("psX")])
            for hl in range(4):
                P.add("pe", lambda e, hl=hl: e.matmul(psZ[:, hl, :], Xb[0][:, hl, :], Zb[0][:, hl, :], start=True, stop=True),
                      reads=[K_("X", 0), K_("Z", 0)], writes=[K_("psZ")])
            P.add("act", lambda e: e.activation(out=Xb[1][:], in_=psX[:], func=AF.Copy), reads=[K_("psX")], writes=[K_("X", 1)])
            P.add("dve", lambda e: e.tensor_tensor(out=Zb[1][:], in0=psZ[:], in1=Zb[0][:], op=ALU.add),
                  reads=[K_("psZ"), K_("Z", 0)], writes=[K_("Z", 1)])
            yield
            for hl in range(4):
                P.add("pe", lambda e, hl=hl: e.matmul(psZ[:, hl, :], Xb[1][:, hl, :], Zb[1][:, hl, :], start=True, stop=True),
                      reads=[K_("X", 1), K_("Z", 1)], writes=[K_("psZ")])
            P.add("dve", lambda e: e.tensor_tensor(out=Zb[0][:], in0=psZ[:], in1=Zb[1][:], op=ALU.add),
                  reads=[K_("psZ"), K_("Z", 1)], writes=[K_("Z", 0)])
            yield
            for hl in range(4):
                P.add("pe", lambda e, hl=hl: e.transpose(out=psT[:, hl, :], in_=Zb[0][:, hl, :], identity=k.identb[:]),
                      reads=[K_("Z", 0)], writes=[K_("psT")])
            P.add("act", lambda e: e.activation(out=Ab[0][:], in_=psT[:, 0:4, :], func=AF.Copy), reads=[K_("psT")], writes=[K_("A", 0)])
            yield
            Bcur, Bkey = Zb[0], K_("Z", 0)
            Acur, Akey = Ab[0], K_("A", 0)
            for q in range(1, 5):
                last = (q == 4)
                nb_ = q % 2
                if not last:
                    for hl in range(4):
                        P.add("pe", lambda e, hl=hl, q=q, Acur=Acur: e.matmul(psX[:, hl, :], Ltm[q][:, hl, :], Acur[:, hl, :], start=True, stop=True),
                              reads=[K_("Ltm", q), Akey], writes=[K_("psX")])
                for hl in range(4):
                    P.add("pe", lambda e, hl=hl, q=q, Bcur=Bcur: e.matmul(psY[:, hl, :], Lm[q][:, hl, :], Bcur[:, hl, :], start=True, stop=True),
                          reads=[K_("Lm", q), Bkey], writes=[K_("psY")])
                if not last:
                    P.add("act", lambda e: e.activation(out=T1a[:], in_=psX[:], func=AF.Copy), reads=[K_("psX")], writes=[K_("T1a")])
                P.add("act", lambda e: e.activation(out=T1b[:], in_=psY[:], func=AF.Copy), reads=[K_("psY")], writes=[K_("T1b")])
                yield
                if not last:
                    for hl in range(4):
                        P.add("pe", lambda e, hl=hl, Bcur=Bcur: e.matmul(psX[:, hl, :], Bcur[:, hl, :], T1a[:, hl, :], start=True, stop=True),
                              reads=[Bkey, K_("T1a")], writes=[K_("psX")])
                for hl in range(4):
                    P.add("pe", lambda e, hl=hl, Acur=Acur: e.matmul(psZ[:, hl, :], Acur[:, hl, :], T1b[:, hl, :], start=True, stop=True),
                          reads=[Akey, K_("T1b")], writes=[K_("psZ")])
                if not last:
                    P.add("dve", lambda e, nb_=nb_, Acur=Acur: e.tensor_tensor(out=Ab[nb_][:], in0=Acur[:], in1=psX[:], op=ALU.subtract),
                          reads=[K_("psX"), Akey], writes=[K_("A", nb_)])
                Bnew = Bb[nb_] if not last else Zb[0]
                Bnk = K_("B", nb_) if not last else K_("Z", 0)
                P.add("dve", lambda e, Bnew=Bnew, Bcur=Bcur: e.tensor_tensor(out=Bnew[:], in0=Bcur[:], in1=psZ[:], op=ALU.subtract),
                      reads=[K_("psZ"), Bkey], writes=[Bnk])
                Bcur, Bkey = Bnew, Bnk
                if not last:
                    Acur, Akey = Ab[nb_], K_("A", nb_)
                yield
            ZF = Zb[0]
            for hl in range(4):
                P.add("pe", lambda e, hl=hl: e.matmul(psX[:, hl, :], ZF[:, hl, :], vb[:, hl, :], start=True, stop=True),
                      reads=[K_("Z", 0)] + [K_("vb", q_) for q_ in range(4)], writes=[K_("psX")])
            for hl in range(4):
                P.add("pe", lambda e, hl=hl: e.matmul(psY[:, hl, :], kbg[:, hl, :], ZF[:, hl, :], start=True, stop=True),
                      reads=[K_("Z", 0)] + [K_("kbg", q_) for q_ in range(4)], writes=[K_("psY")])
            P.add("act", lambda e: e.activation(out=Uo, in_=psX[:], func=AF.Copy), reads=[K_("psX")], writes=[K2("Uo")])
            P.add("act", lambda e: e.activation(out=Wo, in_=psY[:], func=AF.Copy), reads=[K_("psY")], writes=[K2("Wo")])
            u = (c * 2 + d) * 8 + hg
            P.add("sp", lambda e: e.dma_start(out=k.DPH[u], in_=PH[:]),
                  reads=[K2("Uo"), K2("Wo"), K2("Qo")] + [K2("kd", q_) for q_ in range(4)], writes=[("DPH", u)], dma=True)
            yield

        def load_chunk(c):
            cb = c % 2
            tsl = slice(c * 128, (c + 1) * 128)
            P.add("sp", lambda e: e.dma_start(out=kTc[cb][:], in_=k.DKT.rearrange("h p t -> p h t")[:, :, tsl]), writes=[("kTc", cb)], dma=True)
            P.add("sp", lambda e: e.dma_start(out=qTc[cb][:], in_=k.DQT.rearrange("h p t -> p h t")[:, :, tsl]), writes=[("qTc", cb)], dma=True)
            P.add("sp", lambda e: e.dma_start(out=ktm[cb][:], in_=k.DKTM[c].rearrange("p (h d) -> p h d", d=128)), writes=[("ktm", cb)], dma=True)
            P.add("sp", lambda e: e.dma_start(out=vtm[cb][:], in_=k.DVTM[c].rearrange("p (h d) -> p h d", d=128)), writes=[("vtm", cb)], dma=True)

        for c in range(18):
            load_chunk(c)
            units = [(d, hg) for d in range(2) for hg in range(8)]
            for p0 in range(0, len(units), NL):
                gens = [unit(ln, c, units[p0 + ln][0], units[p0 + ln][1], c % 2) for ln in range(NL)]
                alive = list(gens)
                while alive:
                    nxt = []
                    for g in alive:
                        try:
                            next(g)
                            nxt.append(g)
                        except StopIteration:
                            pass
                    alive = nxt


def dn_scan(k, l):
    P = k.P
    order = [list(range(18)), [1, 0] + list(range(17, 1, -1))]
    with contextlib.ExitStack() as st:
        Sf = P.sb([128, 64, 128], F32, st)
        Sb = P.sb([128, 64, 128], BF16, st)
        NB = 3
        ph = [P.sb([128, 4, 4, 128], BF16, st) for _ in range(NB)]
        qT = [P.sb([128, 2, 128], BF16, st) for _ in range(NB)]
        vnew = [P.sb([128, 4, 128], BF16, st) for _ in range(2)]
        o1 = [P.sb([128, 4, 128], F32, st) for _ in range(NB)]
        ob_ = [P.sb([128, 4, 128], F32, st) for _ in range(NB)]
        psP = [P.ps([128, 4, 128], F32, st) for _ in range(2)]
        psO1 = [P.ps([128, 4, 128], F32, st) for _ in range(2)]
        psO2 = [P.ps([128, 4, 128], F32, st) for _ in range(2)]
        psS = [P.ps([128, 4, 128], F32, st) for _ in range(2)]
        P.add("pool", lambda e: e.memset(Sf[:], 0.0), writes=[("Sf", hh) for hh in range(16)])
        P.add("pool", lambda e: e.memset(Sb[:], 0.0), writes=[("Sb", hh) for hh in range(16)])
        P.barrier()
        units = [(s, d, hg) for s in range(18) for d in range(2) for hg in range(8)]

        def stage1(idx):
            s, d, hg = units[idx]
            c = order[d][s]
            u = (c * 2 + d) * 8 + hg
            nb = idx % NB; pb = idx % 2
            g16 = d * 8 + hg
            P.add("sp", lambda e: e.dma_start(out=ph[nb][:], in_=k.DPH[u]), writes=[("ph", nb)], dma=True)
            P.add("sp", lambda e: e.dma_start(out=qT[nb][:], in_=k.DQT[hg * 2:hg * 2 + 2, :, c * 128:(c + 1) * 128].rearrange("h p t -> p h t")),
                  writes=[("qT", nb)], dma=True)
            for hl in range(4):
                hs = d * 32 + hg * 4 + hl
                P.add("pe", lambda e, hl=hl, hs=hs: e.matmul(psP[pb][:, hl, :], ph[nb][:, 1, hl, :], Sb[:, hs, :], start=True, stop=True),
                      reads=[("ph", nb), ("Sb", g16)], writes=[("psP", pb)])
            for hl in range(4):
                hs = d * 32 + hg * 4 + hl
                P.add("pe", lambda e, hl=hl, hs=hs: e.matmul(psO1[pb][:, hl, :], qT[nb][:, hl // 2, :], Sb[:, hs, :], start=True, stop=True),
                      reads=[("qT", nb), ("Sb", g16)], writes=[("psO1", pb)])
            P.add("dve", lambda e: e.tensor_tensor(out=vnew[pb][:], in0=ph[nb][:, 0, :, :], in1=psP[pb][:], op=ALU.subtract),
                  reads=[("ph", nb), ("psP", pb)], writes=[("vnew", pb)])
            P.add("act", lambda e: e.activation(out=o1[nb][:], in_=psO1[pb][:], func=AF.Copy), reads=[("psO1", pb)], writes=[("o1", nb)])

        def stage2(idx):
            s, d, hg = units[idx]
            c = order[d][s]
            nb = idx % NB; pb = idx % 2
            g16 = d * 8 + hg
            h0 = d * 32 + hg * 4
            for hl in range(4):
                P.add("pe", lambda e, hl=hl: e.matmul(psO2[pb][:, hl, :], ph[nb][:, 2, hl, :], vnew[pb][:, hl, :], start=True, stop=True),
                      reads=[("ph", nb), ("vnew", pb)], writes=[("psO2", pb)])
            for hl in range(4):
                P.add("pe", lambda e, hl=hl: e.matmul(psS[pb][:, hl, :], ph[nb][:, 3, hl, :], vnew[pb][:, hl, :], start=True, stop=True),
                      reads=[("ph", nb), ("vnew", pb)], writes=[("psS", pb)])
            for hl in range(4):
                P.add("dve", lambda e, hl=hl: e.scalar_tensor_tensor(
                    out=ob_[nb][:, hl, :], in0=o1[nb][:, hl, :], scalar=k.EG[:, c, h0 + hl:h0 + hl + 1], in1=psO2[pb][:, hl, :],
                    op0=ALU.mult, op1=ALU.add), reads=[("psO2", pb), ("o1", nb)], writes=[("ob", nb, hl)])
            for hl in range(4):
                P.add("dve", lambda e, hl=hl: e.scalar_tensor_tensor(
                    out=Sf[:, h0 + hl, :], in0=Sf[:, h0 + hl, :], scalar=k.EL[:, c, h0 + hl:h0 + hl + 1], in1=psS[pb][:, hl, :],
                    op0=ALU.mult, op1=ALU.add), reads=[("psS", pb), ("Sf", g16, hl)], writes=[("Sf", g16, hl)])
            P.add("act", lambda e: e.activation(out=Sb[:, h0:h0 + 4, :], in_=Sf[:, h0:h0 + 4, :], func=AF.Copy),
                  reads=[("Sf", g16, hl) for hl in range(4)], writes=[("Sb", g16)])
            dst = k.DO[d, c, :, hg * 512:(hg + 1) * 512]
            P.add("sp", lambda e: e.dma_start(out=dst, in_=ob_[nb][:].rearrange("p h j -> p (h j)")),
                  reads=[("ob", nb, hl) for hl in range(4)], writes=[("DO", d, c, hg)], dma=True)

        for idx in range(len(units) + 1):
            if idx < len(units):
                stage1(idx)
            if idx >= 1:
                stage2(idx - 1)


def dn_outnorm(k, l):
    P = k.P
    i = l // 2
    ng = VOFF["dngrep"][0] + i * 128
    with contextlib.ExitStack() as st:
        of = [P.sb([128, 32, 128], F32, st) for _ in range(2)]
        obk = [P.sb([128, 32, 128], F32, st) for _ in range(2)]
        zz = [P.sb([128, 32, 128], BF16, st) for _ in range(2)]
        sq = P.sb([128, 32, 128], F32, st)
        ss = P.sb([128, 32], F32, st)
        y = [P.sb([128, 32, 128], BF16, st) for _ in range(2)]
        yT = [P.sb([128, 32, 128], BF16, st) for _ in range(2)]
        pst = [P.ps([128, 8, 128], BF16, st) for _ in range(2)]
        npt = 0
        for c in range(18):
            b = c % 2
            P.add("sp", lambda e, b=b, c=c: e.dma_start(out=of[b][:], in_=k.DO[0, c].rearrange("p (h j) -> p h j", j=128)), writes=[("of", b)], dma=True)
            P.add("sp", lambda e, b=b, c=c: e.dma_start(out=obk[b][:], in_=k.DO[1, c].rearrange("p (h j) -> p h j", j=128)), writes=[("obk", b)], dma=True)
            P.add("sp", lambda e, b=b, c=c: e.dma_start(out=zz[b][:], in_=k.DZ[c].rearrange("p (h j) -> p h j", j=128)), writes=[("zz", b)], dma=True)
            P.add("dve", lambda e, b=b: e.tensor_tensor(out=of[b][:], in0=of[b][:], in1=obk[b][:], op=ALU.add),
                  reads=[("of", b), ("obk", b)], writes=[("of", b)])
            P.add("act", lambda e, b=b: e.activation(out=sq[:], in_=of[b][:], func=AF.Square), reads=[("of", b)], writes=["sq"])
            P.add("dve", lambda e: e.tensor_reduce(out=ss[:], in_=sq[:], axis=AX.X, op=ALU.add), reads=["sq"], writes=["ss"])
            P.add("act", lambda e: e.activation(out=ss[:], in_=ss[:], func=AF.Sqrt, bias=EPS, scale=1.0 / 128), reads=["ss"], writes=["ss"])
            P.add("dve", lambda e: e.reciprocal(ss[:], ss[:]), reads=["ss"], writes=["ss"])
            P.add("dve", lambda e, b=b: e.tensor_tensor(out=of[b][:], in0=of[b][:], in1=bc(ss[:].unsqueeze(2), [128, 32, 128]), op=ALU.mult),
                  reads=[("of", b), "ss"], writes=[("of", b)])
            P.add("dve", lambda e, b=b: e.tensor_tensor(out=of[b][:], in0=of[b][:], in1=bc(k.vecs[:, ng:ng + 128].unsqueeze(1), [128, 32, 128]), op=ALU.mult),
                  reads=[("of", b)], writes=[("of", b)])
            P.add("dve", lambda e, b=b: e.tensor_tensor(out=y[b][:], in0=of[b][:], in1=zz[b][:], op=ALU.mult),
                  reads=[("of", b), ("zz", b)], writes=[("y", b)])
            for h8 in range(4):
                tb = npt % 2; npt += 1
                for hh in range(8):
                    P.add("pe", lambda e, tb=tb, hh=hh, h8=h8, b=b: e.transpose(out=pst[tb][:, hh, :], in_=y[b][:, h8 * 8 + hh, :], identity=k.identb[:]),
                          reads=[("y", b)], writes=[("pst", tb)])
                P.add("act", lambda e, tb=tb, h8=h8, b=b: e.activation(out=yT[b][:, h8 * 8:(h8 + 1) * 8, :], in_=pst[tb][:], func=AF.Copy),
                      reads=[("pst", tb)], writes=[("yT", b, h8)])
            P.add("sp", lambda e, b=b, c=c: e.dma_start(out=k.DYT[:, :, c * 128:(c + 1) * 128].rearrange("h p t -> p h t"), in_=yT[b][:]),
                  reads=[("yT", b, h8) for h8 in range(4)], writes=[("DYT", c)], dma=True)


def dn_outproj(k, l):
    P = k.P
    i = l // 2
    hsrc = k.hT.rearrange("(c p) t -> p c t", p=128)
    HALF = NT // 2
    SUB = 384
    with contextlib.ExitStack() as st:
        cat = P.sb([128, 32, HALF], BF16, st)
        wb = [P.sb([128, 32, 256], BF16, st) for _ in range(2)]
        hrow = [P.sb([128, HALF], F32, st) for _ in range(2)]
        pso = [P.ps([128, 512], F32, st) for _ in range(4)]
        ysrc = k.DYT.rearrange("h p t -> p h t")
        nps = 0
        for half in range(2):
            h0 = half * HALF
            for q in range(4):
                P.add("sp", lambda e, q=q, h0=h0: e.dma_start(out=cat[:, q * 8:(q + 1) * 8, :], in_=ysrc[:, q * 8:(q + 1) * 8, h0:h0 + HALF]),
                      writes=[("cat", q)], dma=True)
            for blk in range(8):
                b = blk % 2
                for q in range(2):
                    P.add("pool", lambda e, b=b, blk=blk, q=q: e.dma_start(
                        out=wb[b][:, q * 16:(q + 1) * 16, :], in_=k.dn_w_out[i, blk][:, q * 16:(q + 1) * 16, :]),
                        writes=[("wb", b, q)], dma=True)
                for s in range(2):
                    c = blk * 2 + s
                    hb = c % 2
                    P.add("sp", lambda e, hb=hb, c=c, h0=h0: e.dma_start(out=hrow[hb][:], in_=hsrc[:, c, h0:h0 + HALF]),
                          reads=[("hT", c)], writes=[("hrow", hb)], dma=True)
                    for sub in range(HALF // SUB):
                        u0 = sub * SUB
                        pb = nps % 4; nps += 1
                        for kc in range(32):
                            P.add("pe", lambda e, b=b, s=s, pb=pb, kc=kc, u0=u0: e.matmul(
                                pso[pb][:, :SUB], wb[b][:, kc, s * 128:(s + 1) * 128], cat[:, kc, u0:u0 + SUB],
                                start=(kc == 0), stop=(kc == 31)),
                                reads=[("wb", b, kc // 16), ("cat", kc // 8)], writes=[("pso", pb)])
                        for (g0, gn, sidx) in segs(h0 + u0, SUB):
                            r0 = g0 - h0
                            P.add("dve", lambda e, hb=hb, pb=pb, r0=r0, gn=gn, u0=u0, c=c, sidx=sidx: e.scalar_tensor_tensor(
                                out=hrow[hb][:, r0:r0 + gn], in0=pso[pb][:, r0 - u0:r0 - u0 + gn], scalar=modv(k, 2, c, sidx),
                                in1=hrow[hb][:, r0:r0 + gn], op0=ALU.mult, op1=ALU.add),
                                reads=[("pso", pb), ("hrow", hb), "mod"], writes=[("hrow", hb)])
                    P.add("sp", lambda e, hb=hb, c=c, h0=h0: e.dma_start(out=hsrc[:, c, h0:h0 + HALF], in_=hrow[hb][:]),
                          reads=[("hrow", hb)], writes=[("hT", c)], dma=True)


FULL_PLAN = [(l, ("mix", "ffn")) for l in range(DEPTH)]


def tile_w(W, cols=None, width=256):
    if cols is not None:
        W = W[:, cols]
    K_, N_ = W.shape
    return np.ascontiguousarray(W.reshape(K_ // 128, 128, N_ // width, width).transpose(2, 1, 0, 3))


def tile_weights(inp):
    out = {}
    out["w_mod"] = np.stack([tile_w(inp["w_mod"][l]) for l in range(DEPTH)])
    up_cols = np.concatenate([np.concatenate([np.arange(j * 128, (j + 1) * 128), DFF + np.arange(j * 128, (j + 1) * 128)]) for j in range(NFF)])
    out["ffn_w_up"] = np.stack([tile_w(inp["ffn_w_up"][l], up_cols) for l in range(DEPTH)])
    out["ffn_w_down"] = np.stack([tile_w(inp["ffn_w_down"][l]) for l in range(DEPTH)])
    ev_cols = np.concatenate([np.arange(0, 1536)] + [np.concatenate([1536 + np.arange(j * 128, (j + 1) * 128), 2560 + np.arange(j * 128, (j + 1) * 128)])
                                                      for j in range(8)])
    out["even_w_in"] = np.stack([tile_w(inp["even_w_in"][i], ev_cols) for i in range(2)])
    out["even_w_out"] = np.stack([tile_w(inp["even_w_out"][i]) for i in range(2)])
    out["dn_w_in"] = np.stack([tile_w(inp["dn_w_in"][i][:, :12288]) for i in range(2)])
    out["dn_w_ba"] = np.stack([tile_w(inp["dn_w_in"][i][:, 12288:12416], width=128)[0] for i in range(2)])
    out["dn_w_out"] = np.stack([tile_w(inp["dn_w_out"][i]) for i in range(2)])
    return out


def make_core_inputs(inp, b, consts, tiled=None):
    h_in = np.concatenate([inp["ctx"][b].T, inp["x"][b].T], axis=1)
    return {
        "h_in": np.ascontiguousarray(h_in, dtype=np.float32),
        "vecs": pack_vecs(inp, b),
        "consts": consts[0], "rope": consts[1],
        **(tiled if tiled is not None else {}),
    }


def kernel(**inputs):
    inp = {k_: np.asarray(v) for k_, v in inputs.items()}
    consts = make_consts()
    nc = build_program(FULL_PLAN)
    tiled = tile_weights(inp)
    in_maps = [make_core_inputs(inp, c % 4, consts, tiled) for c in range(8)]
    res = run_bass_kernel_spmd(nc, in_maps, core_ids=list(range(8)))
    out = np.stack([res.results[b]["hT"][:, LCTX:].T for b in range(4)], axis=0)
    return np.ascontiguousarray(out.astype(np.float32))
```

```python
import contextlib
import numpy as np
import concourse.bass as bass
import concourse.mybir as mybir
from concourse.bass_utils import run_bass_kernel_spmd

F32 = mybir.dt.float32
BF16 = mybir.dt.bfloat16
AF = mybir.ActivationFunctionType
ALU = mybir.AluOpType
AX = mybir.AxisListType

D = 2048
NCH = 16
LCTX = 256
TLAT = 2048
NT = LCTX + TLAT
DEPTH = 4
DFF = 5632
NFF = DFF // 128
EPS = 1e-6
TILES = [(0, 256), (256, 512), (768, 512), (1280, 512), (1792, 512)]
EVEN_IN_W = 3584
DN_IN_W = 12416

ENGS = ("pe", "act", "dve", "pool", "sp")
NDSEM = 6


class Op:
    __slots__ = ("eng", "fn", "idx", "dma", "slot", "val", "deps", "needed", "cnt", "waits", "clock")


class Prog:
    def __init__(self, nc, same_engine_sync=True):
        self.nc = nc
        self.ops = {e: [] for e in ENGS}
        self.order = []
        self.last_w = {}
        self.readers = {}
        self.same_engine_sync = same_engine_sync
        self.dma_rr = {e: 0 for e in ENGS}
        self.dma_val = {}
        self.stack = contextlib.ExitStack()
        self.sems = {}
        self.dsems = {}
        self._n = 0

    def sb(self, shape, dtype, stack=None):
        self._n += 1
        return (stack or self.stack).enter_context(self.nc.sbuf_tensor(f"sb{self._n}", list(shape), dtype))

    def ps(self, shape, dtype, stack=None):
        self._n += 1
        return (stack or self.stack).enter_context(self.nc.psum_tensor(f"ps{self._n}", list(shape), dtype))

    def add(self, eng, fn, reads=(), writes=(), dma=False):
        op = Op()
        op.eng = eng; op.fn = fn; op.dma = dma; op.needed = False; op.waits = None
        op.idx = len(self.ops[eng]); op.slot = None; op.val = None
        deps = set()
        writes = list(writes)
        if dma:
            s = self.dma_rr[eng] % NDSEM
            self.dma_rr[eng] += 1
            op.slot = (eng, s)
            writes.append(("dsem", eng, s))
            self.dma_val[op.slot] = self.dma_val.get(op.slot, 0) + 16
            op.val = self.dma_val[op.slot]
        for r in reads:
            w = self.last_w.get(r)
            if w is not None:
                deps.add(w)
        for r in writes:
            w = self.last_w.get(r)
            if w is not None:
                deps.add(w)
            for rd in self.readers.get(r, ()):
                deps.add(rd)
        for r in writes:
            self.last_w[r] = op
            self.readers[r] = []
        for r in reads:
            self.readers.setdefault(r, []).append(op)
        deps.discard(op)
        op.deps = deps
        self.ops[eng].append(op)
        self.order.append(op)
        return op

    def barrier(self):
        lasts = []
        for e in ENGS:
            for o in reversed(self.ops[e]):
                if o.fn is not None and not o.dma:
                    lasts.append(o)
                    break
        latest = {}
        for e in ENGS:
            for o in reversed(self.ops[e]):
                if o.dma and o.slot not in latest:
                    latest[o.slot] = o
        for e in ENGS:
            op = Op()
            op.eng = e; op.fn = None; op.dma = False; op.needed = False; op.waits = None
            op.idx = len(self.ops[e]); op.slot = None; op.val = None
            op.deps = set(o for o in lasts if o.eng != e and o.fn is not None and not o.dma) | set(latest.values())
            self.ops[e].append(op)
            self.order.append(op)
        self.last_w.clear(); self.readers.clear()

    def emit(self):
        nc = self.nc
        seen = {e: {} for e in ENGS}
        for op in self.order:
            s = seen[op.eng]
            waits = []
            for d in sorted(op.deps, key=lambda o: (o.eng, o.idx)):
                if d.dma:
                    key = ("d",) + d.slot
                    if s.get(key, 0) >= d.val:
                        continue
                    waits.append(d)
                    s[key] = d.val
                else:
                    if d.fn is None:
                        continue
                    if d.eng == op.eng and (d.eng == "pe" or not self.same_engine_sync or op.fn is None):
                        continue
                    if s.get(d.eng, -1) >= d.idx:
                        continue
                    waits.append(d)
                    s[d.eng] = d.idx
                for k, v in d.clock.items():
                    if k == op.eng:
                        continue
                    if s.get(k, -1) < v:
                        s[k] = v
            for d in waits:
                d.needed = True
            op.waits = waits
            op.clock = dict(s)
        for e in ENGS:
            c = 0
            for op in self.ops[e]:
                if op.needed and not op.dma:
                    c += 1
                op.cnt = c
        st = self.stack
        for e in ENGS:
            self.sems[e] = st.enter_context(nc.semaphore(f"s_{e}"))
        for key in self.dma_val:
            self.dsems[key] = st.enter_context(nc.semaphore(f"d_{key[0]}{key[1]}"))
        block = st.enter_context(nc.Block())
        handles = {"pe": block.tensor, "act": block.scalar, "dve": block.vector, "pool": block.gpsimd, "sp": block.sync}

        def mk(e):
            def body(h):
                for op in self.ops[e]:
                    for d in op.waits:
                        if d.dma:
                            h.wait_ge(self.dsems[d.slot], d.val)
                        else:
                            h.wait_ge(self.sems[d.eng], d.cnt)
                    if op.fn is None:
                        continue
                    ins = op.fn(h)
                    if op.dma:
                        ins.then_inc(self.dsems[op.slot], 16)
                    elif op.needed:
                        ins.then_inc(self.sems[e], 1)
            return body
        for e in ENGS:
            handles[e](mk(e))

    def finish(self):
        self.barrier()
        self.emit()
        self.stack.close()


def _layout(items):
    off = {}
    o = 0
    for name, w in items:
        off[name] = (o, w)
        o += w
    return off, o


VEC_ITEMS = [
    ("bmod", 4 * 96), ("n1g", 64), ("n2g", 64), ("fcw", 4 * 3 * NFF), ("fcb", 4 * NFF),
    ("qg", 2), ("kg", 2), ("sink", 16), ("cdw", 2 * 31 * 8), ("cdb", 16), ("lng", 16), ("lnb", 16),
    ("dcw", 2 * 5 * 64), ("alog", 128), ("dtb", 128), ("dngrep", 256), ("c", 16), ("cctx", 16),
]
VOFF, NV = _layout(VEC_ITEMS)

CONST_ITEMS = [
    ("ident", 128), ("ropeT", 128), ("band", 384),
    ("m_le", 128), ("m_ge", 128), ("m_lt", 128), ("m_gt", 128),
    ("mblk", 5 * 128),
]
COFF, NCONST = _layout(CONST_ITEMS)


def fm(v):
    return np.ascontiguousarray(v.reshape(-1, 128).T)


def pack_vecs(inp, b):
    V = np.zeros((128, NV), np.float32)

    def put(name, arr):
        o, w = VOFF[name]
        assert arr.shape == (128, w), (name, arr.shape, w)
        V[:, o:o + w] = arr
    put("bmod", np.concatenate([fm(inp["b_mod"][l]) for l in range(4)], axis=1))
    put("n1g", np.concatenate([fm(inp["norm1_g"][l]) for l in range(4)], axis=1))
    put("n2g", np.concatenate([fm(inp["norm2_g"][l]) for l in range(4)], axis=1))
    put("fcw", np.concatenate([fm(inp["ffn_conv_w"][l, k]) for l in range(4) for k in range(3)], axis=1))
    put("fcb", np.concatenate([fm(inp["ffn_conv_b"][l]) for l in range(4)], axis=1))
    put("qg", inp["attn_q_norm_g"].T.copy())
    put("kg", inp["attn_k_norm_g"].T.copy())
    put("sink", np.broadcast_to(inp["attn_sink"].reshape(1, 16), (128, 16)).copy())
    put("cdw", np.concatenate([fm(inp["conv_dw_w"][i, k]) for i in range(2) for k in range(31)], axis=1))
    put("cdb", np.concatenate([fm(inp["conv_dw_b"][i]) for i in range(2)], axis=1))
    put("lng", np.concatenate([fm(inp["conv_ln_g"][i]) for i in range(2)], axis=1))
    put("lnb", np.concatenate([fm(inp["conv_ln_b"][i]) for i in range(2)], axis=1))
    put("dcw", np.concatenate([fm(inp["dn_conv_w"][i, k]) for i in range(2) for k in range(5)], axis=1))
    put("alog", np.broadcast_to(inp["dn_a_log"].reshape(1, 128), (128, 128)).copy())
    put("dtb", np.broadcast_to(inp["dn_dt_bias"].reshape(1, 128), (128, 128)).copy())
    put("dngrep", np.broadcast_to(inp["dn_norm_g"].reshape(1, 256), (128, 256)).copy())
    put("c", fm(inp["c"][b]))
    put("cctx", fm(inp["c_ctx"]))
    return V


def make_consts():
    C = np.zeros((128, NCONST), np.float32)

    def put(name, arr):
        o, w = COFF[name]
        assert arr.shape == (128, w), (name, arr.shape)
        C[:, o:o + w] = arr
    idx = np.arange(128)
    put("ident", np.eye(128, dtype=np.float32))
    R = np.zeros((128, 128), np.float32)
    for i in range(32):
        R[i, 32 + i] = -1.0
        R[32 + i, i] = 1.0
        R[64 + i, 96 + i] = -1.0
        R[96 + i, 64 + i] = 1.0
    put("ropeT", R.T.copy())
    kk = idx[:, None]; qq = idx[None, :]
    band = np.concatenate([(kk <= qq), np.ones((128, 128), bool), (qq <= kk)], axis=1).astype(np.float32)
    put("band", band)
    put("m_le", (kk <= qq).astype(np.float32))
    put("m_ge", (kk >= qq).astype(np.float32))
    put("m_lt", (kk < qq).astype(np.float32))
    put("m_gt", (kk > qq).astype(np.float32))
    mb = [(kk // 8 == qq // 8)]
    for bsz in (8, 16, 32, 64):
        mb.append((kk // (2 * bsz) == qq // (2 * bsz)) & (kk // bsz != qq // bsz))
    put("mblk", np.concatenate(mb, axis=1).astype(np.float32))
    GRID_W = 64
    rows = TLAT // GRID_W
    row = np.repeat(np.arange(rows, dtype=np.float32), GRID_W)
    col = np.tile(np.arange(GRID_W, dtype=np.float32), rows)
    half = 64
    inv_freq = (10000.0 ** (-np.arange(0, half, 2, dtype=np.float32) / half)).astype(np.float32)
    ang_r = row[:, None] * inv_freq
    ang_c = col[:, None] * inv_freq
    ang = np.concatenate([ang_r, ang_r, ang_c, ang_c], axis=-1)
    rope = np.concatenate([np.cos(ang).T, np.sin(ang).T], axis=1).astype(np.float32)
    return C, np.ascontiguousarray(rope)


class K:
    pass


def segs(t0, n):
    out = []
    if t0 < LCTX:
        e = min(t0 + n, LCTX)
        out.append((t0, e - t0, 1))
        if t0 + n > LCTX:
            out.append((LCTX, t0 + n - LCTX, 0))
    else:
        out.append((t0, n, 0))
    return out


def bc(ap, shape):
    return ap.broadcast_to(list(shape))


def stage_adaln(k, l):
    P = k.P
    with contextlib.ExitStack() as st:
        wb = [P.sb([128, 16, 256], BF16, st) for _ in range(2)]
        pm = P.ps([128, 96, 2], F32, st)
        for blk in range(48):
            b = blk % 2
            P.add("pool", lambda e, b=b, blk=blk: e.dma_start(out=wb[b][:], in_=k.w_mod[l, blk]),
                  writes=[("wb", b)], dma=True)
            for s in range(2):
                j = blk * 2 + s
                for kc in range(16):
                    P.add("pe", lambda e, b=b, s=s, j=j, kc=kc: e.matmul(
                        pm[:, j, :], wb[b][:, kc, s * 128:(s + 1) * 128], k.siluc[:, kc, :],
                        start=(kc == 0), stop=(kc == 15)),
                        reads=[("wb", b), "siluc"], writes=["pm"])
        bo = VOFF["bmod"][0] + l * 96
        P.add("dve", lambda e: e.tensor_tensor(out=k.mod[:], in0=pm[:], in1=bc(k.vecs[:, bo:bo + 96].unsqueeze(2), [128, 96, 2]),
                                               op=ALU.add), reads=["pm"], writes=["mod"])
        for (dst, m, gname) in ((k.A1, 1, "n1g"), (k.A2, 4, "n2g")):
            go = VOFF[gname][0] + l * 16
            P.add("dve", lambda e, dst=dst, m=m: e.tensor_scalar_add(out=dst[:], in0=k.mod[:, m * 16:(m + 1) * 16, :], scalar1=1.0),
                  reads=["mod"], writes=[("A", m)])
            P.add("dve", lambda e, dst=dst, go=go: e.tensor_tensor(
                out=dst[:], in0=dst[:], in1=bc(k.vecs[:, go:go + 16].unsqueeze(2), [128, 16, 2]), op=ALU.mult),
                reads=[("A", m)], writes=[("A", m)])
    P.barrier()


def modv(k, m, c, s):
    return k.mod[:, m * 16 + c, s:s + 1]


def stage_norm(k, which, nx, nbuf=2):
    P = k.P
    A = k.A1 if which == 1 else k.A2
    msh = 0 if which == 1 else 3
    hsrc = k.hT.rearrange("(c p) t -> p c t", p=128)
    with contextlib.ExitStack() as st:
        hs = [P.sb([128, 16, 512], F32, st) for _ in range(nbuf)]
        sq = P.sb([128, 16, 512], BF16, st)
        rstd = P.sb([128, 512], F32, st)
        pss = P.ps([128, 512], F32, st)
        for ti, (t0, n) in enumerate(TILES):
            b = ti % nbuf
            P.add("sp", lambda e, b=b, t0=t0, n=n: e.dma_start(out=hs[b][:, :, :n], in_=hsrc[:, :, t0:t0 + n]),
                  reads=["hT"], writes=[("hs", b)], dma=True)
            P.add("act", lambda e, b=b, n=n: e.activation(out=sq[:, :, :n], in_=hs[b][:, :, :n], func=AF.Square),
                  reads=[("hs", b)], writes=["sq"])
            for c in range(16):
                P.add("pe", lambda e, c=c, n=n: e.matmul(pss[:, :n], k.onesD[:], sq[:, c, :n], start=(c == 0), stop=(c == 15)),
                      reads=["sq"], writes=["pss"])
            P.add("act", lambda e, n=n: e.activation(out=rstd[:, :n], in_=pss[:, :n], func=AF.Sqrt, bias=EPS, scale=1.0),
                  reads=["pss"], writes=["rstd"])
            P.add("dve", lambda e, n=n: e.reciprocal(rstd[:, :n], rstd[:, :n]), reads=["rstd"], writes=["rstd"])
            P.add("dve", lambda e, b=b, n=n: e.tensor_tensor(out=hs[b][:, :, :n], in0=hs[b][:, :, :n],
                                                             in1=bc(rstd[:, :n].unsqueeze(1), [128, 16, n]), op=ALU.mult),
                  reads=["rstd", ("hs", b)], writes=[("hs", b)])
            s = 1 if t0 < LCTX else 0
            for c in range(16):
                P.add("act", lambda e, b=b, c=c, t0=t0, n=n, s=s: e.activation(
                    out=nx[:, c, t0:t0 + n], in_=hs[b][:, c, :n], func=AF.Identity,
                    bias=modv(k, msh, c, s), scale=A[:, c, s:s + 1]),
                    reads=[("hs", b), "mod", ("A", 1), ("A", 4)], writes=[("nx", ti, c)])


def stage_ffn(k, l):
    P = k.P
    with contextlib.ExitStack() as st0:
        nx = P.sb([128, 16, NT], BF16, st0)
        stage_norm(k, 2, nx)
        P.barrier()
        stage_ffn_up(k, l, nx)
        P.barrier()
    stage_ffn_down(k, l)


def stage_ffn_up(k, l, nx):
    P = k.P
    with contextlib.ExitStack() as st:
        wb = [P.sb([128, 16, 256], BF16, st) for _ in range(3)]
        G = [P.sb([128, NT], F32, st) for _ in range(2)]
        Vv = [P.sb([128, NT], BF16, st) for _ in range(2)]
        acc = [P.sb([128, NT], F32, st) for _ in range(2)]
        sg = [P.sb([128, NT], BF16, st) for _ in range(2)]
        hh = [P.sb([128, NT], BF16, st) for _ in range(2)]
        psg = [P.ps([128, 512], F32, st) for _ in range(2)]
        psv = [P.ps([128, 512], F32, st) for _ in range(2)]
        cw = VOFF["fcw"][0] + l * 3 * NFF
        cb = VOFF["fcb"][0] + l * NFF

        def make_post(j):
            b = j % 2
            Gk = [("G", b, ti) for ti in range(5)]
            Vk = [("V", b, ti) for ti in range(5)]
            w0 = k.vecs[:, cw + 0 * NFF + j:cw + 0 * NFF + j + 1]
            w1 = k.vecs[:, cw + 1 * NFF + j:cw + 1 * NFF + j + 1]
            w2 = k.vecs[:, cw + 2 * NFF + j:cw + 2 * NFF + j + 1]
            bb = k.vecs[:, cb + j:cb + j + 1]
            conv = []
            conv.append(lambda: P.add("dve", lambda e: e.tensor_scalar(out=acc[b][:], in0=G[b][:], scalar1=w1, scalar2=bb,
                                                                       op0=ALU.mult, op1=ALU.add), reads=Gk, writes=[("acc", b)]))
            for (s0, sn) in ((0, LCTX), (LCTX, TLAT)):
                conv.append(lambda s0=s0, sn=sn: P.add("dve", lambda e: e.scalar_tensor_tensor(
                    out=acc[b][:, s0 + 1:s0 + sn], in0=G[b][:, s0:s0 + sn - 1], scalar=w0, in1=acc[b][:, s0 + 1:s0 + sn],
                    op0=ALU.mult, op1=ALU.add), reads=Gk + [("acc", b)], writes=[("acc", b)]))
                conv.append(lambda s0=s0, sn=sn: P.add("dve", lambda e: e.scalar_tensor_tensor(
                    out=acc[b][:, s0:s0 + sn - 1], in0=G[b][:, s0 + 1:s0 + sn], scalar=w2, in1=acc[b][:, s0:s0 + sn - 1],
                    op0=ALU.mult, op1=ALU.add), reads=Gk + [("acc", b)], writes=[("acc", b)]))

            def tail():
                P.add("act", lambda e: e.activation(out=sg[b][:], in_=acc[b][:], func=AF.Silu), reads=[("acc", b)], writes=[("sg", b)])
                P.add("pool", lambda e: e.tensor_tensor(out=hh[b][:], in0=sg[b][:], in1=Vv[b][:], op=ALU.mult),
                      reads=[("sg", b)] + Vk, writes=[("hh", b)])
                P.add("sp", lambda e: e.dma_start(out=k.HF[j * 128:(j + 1) * 128, :], in_=hh[b][:]),
                      reads=[("hh", b)], writes=["HF"], dma=True)
            return conv, tail

        pending = None
        for j in range(NFF):
            b = j % 2
            wi = j % 3
            P.add("pool", lambda e, wi=wi, j=j: e.dma_start(out=wb[wi][:], in_=k.ffn_w_up[l, j]),
                  writes=[("wb", wi)], dma=True)
            conv_prev = list(pending[0]) if pending else []
            for ti, (t0, n) in enumerate(TILES):
                pb = ti % 2
                for kc in range(16):
                    P.add("pe", lambda e, wi=wi, pb=pb, kc=kc, t0=t0, n=n: e.matmul(
                        psg[pb][:, :n], wb[wi][:, kc, 0:128], nx[:, kc, t0:t0 + n], start=(kc == 0), stop=(kc == 15)),
                        reads=[("wb", wi)], writes=[("psg", pb)])
                for kc in range(16):
                    P.add("pe", lambda e, wi=wi, pb=pb, kc=kc, t0=t0, n=n: e.matmul(
                        psv[pb][:, :n], wb[wi][:, kc, 128:256], nx[:, kc, t0:t0 + n], start=(kc == 0), stop=(kc == 15)),
                        reads=[("wb", wi)], writes=[("psv", pb)])
                P.add("act", lambda e, b=b, pb=pb, t0=t0, n=n: e.activation(out=G[b][:, t0:t0 + n], in_=psg[pb][:, :n], func=AF.Copy),
                      reads=[("psg", pb)], writes=[("G", b, ti)])
                P.add("dve", lambda e, b=b, pb=pb, t0=t0, n=n: e.tensor_copy(Vv[b][:, t0:t0 + n], psv[pb][:, :n]),
                      reads=[("psv", pb)], writes=[("V", b, ti)])
                if conv_prev:
                    conv_prev.pop(0)()
            for c_ in conv_prev:
                c_()
            if pending:
                pending[1]()
            pending = make_post(j)
        for c_ in pending[0]:
            c_()
        pending[1]()


def stage_ffn_down(k, l):
    P = k.P
    HALF = NT // 2
    SUB = 384
    with contextlib.ExitStack() as st:
        hres = P.sb([128, NFF, HALF], BF16, st)
        wb = [P.sb([128, NFF, 256], BF16, st) for _ in range(2)]
        hrow = [P.sb([128, HALF], F32, st) for _ in range(2)]
        pso = [P.ps([128, 512], F32, st) for _ in range(4)]
        hfsrc = k.HF.rearrange("(j p) t -> p j t", p=128)
        hsrc = k.hT.rearrange("(c p) t -> p c t", p=128)
        nps = 0
        for half in range(2):
            h0 = half * HALF
            for q in range(4):
                P.add("sp", lambda e, q=q, h0=h0: e.dma_start(out=hres[:, q * 11:(q + 1) * 11, :], in_=hfsrc[:, q * 11:(q + 1) * 11, h0:h0 + HALF]),
                      reads=["HF"], writes=[("hres", q)], dma=True)
            for blk in range(8):
                b = blk % 2
                for q in range(2):
                    P.add("pool", lambda e, b=b, blk=blk, q=q: e.dma_start(
                        out=wb[b][:, q * 22:(q + 1) * 22, :], in_=k.ffn_w_down[l, blk][:, q * 22:(q + 1) * 22, :]),
                        writes=[("wb", b, q)], dma=True)
                for s in range(2):
                    c = blk * 2 + s
                    hb = c % 2
                    P.add("sp", lambda e, hb=hb, c=c, h0=h0: e.dma_start(out=hrow[hb][:], in_=hsrc[:, c, h0:h0 + HALF]),
                          reads=[("hT", c)], writes=[("hrow", hb)], dma=True)
                    for sub in range(HALF // SUB):
                        u0 = sub * SUB
                        pb = nps % 4
                        nps += 1
                        for kc in range(NFF):
                            P.add("pe", lambda e, b=b, s=s, pb=pb, kc=kc, u0=u0: e.matmul(
                                pso[pb][:, :SUB], wb[b][:, kc, s * 128:(s + 1) * 128], hres[:, kc, u0:u0 + SUB],
                                start=(kc == 0), stop=(kc == NFF - 1)),
                                reads=[("wb", b, kc // 22), ("hres", kc // 11)], writes=[("pso", pb)])
                        for (g0, gn, sidx) in segs(h0 + u0, SUB):
                            r0 = g0 - h0
                            P.add("dve", lambda e, hb=hb, pb=pb, r0=r0, gn=gn, u0=u0, c=c, sidx=sidx: e.scalar_tensor_tensor(
                                out=hrow[hb][:, r0:r0 + gn], in0=pso[pb][:, r0 - u0:r0 - u0 + gn], scalar=modv(k, 5, c, sidx),
                                in1=hrow[hb][:, r0:r0 + gn], op0=ALU.mult, op1=ALU.add),
                                reads=[("pso", pb), ("hrow", hb), "mod"], writes=[("hrow", hb)])
                    P.add("sp", lambda e, hb=hb, c=c, h0=h0: e.dma_start(out=hsrc[:, c, h0:h0 + HALF], in_=hrow[hb][:]),
                          reads=[("hrow", hb)], writes=[("hT", c)], dma=True)
    P.barrier()


DEBUG_OUT = ()


def build_program(plan, test_out_ctx=False):
    nc = bass.Bass("TRN2", target_bir_lowering=False)
    k = K()
    k.nc = nc
    P = Prog(nc)
    k.P = P
    def dt(name, shape, dtype, kind="Internal"):
        if name in DEBUG_OUT:
            kind = "ExternalOutput"
        return nc.dram_tensor(name, shape, dtype, kind=kind)
    k.h_in = dt("h_in", [D, NT], F32, kind="ExternalInput").ap()
    k.vecs_d = dt("vecs", [128, NV], F32, kind="ExternalInput").ap()
    k.consts_d = dt("consts", [128, NCONST], F32, kind="ExternalInput").ap()
    k.rope_d = dt("rope", [128, 2 * TLAT], F32, kind="ExternalInput").ap()
    k.w_mod = dt("w_mod", [DEPTH, 48, 128, 16, 256], F32, kind="ExternalInput").ap()
    k.ffn_w_up = dt("ffn_w_up", [DEPTH, NFF, 128, 16, 256], F32, kind="ExternalInput").ap()
    k.ffn_w_down = dt("ffn_w_down", [DEPTH, 8, 128, NFF, 256], F32, kind="ExternalInput").ap()
    k.even_w_in = dt("even_w_in", [2, 14, 128, 16, 256], F32, kind="ExternalInput").ap()
    k.even_w_out = dt("even_w_out", [2, 8, 128, 16, 256], F32, kind="ExternalInput").ap()
    k.dn_w_in = dt("dn_w_in", [2, 48, 128, 16, 256], F32, kind="ExternalInput").ap()
    k.dn_w_ba = dt("dn_w_ba", [2, 128, 16, 128], F32, kind="ExternalInput").ap()
    k.dn_w_out = dt("dn_w_out", [2, 8, 128, 32, 256], F32, kind="ExternalInput").ap()
    k.hT = dt("hT", [D, NT], F32, kind="ExternalOutput").ap()
    k.HF = dt("HF", [DFF, NT], BF16, kind="Internal").ap()
    k.QT = dt("QT", [8, 128, NT], BF16, kind="Internal").ap()
    k.KT = dt("KT", [2, 128, NT], BF16, kind="Internal").ap()
    k.VTM = dt("VTM", [18, 128, 256], BF16, kind="Internal").ap()
    k.CAT = dt("CAT", [16, 128, NT], BF16, kind="Internal").ap()
    k.DQT = dt("DQT", [16, 128, NT], BF16, kind="Internal").ap()
    k.DKT = dt("DKT", [16, 128, NT], BF16, kind="Internal").ap()
    k.DKTM = dt("DKTM", [18, 128, 2048], BF16, kind="Internal").ap()
    k.DVTM = dt("DVTM", [18, 128, 4096], BF16, kind="Internal").ap()
    k.DZ = dt("DZ", [18, 128, 4096], BF16, kind="Internal").ap()
    k.DPH = dt("DPH", [288, 128, 4, 4, 128], BF16, kind="Internal").ap()
    k.DO = dt("DO", [2, 18, 128, 4096], F32, kind="Internal").ap()
    k.DYT = dt("DYT", [32, 128, NT], BF16, kind="Internal").ap()
    k.DBG_G = dt("DBG_G", [5, 128, 18, 64], F32, kind="Internal").ap()

    k.vecs = P.sb([128, NV], F32)
    k.consts = P.sb([128, NCONST], F32)
    k.siluc = P.sb([128, 16, 2], BF16)
    k.mod = P.sb([128, 96, 2], F32)
    k.A1 = P.sb([128, 16, 2], F32)
    k.A2 = P.sb([128, 16, 2], F32)
    k.onesD = P.sb([128, 128], BF16)
    k.ones128 = P.sb([128, 128], BF16)
    k.ones1 = P.sb([128, 128], BF16)
    k.onesf128 = P.sb([128, 128], F32)
    k.bandb = P.sb([128, 384], BF16)
    k.identb = P.sb([128, 128], BF16)
    k.maskb = P.sb([128, 5, 128], BF16)

    P.add("sp", lambda e: e.dma_start(out=k.vecs[:], in_=k.vecs_d[:, :]), writes=["vecs"], dma=True)
    P.add("sp", lambda e: e.dma_start(out=k.consts[:], in_=k.consts_d[:, :]), writes=["consts"], dma=True)
    P.add("pool", lambda e: e.memset(k.onesD[:], 1.0 / D), writes=["onesD"])
    P.add("pool", lambda e: e.memset(k.ones128[:], 1.0 / 128), writes=["ones128"])
    P.add("pool", lambda e: e.memset(k.ones1[:], 1.0), writes=["ones1"])
    P.add("pool", lambda e: e.memset(k.onesf128[:], 1.0 / 128), writes=["onesf128"])
    P.add("dve", lambda e: e.tensor_copy(k.bandb[:], k.consts[:, COFF["band"][0]:COFF["band"][0] + 384]), reads=["consts"], writes=["bandb"])
    P.add("dve", lambda e: e.tensor_copy(k.maskb[:].rearrange("p q j -> p (q j)"), k.consts[:, COFF["mblk"][0]:COFF["mblk"][0] + 640]), reads=["consts"], writes=["maskb"])
    P.add("dve", lambda e: e.tensor_copy(k.identb[:], k.consts[:, COFF["ident"][0]:COFF["ident"][0] + 128]), reads=["consts"], writes=["identb"])
    co = VOFF["c"][0]
    P.add("act", lambda e: e.activation(out=k.siluc[:, :, 0], in_=k.vecs[:, co:co + 16], func=AF.Silu), reads=["vecs"], writes=["siluc"])
    co2 = VOFF["cctx"][0]
    P.add("act", lambda e: e.activation(out=k.siluc[:, :, 1], in_=k.vecs[:, co2:co2 + 16], func=AF.Silu), reads=["vecs", "siluc"], writes=["siluc"])
    with contextlib.ExitStack() as st:
        tmp = [P.sb([128, 4, NT], F32, st) for _ in range(2)]
        src = k.h_in.rearrange("(c p) t -> p c t", p=128)
        dst = k.hT.rearrange("(c p) t -> p c t", p=128)
        for q in range(4):
            b = q % 2
            P.add("sp", lambda e, b=b, q=q: e.dma_start(out=tmp[b][:], in_=src[:, q * 4:(q + 1) * 4, :]), writes=[("tmp", b)], dma=True)
            P.add("sp", lambda e, b=b, q=q: e.dma_start(out=dst[:, q * 4:(q + 1) * 4, :], in_=tmp[b][:]), reads=[("tmp", b)], writes=["hT"], dma=True)
    P.barrier()

    for (l, parts) in plan:
        stage_adaln(k, l)
        if "mix" in parts:
            if l % 2 == 0:
                stage_even(k, l)
            else:
                stage_dn(k, l)
        if "ffn" in parts:
            stage_ffn(k, l)
    P.finish()
    return nc


def stage_even(k, l):
    P = k.P
    with contextlib.ExitStack() as st0:
        nx = P.sb([128, 16, NT], BF16, st0)
        stage_norm(k, 1, nx)
        P.barrier()
        even_qkv(k, l, nx)
        P.barrier()
        even_glu(k, l, nx)
        P.barrier()
    even_attn(k, l)
    P.barrier()
    even_out(k, l)
    P.barrier()


def rsqrt_ops(P, out_ap, in_ap, reads, wkey):
    P.add("act", lambda e: e.activation(out=out_ap, in_=in_ap, func=AF.Sqrt, bias=EPS, scale=1.0), reads=reads, writes=[wkey])
    P.add("dve", lambda e: e.reciprocal(out_ap, out_ap), reads=[wkey], writes=[wkey])


def even_qkv(k, l, nx):
    P = k.P
    i = l // 2
    cos0 = 0; sin0 = TLAT; rp0 = COFF["ropeT"][0]
    with contextlib.ExitStack() as st:
        rope = P.sb([128, 2 * TLAT], F32, st)
        P.add("sp", lambda e: e.dma_start(out=rope[:], in_=k.rope_d[:, :]), writes=["rope"], dma=True)
        wb = [P.sb([128, 16, 256], BF16, st) for _ in range(2)]
        psa = [P.ps([128, 512], F32, st) for _ in range(2)]
        psb = P.ps([128, 512], F32, st)
        psc = P.ps([128, 512], F32, st)
        x0 = [P.sb([128, 512], F32, st) for _ in range(2)]
        sqb = P.sb([128, 512], BF16, st)
        rr = P.sb([128, 512], F32, st)
        qn = P.sb([128, 512], F32, st)
        t1 = P.sb([128, 512], F32, st)
        t2 = P.sb([128, 512], F32, st)
        qo = [P.sb([128, NT], BF16, st) for _ in range(2)]
        vt = P.sb([128, 18, 256], BF16, st)
        gs = P.sb([128, 2], F32, st)
        qg0 = VOFF["qg"][0] + i; kg0 = VOFF["kg"][0] + i
        P.add("act", lambda e: e.mul(gs[:, 0:1], k.vecs[:, qg0:qg0 + 1], 128 ** -0.5), writes=["gs"])
        P.add("act", lambda e: e.copy(gs[:, 1:2], k.vecs[:, kg0:kg0 + 1]), reads=["gs"], writes=["gs"])
        nblk = 0
        nchunk = 0
        npa = 0
        for blk in range(5):
            b = nblk % 2; nblk += 1
            P.add("pool", lambda e, b=b, blk=blk: e.dma_start(out=wb[b][:], in_=k.even_w_in[i, blk]),
                  writes=[("wb", b)], dma=True)
            for s in range(2):
                ch = blk * 2 + s
                isq = ch < 8
                qb = nchunk % 2; nchunk += 1
                gcol = gs[:, 0:1] if isq else gs[:, 1:2]
                for ti, (t0, n) in enumerate(TILES):
                    pb = npa % 2; npa += 1
                    for kc in range(16):
                        P.add("pe", lambda e, b=b, s=s, pb=pb, kc=kc, t0=t0, n=n: e.matmul(
                            psa[pb][:, :n], wb[b][:, kc, s * 128:(s + 1) * 128], nx[:, kc, t0:t0 + n], start=(kc == 0), stop=(kc == 15)),
                            reads=[("wb", b)], writes=[("psa", pb)])
                    P.add("act", lambda e, pb=pb, n=n: e.activation(out=x0[pb][:, :n], in_=psa[pb][:, :n], func=AF.Copy),
                          reads=[("psa", pb)], writes=[("x0", pb)])
                    P.add("act", lambda e, pb=pb, n=n: e.activation(out=sqb[:, :n], in_=psa[pb][:, :n], func=AF.Square),
                          reads=[("psa", pb)], writes=["sqb"])
                    P.add("pe", lambda e, n=n: e.matmul(psb[:, :n], k.ones128[:], sqb[:, :n], start=True, stop=True),
                          reads=["sqb"], writes=["psb"])
                    rsqrt_ops(P, rr[:, :n], psb[:, :n], ["psb"], "rr")
                    if t0 < LCTX:
                        P.add("dve", lambda e, pb=pb, qb=qb, n=n, t0=t0, gcol=gcol: e.scalar_tensor_tensor(
                            out=qo[qb][:, t0:t0 + n], in0=x0[pb][:, :n], scalar=gcol, in1=rr[:, :n], op0=ALU.mult, op1=ALU.mult),
                            reads=[("x0", pb), "rr", "gs"], writes=[("qo", qb, ti)])
                    else:
                        P.add("dve", lambda e, pb=pb, n=n, gcol=gcol: e.scalar_tensor_tensor(
                            out=qn[:, :n], in0=x0[pb][:, :n], scalar=gcol, in1=rr[:, :n], op0=ALU.mult, op1=ALU.mult),
                            reads=[("x0", pb), "rr", "gs"], writes=["qn"])
                        P.add("pe", lambda e, n=n: e.matmul(psc[:, :n], k.consts[:, rp0:rp0 + 128], qn[:, :n], start=True, stop=True),
                              reads=["qn"], writes=["psc"])
                        c0 = cos0 + t0 - LCTX; s0 = sin0 + t0 - LCTX
                        P.add("dve", lambda e, n=n, c0=c0: e.tensor_tensor(out=t1[:, :n], in0=qn[:, :n], in1=rope[:, c0:c0 + n], op=ALU.mult),
                              reads=["qn", "rope"], writes=["t1"])
                        P.add("dve", lambda e, n=n, s0=s0: e.tensor_tensor(out=t2[:, :n], in0=psc[:, :n], in1=rope[:, s0:s0 + n], op=ALU.mult),
                              reads=["psc", "rope"], writes=["t2"])
                        P.add("pool", lambda e, qb=qb, n=n, t0=t0: e.tensor_tensor(out=qo[qb][:, t0:t0 + n], in0=t1[:, :n], in1=t2[:, :n], op=ALU.add),
                              reads=["t1", "t2"], writes=[("qo", qb, ti)])
                dst = k.QT[ch] if isq else k.KT[ch - 8]
                P.add("sp", lambda e, qb=qb, dst=dst: e.dma_start(out=dst, in_=qo[qb][:]),
                      reads=[("qo", qb, ti) for ti in range(5)], writes=[("qkT", ch)], dma=True)
        b = nblk % 2; nblk += 1
        P.add("pool", lambda e, b=b: e.dma_start(out=wb[b][:], in_=k.even_w_in[i, 5]), writes=[("wb", b)], dma=True)
        for tt in range(18):
            pb = npa % 2; npa += 1
            for kc in range(16):
                P.add("pe", lambda e, b=b, pb=pb, kc=kc, tt=tt: e.matmul(
                    psa[pb][:, :256], nx[:, kc, tt * 128:(tt + 1) * 128], wb[b][:, kc, :], start=(kc == 0), stop=(kc == 15)),
                    reads=[("wb", b)], writes=[("psa", pb)])
            P.add("act", lambda e, pb=pb, tt=tt: e.activation(out=vt[:, tt, :], in_=psa[pb][:, :256], func=AF.Copy),
                  reads=[("psa", pb)], writes=[("vt", tt)])
        P.add("sp", lambda e: e.dma_start(out=k.VTM.rearrange("t p c -> p t c"), in_=vt[:]),
              reads=[("vt", tt) for tt in range(18)], writes=["VTM"], dma=True)


def even_glu(k, l, nx):
    P = k.P
    i = l // 2
    cdw = VOFF["cdw"][0] + i * 31 * 8
    cdb = VOFF["cdb"][0] + i * 8
    lng = VOFF["lng"][0] + i * 8
    lnb = VOFF["lnb"][0] + i * 8
    PADW = NT + 60

    def pos(t):
        return t + 15 if t < LCTX else t + 45
    with contextlib.ExitStack() as st:
        wb = [P.sb([128, 16, 256], BF16, st) for _ in range(2)]
        psa = [P.ps([128, 512], F32, st) for _ in range(2)]
        psb = [P.ps([128, 512], F32, st) for _ in range(2)]
        psc = [P.ps([128, 512], F32, st) for _ in range(2)]
        psm = P.ps([128, 512], F32, st)
        psq = P.ps([128, 512], F32, st)
        sgm = [P.sb([128, 512], F32, st) for _ in range(2)]
        Ub = [P.sb([128, PADW], BF16, st) for _ in range(2)]
        Dg = [P.sb([128, 31, 128], BF16, st) for _ in range(2)]
        accA = [P.sb([128, NT], F32, st) for _ in range(2)]
        hsq = P.sb([128, NT], F32, st)
        co = [P.sb([128, NT], BF16, st) for _ in range(2)]
        msb = P.sb([128, 512], F32, st)
        m2 = P.sb([128, 512], F32, st)
        var = P.sb([128, 512], F32, st)
        dd = P.sb([128, 512], F32, st)
        for b in range(2):
            P.add("pool", lambda e, b=b: e.memset(Ub[b][:], 0.0), writes=[("Ub", b, ti) for ti in range(5)] + [("Ubz", b)])

        def proj(j):
            b = j % 2
            P.add("pool", lambda e: e.dma_start(out=wb[b][:], in_=k.even_w_in[i, 6 + j]), writes=[("wb", b)], dma=True)
            for kk in range(31):
                w = k.vecs[:, cdw + kk * 8 + j:cdw + kk * 8 + j + 1]
                P.add("dve", lambda e, kk=kk, w=w: e.tensor_scalar(out=Dg[b][:, kk, :], in0=k.identb[:], scalar1=w, scalar2=None, op0=ALU.mult),
                      writes=[("Dg", b, kk)])
            for ti, (t0, n) in enumerate(TILES):
                pb = ti % 2
                for kc in range(16):
                    P.add("pe", lambda e, pb=pb, kc=kc, t0=t0, n=n: e.matmul(
                        psa[pb][:, :n], wb[b][:, kc, 0:128], nx[:, kc, t0:t0 + n], start=(kc == 0), stop=(kc == 15)),
                        reads=[("wb", b)], writes=[("psa", pb)])
                for kc in range(16):
                    P.add("pe", lambda e, pb=pb, kc=kc, t0=t0, n=n: e.matmul(
                        psb[pb][:, :n], wb[b][:, kc, 128:256], nx[:, kc, t0:t0 + n], start=(kc == 0), stop=(kc == 15)),
                        reads=[("wb", b)], writes=[("psb", pb)])
                P.add("act", lambda e, pb=pb, n=n: e.activation(out=sgm[pb][:, :n], in_=psb[pb][:, :n], func=AF.Sigmoid),
                      reads=[("psb", pb)], writes=[("sgm", pb)])
                p0 = pos(t0)
                P.add("dve", lambda e, pb=pb, p0=p0, n=n: e.tensor_tensor(out=Ub[b][:, p0:p0 + n], in0=psa[pb][:, :n], in1=sgm[pb][:, :n], op=ALU.mult),
                      reads=[("psa", pb), ("sgm", pb), ("Ubz", b)], writes=[("Ub", b, ti)])

        def conv(j):
            b = j % 2
            Uk = [("Ub", b, ti) for ti in range(5)] + [("Ubz", b)]
            bias = k.vecs[:, cdb + j:cdb + j + 1]
            for ti, (t0, n) in enumerate(TILES):
                pb = ti % 2
                p0 = pos(t0)
                for kk in range(31):
                    P.add("pe", lambda e, pb=pb, kk=kk, p0=p0, n=n: e.matmul(
                        psc[pb][:, :n], Dg[b][:, kk, :], Ub[b][:, p0 + kk - 15:p0 + kk - 15 + n], start=(kk == 0), stop=(kk == 30)),
                        reads=Uk + [("Dg", b, kk)], writes=[("psc", pb)])
                P.add("act", lambda e, pb=pb, t0=t0, n=n: e.activation(out=accA[b][:, t0:t0 + n], in_=psc[pb][:, :n], func=AF.Identity,
                                                                      bias=bias, scale=1.0),
                      reads=[("psc", pb)], writes=[("accA", b, ti)])

        def lnorm(j):
            b = j % 2
            Ak = [("accA", b, ti) for ti in range(5)]
            P.add("act", lambda e: e.activation(out=hsq[:], in_=accA[b][:], func=AF.Square), reads=Ak, writes=["hsq"])
            for ti, (t0, n) in enumerate(TILES):
                P.add("pe", lambda e, t0=t0, n=n: e.matmul(psm[:, :n], k.onesf128[:], accA[b][:, t0:t0 + n], start=True, stop=True),
                      reads=Ak, writes=["psm"])
                P.add("pe", lambda e, t0=t0, n=n: e.matmul(psq[:, :n], k.onesf128[:], hsq[:, t0:t0 + n], start=True, stop=True),
                      reads=["hsq"], writes=["psq"])
                P.add("act", lambda e, n=n: e.activation(out=msb[:, :n], in_=psm[:, :n], func=AF.Copy), reads=["psm"], writes=["msb"])
                P.add("act", lambda e, n=n: e.activation(out=m2[:, :n], in_=psm[:, :n], func=AF.Square), reads=["psm"], writes=["m2"])
                P.add("dve", lambda e, n=n: e.tensor_tensor(out=var[:, :n], in0=psq[:, :n], in1=m2[:, :n], op=ALU.subtract),
                      reads=["psq", "m2"], writes=["var"])
                P.add("dve", lambda e, n=n: e.tensor_scalar_max(out=var[:, :n], in0=var[:, :n], scalar1=0.0), reads=["var"], writes=["var"])
                rsqrt_ops(P, var[:, :n], var[:, :n], ["var"], "var")
                P.add("dve", lambda e, t0=t0, n=n: e.tensor_tensor(out=dd[:, :n], in0=accA[b][:, t0:t0 + n], in1=msb[:, :n], op=ALU.subtract),
                      reads=Ak + ["msb"], writes=["dd"])
                P.add("dve", lambda e, n=n: e.tensor_tensor(out=dd[:, :n], in0=dd[:, :n], in1=var[:, :n], op=ALU.mult),
                      reads=["dd", "var"], writes=["dd"])
                P.add("act", lambda e, t0=t0, n=n: e.activation(
                    out=co[b][:, t0:t0 + n], in_=dd[:, :n], func=AF.Silu,
                    bias=k.vecs[:, lnb + j:lnb + j + 1], scale=k.vecs[:, lng + j:lng + j + 1]),
                    reads=["dd"], writes=[("co", b, ti)])
            P.add("sp", lambda e: e.dma_start(out=k.CAT[8 + j], in_=co[b][:]),
                  reads=[("co", b, ti) for ti in range(5)], writes=[("CAT", 8 + j)], dma=True)

        for j in range(8):
            proj(j)
            if j >= 1:
                lnorm(j - 1)
            conv(j)
        lnorm(7)


def even_attn(k, l):
    P = k.P
    i = l // 2
    band0 = COFF["band"][0]
    with contextlib.ExitStack() as st:
        kT = P.sb([128, 2, NT], BF16, st)
        V = P.sb([128, 18, 256], BF16, st)
        qh = [P.sb([128, NT], BF16, st) for _ in range(2)]
        PC = [[P.sb([128, NT], BF16, st) for _ in range(2)] for _ in range(2)]
        PB = [P.sb([128, 16, 384], BF16, st) for _ in range(2)]
        tmpE = [P.sb([128, 384], BF16, st) for _ in range(2)]
        ao = [P.sb([128, NT], BF16, st) for _ in range(2)]
        den = P.sb([128, 512], F32, st)
        esink = P.sb([128, 8], F32, st)
        pss = [P.ps([128, 512], F32, st) for _ in range(2)]
        pso = [P.ps([128, 512], F32, st) for _ in range(2)]
        psd = [P.ps([128, 512], F32, st) for _ in range(2)]
        so = VOFF["sink"][0] + i * 8
        P.add("act", lambda e: e.activation(out=esink[:], in_=k.vecs[:, so:so + 8], func=AF.Exp), writes=["esink"])
        for g in range(2):
            P.add("sp", lambda e, g=g: e.dma_start(out=kT[:, g, :], in_=k.KT[g]), writes=[("kT", g)], dma=True)
        P.add("sp", lambda e: e.dma_start(out=V[:], in_=k.VTM.rearrange("t p c -> p t c")), writes=["V"], dma=True)
        nps = 0
        npo = 0
        for h in range(8):
            g = h // 4
            hb = h % 2
            P.add("sp", lambda e, hb=hb, h=h: e.dma_start(out=qh[hb][:], in_=k.QT[h]), writes=[("qh", hb)], dma=True)
            for cb in range(2):
                for ti, (t0, n) in enumerate(TILES):
                    pb = nps % 2; nps += 1
                    P.add("pe", lambda e, pb=pb, g=g, cb=cb, hb=hb, t0=t0, n=n: e.matmul(
                        pss[pb][:, :n], kT[:, g, cb * 128:(cb + 1) * 128], qh[hb][:, t0:t0 + n], start=True, stop=True),
                        reads=[("kT", g), ("qh", hb)], writes=[("pss", pb)])
                    P.add("act", lambda e, pb=pb, hb=hb, cb=cb, t0=t0, n=n: e.activation(
                        out=PC[hb][cb][:, t0:t0 + n], in_=pss[pb][:, :n], func=AF.Exp),
                        reads=[("pss", pb)], writes=[("PC", hb, cb, ti)])
            for jb in range(16):
                lo = max(jb - 1, 0); hi = min(jb + 1, 15)
                n = (hi - lo + 1) * 128
                off = (lo - (jb - 1)) * 128
                pb = nps % 2; nps += 1
                eb = jb % 2
                P.add("pe", lambda e, pb=pb, g=g, jb=jb, hb=hb, lo=lo, n=n: e.matmul(
                    pss[pb][:, :n], kT[:, g, LCTX + jb * 128:LCTX + (jb + 1) * 128], qh[hb][:, LCTX + lo * 128:LCTX + lo * 128 + n],
                    start=True, stop=True), reads=[("kT", g), ("qh", hb)], writes=[("pss", pb)])
                P.add("act", lambda e, pb=pb, eb=eb, n=n: e.activation(out=tmpE[eb][:, :n], in_=pss[pb][:, :n], func=AF.Exp),
                      reads=[("pss", pb)], writes=[("tmpE", eb)])
                P.add("pool", lambda e, eb=eb, hb=hb, jb=jb, off=off, n=n: e.tensor_tensor(
                    out=PB[hb][:, jb, off:off + n], in0=tmpE[eb][:, :n], in1=k.bandb[:, off:off + n], op=ALU.mult),
                    reads=[("tmpE", eb)], writes=[("PB", hb, jb)])
            for ti, (t0, n) in enumerate(TILES):
                ob = npo % 2; npo += 1
                mms = []
                for cb in range(2):
                    mms.append((V[:, cb, g * 128:(g + 1) * 128], PC[hb][cb][:, t0:t0 + n], 0, n, [("PC", hb, cb, ti)]))
                if t0 >= LCTX:
                    qt = (t0 - LCTX) // 512
                    for nb in range(4 * qt, 4 * qt + 4):
                        for jb in (nb - 1, nb, nb + 1):
                            if 0 <= jb <= 15:
                                c0 = (nb - (jb - 1)) * 128
                                mms.append((V[:, 2 + jb, g * 128:(g + 1) * 128], PB[hb][:, jb, c0:c0 + 128], (nb - 4 * qt) * 128, 128,
                                            [("PB", hb, jb)]))
                for mi, (lhs, rhs, o0, on, rk) in enumerate(mms):
                    P.add("pe", lambda e, ob=ob, lhs=lhs, rhs=rhs, o0=o0, on=on, mi=mi, last=(mi == len(mms) - 1): e.matmul(
                        pso[ob][:, o0:o0 + on], lhs, rhs, start=(mi == 0), stop=last, skip_group_check=True),
                        reads=rk + ["V"], writes=[("pso", ob)])
                for mi, (lhs, rhs, o0, on, rk) in enumerate(mms):
                    P.add("pe", lambda e, ob=ob, rhs=rhs, o0=o0, on=on, mi=mi, last=(mi == len(mms) - 1): e.matmul(
                        psd[ob][:, o0:o0 + on], k.ones1[:], rhs, start=(mi == 0), stop=last, skip_group_check=True),
                        reads=rk, writes=[("psd", ob)])
                P.add("dve", lambda e, ob=ob, n=n, h=h: e.tensor_scalar_add(out=den[:, :n], in0=psd[ob][:, :n], scalar1=esink[:, h:h + 1]),
                      reads=[("psd", ob), "esink"], writes=["den"])
                P.add("dve", lambda e, n=n: e.reciprocal(den[:, :n], den[:, :n]), reads=["den"], writes=["den"])
                P.add("dve", lambda e, ob=ob, hb=hb, t0=t0, n=n: e.tensor_tensor(out=ao[hb][:, t0:t0 + n], in0=pso[ob][:, :n], in1=den[:, :n], op=ALU.mult),
                      reads=[("pso", ob), "den"], writes=[("ao", hb, ti)])
            P.add("sp", lambda e, hb=hb, h=h: e.dma_start(out=k.CAT[h], in_=ao[hb][:]),
                  reads=[("ao", hb, ti) for ti in range(5)], writes=[("CAT", h)], dma=True)


def proj_out_residual(k, wsrc, nkc, cat, gate_m, wkeys_extra=()):
    P = k.P
    hsrc = k.hT.rearrange("(c p) t -> p c t", p=128)
    with contextlib.ExitStack() as st:
        wb = [P.sb([128, nkc, 256], BF16, st) for _ in range(2)]
        hrow = [P.sb([128, NT], F32, st) for _ in range(2)]
        pso = [P.ps([128, 512], F32, st) for _ in range(4)]
        nps = 0
        for blk in range(8):
            b = blk % 2
            P.add("pool", lambda e, b=b, blk=blk: e.dma_start(out=wb[b][:], in_=wsrc(blk)),
                  writes=[("wb", b)], dma=True)
            for s in range(2):
                c = blk * 2 + s
                hb = c % 2
                P.add("sp", lambda e, hb=hb, c=c: e.dma_start(out=hrow[hb][:], in_=hsrc[:, c, :]),
                      reads=[("hT", c)], writes=[("hrow", hb)], dma=True)
                for ti, (t0, n) in enumerate(TILES):
                    pb = nps % 4; nps += 1
                    for kc in range(nkc):
                        P.add("pe", lambda e, b=b, s=s, pb=pb, kc=kc, t0=t0, n=n: e.matmul(
                            pso[pb][:, :n], wb[b][:, kc, s * 128:(s + 1) * 128], cat[:, kc, t0:t0 + n],
                            start=(kc == 0), stop=(kc == nkc - 1)),
                            reads=[("wb", b), ("cat", kc)], writes=[("pso", pb)])
                    sidx = 1 if t0 < LCTX else 0
                    P.add("dve", lambda e, hb=hb, pb=pb, t0=t0, n=n, c=c, sidx=sidx: e.scalar_tensor_tensor(
                        out=hrow[hb][:, t0:t0 + n], in0=pso[pb][:, :n], scalar=modv(k, gate_m, c, sidx),
                        in1=hrow[hb][:, t0:t0 + n], op0=ALU.mult, op1=ALU.add),
                        reads=[("pso", pb), ("hrow", hb), "mod"], writes=[("hrow", hb)])
                P.add("sp", lambda e, hb=hb, c=c: e.dma_start(out=hsrc[:, c, :], in_=hrow[hb][:]),
                      reads=[("hrow", hb)], writes=[("hT", c)], dma=True)


def even_out(k, l):
    P = k.P
    i = l // 2
    wsrc = lambda blk: k.even_w_out[i, blk]
    with contextlib.ExitStack() as st:
        cat = P.sb([128, 16, NT], BF16, st)
        for c in range(16):
            P.add("sp", lambda e, c=c: e.dma_start(out=cat[:, c, :], in_=k.CAT[c]), writes=[("cat", c)], dma=True)
        proj_out_residual(k, wsrc, 16, cat, 2)


DN_STOP = 99


def stage_dn(k, l):
    P = k.P
    with contextlib.ExitStack() as st1:
        k.BETA = P.sb([128, 18, 64], F32, st1)
        k.GG = P.sb([128, 18, 64], F32, st1)
        k.EG = P.sb([128, 18, 64], F32, st1)
        k.EL = P.sb([128, 18, 64], F32, st1)
        k.BEG = P.sb([128, 18, 64], F32, st1)
        k.ER = P.sb([128, 18, 64], F32, st1)
        with contextlib.ExitStack() as st0:
            nx = P.sb([128, 16, NT], BF16, st0)
            stage_norm(k, 1, nx, nbuf=1)
            P.barrier()
            if DN_STOP >= 1:
                dn_proj_qkv(k, l, nx)
                P.barrier()
            if DN_STOP >= 2:
                dn_proj_z(k, l, nx)
                P.barrier()
            if DN_STOP >= 3:
                dn_proj_ba(k, l, nx)
                P.barrier()
        if DN_STOP >= 4:
            dn_gates(k, l)
            P.barrier()
            if "DBG_G" in DEBUG_OUT:
                for ii, t in enumerate((k.BETA, k.GG, k.EG, k.EL, k.ER)):
                    P.add("sp", lambda e, ii=ii, t=t: e.dma_start(out=k.DBG_G[ii], in_=t[:]), dma=True)
                P.barrier()
        if DN_STOP >= 5:
            dn_prep(k, l)
            P.barrier()
        if DN_STOP >= 6:
            dn_scan(k, l)
            P.barrier()
    if DN_STOP >= 7:
        dn_outnorm(k, l)
        P.barrier()
    if DN_STOP >= 8:
        dn_outproj(k, l)
        P.barrier()


def dn_proj_qkv(k, l, nx):
    P = k.P
    i = l // 2
    dcw = VOFF["dcw"][0] + i * 5 * 64
    with contextlib.ExitStack() as st:
        wb = [P.sb([128, 16, 256], BF16, st) for _ in range(2)]
        psa = [P.ps([128, 512], F32, st) for _ in range(2)]
        psb = P.ps([128, 512], F32, st)
        pst = [P.ps([128, 4, 128], BF16, st) for _ in range(2)]
        U = [P.sb([128, NT], F32, st) for _ in range(2)]
        accs = [P.sb([128, NT], F32, st) for _ in range(2)]
        sq = P.sb([128, NT], BF16, st)
        rr = P.sb([128, 512], F32, st)
        qo = [P.sb([128, NT], BF16, st) for _ in range(2)]
        tm = [P.sb([128, 18, 128], BF16, st)] * 2
        npa = 0
        npt = 0
        for blk in range(32):
            b = blk % 2
            P.add("pool", lambda e, b=b, blk=blk: e.dma_start(out=wb[b][:], in_=k.dn_w_in[i, blk]),
                  writes=[("wb", b)], dma=True)
            for s in range(2):
                ch = blk * 2 + s
                ub = ch % 2
                acc = accs[ch % 2]
                ak = ("acc", ch % 2)
                kind = "q" if ch < 16 else ("k" if ch < 32 else "v")
                for ti, (t0, n) in enumerate(TILES):
                    pb = npa % 2; npa += 1
                    for kc in range(16):
                        P.add("pe", lambda e, b=b, s=s, pb=pb, kc=kc, t0=t0, n=n: e.matmul(
                            psa[pb][:, :n], wb[b][:, kc, s * 128:(s + 1) * 128], nx[:, kc, t0:t0 + n], start=(kc == 0), stop=(kc == 15)),
                            reads=[("wb", b)], writes=[("psa", pb)])
                    P.add("act", lambda e, ub=ub, pb=pb, t0=t0, n=n: e.activation(out=U[ub][:, t0:t0 + n], in_=psa[pb][:, :n], func=AF.Copy),
                          reads=[("psa", pb)], writes=[("U", ub, ti)])
                Uk = [("U", ub, ti) for ti in range(5)]
                wc = lambda kk, ch=ch: k.vecs[:, dcw + kk * 64 + ch:dcw + kk * 64 + ch + 1]
                P.add("dve", lambda e, acc=acc, ub=ub, w=wc(2): e.tensor_scalar(out=acc[:], in0=U[ub][:], scalar1=w, scalar2=None, op0=ALU.mult),
                      reads=Uk, writes=[ak])
                for kk in (0, 1, 3, 4):
                    sh = kk - 2
                    for (s0, sn) in ((0, LCTX), (LCTX, TLAT)):
                        lo = max(0, -sh); hi = sn - max(0, sh)
                        P.add("dve", lambda e, acc=acc, ub=ub, s0=s0, lo=lo, hi=hi, sh=sh, w=wc(kk): e.scalar_tensor_tensor(
                            out=acc[:, s0 + lo:s0 + hi], in0=U[ub][:, s0 + lo + sh:s0 + hi + sh], scalar=w, in1=acc[:, s0 + lo:s0 + hi],
                            op0=ALU.mult, op1=ALU.add), reads=Uk + [ak], writes=[ak])
                qb = ch % 2
                if kind == "v":
                    P.add("act", lambda e, acc=acc, qb=qb: e.activation(out=qo[qb][:], in_=acc[:], func=AF.Silu), reads=[ak], writes=[("qo", qb)])
                else:
                    P.add("act", lambda e, acc=acc: e.activation(out=acc[:], in_=acc[:], func=AF.Silu), reads=[ak], writes=[ak])
                    P.add("act", lambda e, acc=acc: e.activation(out=sq[:], in_=acc[:], func=AF.Square), reads=[ak], writes=["sq"])
                    for ti, (t0, n) in enumerate(TILES):
                        P.add("pe", lambda e, t0=t0, n=n: e.matmul(psb[:, :n], k.ones1[:], sq[:, t0:t0 + n], start=True, stop=True),
                              reads=["sq"], writes=["psb"])
                        rsqrt_ops(P, rr[:, :n], psb[:, :n], ["psb"], "rr")
                        sc = (128 ** -0.5) if kind == "q" else 1.0
                        P.add("dve", lambda e, acc=acc, qb=qb, t0=t0, n=n, sc=sc: e.scalar_tensor_tensor(
                            out=qo[qb][:, t0:t0 + n], in0=acc[:, t0:t0 + n], scalar=sc, in1=rr[:, :n], op0=ALU.mult, op1=ALU.mult),
                            reads=[ak, "rr"], writes=[("qo", qb)])
                if kind == "q":
                    P.add("sp", lambda e, qb=qb, ch=ch: e.dma_start(out=k.DQT[ch], in_=qo[qb][:]), reads=[("qo", qb)], writes=[("DQT", ch)], dma=True)
                else:
                    if kind == "k":
                        P.add("sp", lambda e, qb=qb, ch=ch: e.dma_start(out=k.DKT[ch - 16], in_=qo[qb][:]), reads=[("qo", qb)],
                              writes=[("DKT", ch)], dma=True)
                    for c4 in range(0, 18, 4):
                        nn = min(4, 18 - c4)
                        tb = npt % 2; npt += 1
                        for cc in range(nn):
                            c = c4 + cc
                            P.add("pe", lambda e, tb=tb, cc=cc, c=c, qb=qb: e.transpose(
                                out=pst[tb][:, cc, :], in_=qo[qb][:, c * 128:(c + 1) * 128], identity=k.identb[:]),
                                reads=[("qo", qb)], writes=[("pst", tb)])
                        P.add("act", lambda e, tb=tb, qb=qb, c4=c4, nn=nn: e.activation(out=tm[qb][:, c4:c4 + nn, :], in_=pst[tb][:, :nn, :], func=AF.Copy),
                              reads=[("pst", tb)], writes=[("tm", 0)])
                    if kind == "k":
                        dst = k.DKTM[:, :, (ch - 16) * 128:(ch - 15) * 128]
                    else:
                        dst = k.DVTM[:, :, (ch - 32) * 128:(ch - 31) * 128]
                    P.add("sp", lambda e, qb=qb, dst=dst: e.dma_start(out=dst.rearrange("c p d -> p c d"), in_=tm[qb][:]),
                          reads=[("tm", 0)], writes=[("DTM", ch)], dma=True)


def dn_proj_z(k, l, nx):
    P = k.P
    i = l // 2
    with contextlib.ExitStack() as st:
        wb = [P.sb([128, 16, 512], BF16, st) for _ in range(2)]
        psa = [P.ps([128, 512], F32, st) for _ in range(4)]
        zs = [P.sb([128, 512], BF16, st) for _ in range(4)]
        npa = 0
        for blk in range(8):
            b = blk % 2
            for q in range(2):
                P.add("pool", lambda e, b=b, blk=blk, q=q: e.dma_start(
                    out=wb[b][:, :, q * 256:(q + 1) * 256], in_=k.dn_w_in[i, 32 + blk * 2 + q]),
                    writes=[("wb", b, q)], dma=True)
            for c in range(18):
                pb = npa % 4; npa += 1
                for kc in range(16):
                    P.add("pe", lambda e, b=b, pb=pb, kc=kc, c=c: e.matmul(
                        psa[pb][:], nx[:, kc, c * 128:(c + 1) * 128], wb[b][:, kc, :], start=(kc == 0), stop=(kc == 15)),
                        reads=[("wb", b, 0), ("wb", b, 1)], writes=[("psa", pb)])
                P.add("act", lambda e, pb=pb: e.activation(out=zs[pb][:], in_=psa[pb][:], func=AF.Silu),
                      reads=[("psa", pb)], writes=[("zs", pb)])
                P.add("sp", lambda e, pb=pb, c=c, blk=blk: e.dma_start(out=k.DZ[c, :, blk * 512:(blk + 1) * 512], in_=zs[pb][:]),
                      reads=[("zs", pb)], writes=[("DZ", c, blk)], dma=True)


def dn_proj_ba(k, l, nx):
    P = k.P
    i = l // 2
    with contextlib.ExitStack() as st:
        wb = P.sb([128, 16, 128], BF16, st)
        BA = P.sb([128, 18, 128], F32, st)
        psa = [P.ps([128, 512], F32, st) for _ in range(2)]
        P.add("pool", lambda e: e.dma_start(out=wb[:], in_=k.dn_w_ba[i]), writes=["wb"], dma=True)
        for c in range(18):
            pb = c % 2
            for kc in range(16):
                P.add("pe", lambda e, pb=pb, kc=kc, c=c: e.matmul(
                    psa[pb][:, 0:128], nx[:, kc, c * 128:(c + 1) * 128], wb[:, kc, :], start=(kc == 0), stop=(kc == 15)),
                    reads=["wb"], writes=[("psa", pb)])
            P.add("act", lambda e, pb=pb, c=c: e.activation(out=BA[:, c, :], in_=psa[pb][:, 0:128], func=AF.Copy),
                  reads=[("psa", pb)], writes=[("BA", c)])
        BAk = [("BA", c) for c in range(18)]
        P.add("act", lambda e: e.activation(out=k.BETA[:], in_=BA[:, :, 0:64], func=AF.Sigmoid), reads=BAk, writes=["BETA"])
        P.add("dve", lambda e: e.tensor_copy(k.GG[:], BA[:, :, 64:128]), reads=BAk, writes=["GG"])
        P.barrier()


def dn_gates(k, l):
    P = k.P
    i = l // 2
    al = VOFF["alog"][0] + i * 64
    db = VOFF["dtb"][0] + i * 64
    with contextlib.ExitStack() as st:
        x = P.sb([128, 18, 64], F32, st)
        ax = P.sb([128, 18, 64], F32, st)
        nA = P.sb([128, 64], F32, st)
        onesf = P.sb([128, 128], F32, st)
        ps = [P.ps([128, 3, 64], F32, st) for _ in range(2)]
        P.add("pool", lambda e: e.memset(onesf[:], 1.0), writes=["onesf"])
        P.add("dve", lambda e: e.tensor_tensor(out=x[:], in0=k.GG[:], in1=bc(k.vecs[:, db:db + 64].unsqueeze(1), [128, 18, 64]), op=ALU.add),
              writes=["x"])
        P.add("act", lambda e: e.activation(out=ax[:], in_=x[:], func=AF.Abs), reads=["x"], writes=["ax"])
        P.add("act", lambda e: e.activation(out=ax[:], in_=ax[:], func=AF.Exp, scale=-1.0), reads=["ax"], writes=["ax"])
        P.add("act", lambda e: e.activation(out=ax[:], in_=ax[:], func=AF.Ln, bias=1.0), reads=["ax"], writes=["ax"])
        P.add("dve", lambda e: e.tensor_scalar_max(out=x[:], in0=x[:], scalar1=0.0), reads=["x"], writes=["x"])
        P.add("dve", lambda e: e.tensor_tensor(out=x[:], in0=x[:], in1=ax[:], op=ALU.add), reads=["x", "ax"], writes=["x"])
        P.add("act", lambda e: e.activation(out=nA[:], in_=k.vecs[:, al:al + 64], func=AF.Exp), writes=["nA"])
        P.add("dve", lambda e: e.scalar_tensor_tensor(out=k.GG[:], in0=x[:], scalar=-1.0, in1=bc(nA[:].unsqueeze(1), [128, 18, 64]),
                                                      op0=ALU.mult, op1=ALU.mult), reads=["x", "nA"], writes=["GG"])
        mo = {n: COFF[n][0] for n in ("m_le", "m_ge", "m_lt", "m_gt")}
        for c in range(18):
            pb = c % 2
            for d in range(2):
                m_gc = mo["m_le"] if d == 0 else mo["m_ge"]
                m_rm = mo["m_gt"] if d == 0 else mo["m_lt"]
                rhs = k.GG[:, c, d * 32:(d + 1) * 32]
                P.add("pe", lambda e, pb=pb, d=d, m=m_gc, rhs=rhs: e.matmul(ps[pb][:, 0, d * 32:(d + 1) * 32], k.consts[:, m:m + 128], rhs, start=True, stop=True),
                      reads=["GG"], writes=[("ps", pb)])
                P.add("pe", lambda e, pb=pb, d=d, m=m_rm, rhs=rhs: e.matmul(ps[pb][:, 1, d * 32:(d + 1) * 32], k.consts[:, m:m + 128], rhs, start=True, stop=True),
                      reads=["GG"], writes=[("ps", pb)])
                P.add("pe", lambda e, pb=pb, d=d, rhs=rhs: e.matmul(ps[pb][:, 2, d * 32:(d + 1) * 32], onesf[:], rhs, start=True, stop=True),
                      reads=["GG", "onesf"], writes=[("ps", pb)])
            P.add("act", lambda e, pb=pb, c=c: e.activation(out=k.EG[:, c, :], in_=ps[pb][:, 0, :], func=AF.Exp), reads=[("ps", pb)], writes=[("EG", c)])
            P.add("act", lambda e, pb=pb, c=c: e.activation(out=k.ER[:, c, :], in_=ps[pb][:, 1, :], func=AF.Exp), reads=[("ps", pb)], writes=[("ER", c)])
            P.add("act", lambda e, pb=pb, c=c: e.activation(out=k.EL[:, c, :], in_=ps[pb][:, 2, :], func=AF.Exp), reads=[("ps", pb)], writes=[("EL", c)])
            P.add("dve", lambda e, c=c: e.tensor_tensor(out=k.BEG[:, c, :], in0=k.BETA[:, c, :], in1=k.EG[:, c, :], op=ALU.mult),
                  reads=[("EG", c)], writes=[("BEG", c)])


def dn_prep(k, l):
    P = k.P
    mo = {n: COFF[n][0] for n in ("m_le", "m_ge", "m_lt", "m_gt", "ident")}
    NL = 2
    with contextlib.ExitStack() as st:
        kTc = [P.sb([128, 16, 128], BF16, st) for _ in range(2)]
        qTc = [P.sb([128, 16, 128], BF16, st) for _ in range(2)]
        ktm = [P.sb([128, 16, 128], BF16, st) for _ in range(2)]
        vtm = [P.sb([128, 32, 128], BF16, st) for _ in range(2)]
        lanes = []
        for ln in range(NL):
            B_ = K()
            B_.Gm = P.sb([128, 2, 128], F32, st); B_.QKm = P.sb([128, 2, 128], F32, st)
            B_.rhs2 = P.sb([128, 4, 128], F32, st); B_.eD = P.sb([128, 4, 128], F32, st)
            B_.L0 = P.sb([128, 4, 128], BF16, st); B_.QKD = P.sb([128, 4, 128], BF16, st)
            B_.LtQ = P.sb([128, 8, 128], BF16, st)
            B_.Xb = [P.sb([128, 4, 128], BF16, st) for _ in range(2)]
            B_.Yb = P.sb([128, 4, 128], BF16, st)
            B_.Zb = [P.sb([128, 4, 128], BF16, st) for _ in range(2)]
            B_.Ab = [P.sb([128, 4, 128], BF16, st) for _ in range(2)]
            B_.Bb = [P.sb([128, 4, 128], BF16, st) for _ in range(2)]
            B_.Lm = [P.sb([128, 4, 128], BF16, st) for _ in range(5)]
            B_.Ltm = [P.sb([128, 4, 128], BF16, st) for _ in range(4)]
            B_.T1a = P.sb([128, 4, 128], BF16, st); B_.T1b = P.sb([128, 4, 128], BF16, st)
            B_.vb = P.sb([128, 4, 128], BF16, st); B_.kbg = P.sb([128, 4, 128], BF16, st)
            B_.PH = [P.sb([128, 4, 4, 128], BF16, st) for _ in range(2)]
            B_.cnt = 0
            B_.psT = P.ps([128, 8, 128], BF16, st)
            B_.psX = P.ps([128, 4, 128], F32, st); B_.psY = P.ps([128, 4, 128], F32, st); B_.psZ = P.ps([128, 4, 128], F32, st)
            lanes.append(B_)

        def unit(ln, c, d, hg, cb):
            B_ = lanes[ln]
            K_ = lambda name, *a: (name, ln) + a
            Gm, QKm, rhs2, eD, L0, QKD, LtQ = B_.Gm, B_.QKm, B_.rhs2, B_.eD, B_.L0, B_.QKD, B_.LtQ
            Xb, Yb, Zb, Ab, Bb, Lm, Ltm, T1a, T1b = B_.Xb, B_.Yb, B_.Zb, B_.Ab, B_.Bb, B_.Lm, B_.Ltm, B_.T1a, B_.T1b
            vb, kbg = B_.vb, B_.kbg
            pi = B_.cnt % 2
            B_.cnt += 1
            PH = B_.PH[pi]
            Uo, Wo, Qo, kd = PH[:, 0], PH[:, 1], PH[:, 2], PH[:, 3]
            K2 = lambda name, *a: (name, ln, pi) + a
            psT, psX, psY, psZ = B_.psT, B_.psX, B_.psY, B_.psZ
            m_strict = mo["m_gt"] if d == 0 else mo["m_lt"]
            m_incl = mo["m_ge"] if d == 0 else mo["m_le"]
            m_l = mo["m_le"] if d == 0 else mo["m_ge"]
            m_r = mo["m_gt"] if d == 0 else mo["m_lt"]
            h0 = d * 32 + hg * 4
            kh0 = hg * 2
            P.add("pool", lambda e: e.tensor_tensor(
                out=rhs2[:], in0=bc(k.consts[:, m_r:m_r + 128].unsqueeze(1), [128, 4, 128]),
                in1=bc(k.GG[:, c, h0:h0 + 4].unsqueeze(2), [128, 4, 128]), op=ALU.mult), writes=[K_("rhs2")])
            for a in range(2):
                P.add("pe", lambda e, a=a: e.matmul(psX[:, a, :], kTc[cb][:, kh0 + a, :], kTc[cb][:, kh0 + a, :], start=True, stop=True),
                      reads=[("kTc", cb)], writes=[K_("psX")])
                P.add("pe", lambda e, a=a: e.matmul(psX[:, 2 + a, :], qTc[cb][:, kh0 + a, :], kTc[cb][:, kh0 + a, :], start=True, stop=True),
                      reads=[("kTc", cb), ("qTc", cb)], writes=[K_("psX")])
            P.add("pe", lambda e: e.matmul(psY[:], k.consts[:, m_l:m_l + 128], rhs2[:], start=True, stop=True),
                  reads=[K_("rhs2")], writes=[K_("psY")])
            P.add("dve", lambda e: e.tensor_tensor(out=Gm[:], in0=psX[:, 0:2, :], in1=bc(k.consts[:, m_strict:m_strict + 128].unsqueeze(1), [128, 2, 128]), op=ALU.mult),
                  reads=[K_("psX")], writes=[K_("Gm")])
            P.add("dve", lambda e: e.tensor_tensor(out=QKm[:], in0=psX[:, 2:4, :], in1=bc(k.consts[:, m_incl:m_incl + 128].unsqueeze(1), [128, 2, 128]), op=ALU.mult),
                  reads=[K_("psX")], writes=[K_("QKm")])
            P.add("act", lambda e: e.activation(out=eD[:], in_=psY[:], func=AF.Exp), reads=[K_("psY")], writes=[K_("eD")])
            yield
            for hl in range(4):
                P.add("dve", lambda e, hl=hl: e.scalar_tensor_tensor(
                    out=L0[:, hl, :], in0=Gm[:, hl // 2, :], scalar=k.BETA[:, c, h0 + hl:h0 + hl + 1], in1=eD[:, hl, :],
                    op0=ALU.mult, op1=ALU.mult), reads=[K_("Gm"), K_("eD")], writes=[K_("L0", hl)])
            P.add("dve", lambda e: e.tensor_tensor(
                out=QKD[:].rearrange("p (a b) j -> p a b j", b=2), in0=bc(QKm[:].unsqueeze(2), [128, 2, 2, 128]),
                in1=eD[:].rearrange("p (a b) j -> p a b j", b=2), op=ALU.mult), reads=[K_("QKm"), K_("eD")], writes=[K_("QKD")])
            L0k = [K_("L0", hl) for hl in range(4)]
            for hl in range(4):
                P.add("pe", lambda e, hl=hl: e.transpose(out=psT[:, hl, :], in_=L0[:, hl, :], identity=k.identb[:]),
                      reads=L0k, writes=[K_("psT")])
            for hl in range(4):
                P.add("pe", lambda e, hl=hl: e.transpose(out=psT[:, 4 + hl, :], in_=QKD[:, hl, :], identity=k.identb[:]),
                      reads=[K_("QKD")], writes=[K_("psT")])
            P.add("act", lambda e: e.activation(out=LtQ[:, 0:4, :], in_=psT[:, 0:4, :], func=AF.Copy), reads=[K_("psT")], writes=[K_("LtQ")])
            P.add("act", lambda e: e.activation(out=Qo, in_=psT[:, 4:8, :], func=AF.Copy), reads=[K_("psT")], writes=[K2("Qo")])
            yield
            Lt = LtQ[:, 0:4, :]
            mk = lambda q: bc(k.maskb[:, q, :].unsqueeze(1), [128, 4, 128])
            P.add("dve", lambda e: e.tensor_tensor(out=Lm[0][:], in0=L0[:], in1=mk(0), op=ALU.mult), reads=L0k, writes=[K_("Lm", 0)])
            P.add("dve", lambda e: e.tensor_tensor(out=Ltm[0][:], in0=Lt, in1=mk(0), op=ALU.mult), reads=[K_("LtQ")], writes=[K_("Ltm", 0)])
            for q in range(1, 5):
                eng_m = "pool"
                P.add(eng_m, lambda e, q=q: e.tensor_tensor(out=Lm[q][:], in0=L0[:], in1=mk(q), op=ALU.mult), reads=L0k, writes=[K_("Lm", q)])
                if q < 4:
                    P.add(eng_m, lambda e, q=q: e.tensor_tensor(out=Ltm[q][:], in0=Lt, in1=mk(q), op=ALU.mult), reads=[K_("LtQ")], writes=[K_("Ltm", q)])
            for hl in range(4):
                P.add("act", lambda e, hl=hl: e.activation(out=vb[:, hl, :], in_=vtm[cb][:, hg * 4 + hl, :], func=AF.Identity,
                                                           scale=k.BETA[:, c, h0 + hl:h0 + hl + 1]), reads=[("vtm", cb)], writes=[K_("vb", hl)])
                P.add("act", lambda e, hl=hl: e.activation(out=kbg[:, hl, :], in_=ktm[cb][:, kh0 + hl // 2, :], func=AF.Identity,
                                                           scale=k.BEG[:, c, h0 + hl:h0 + hl + 1]), reads=[("ktm", cb)], writes=[K_("kbg", hl)])
                P.add("act", lambda e, hl=hl: e.activation(out=kd[:, hl, :], in_=ktm[cb][:, kh0 + hl // 2, :], func=AF.Identity,
                                                           scale=k.ER[:, c, h0 + hl:h0 + hl + 1]), reads=[("ktm", cb)], writes=[K2("kd", hl)])
            for hl in range(4):
                P.add("pe", lambda e, hl=hl: e.matmul(psX[:, hl, :], Ltm[0][:, hl, :], Lm[0][:, hl, :], start=True, stop=True),
                      reads=[K_("Lm", 0), K_("Ltm", 0)], writes=[K_("psX")])
            for hl in range(4):
                P.add("pe", lambda e, hl=hl: e.matmul(psY[:, hl, :], Lm[0][:, hl, :], Ltm[0][:, hl, :], start=True, stop=True),
                      reads=[K_("Lm", 0), K_("Ltm", 0)], writes=[K_("psY")])
            P.add("act", lambda e: e.activation(out=Xb[0][:], in_=psX[:], func=AF.Copy), reads=[K_("psX")], writes=[K_("X", 0)])
            P.add("act", lambda e: e.activation(out=Yb[:], in_=psY[:], func=AF.Copy), reads=[K_("psY")], writes=[K_("Y")])
            P.add("dve", lambda e: e.scalar_tensor_tensor(
                out=Zb[0][:], in0=Ltm[0][:], scalar=-1.0, in1=bc(k.identb[:].unsqueeze(1), [128, 4, 128]),
                op0=ALU.mult, op1=ALU.add), reads=[K_("Ltm", 0)], writes=[K_("Z", 0)])
            yield
            for hl in range(4):
                P.add("pe", lambda e, hl=hl: e.matmul(psX[:, hl, :], Yb[:, hl, :], Xb[0][:, hl, :], start=True, stop=True),
                      reads=[K_("X", 0), K_("Y")], writes=[K_("psX")])
            for hl in range(4):
                P.add("pe", lambda e, hl=hl: e.matmul(psZ[:, hl, :], Xb[0][:, hl, :], Zb[0][:, hl, :], start=True, stop=True),
                      reads=[K_("X", 0), K_("Z", 0)], writes=[K_("psZ")])
            P.add("act", lambda e: e.activation(out=Xb[1][:], in_=psX[:], func=AF.Copy), reads=[K_("psX")], writes=[K_("X", 1)])
            P.add("dve", lambda e: e.tensor_tensor(out=Zb[1][:], in0=psZ[:], in1=Zb[0][:], op=ALU.add),
                  reads=[K_("psZ"), K_("Z", 0)], writes=[K_("Z", 1)])
            yield
            for hl in range(4):
                P.add("pe", lambda e, hl=hl: e.matmul(psZ[:, hl, :], Xb[1][:, hl, :], Zb[1][:, hl, :], start=True, stop=True),
                      reads=[K_("X", 1), K_("Z", 1)], writes=[K_("psZ")])
            P.add("dve", lambda e: e.tensor_tensor(out=Zb[0][:], in0=psZ[:], in1=Zb[1][:], op=ALU.add),
                  reads=[K_("psZ"), K_("Z", 1)], writes=[K_("Z", 0)])
            yield
            for hl in range(4):
                P.add("pe", lambda e, hl=hl: e.transpose(out=psT[:, hl, :], in_=Zb[0][:, hl, :], identity=k.identb[:]),
                      reads=[K_("Z", 0)], writes=[K_("psT")])
            P.add("act", lambda e: e.activation(out=Ab[0][:], in_=psT[:, 0:4, :], func=AF.Copy), reads=[K_("psT")], writes=[K_("A", 0)])
            yield
            Bcur, Bkey = Zb[0], K_("Z", 0)
            Acur, Akey = Ab[0], K_("A", 0)
            for q in range(1, 5):
                last = (q == 4)
                nb_ = q % 2
                if not last:
                    for hl in range(4):
                        P.add("pe", lambda e, hl=hl, q=q, Acur=Acur: e.matmul(psX[:, hl, :], Ltm[q][:, hl, :], Acur[:, hl, :], start=True, stop=True),
                              reads=[K_("Ltm", q), Akey], writes=[K_("psX")])
                for hl in range(4):
                    P.add("pe", lambda e, hl=hl, q=q, Bcur=Bcur: e.matmul(psY[:, hl, :], Lm[q][:, hl, :], Bcur[:, hl, :], start=True, stop=True),
                          reads=[K_("Lm", q), Bkey], writes=[K_("psY")])
                if not last:
                    P.add("act", lambda e: e.activation(out=T1a[:], in_=psX[:], func=AF.Copy), reads=[K_("psX")], writes=[K_("T1a")])
                P.add("act", lambda e: e.activation(out=T1b[:], in_=psY[:], func=AF.Copy), reads=[K_("psY")], writes=[K_("T1b")])
                yield
                if not last:
                    for hl in range(4):
                        P.add("pe", lambda e, hl=hl, Bcur=Bcur: e.matmul(psX[:, hl, :], Bcur[:, hl, :], T1a[:, hl, :], start=True, stop=True),
                              reads=[Bkey, K_("T1a")], writes=[K_("psX")])
                for hl in range(4):
                    P.add("pe", lambda e, hl=hl, Acur=Acur: e.matmul(psZ[:, hl, :], Acur[:, hl, :], T1b[:, hl, :], start=True, stop=True),
                          reads=[Akey, K_("T1b")], writes=[K_("psZ")])
                if not last:
                    P.add("dve", lambda e, nb_=nb_, Acur=Acur: e.tensor_tensor(out=Ab[nb_][:], in0=Acur[:], in1=psX[:], op=ALU.subtract),
                          reads=[K_("psX"), Akey], writes=[K_("A", nb_)])
                Bnew = Bb[nb_] if not last else Zb[0]
                Bnk = K_("B", nb_) if not last else K_("Z", 0)
                P.add("dve", lambda e, Bnew=Bnew, Bcur=Bcur: e.tensor_tensor(out=Bnew[:], in0=Bcur[:], in1=psZ[:], op=ALU.subtract),
                      reads=[K_("psZ"), Bkey], writes=[Bnk])
                Bcur, Bkey = Bnew, Bnk
                if not last:
                    Acur, Akey = Ab[nb_], K_("A", nb_)
                yield
            ZF = Zb[0]
            for hl in range(4):
                P.add("pe", lambda e, hl=hl: e.matmul(psX[:, hl, :], ZF[:, hl, :], vb[:, hl, :], start=True, stop=True),
                      reads=[K_("Z", 0)] + [K_("vb", q_) for q_ in range(4)], writes=[K_("psX")])
            for hl in range(4):
                P.add("pe", lambda e, hl=hl: e.matmul(psY[:, hl, :], kbg[:, hl, :], ZF[:, hl, :], start=True, stop=True),
                      reads=[K_("Z", 0)] + [K_("kbg", q_) for q_ in range(4)], writes=[K_("psY")])
            P.add("act", lambda e: e.activation(out=Uo, in_=psX[:], func=AF.Copy), reads=[K_("psX")], writes=[K2("Uo")])
            P.add("act", lambda e: e.activation(out=Wo, in_=psY[:], func=AF.Copy), reads=[K_("psY")], writes=[K2("Wo")])
            u = (c * 2 + d) * 8 + hg
            P.add("sp", lambda e: e.dma_start(out=k.DPH[u], in_=PH[:]),
                  reads=[K2("Uo"), K2("Wo"), K2("Qo")] + [K2("kd", q_) for q_ in range(4)], writes=[("DPH", u)], dma=True)
            yield

        def load_chunk(c):
            cb = c % 2
            tsl = slice(c * 128, (c + 1) * 128)
            P.add("sp", lambda e: e.dma_start(out=kTc[cb][:], in_=k.DKT.rearrange("h p t -> p h t")[:, :, tsl]), writes=[("kTc", cb)], dma=True)
            P.add("sp", lambda e: e.dma_start(out=qTc[cb][:], in_=k.DQT.rearrange("h p t -> p h t")[:, :, tsl]), writes=[("qTc", cb)], dma=True)
            P.add("sp", lambda e: e.dma_start(out=ktm[cb][:], in_=k.DKTM[c].rearrange("p (h d) -> p h d", d=128)), writes=[("ktm", cb)], dma=True)
            P.add("sp", lambda e: e.dma_start(out=vtm[cb][:], in_=k.DVTM[c].rearrange("p (h d) -> p h d", d=128)), writes=[("vtm", cb)], dma=True)

        for c in range(18):
            load_chunk(c)
            units = [(d, hg) for d in range(2) for hg in range(8)]
            for p0 in range(0, len(units), NL):
                gens = [unit(ln, c, units[p0 + ln][0], units[p0 + ln][1], c % 2) for ln in range(NL)]
                alive = list(gens)
                while alive:
                    nxt = []
                    for g in alive:
                        try:
                            next(g)
                            nxt.append(g)
                        except StopIteration:
                            pass
                    alive = nxt


def dn_scan(k, l):
    P = k.P
    order = [list(range(18)), [1, 0] + list(range(17, 1, -1))]
    with contextlib.ExitStack() as st:
        Sf = P.sb([128, 64, 128], F32, st)
        Sb = P.sb([128, 64, 128], BF16, st)
        NB = 3
        ph = [P.sb([128, 4, 4, 128], BF16, st) for _ in range(NB)]
        qT = [P.sb([128, 2, 128], BF16, st) for _ in range(NB)]
        vnew = [P.sb([128, 4, 128], BF16, st) for _ in range(2)]
        o1 = [P.sb([128, 4, 128], F32, st) for _ in range(NB)]
        ob_ = [P.sb([128, 4, 128], F32, st) for _ in range(NB)]
        psP = [P.ps([128, 4, 128], F32, st) for _ in range(2)]
        psO1 = [P.ps([128, 4, 128], F32, st) for _ in range(2)]
        psO2 = [P.ps([128, 4, 128], F32, st) for _ in range(2)]
        psS = [P.ps([128, 4, 128], F32, st) for _ in range(2)]
        P.add("pool", lambda e: e.memset(Sf[:], 0.0), writes=[("Sf", hh) for hh in range(16)])
        P.add("pool", lambda e: e.memset(Sb[:], 0.0), writes=[("Sb", hh) for hh in range(16)])
        P.barrier()
        units = [(s, d, hg) for s in range(18) for d in range(2) for hg in range(8)]

        def stage1(idx):
            s, d, hg = units[idx]
            c = order[d][s]
            u = (c * 2 + d) * 8 + hg
            nb = idx % NB; pb = idx % 2
            g16 = d * 8 + hg
            P.add("sp", lambda e: e.dma_start(out=ph[nb][:], in_=k.DPH[u]), writes=[("ph", nb)], dma=True)
            P.add("sp", lambda e: e.dma_start(out=qT[nb][:], in_=k.DQT[hg * 2:hg * 2 + 2, :, c * 128:(c + 1) * 128].rearrange("h p t -> p h t")),
                  writes=[("qT", nb)], dma=True)
            for hl in range(4):
                hs = d * 32 + hg * 4 + hl
                P.add("pe", lambda e, hl=hl, hs=hs: e.matmul(psP[pb][:, hl, :], ph[nb][:, 1, hl, :], Sb[:, hs, :], start=True, stop=True),
                      reads=[("ph", nb), ("Sb", g16)], writes=[("psP", pb)])
            for hl in range(4):
                hs = d * 32 + hg * 4 + hl
                P.add("pe", lambda e, hl=hl, hs=hs: e.matmul(psO1[pb][:, hl, :], qT[nb][:, hl // 2, :], Sb[:, hs, :], start=True, stop=True),
                      reads=[("qT", nb), ("Sb", g16)], writes=[("psO1", pb)])
            P.add("dve", lambda e: e.tensor_tensor(out=vnew[pb][:], in0=ph[nb][:, 0, :, :], in1=psP[pb][:], op=ALU.subtract),
                  reads=[("ph", nb), ("psP", pb)], writes=[("vnew", pb)])
            P.add("act", lambda e: e.activation(out=o1[nb][:], in_=psO1[pb][:], func=AF.Copy), reads=[("psO1", pb)], writes=[("o1", nb)])

        def stage2(idx):
            s, d, hg = units[idx]
            c = order[d][s]
            nb = idx % NB; pb = idx % 2
            g16 = d * 8 + hg
            h0 = d * 32 + hg * 4
            for hl in range(4):
                P.add("pe", lambda e, hl=hl: e.matmul(psO2[pb][:, hl, :], ph[nb][:, 2, hl, :], vnew[pb][:, hl, :], start=True, stop=True),
                      reads=[("ph", nb), ("vnew", pb)], writes=[("psO2", pb)])
            for hl in range(4):
                P.add("pe", lambda e, hl=hl: e.matmul(psS[pb][:, hl, :], ph[nb][:, 3, hl, :], vnew[pb][:, hl, :], start=True, stop=True),
                      reads=[("ph", nb), ("vnew", pb)], writes=[("psS", pb)])
            for hl in range(4):
                P.add("dve", lambda e, hl=hl: e.scalar_tensor_tensor(
                    out=ob_[nb][:, hl, :], in0=o1[nb][:, hl, :], scalar=k.EG[:, c, h0 + hl:h0 + hl + 1], in1=psO2[pb][:, hl, :],
                    op0=ALU.mult, op1=ALU.add), reads=[("psO2", pb), ("o1", nb)], writes=[("ob", nb, hl)])
            for hl in range(4):
                P.add("dve", lambda e, hl=hl: e.scalar_tensor_tensor(
                    out=Sf[:, h0 + hl, :], in0=Sf[:, h0 + hl, :], scalar=k.EL[:, c, h0 + hl:h0 + hl + 1], in1=psS[pb][:, hl, :],
                    op0=ALU.mult, op1=ALU.add), reads=[("psS", pb), ("Sf", g16, hl)], writes=[("Sf", g16, hl)])
            P.add("act", lambda e: e.activation(out=Sb[:, h0:h0 + 4, :], in_=Sf[:, h0:h0 + 4, :], func=AF.Copy),
                  reads=[("Sf", g16, hl) for hl in range(4)], writes=[("Sb", g16)])
            dst = k.DO[d, c, :, hg * 512:(hg + 1) * 512]
            P.add("sp", lambda e: e.dma_start(out=dst, in_=ob_[nb][:].rearrange("p h j -> p (h j)")),
                  reads=[("ob", nb, hl) for hl in range(4)], writes=[("DO", d, c, hg)], dma=True)

        for idx in range(len(units) + 1):
            if idx < len(units):
                stage1(idx)
            if idx >= 1:
                stage2(idx - 1)


def dn_outnorm(k, l):
    P = k.P
    i = l // 2
    ng = VOFF["dngrep"][0] + i * 128
    with contextlib.ExitStack() as st:
        of = [P.sb([128, 32, 128], F32, st) for _ in range(2)]
        obk = [P.sb([128, 32, 128], F32, st) for _ in range(2)]
        zz = [P.sb([128, 32, 128], BF16, st) for _ in range(2)]
        sq = P.sb([128, 32, 128], F32, st)
        ss = P.sb([128, 32], F32, st)
        y = [P.sb([128, 32, 128], BF16, st) for _ in range(2)]
        yT = [P.sb([128, 32, 128], BF16, st) for _ in range(2)]
        pst = [P.ps([128, 8, 128], BF16, st) for _ in range(2)]
        npt = 0
        for c in range(18):
            b = c % 2
            P.add("sp", lambda e, b=b, c=c: e.dma_start(out=of[b][:], in_=k.DO[0, c].rearrange("p (h j) -> p h j", j=128)), writes=[("of", b)], dma=True)
            P.add("sp", lambda e, b=b, c=c: e.dma_start(out=obk[b][:], in_=k.DO[1, c].rearrange("p (h j) -> p h j", j=128)), writes=[("obk", b)], dma=True)
            P.add("sp", lambda e, b=b, c=c: e.dma_start(out=zz[b][:], in_=k.DZ[c].rearrange("p (h j) -> p h j", j=128)), writes=[("zz", b)], dma=True)
            P.add("dve", lambda e, b=b: e.tensor_tensor(out=of[b][:], in0=of[b][:], in1=obk[b][:], op=ALU.add),
                  reads=[("of", b), ("obk", b)], writes=[("of", b)])
            P.add("act", lambda e, b=b: e.activation(out=sq[:], in_=of[b][:], func=AF.Square), reads=[("of", b)], writes=["sq"])
            P.add("dve", lambda e: e.tensor_reduce(out=ss[:], in_=sq[:], axis=AX.X, op=ALU.add), reads=["sq"], writes=["ss"])
            P.add("act", lambda e: e.activation(out=ss[:], in_=ss[:], func=AF.Sqrt, bias=EPS, scale=1.0 / 128), reads=["ss"], writes=["ss"])
            P.add("dve", lambda e: e.reciprocal(ss[:], ss[:]), reads=["ss"], writes=["ss"])
            P.add("dve", lambda e, b=b: e.tensor_tensor(out=of[b][:], in0=of[b][:], in1=bc(ss[:].unsqueeze(2), [128, 32, 128]), op=ALU.mult),
                  reads=[("of", b), "ss"], writes=[("of", b)])
            P.add("dve", lambda e, b=b: e.tensor_tensor(out=of[b][:], in0=of[b][:], in1=bc(k.vecs[:, ng:ng + 128].unsqueeze(1), [128, 32, 128]), op=ALU.mult),
                  reads=[("of", b)], writes=[("of", b)])
            P.add("dve", lambda e, b=b: e.tensor_tensor(out=y[b][:], in0=of[b][:], in1=zz[b][:], op=ALU.mult),
                  reads=[("of", b), ("zz", b)], writes=[("y", b)])
            for h8 in range(4):
                tb = npt % 2; npt += 1
                for hh in range(8):
                    P.add("pe", lambda e, tb=tb, hh=hh, h8=h8, b=b: e.transpose(out=pst[tb][:, hh, :], in_=y[b][:, h8 * 8 + hh, :], identity=k.identb[:]),
                          reads=[("y", b)], writes=[("pst", tb)])
                P.add("act", lambda e, tb=tb, h8=h8, b=b: e.activation(out=yT[b][:, h8 * 8:(h8 + 1) * 8, :], in_=pst[tb][:], func=AF.Copy),
                      reads=[("pst", tb)], writes=[("yT", b, h8)])
            P.add("sp", lambda e, b=b, c=c: e.dma_start(out=k.DYT[:, :, c * 128:(c + 1) * 128].rearrange("h p t -> p h t"), in_=yT[b][:]),
                  reads=[("yT", b, h8) for h8 in range(4)], writes=[("DYT", c)], dma=True)


def dn_outproj(k, l):
    P = k.P
    i = l // 2
    hsrc = k.hT.rearrange("(c p) t -> p c t", p=128)
    HALF = NT // 2
    SUB = 384
    with contextlib.ExitStack() as st:
        cat = P.sb([128, 32, HALF], BF16, st)
        wb = [P.sb([128, 32, 256], BF16, st) for _ in range(2)]
        hrow = [P.sb([128, HALF], F32, st) for _ in range(2)]
        pso = [P.ps([128, 512], F32, st) for _ in range(4)]
        ysrc = k.DYT.rearrange("h p t -> p h t")
        nps = 0
        for half in range(2):
            h0 = half * HALF
            for q in range(4):
                P.add("sp", lambda e, q=q, h0=h0: e.dma_start(out=cat[:, q * 8:(q + 1) * 8, :], in_=ysrc[:, q * 8:(q + 1) * 8, h0:h0 + HALF]),
                      writes=[("cat", q)], dma=True)
            for blk in range(8):
                b = blk % 2
                for q in range(2):
                    P.add("pool", lambda e, b=b, blk=blk, q=q: e.dma_start(
                        out=wb[b][:, q * 16:(q + 1) * 16, :], in_=k.dn_w_out[i, blk][:, q * 16:(q + 1) * 16, :]),
                        writes=[("wb", b, q)], dma=True)
                for s in range(2):
                    c = blk * 2 + s
                    hb = c % 2
                    P.add("sp", lambda e, hb=hb, c=c, h0=h0: e.dma_start(out=hrow[hb][:], in_=hsrc[:, c, h0:h0 + HALF]),
                          reads=[("hT", c)], writes=[("hrow", hb)], dma=True)
                    for sub in range(HALF // SUB):
                        u0 = sub * SUB
                        pb = nps % 4; nps += 1
                        for kc in range(32):
                            P.add("pe", lambda e, b=b, s=s, pb=pb, kc=kc, u0=u0: e.matmul(
                                pso[pb][:, :SUB], wb[b][:, kc, s * 128:(s + 1) * 128], cat[:, kc, u0:u0 + SUB],
                                start=(kc == 0), stop=(kc == 31)),
                                reads=[("wb", b, kc // 16), ("cat", kc // 8)], writes=[("pso", pb)])
                        for (g0, gn, sidx) in segs(h0 + u0, SUB):
                            r0 = g0 - h0
                            P.add("dve", lambda e, hb=hb, pb=pb, r0=r0, gn=gn, u0=u0, c=c, sidx=sidx: e.scalar_tensor_tensor(
                                out=hrow[hb][:, r0:r0 + gn], in0=pso[pb][:, r0 - u0:r0 - u0 + gn], scalar=modv(k, 2, c, sidx),
                                in1=hrow[hb][:, r0:r0 + gn], op0=ALU.mult, op1=ALU.add),
                                reads=[("pso", pb), ("hrow", hb), "mod"], writes=[("hrow", hb)])
                    P.add("sp", lambda e, hb=hb, c=c, h0=h0: e.dma_start(out=hsrc[:, c, h0:h0 + HALF], in_=hrow[hb][:]),
                          reads=[("hrow", hb)], writes=[("hT", c)], dma=True)


FULL_PLAN = [(l, ("mix", "ffn")) for l in range(DEPTH)]


def tile_w(W, cols=None, width=256):
    if cols is not None:
        W = W[:, cols]
    K_, N_ = W.shape
    return np.ascontiguousarray(W.reshape(K_ // 128, 128, N_ // width, width).transpose(2, 1, 0, 3))


def tile_weights(inp):
    out = {}
    out["w_mod"] = np.stack([tile_w(inp["w_mod"][l]) for l in range(DEPTH)])
    up_cols = np.concatenate([np.concatenate([np.arange(j * 128, (j + 1) * 128), DFF + np.arange(j * 128, (j + 1) * 128)]) for j in range(NFF)])
    out["ffn_w_up"] = np.stack([tile_w(inp["ffn_w_up"][l], up_cols) for l in range(DEPTH)])
    out["ffn_w_down"] = np.stack([tile_w(inp["ffn_w_down"][l]) for l in range(DEPTH)])
    ev_cols = np.concatenate([np.arange(0, 1536)] + [np.concatenate([1536 + np.arange(j * 128, (j + 1) * 128), 2560 + np.arange(j * 128, (j + 1) * 128)])
                                                      for j in range(8)])
    out["even_w_in"] = np.stack([tile_w(inp["even_w_in"][i], ev_cols) for i in range(2)])
    out["even_w_out"] = np.stack([tile_w(inp["even_w_out"][i]) for i in range(2)])
    out["dn_w_in"] = np.stack([tile_w(inp["dn_w_in"][i][:, :12288]) for i in range(2)])
    out["dn_w_ba"] = np.stack([tile_w(inp["dn_w_in"][i][:, 12288:12416], width=128)[0] for i in range(2)])
    out["dn_w_out"] = np.stack([tile_w(inp["dn_w_out"][i]) for i in range(2)])
    return out


def make_core_inputs(inp, b, consts, tiled=None):
    h_in = np.concatenate([inp["ctx"][b].T, inp["x"][b].T], axis=1)
    return {
        "h_in": np.ascontiguousarray(h_in, dtype=np.float32),
        "vecs": pack_vecs(inp, b),
        "consts": consts[0], "rope": consts[1],
        **(tiled if tiled is not None else {}),
    }


def kernel(**inputs):
    inp = {k_: np.asarray(v) for k_, v in inputs.items()}
    consts = make_consts()
    nc = build_program(FULL_PLAN)
    tiled = tile_weights(inp)
    in_maps = [make_core_inputs(inp, c % 4, consts, tiled) for c in range(8)]
    res = run_bass_kernel_spmd(nc, in_maps, core_ids=list(range(8)))
    out = np.stack([res.results[b]["hT"][:, LCTX:].T for b in range(4)], axis=0)
    return np.ascontiguousarray(out.astype(np.float32))
```

```python
import contextlib
import numpy as np
import concourse.bass as bass
import concourse.mybir as mybir
from concourse.bass_utils import run_bass_kernel_spmd

F32 = mybir.dt.float32
BF16 = mybir.dt.bfloat16
AF = mybir.ActivationFunctionType
ALU = mybir.AluOpType
AX = mybir.AxisListType

D = 2048
NCH = 16
LCTX = 256
TLAT = 2048
NT = LCTX + TLAT
DEPTH = 4
DFF = 5632
NFF = DFF // 128
EPS = 1e-6
TILES = [(0, 256), (256, 512), (768, 512), (1280, 512), (1792, 512)]
EVEN_IN_W = 3584
DN_IN_W = 12416

ENGS = ("pe", "act", "dve", "pool", "sp")
NDSEM = 6


class Op:
    __slots__ = ("eng", "fn", "idx", "dma", "slot", "val", "deps", "needed", "cnt", "waits", "clock")


class Prog:
    def __init__(self, nc, same_engine_sync=True):
        self.nc = nc
        self.ops = {e: [] for e in ENGS}
        self.order = []
        self.last_w = {}
        self.readers = {}
        self.same_engine_sync = same_engine_sync
        self.dma_rr = {e: 0 for e in ENGS}
        self.dma_val = {}
        self.stack = contextlib.ExitStack()
        self.sems = {}
        self.dsems = {}
        self._n = 0

    def sb(self, shape, dtype, stack=None):
        self._n += 1
        return (stack or self.stack).enter_context(self.nc.sbuf_tensor(f"sb{self._n}", list(shape), dtype))

    def ps(self, shape, dtype, stack=None):
        self._n += 1
        return (stack or self.stack).enter_context(self.nc.psum_tensor(f"ps{self._n}", list(shape), dtype))

    def add(self, eng, fn, reads=(), writes=(), dma=False):
        op = Op()
        op.eng = eng; op.fn = fn; op.dma = dma; op.needed = False; op.waits = None
        op.idx = len(self.ops[eng]); op.slot = None; op.val = None
        deps = set()
        writes = list(writes)
        if dma:
            s = self.dma_rr[eng] % NDSEM
            self.dma_rr[eng] += 1
            op.slot = (eng, s)
            writes.append(("dsem", eng, s))
            self.dma_val[op.slot] = self.dma_val.get(op.slot, 0) + 16
            op.val = self.dma_val[op.slot]
        for r in reads:
            w = self.last_w.get(r)
            if w is not None:
                deps.add(w)
        for r in writes:
            w = self.last_w.get(r)
            if w is not None:
                deps.add(w)
            for rd in self.readers.get(r, ()):
                deps.add(rd)
        for r in writes:
            self.last_w[r] = op
            self.readers[r] = []
        for r in reads:
            self.readers.setdefault(r, []).append(op)
        deps.discard(op)
        op.deps = deps
        self.ops[eng].append(op)
        self.order.append(op)
        return op

    def barrier(self):
        lasts = []
        for e in ENGS:
            for o in reversed(self.ops[e]):
                if o.fn is not None and not o.dma:
                    lasts.append(o)
                    break
        latest = {}
        for e in ENGS:
            for o in reversed(self.ops[e]):
                if o.dma and o.slot not in latest:
                    latest[o.slot] = o
        for e in ENGS:
            op = Op()
            op.eng = e; op.fn = None; op.dma = False; op.needed = False; op.waits = None
            op.idx = len(self.ops[e]); op.slot = None; op.val = None
            op.deps = set(o for o in lasts if o.eng != e and o.fn is not None and not o.dma) | set(latest.values())
            self.ops[e].append(op)
            self.order.append(op)
        self.last_w.clear(); self.readers.clear()

    def emit(self):
        nc = self.nc
        seen = {e: {} for e in ENGS}
        for op in self.order:
            s = seen[op.eng]
            waits = []
            for d in sorted(op.deps, key=lambda o: (o.eng, o.idx)):
                if d.dma:
                    key = ("d",) + d.slot
                    if s.get(key, 0) >= d.val:
                        continue
                    waits.append(d)
                    s[key] = d.val
                else:
                    if d.fn is None:
                        continue
                    if d.eng == op.eng and (d.eng == "pe" or not self.same_engine_sync or op.fn is None):
                        continue
                    if s.get(d.eng, -1) >= d.idx:
                        continue
                    waits.append(d)
                    s[d.eng] = d.idx
                for k, v in d.clock.items():
                    if k == op.eng:
                        continue
                    if s.get(k, -1) < v:
                        s[k] = v
            for d in waits:
                d.needed = True
            op.waits = waits
            op.clock = dict(s)
        for e in ENGS:
            c = 0
            for op in self.ops[e]:
                if op.needed and not op.dma:
                    c += 1
                op.cnt = c
        st = self.stack
        for e in ENGS:
            self.sems[e] = st.enter_context(nc.semaphore(f"s_{e}"))
        for key in self.dma_val:
            self.dsems[key] = st.enter_context(nc.semaphore(f"d_{key[0]}{key[1]}"))
        block = st.enter_context(nc.Block())
        handles = {"pe": block.tensor, "act": block.scalar, "dve": block.vector, "pool": block.gpsimd, "sp": block.sync}

        def mk(e):
            def body(h):
                for op in self.ops[e]:
                    for d in op.waits:
                        if d.dma:
                            h.wait_ge(self.dsems[d.slot], d.val)
                        else:
                            h.wait_ge(self.sems[d.eng], d.cnt)
                    if op.fn is None:
                        continue
                    ins = op.fn(h)
                    if op.dma:
                        ins.then_inc(self.dsems[op.slot], 16)
                    elif op.needed:
                        ins.then_inc(self.sems[e], 1)
            return body
        for e in ENGS:
            handles[e](mk(e))

    def finish(self):
        self.barrier()
        self.emit()
        self.stack.close()


def _layout(items):
    off = {}
    o = 0
    for name, w in items:
        off[name] = (o, w)
        o += w
    return off, o


VEC_ITEMS = [
    ("bmod", 4 * 96), ("n1g", 64), ("n2g", 64), ("fcw", 4 * 3 * NFF), ("fcb", 4 * NFF),
    ("qg", 2), ("kg", 2), ("sink", 16), ("cdw", 2 * 31 * 8), ("cdb", 16), ("lng", 16), ("lnb", 16),
    ("dcw", 2 * 5 * 64), ("alog", 128), ("dtb", 128), ("dngrep", 256), ("c", 16), ("cctx", 16),
]
VOFF, NV = _layout(VEC_ITEMS)

CONST_ITEMS = [
    ("ident", 128), ("ropeT", 128), ("band", 384),
    ("m_le", 128), ("m_ge", 128), ("m_lt", 128), ("m_gt", 128),
    ("mblk", 5 * 128),
]
COFF, NCONST = _layout(CONST_ITEMS)


def fm(v):
    return np.ascontiguousarray(v.reshape(-1, 128).T)


def pack_vecs(inp, b):
    V = np.zeros((128, NV), np.float32)

    def put(name, arr):
        o, w = VOFF[name]
        assert arr.shape == (128, w), (name, arr.shape, w)
        V[:, o:o + w] = arr
    put("bmod", np.concatenate([fm(inp["b_mod"][l]) for l in range(4)], axis=1))
    put("n1g", np.concatenate([fm(inp["norm1_g"][l]) for l in range(4)], axis=1))
    put("n2g", np.concatenate([fm(inp["norm2_g"][l]) for l in range(4)], axis=1))
    put("fcw", np.concatenate([fm(inp["ffn_conv_w"][l, k]) for l in range(4) for k in range(3)], axis=1))
    put("fcb", np.concatenate([fm(inp["ffn_conv_b"][l]) for l in range(4)], axis=1))
    put("qg", inp["attn_q_norm_g"].T.copy())
    put("kg", inp["attn_k_norm_g"].T.copy())
    put("sink", np.broadcast_to(inp["attn_sink"].reshape(1, 16), (128, 16)).copy())
    put("cdw", np.concatenate([fm(inp["conv_dw_w"][i, k]) for i in range(2) for k in range(31)], axis=1))
    put("cdb", np.concatenate([fm(inp["conv_dw_b"][i]) for i in range(2)], axis=1))
    put("lng", np.concatenate([fm(inp["conv_ln_g"][i]) for i in range(2)], axis=1))
    put("lnb", np.concatenate([fm(inp["conv_ln_b"][i]) for i in range(2)], axis=1))
    put("dcw", np.concatenate([fm(inp["dn_conv_w"][i, k]) for i in range(2) for k in range(5)], axis=1))
    put("alog", np.broadcast_to(inp["dn_a_log"].reshape(1, 128), (128, 128)).copy())
    put("dtb", np.broadcast_to(inp["dn_dt_bias"].reshape(1, 128), (128, 128)).copy())
    put("dngrep", np.broadcast_to(inp["dn_norm_g"].reshape(1, 256), (128, 256)).copy())
    put("c", fm(inp["c"][b]))
    put("cctx", fm(inp["c_ctx"]))
    return V


def make_consts():
    C = np.zeros((128, NCONST), np.float32)

    def put(name, arr):
        o, w = COFF[name]
        assert arr.shape == (128, w), (name, arr.shape)
        C[:, o:o + w] = arr
    idx = np.arange(128)
    put("ident", np.eye(128, dtype=np.float32))
    R = np.zeros((128, 128), np.float32)
    for i in range(32):
        R[i, 32 + i] = -1.0
        R[32 + i, i] = 1.0
        R[64 + i, 96 + i] = -1.0
        R[96 + i, 64 + i] = 1.0
    put("ropeT", R.T.copy())
    kk = idx[:, None]; qq = idx[None, :]
    band = np.concatenate([(kk <= qq), np.ones((128, 128), bool), (qq <= kk)], axis=1).astype(np.float32)
    put("band", band)
    put("m_le", (kk <= qq).astype(np.float32))
    put("m_ge", (kk >= qq).astype(np.float32))
    put("m_lt", (kk < qq).astype(np.float32))
    put("m_gt", (kk > qq).astype(np.float32))
    mb = [(kk // 8 == qq // 8)]
    for bsz in (8, 16, 32, 64):
        mb.append((kk // (2 * bsz) == qq // (2 * bsz)) & (kk // bsz != qq // bsz))
    put("mblk", np.concatenate(mb, axis=1).astype(np.float32))
    GRID_W = 64
    rows = TLAT // GRID_W
    row = np.repeat(np.arange(rows, dtype=np.float32), GRID_W)
    col = np.tile(np.arange(GRID_W, dtype=np.float32), rows)
    half = 64
    inv_freq = (10000.0 ** (-np.arange(0, half, 2, dtype=np.float32) / half)).astype(np.float32)
    ang_r = row[:, None] * inv_freq
    ang_c = col[:, None] * inv_freq
    ang = np.concatenate([ang_r, ang_r, ang_c, ang_c], axis=-1)
    rope = np.concatenate([np.cos(ang).T, np.sin(ang).T], axis=1).astype(np.float32)
    return C, np.ascontiguousarray(rope)


class K:
    pass


def segs(t0, n):
    out = []
    if t0 < LCTX:
        e = min(t0 + n, LCTX)
        out.append((t0, e - t0, 1))
        if t0 + n > LCTX:
            out.append((LCTX, t0 + n - LCTX, 0))
    else:
        out.append((t0, n, 0))
    return out


def bc(ap, shape):
    return ap.broadcast_to(list(shape))


def stage_adaln(k, l):
    P = k.P
    with contextlib.ExitStack() as st:
        wb = [P.sb([128, 16, 256], BF16, st) for _ in range(2)]
        pm = P.ps([128, 96, 2], F32, st)
        for blk in range(48):
            b = blk % 2
            P.add("pool", lambda e, b=b, blk=blk: e.dma_start(out=wb[b][:], in_=k.w_mod[l, blk]),
                  writes=[("wb", b)], dma=True)
            for s in range(2):
                j = blk * 2 + s
                for kc in range(16):
                    P.add("pe", lambda e, b=b, s=s, j=j, kc=kc: e.matmul(
                        pm[:, j, :], wb[b][:, kc, s * 128:(s + 1) * 128], k.siluc[:, kc, :],
                        start=(kc == 0), stop=(kc == 15)),
                        reads=[("wb", b), "siluc"], writes=["pm"])
        bo = VOFF["bmod"][0] + l * 96
        P.add("dve", lambda e: e.tensor_tensor(out=k.mod[:], in0=pm[:], in1=bc(k.vecs[:, bo:bo + 96].unsqueeze(2), [128, 96, 2]),
                                               op=ALU.add), reads=["pm"], writes=["mod"])
        for (dst, m, gname) in ((k.A1, 1, "n1g"), (k.A2, 4, "n2g")):
            go = VOFF[gname][0] + l * 16
            P.add("dve", lambda e, dst=dst, m=m: e.tensor_scalar_add(out=dst[:], in0=k.mod[:, m * 16:(m + 1) * 16, :], scalar1=1.0),
                  reads=["mod"], writes=[("A", m)])
            P.add("dve", lambda e, dst=dst, go=go: e.tensor_tensor(
                out=dst[:], in0=dst[:], in1=bc(k.vecs[:, go:go + 16].unsqueeze(2), [128, 16, 2]), op=ALU.mult),
                reads=[("A", m)], writes=[("A", m)])
    P.barrier()


def modv(k, m, c, s):
    return k.mod[:, m * 16 + c, s:s + 1]


def stage_norm(k, which, nx, nbuf=2):
    P = k.P
    A = k.A1 if which == 1 else k.A2
    msh = 0 if which == 1 else 3
    hsrc = k.hT.rearrange("(c p) t -> p c t", p=128)
    with contextlib.ExitStack() as st:
        hs = [P.sb([128, 16, 512], F32, st) for _ in range(nbuf)]
        sq = P.sb([128, 16, 512], BF16, st)
        rstd = P.sb([128, 512], F32, st)
        pss = P.ps([128, 512], F32, st)
        for ti, (t0, n) in enumerate(TILES):
            b = ti % nbuf
            P.add("sp", lambda e, b=b, t0=t0, n=n: e.dma_start(out=hs[b][:, :, :n], in_=hsrc[:, :, t0:t0 + n]),
                  reads=["hT"], writes=[("hs", b)], dma=True)
            P.add("act", lambda e, b=b, n=n: e.activation(out=sq[:, :, :n], in_=hs[b][:, :, :n], func=AF.Square),
                  reads=[("hs", b)], writes=["sq"])
            for c in range(16):
                P.add("pe", lambda e, c=c, n=n: e.matmul(pss[:, :n], k.onesD[:], sq[:, c, :n], start=(c == 0), stop=(c == 15)),
                      reads=["sq"], writes=["pss"])
            P.add("act", lambda e, n=n: e.activation(out=rstd[:, :n], in_=pss[:, :n], func=AF.Sqrt, bias=EPS, scale=1.0),
                  reads=["pss"], writes=["rstd"])
            P.add("dve", lambda e, n=n: e.reciprocal(rstd[:, :n], rstd[:, :n]), reads=["rstd"], writes=["rstd"])
            P.add("dve", lambda e, b=b, n=n: e.tensor_tensor(out=hs[b][:, :, :n], in0=hs[b][:, :, :n],
                                                             in1=bc(rstd[:, :n].unsqueeze(1), [128, 16, n]), op=ALU.mult),
                  reads=["rstd", ("hs", b)], writes=[("hs", b)])
            s = 1 if t0 < LCTX else 0
            for c in range(16):
                P.add("act", lambda e, b=b, c=c, t0=t0, n=n, s=s: e.activation(
                    out=nx[:, c, t0:t0 + n], in_=hs[b][:, c, :n], func=AF.Identity,
                    bias=modv(k, msh, c, s), scale=A[:, c, s:s + 1]),
                    reads=[("hs", b), "mod", ("A", 1), ("A", 4)], writes=[("nx", ti, c)])


def stage_ffn(k, l):
    P = k.P
    with contextlib.ExitStack() as st0:
        nx = P.sb([128, 16, NT], BF16, st0)
        stage_norm(k, 2, nx)
        P.barrier()
        stage_ffn_up(k, l, nx)
        P.barrier()
    stage_ffn_down(k, l)


def stage_ffn_up(k, l, nx):
    P = k.P
    with contextlib.ExitStack() as st:
        wb = [P.sb([128, 16, 256], BF16, st) for _ in range(3)]
        G = [P.sb([128, NT], F32, st) for _ in range(2)]
        Vv = [P.sb([128, NT], BF16, st) for _ in range(2)]
        acc = [P.sb([128, NT], F32, st) for _ in range(2)]
        sg = [P.sb([128, NT], BF16, st) for _ in range(2)]
        hh = [P.sb([128, NT], BF16, st) for _ in range(2)]
        psg = [P.ps([128, 512], F32, st) for _ in range(2)]
        psv = [P.ps([128, 512], F32, st) for _ in range(2)]
        cw = VOFF["fcw"][0] + l * 3 * NFF
        cb = VOFF["fcb"][0] + l * NFF

        def make_post(j):
            b = j % 2
            Gk = [("G", b, ti) for ti in range(5)]
            Vk = [("V", b, ti) for ti in range(5)]
            w0 = k.vecs[:, cw + 0 * NFF + j:cw + 0 * NFF + j + 1]
            w1 = k.vecs[:, cw + 1 * NFF + j:cw + 1 * NFF + j + 1]
            w2 = k.vecs[:, cw + 2 * NFF + j:cw + 2 * NFF + j + 1]
            bb = k.vecs[:, cb + j:cb + j + 1]
            conv = []
            conv.append(lambda: P.add("dve", lambda e: e.tensor_scalar(out=acc[b][:], in0=G[b][:], scalar1=w1, scalar2=bb,
                                                                       op0=ALU.mult, op1=ALU.add), reads=Gk, writes=[("acc", b)]))
            for (s0, sn) in ((0, LCTX), (LCTX, TLAT)):
                conv.append(lambda s0=s0, sn=sn: P.add("dve", lambda e: e.scalar_tensor_tensor(
                    out=acc[b][:, s0 + 1:s0 + sn], in0=G[b][:, s0:s0 + sn - 1], scalar=w0, in1=acc[b][:, s0 + 1:s0 + sn],
                    op0=ALU.mult, op1=ALU.add), reads=Gk + [("acc", b)], writes=[("acc", b)]))
                conv.append(lambda s0=s0, sn=sn: P.add("dve", lambda e: e.scalar_tensor_tensor(
                    out=acc[b][:, s0:s0 + sn - 1], in0=G[b][:, s0 + 1:s0 + sn], scalar=w2, in1=acc[b][:, s0:s0 + sn - 1],
                    op0=ALU.mult, op1=ALU.add), reads=Gk + [("acc", b)], writes=[("acc", b)]))

            def tail():
                P.add("act", lambda e: e.activation(out=sg[b][:], in_=acc[b][:], func=AF.Silu), reads=[("acc", b)], writes=[("sg", b)])
                P.add("pool", lambda e: e.tensor_tensor(out=hh[b][:], in0=sg[b][:], in1=Vv[b][:], op=ALU.mult),
                      reads=[("sg", b)] + Vk, writes=[("hh", b)])
                P.add("sp", lambda e: e.dma_start(out=k.HF[j * 128:(j + 1) * 128, :], in_=hh[b][:]),
                      reads=[("hh", b)], writes=["HF"], dma=True)
            return conv, tail

        pending = None
        for j in range(NFF):
            b = j % 2
            wi = j % 3
            P.add("pool", lambda e, wi=wi, j=j: e.dma_start(out=wb[wi][:], in_=k.ffn_w_up[l, j]),
                  writes=[("wb", wi)], dma=True)
            conv_prev = list(pending[0]) if pending else []
            for ti, (t0, n) in enumerate(TILES):
                pb = ti % 2
                for kc in range(16):
                    P.add("pe", lambda e, wi=wi, pb=pb, kc=kc, t0=t0, n=n: e.matmul(
                        psg[pb][:, :n], wb[wi][:, kc, 0:128], nx[:, kc, t0:t0 + n], start=(kc == 0), stop=(kc == 15)),
                        reads=[("wb", wi)], writes=[("psg", pb)])
                for kc in range(16):
                    P.add("pe", lambda e, wi=wi, pb=pb, kc=kc, t0=t0, n=n: e.matmul(
                        psv[pb][:, :n], wb[wi][:, kc, 128:256], nx[:, kc, t0:t0 + n], start=(kc == 0), stop=(kc == 15)),
                        reads=[("wb", wi)], writes=[("psv", pb)])
                P.add("act", lambda e, b=b, pb=pb, t0=t0, n=n: e.activation(out=G[b][:, t0:t0 + n], in_=psg[pb][:, :n], func=AF.Copy),
                      reads=[("psg", pb)], writes=[("G", b, ti)])
                P.add("dve", lambda e, b=b, pb=pb, t0=t0, n=n: e.tensor_copy(Vv[b][:, t0:t0 + n], psv[pb][:, :n]),
                      reads=[("psv", pb)], writes=[("V", b, ti)])
                if conv_prev:
                    conv_prev.pop(0)()
            for c_ in conv_prev:
                c_()
            if pending:
                pending[1]()
            pending = make_post(j)
        for c_ in pending[0]:
            c_()
        pending[1]()


def stage_ffn_down(k, l):
    P = k.P
    HALF = NT // 2
    SUB = 384
    with contextlib.ExitStack() as st:
        hres = P.sb([128, NFF, HALF], BF16, st)
        wb = [P.sb([128, NFF, 256], BF16, st) for _ in range(2)]
        hrow = [P.sb([128, HALF], F32, st) for _ in range(2)]
        pso = [P.ps([128, 512], F32, st) for _ in range(4)]
        hfsrc = k.HF.rearrange("(j p) t -> p j t", p=128)
        hsrc = k.hT.rearrange("(c p) t -> p c t", p=128)
        nps = 0
        for half in range(2):
            h0 = half * HALF
            for q in range(4):
                P.add("sp", lambda e, q=q, h0=h0: e.dma_start(out=hres[:, q * 11:(q + 1) * 11, :], in_=hfsrc[:, q * 11:(q + 1) * 11, h0:h0 + HALF]),
                      reads=["HF"], writes=[("hres", q)], dma=True)
            for blk in range(8):
                b = blk % 2
                for q in range(2):
                    P.add("pool", lambda e, b=b, blk=blk, q=q: e.dma_start(
                        out=wb[b][:, q * 22:(q + 1) * 22, :], in_=k.ffn_w_down[l, blk][:, q * 22:(q + 1) * 22, :]),
                        writes=[("wb", b, q)], dma=True)
                for s in range(2):
                    c = blk * 2 + s
                    hb = c % 2
                    P.add("sp", lambda e, hb=hb, c=c, h0=h0: e.dma_start(out=hrow[hb][:], in_=hsrc[:, c, h0:h0 + HALF]),
                          reads=[("hT", c)], writes=[("hrow", hb)], dma=True)
                    for sub in range(HALF // SUB):
                        u0 = sub * SUB
                        pb = nps % 4
                        nps += 1
                        for kc in range(NFF):
                            P.add("pe", lambda e, b=b, s=s, pb=pb, kc=kc, u0=u0: e.matmul(
                                pso[pb][:, :SUB], wb[b][:, kc, s * 128:(s + 1) * 128], hres[:, kc, u0:u0 + SUB],
                                start=(kc == 0), stop=(kc == NFF - 1)),
                                reads=[("wb", b, kc // 22), ("hres", kc // 11)], writes=[("pso", pb)])
                        for (g0, gn, sidx) in segs(h0 + u0, SUB):
                            r0 = g0 - h0
                            P.add("dve", lambda e, hb=hb, pb=pb, r0=r0, gn=gn, u0=u0, c=c, sidx=sidx: e.scalar_tensor_tensor(
                                out=hrow[hb][:, r0:r0 + gn], in0=pso[pb][:, r0 - u0:r0 - u0 + gn], scalar=modv(k, 5, c, sidx),
                                in1=hrow[hb][:, r0:r0 + gn], op0=ALU.mult, op1=ALU.add),
                                reads=[("pso", pb), ("hrow", hb), "mod"], writes=[("hrow", hb)])
                    P.add("sp", lambda e, hb=hb, c=c, h0=h0: e.dma_start(out=hsrc[:, c, h0:h0 + HALF], in_=hrow[hb][:]),
                          reads=[("hrow", hb)], writes=[("hT", c)], dma=True)
    P.barrier()


DEBUG_OUT = ()


def build_program(plan, test_out_ctx=False):
    nc = bass.Bass("TRN2", target_bir_lowering=False)
    k = K()
    k.nc = nc
    P = Prog(nc)
    k.P = P
    def dt(name, shape, dtype, kind="Internal"):
        if name in DEBUG_OUT:
            kind = "ExternalOutput"
        return nc.dram_tensor(name, shape, dtype, kind=kind)
    k.h_in = dt("h_in", [D, NT], F32, kind="ExternalInput").ap()
    k.vecs_d = dt("vecs", [128, NV], F32, kind="ExternalInput").ap()
    k.consts_d = dt("consts", [128, NCONST], F32, kind="ExternalInput").ap()
    k.rope_d = dt("rope", [128, 2 * TLAT], F32, kind="ExternalInput").ap()
    k.w_mod = dt("w_mod", [DEPTH, 48, 128, 16, 256], F32, kind="ExternalInput").ap()
    k.ffn_w_up = dt("ffn_w_up", [DEPTH, NFF, 128, 16, 256], F32, kind="ExternalInput").ap()
    k.ffn_w_down = dt("ffn_w_down", [DEPTH, 8, 128, NFF, 256], F32, kind="ExternalInput").ap()
    k.even_w_in = dt("even_w_in", [2, 14, 128, 16, 256], F32, kind="ExternalInput").ap()
    k.even_w_out = dt("even_w_out", [2, 8, 128, 16, 256], F32, kind="ExternalInput").ap()
    k.dn_w_in = dt("dn_w_in", [2, 48, 128, 16, 256], F32, kind="ExternalInput").ap()
    k.dn_w_ba = dt("dn_w_ba", [2, 128, 16, 128], F32, kind="ExternalInput").ap()
    k.dn_w_out = dt("dn_w_out", [2, 8, 128, 32, 256], F32, kind="ExternalInput").ap()
    k.hT = dt("hT", [D, NT], F32, kind="ExternalOutput").ap()
    k.HF = dt("HF", [DFF, NT], BF16, kind="Internal").ap()
    k.QT = dt("QT", [8, 128, NT], BF16, kind="Internal").ap()
    k.KT = dt("KT", [2, 128, NT], BF16, kind="Internal").ap()
    k.VTM = dt("VTM", [18, 128, 256], BF16, kind="Internal").ap()
    k.CAT = dt("CAT", [16, 128, NT], BF16, kind="Internal").ap()
    k.DQT = dt("DQT", [16, 128, NT], BF16, kind="Internal").ap()
    k.DKT = dt("DKT", [16, 128, NT], BF16, kind="Internal").ap()
    k.DKTM = dt("DKTM", [18, 128, 2048], BF16, kind="Internal").ap()
    k.DVTM = dt("DVTM", [18, 128, 4096], BF16, kind="Internal").ap()
    k.DZ = dt("DZ", [18, 128, 4096], BF16, kind="Internal").ap()
    k.DPH = dt("DPH", [288, 128, 4, 4, 128], BF16, kind="Internal").ap()
    k.DO = dt("DO", [2, 18, 128, 4096], F32, kind="Internal").ap()
    k.DYT = dt("DYT", [32, 128, NT], BF16, kind="Internal").ap()
    k.DBG_G = dt("DBG_G", [5, 128, 18, 64], F32, kind="Internal").ap()

    k.vecs = P.sb([128, NV], F32)
    k.consts = P.sb([128, NCONST], F32)
    k.siluc = P.sb([128, 16, 2], BF16)
    k.mod = P.sb([128, 96, 2], F32)
    k.A1 = P.sb([128, 16, 2], F32)
    k.A2 = P.sb([128, 16, 2], F32)
    k.onesD = P.sb([128, 128], BF16)
    k.ones128 = P.sb([128, 128], BF16)
    k.ones1 = P.sb([128, 128], BF16)
    k.onesf128 = P.sb([128, 128], F32)
    k.bandb = P.sb([128, 384], BF16)
    k.identb = P.sb([128, 128], BF16)
    k.maskb = P.sb([128, 5, 128], BF16)

    P.add("sp", lambda e: e.dma_start(out=k.vecs[:], in_=k.vecs_d[:, :]), writes=["vecs"], dma=True)
    P.add("sp", lambda e: e.dma_start(out=k.consts[:], in_=k.consts_d[:, :]), writes=["consts"], dma=True)
    P.add("pool", lambda e: e.memset(k.onesD[:], 1.0 / D), writes=["onesD"])
    P.add("pool", lambda e: e.memset(k.ones128[:], 1.0 / 128), writes=["ones128"])
    P.add("pool", lambda e: e.memset(k.ones1[:], 1.0), writes=["ones1"])
    P.add("pool", lambda e: e.memset(k.onesf128[:], 1.0 / 128), writes=["onesf128"])
    P.add("dve", lambda e: e.tensor_copy(k.bandb[:], k.consts[:, COFF["band"][0]:COFF["band"][0] + 384]), reads=["consts"], writes=["bandb"])
    P.add("dve", lambda e: e.tensor_copy(k.maskb[:].rearrange("p q j -> p (q j)"), k.consts[:, COFF["mblk"][0]:COFF["mblk"][0] + 640]), reads=["consts"], writes=["maskb"])
    P.add("dve", lambda e: e.tensor_copy(k.identb[:], k.consts[:, COFF["ident"][0]:COFF["ident"][0] + 128]), reads=["consts"], writes=["identb"])
    co = VOFF["c"][0]
    P.add("act", lambda e: e.activation(out=k.siluc[:, :, 0], in_=k.vecs[:, co:co + 16], func=AF.Silu), reads=["vecs"], writes=["siluc"])
    co2 = VOFF["cctx"][0]
    P.add("act", lambda e: e.activation(out=k.siluc[:, :, 1], in_=k.vecs[:, co2:co2 + 16], func=AF.Silu), reads=["vecs", "siluc"], writes=["siluc"])
    with contextlib.ExitStack() as st:
        tmp = [P.sb([128, 4, NT], F32, st) for _ in range(2)]
        src = k.h_in.rearrange("(c p) t -> p c t", p=128)
        dst = k.hT.rearrange("(c p) t -> p c t", p=128)
        for q in range(4):
            b = q % 2
            P.add("sp", lambda e, b=b, q=q: e.dma_start(out=tmp[b][:], in_=src[:, q * 4:(q + 1) * 4, :]), writes=[("tmp", b)], dma=True)
            P.add("sp", lambda e, b=b, q=q: e.dma_start(out=dst[:, q * 4:(q + 1) * 4, :], in_=tmp[b][:]), reads=[("tmp", b)], writes=["hT"], dma=True)
    P.barrier()

    for (l, parts) in plan:
        stage_adaln(k, l)
        if "mix" in parts:
            if l % 2 == 0:
                stage_even(k, l)
            else:
                stage_dn(k, l)
        if "ffn" in parts:
            stage_ffn(k, l)
    P.finish()
    return nc


def stage_even(k, l):
    P = k.P
    with contextlib.ExitStack() as st0:
        nx = P.sb([128, 16, NT], BF16, st0)
        stage_norm(k, 1, nx)
        P.barrier()
        even_qkv(k, l, nx)
        P.barrier()
        even_glu(k, l, nx)
        P.barrier()
    even_attn(k, l)
    P.barrier()
    even_out(k, l)
    P.barrier()


def rsqrt_ops(P, out_ap, in_ap, reads, wkey):
    P.add("act", lambda e: e.activation(out=out_ap, in_=in_ap, func=AF.Sqrt, bias=EPS, scale=1.0), reads=reads, writes=[wkey])
    P.add("dve", lambda e: e.reciprocal(out_ap, out_ap), reads=[wkey], writes=[wkey])


def even_qkv(k, l, nx):
    P = k.P
    i = l // 2
    cos0 = 0; sin0 = TLAT; rp0 = COFF["ropeT"][0]
    with contextlib.ExitStack() as st:
        rope = P.sb([128, 2 * TLAT], F32, st)
        P.add("sp", lambda e: e.dma_start(out=rope[:], in_=k.rope_d[:, :]), writes=["rope"], dma=True)
        wb = [P.sb([128, 16, 256], BF16, st) for _ in range(2)]
        psa = [P.ps([128, 512], F32, st) for _ in range(2)]
        psb = P.ps([128, 512], F32, st)
        psc = P.ps([128, 512], F32, st)
        x0 = [P.sb([128, 512], F32, st) for _ in range(2)]
        sqb = P.sb([128, 512], BF16, st)
        rr = P.sb([128, 512], F32, st)
        qn = P.sb([128, 512], F32, st)
        t1 = P.sb([128, 512], F32, st)
        t2 = P.sb([128, 512], F32, st)
        qo = [P.sb([128, NT], BF16, st) for _ in range(2)]
        vt = P.sb([128, 18, 256], BF16, st)
        gs = P.sb([128, 2], F32, st)
        qg0 = VOFF["qg"][0] + i; kg0 = VOFF["kg"][0] + i
        P.add("act", lambda e: e.mul(gs[:, 0:1], k.vecs[:, qg0:qg0 + 1], 128 ** -0.5), writes=["gs"])
        P.add("act", lambda e: e.copy(gs[:, 1:2], k.vecs[:, kg0:kg0 + 1]), reads=["gs"], writes=["gs"])
        nblk = 0
        nchunk = 0
        npa = 0
        for blk in range(5):
            b = nblk % 2; nblk += 1
            P.add("pool", lambda e, b=b, blk=blk: e.dma_start(out=wb[b][:], in_=k.even_w_in[i, blk]),
                  writes=[("wb", b)], dma=True)
            for s in range(2):
                ch = blk * 2 + s
                isq = ch < 8
                qb = nchunk % 2; nchunk += 1
                gcol = gs[:, 0:1] if isq else gs[:, 1:2]
                for ti, (t0, n) in enumerate(TILES):
                    pb = npa % 2; npa += 1
                    for kc in range(16):
                        P.add("pe", lambda e, b=b, s=s, pb=pb, kc=kc, t0=t0, n=n: e.matmul(
                            psa[pb][:, :n], wb[b][:, kc, s * 128:(s + 1) * 128], nx[:, kc, t0:t0 + n], start=(kc == 0), stop=(kc == 15)),
                            reads=[("wb", b)], writes=[("psa", pb)])
                    P.add("act", lambda e, pb=pb, n=n: e.activation(out=x0[pb][:, :n], in_=psa[pb][:, :n], func=AF.Copy),
                          reads=[("psa", pb)], writes=[("x0", pb)])
                    P.add("act", lambda e, pb=pb, n=n: e.activation(out=sqb[:, :n], in_=psa[pb][:, :n], func=AF.Square),
                          reads=[("psa", pb)], writes=["sqb"])
                    P.add("pe", lambda e, n=n: e.matmul(psb[:, :n], k.ones128[:], sqb[:, :n], start=True, stop=True),
                          reads=["sqb"], writes=["psb"])
                    rsqrt_ops(P, rr[:, :n], psb[:, :n], ["psb"], "rr")
                    if t0 < LCTX:
                        P.add("dve", lambda e, pb=pb, qb=qb, n=n, t0=t0, gcol=gcol: e.scalar_tensor_tensor(
                            out=qo[qb][:, t0:t0 + n], in0=x0[pb][:, :n], scalar=gcol, in1=rr[:, :n], op0=ALU.mult, op1=ALU.mult),
                            reads=[("x0", pb), "rr", "gs"], writes=[("qo", qb, ti)])
                    else:
                        P.add("dve", lambda e, pb=pb, n=n, gcol=gcol: e.scalar_tensor_tensor(
                            out=qn[:, :n], in0=x0[pb][:, :n], scalar=gcol, in1=rr[:, :n], op0=ALU.mult, op1=ALU.mult),
                            reads=[("x0", pb), "rr", "gs"], writes=["qn"])
                        P.add("pe", lambda e, n=n: e.matmul(psc[:, :n], k.consts[:, rp0:rp0 + 128], qn[:, :n], start=True, stop=True),
                              reads=["qn"], writes=["psc"])
                        c0 = cos0 + t0 - LCTX; s0 = sin0 + t0 - LCTX
                        P.add("dve", lambda e, n=n, c0=c0: e.tensor_tensor(out=t1[:, :n], in0=qn[:, :n], in1=rope[:, c0:c0 + n], op=ALU.mult),
                              reads=["qn", "rope"], writes=["t1"])
                        P.add("dve", lambda e, n=n, s0=s0: e.tensor_tensor(out=t2[:, :n], in0=psc[:, :n], in1=rope[:, s0:s0 + n], op=ALU.mult),
                              reads=["psc", "rope"], writes=["t2"])
                        P.add("pool", lambda e, qb=qb, n=n, t0=t0: e.tensor_tensor(out=qo[qb][:, t0:t0 + n], in0=t1[:, :n], in1=t2[:, :n], op=ALU.add),
                              reads=["t1", "t2"], writes=[("qo", qb, ti)])
                dst = k.QT[ch] if isq else k.KT[ch - 8]
                P.add("sp", lambda e, qb=qb, dst=dst: e.dma_start(out=dst, in_=qo[qb][:]),
                      reads=[("qo", qb, ti) for ti in range(5)], writes=[("qkT", ch)], dma=True)
        b = nblk % 2; nblk += 1
        P.add("pool", lambda e, b=b: e.dma_start(out=wb[b][:], in_=k.even_w_in[i, 5]), writes=[("wb", b)], dma=True)
        for tt in range(18):
            pb = npa % 2; npa += 1
            for kc in range(16):
                P.add("pe", lambda e, b=b, pb=pb, kc=kc, tt=tt: e.matmul(
                    psa[pb][:, :256], nx[:, kc, tt * 128:(tt + 1) * 128], wb[b][:, kc, :], start=(kc == 0), stop=(kc == 15)),
                    reads=[("wb", b)], writes=[("psa", pb)])
            P.add("act", lambda e, pb=pb, tt=tt: e.activation(out=vt[:, tt, :], in_=psa[pb][:, :256], func=AF.Copy),
                  reads=[("psa", pb)], writes=[("vt", tt)])
        P.add("sp", lambda e: e.dma_start(out=k.VTM.rearrange("t p c -> p t c"), in_=vt[:]),
              reads=[("vt", tt) for tt in range(18)], writes=["VTM"], dma=True)


def even_glu(k, l, nx):
    P = k.P
    i = l // 2
    cdw = VOFF["cdw"][0] + i * 31 * 8
    cdb = VOFF["cdb"][0] + i * 8
    lng = VOFF["lng"][0] + i * 8
    lnb = VOFF["lnb"][0] + i * 8
    PADW = NT + 60

    def pos(t):
        return t + 15 if t < LCTX else t + 45
    with contextlib.ExitStack() as st:
        wb = [P.sb([128, 16, 256], BF16, st) for _ in range(2)]
        psa = [P.ps([128, 512], F32, st) for _ in range(2)]
        psb = [P.ps([128, 512], F32, st) for _ in range(2)]
        psc = [P.ps([128, 512], F32, st) for _ in range(2)]
        psm = P.ps([128, 512], F32, st)
        psq = P.ps([128, 512], F32, st)
        sgm = [P.sb([128, 512], F32, st) for _ in range(2)]
        Ub = [P.sb([128, PADW], BF16, st) for _ in range(2)]
        Dg = [P.sb([128, 31, 128], BF16, st) for _ in range(2)]
        accA = [P.sb([128, NT], F32, st) for _ in range(2)]
        hsq = P.sb([128, NT], F32, st)
        co = [P.sb([128, NT], BF16, st) for _ in range(2)]
        msb = P.sb([128, 512], F32, st)
        m2 = P.sb([128, 512], F32, st)
        var = P.sb([128, 512], F32, st)
        dd = P.sb([128, 512], F32, st)
        for b in range(2):
            P.add("pool", lambda e, b=b: e.memset(Ub[b][:], 0.0), writes=[("Ub", b, ti) for ti in range(5)] + [("Ubz", b)])

        def proj(j):
            b = j % 2
            P.add("pool", lambda e: e.dma_start(out=wb[b][:], in_=k.even_w_in[i, 6 + j]), writes=[("wb", b)], dma=True)
            for kk in range(31):
                w = k.vecs[:, cdw + kk * 8 + j:cdw + kk * 8 + j + 1]
                P.add("dve", lambda e, kk=kk, w=w: e.tensor_scalar(out=Dg[b][:, kk, :], in0=k.identb[:], scalar1=w, scalar2=None, op0=ALU.mult),
                      writes=[("Dg", b, kk)])
            for ti, (t0, n) in enumerate(TILES):
                pb = ti % 2
                for kc in range(16):
                    P.add("pe", lambda e, pb=pb, kc=kc, t0=t0, n=n: e.matmul(
                        psa[pb][:, :n], wb[b][:, kc, 0:128], nx[:, kc, t0:t0 + n], start=(kc == 0), stop=(kc == 15)),
                        reads=[("wb", b)], writes=[("psa", pb)])
                for kc in range(16):
                    P.add("pe", lambda e, pb=pb, kc=kc, t0=t0, n=n: e.matmul(
                        psb[pb][:, :n], wb[b][:, kc, 128:256], nx[:, kc, t0:t0 + n], start=(kc == 0), stop=(kc == 15)),
                        reads=[("wb", b)], writes=[("psb", pb)])
                P.add("act", lambda e, pb=pb, n=n: e.activation(out=sgm[pb][:, :n], in_=psb[pb][:, :n], func=AF.Sigmoid),
                      reads=[("psb", pb)], writes=[("sgm", pb)])
                p0 = pos(t0)
                P.add("dve", lambda e, pb=pb, p0=p0, n=n: e.tensor_tensor(out=Ub[b][:, p0:p0 + n], in0=psa[pb][:, :n], in1=sgm[pb][:, :n], op=ALU.mult),
                      reads=[("psa", pb), ("sgm", pb), ("Ubz", b)], writes=[("Ub", b, ti)])

        def conv(j):
            b = j % 2
            Uk = [("Ub", b, ti) for ti in range(5)] + [("Ubz", b)]
            bias = k.vecs[:, cdb + j:cdb + j + 1]
            for ti, (t0, n) in enumerate(TILES):
                pb = ti % 2
                p0 = pos(t0)
                for kk in range(31):
                    P.add("pe", lambda e, pb=pb, kk=kk, p0=p0, n=n: e.matmul(
                        psc[pb][:, :n], Dg[b][:, kk, :], Ub[b][:, p0 + kk - 15:p0 + kk - 15 + n], start=(kk == 0), stop=(kk == 30)),
                        reads=Uk + [("Dg", b, kk)], writes=[("psc", pb)])
                P.add("act", lambda e, pb=pb, t0=t0, n=n: e.activation(out=accA[b][:, t0:t0 + n], in_=psc[pb][:, :n], func=AF.Identity,
                                                                      bias=bias, scale=1.0),
                      reads=[("psc", pb)], writes=[("accA", b, ti)])

        def lnorm(j):
            b = j % 2
            Ak = [("accA", b, ti) for ti in range(5)]
            P.add("act", lambda e: e.activation(out=hsq[:], in_=accA[b][:], func=AF.Square), reads=Ak, writes=["hsq"])
            for ti, (t0, n) in enumerate(TILES):
                P.add("pe", lambda e, t0=t0, n=n: e.matmul(psm[:, :n], k.onesf128[:], accA[b][:, t0:t0 + n], start=True, stop=True),
                      reads=Ak, writes=["psm"])
                P.add("pe", lambda e, t0=t0, n=n: e.matmul(psq[:, :n], k.onesf128[:], hsq[:, t0:t0 + n], start=True, stop=True),
                      reads=["hsq"], writes=["psq"])
                P.add("act", lambda e, n=n: e.activation(out=msb[:, :n], in_=psm[:, :n], func=AF.Copy), reads=["psm"], writes=["msb"])
                P.add("act", lambda e, n=n: e.activation(out=m2[:, :n], in_=psm[:, :n], func=AF.Square), reads=["psm"], writes=["m2"])
                P.add("dve", lambda e, n=n: e.tensor_tensor(out=var[:, :n], in0=psq[:, :n], in1=m2[:, :n], op=ALU.subtract),
                      reads=["psq", "m2"], writes=["var"])
                P.add("dve", lambda e, n=n: e.tensor_scalar_max(out=var[:, :n], in0=var[:, :n], scalar1=0.0), reads=["var"], writes=["var"])
                rsqrt_ops(P, var[:, :n], var[:, :n], ["var"], "var")
                P.add("dve", lambda e, t0=t0, n=n: e.tensor_tensor(out=dd[:, :n], in0=accA[b][:, t0:t0 + n], in1=msb[:, :n], op=ALU.subtract),
                      reads=Ak + ["msb"], writes=["dd"])
                P.add("dve", lambda e, n=n: e.tensor_tensor(out=dd[:, :n], in0=dd[:, :n], in1=var[:, :n], op=ALU.mult),
                      reads=["dd", "var"], writes=["dd"])
                P.add("act", lambda e, t0=t0, n=n: e.activation(
                    out=co[b][:, t0:t0 + n], in_=dd[:, :n], func=AF.Silu,
                    bias=k.vecs[:, lnb + j:lnb + j + 1], scale=k.vecs[:, lng + j:lng + j + 1]),
                    reads=["dd"], writes=[("co", b, ti)])
            P.add("sp", lambda e: e.dma_start(out=k.CAT[8 + j], in_=co[b][:]),
                  reads=[("co", b, ti) for ti in range(5)], writes=[("CAT", 8 + j)], dma=True)

        for j in range(8):
            proj(j)
            if j >= 1:
                lnorm(j - 1)
            conv(j)
        lnorm(7)


def even_attn(k, l):
    P = k.P
    i = l // 2
    band0 = COFF["band"][0]
    with contextlib.ExitStack() as st:
        kT = P.sb([128, 2, NT], BF16, st)
        V = P.sb([128, 18, 256], BF16, st)
        qh = [P.sb([128, NT], BF16, st) for _ in range(2)]
        PC = [[P.sb([128, NT], BF16, st) for _ in range(2)] for _ in range(2)]
        PB = [P.sb([128, 16, 384], BF16, st) for _ in range(2)]
        tmpE = [P.sb([128, 384], BF16, st) for _ in range(2)]
        ao = [P.sb([128, NT], BF16, st) for _ in range(2)]
        den = P.sb([128, 512], F32, st)
        esink = P.sb([128, 8], F32, st)
        pss = [P.ps([128, 512], F32, st) for _ in range(2)]
        pso = [P.ps([128, 512], F32, st) for _ in range(2)]
        psd = [P.ps([128, 512], F32, st) for _ in range(2)]
        so = VOFF["sink"][0] + i * 8
        P.add("act", lambda e: e.activation(out=esink[:], in_=k.vecs[:, so:so + 8], func=AF.Exp), writes=["esink"])
        for g in range(2):
            P.add("sp", lambda e, g=g: e.dma_start(out=kT[:, g, :], in_=k.KT[g]), writes=[("kT", g)], dma=True)
        P.add("sp", lambda e: e.dma_start(out=V[:], in_=k.VTM.rearrange("t p c -> p t c")), writes=["V"], dma=True)
        nps = 0
        npo = 0
        for h in range(8):
            g = h // 4
            hb = h % 2
            P.add("sp", lambda e, hb=hb, h=h: e.dma_start(out=qh[hb][:], in_=k.QT[h]), writes=[("qh", hb)], dma=True)
            for cb in range(2):
                for ti, (t0, n) in enumerate(TILES):
                    pb = nps % 2; nps += 1
                    P.add("pe", lambda e, pb=pb, g=g, cb=cb, hb=hb, t0=t0, n=n: e.matmul(
                        pss[pb][:, :n], kT[:, g, cb * 128:(cb + 1) * 128], qh[hb][:, t0:t0 + n], start=True, stop=True),
                        reads=[("kT", g), ("qh", hb)], writes=[("pss", pb)])
                    P.add("act", lambda e, pb=pb, hb=hb, cb=cb, t0=t0, n=n: e.activation(
                        out=PC[hb][cb][:, t0:t0 + n], in_=pss[pb][:, :n], func=AF.Exp),
                        reads=[("pss", pb)], writes=[("PC", hb, cb, ti)])
            for jb in range(16):
                lo = max(jb - 1, 0); hi = min(jb + 1, 15)
                n = (hi - lo + 1) * 128
                off = (lo - (jb - 1)) * 128
                pb = nps % 2; nps += 1
                eb = jb % 2
                P.add("pe", lambda e, pb=pb, g=g, jb=jb, hb=hb, lo=lo, n=n: e.matmul(
                    pss[pb][:, :n], kT[:, g, LCTX + jb * 128:LCTX + (jb + 1) * 128], qh[hb][:, LCTX + lo * 128:LCTX + lo * 128 + n],
                    start=True, stop=True), reads=[("kT", g), ("qh", hb)], writes=[("pss", pb)])
                P.add("act", lambda e, pb=pb, eb=eb, n=n: e.activation(out=tmpE[eb][:, :n], in_=pss[pb][:, :n], func=AF.Exp),
                      reads=[("pss", pb)], writes=[("tmpE", eb)])
                P.add("pool", lambda e, eb=eb, hb=hb, jb=jb, off=off, n=n: e.tensor_tensor(
                    out=PB[hb][:, jb, off:off + n], in0=tmpE[eb][:, :n], in1=k.bandb[:, off:off + n], op=ALU.mult),
                    reads=[("tmpE", eb)], writes=[("PB", hb, jb)])
            for ti, (t0, n) in enumerate(TILES):
                ob = npo % 2; npo += 1
                mms = []
                for cb in range(2):
                    mms.append((V[:, cb, g * 128:(g + 1) * 128], PC[hb][cb][:, t0:t0 + n], 0, n, [("PC", hb, cb, ti)]))
                if t0 >= LCTX:
                    qt = (t0 - LCTX) // 512
                    for nb in range(4 * qt, 4 * qt + 4):
                        for jb in (nb - 1, nb, nb + 1):
                            if 0 <= jb <= 15:
                                c0 = (nb - (jb - 1)) * 128
                                mms.append((V[:, 2 + jb, g * 128:(g + 1) * 128], PB[hb][:, jb, c0:c0 + 128], (nb - 4 * qt) * 128, 128,
                                            [("PB", hb, jb)]))
                for mi, (lhs, rhs, o0, on, rk) in enumerate(mms):
                    P.add("pe", lambda e, ob=ob, lhs=lhs, rhs=rhs, o0=o0, on=on, mi=mi, last=(mi == len(mms) - 1): e.matmul(
                        pso[ob][:, o0:o0 + on], lhs, rhs, start=(mi == 0), stop=last, skip_group_check=True),
                        reads=rk + ["V"], writes=[("pso", ob)])
                for mi, (lhs, rhs, o0, on, rk) in enumerate(mms):
                    P.add("pe", lambda e, ob=ob, rhs=rhs, o0=o0, on=on, mi=mi, last=(mi == len(mms) - 1): e.matmul(
                        psd[ob][:, o0:o0 + on], k.ones1[:], rhs, start=(mi == 0), stop=last, skip_group_check=True),
                        reads=rk, writes=[("psd", ob)])
                P.add("dve", lambda e, ob=ob, n=n, h=h: e.tensor_scalar_add(out=den[:, :n], in0=psd[ob][:, :n], scalar1=esink[:, h:h + 1]),
                      reads=[("psd", ob), "esink"], writes=["den"])
                P.add("dve", lambda e, n=n: e.reciprocal(den[:, :n], den[:, :n]), reads=["den"], writes=["den"])
                P.add("dve", lambda e, ob=ob, hb=hb, t0=t0, n=n: e.tensor_tensor(out=ao[hb][:, t0:t0 + n], in0=pso[ob][:, :n], in1=den[:, :n], op=ALU.mult),
                      reads=[("pso", ob), "den"], writes=[("ao", hb, ti)])
            P.add("sp", lambda e, hb=hb, h=h: e.dma_start(out=k.CAT[h], in_=ao[hb][:]),
                  reads=[("ao", hb, ti) for ti in range(5)], writes=[("CAT", h)], dma=True)


def proj_out_residual(k, wsrc, nkc, cat, gate_m, wkeys_extra=()):
    P = k.P
    hsrc = k.hT.rearrange("(c p) t -> p c t", p=128)
    with contextlib.ExitStack() as st:
        wb = [P.sb([128, nkc, 256], BF16, st) for _ in range(2)]
        hrow = [P.sb([128, NT], F32, st) for _ in range(2)]
        pso = [P.ps([128, 512], F32, st) for _ in range(4)]
        nps = 0
        for blk in range(8):
            b = blk % 2
            P.add("pool", lambda e, b=b, blk=blk: e.dma_start(out=wb[b][:], in_=wsrc(blk)),
                  writes=[("wb", b)], dma=True)
            for s in range(2):
                c = blk * 2 + s
                hb = c % 2
                P.add("sp", lambda e, hb=hb, c=c: e.dma_start(out=hrow[hb][:], in_=hsrc[:, c, :]),
                      reads=[("hT", c)], writes=[("hrow", hb)], dma=True)
                for ti, (t0, n) in enumerate(TILES):
                    pb = nps % 4; nps += 1
                    for kc in range(nkc):
                        P.add("pe", lambda e, b=b, s=s, pb=pb, kc=kc, t0=t0, n=n: e.matmul(
                            pso[pb][:, :n], wb[b][:, kc, s * 128:(s + 1) * 128], cat[:, kc, t0:t0 + n],
                            start=(kc == 0), stop=(kc == nkc - 1)),
                            reads=[("wb", b), ("cat", kc)], writes=[("pso", pb)])
                    sidx = 1 if t0 < LCTX else 0
                    P.add("dve", lambda e, hb=hb, pb=pb, t0=t0, n=n, c=c, sidx=sidx: e.scalar_tensor_tensor(
                        out=hrow[hb][:, t0:t0 + n], in0=pso[pb][:, :n], scalar=modv(k, gate_m, c, sidx),
                        in1=hrow[hb][:, t0:t0 + n], op0=ALU.mult, op1=ALU.add),
                        reads=[("pso", pb), ("hrow", hb), "mod"], writes=[("hrow", hb)])
                P.add("sp", lambda e, hb=hb, c=c: e.dma_start(out=hsrc[:, c, :], in_=hrow[hb][:]),
                      reads=[("hrow", hb)], writes=[("hT", c)], dma=True)


def even_out(k, l):
    P = k.P
    i = l // 2
    wsrc = lambda blk: k.even_w_out[i, blk]
    with contextlib.ExitStack() as st:
        cat = P.sb([128, 16, NT], BF16, st)
        for c in range(16):
            P.add("sp", lambda e, c=c: e.dma_start(out=cat[:, c, :], in_=k.CAT[c]), writes=[("cat", c)], dma=True)
        proj_out_residual(k, wsrc, 16, cat, 2)


DN_STOP = 99


def stage_dn(k, l):
    P = k.P
    with contextlib.ExitStack() as st1:
        k.BETA = P.sb([128, 18, 64], F32, st1)
        k.GG = P.sb([128, 18, 64], F32, st1)
        k.EG = P.sb([128, 18, 64], F32, st1)
        k.EL = P.sb([128, 18, 64], F32, st1)
        k.BEG = P.sb([128, 18, 64], F32, st1)
        k.ER = P.sb([128, 18, 64], F32, st1)
        with contextlib.ExitStack() as st0:
            nx = P.sb([128, 16, NT], BF16, st0)
            stage_norm(k, 1, nx, nbuf=1)
            P.barrier()
            if DN_STOP >= 1:
                dn_proj_qkv(k, l, nx)
                P.barrier()
            if DN_STOP >= 2:
                dn_proj_z(k, l, nx)
                P.barrier()
            if DN_STOP >= 3:
                dn_proj_ba(k, l, nx)
                P.barrier()
        if DN_STOP >= 4:
            dn_gates(k, l)
            P.barrier()
            if "DBG_G" in DEBUG_OUT:
                for ii, t in enumerate((k.BETA, k.GG, k.EG, k.EL, k.ER)):
                    P.add("sp", lambda e, ii=ii, t=t: e.dma_start(out=k.DBG_G[ii], in_=t[:]), dma=True)
                P.barrier()
        if DN_STOP >= 5:
            dn_prep(k, l)
            P.barrier()
        if DN_STOP >= 6:
            dn_scan(k, l)
            P.barrier()
    if DN_STOP >= 7:
        dn_outnorm(k, l)
        P.barrier()
    if DN_STOP >= 8:
        dn_outproj(k, l)
        P.barrier()


def dn_proj_qkv(k, l, nx):
    P = k.P
    i = l // 2
    dcw = VOFF["dcw"][0] + i * 5 * 64
    PADW = NT + 8

    def pos(t):
        return t + 2 if t < LCTX else t + 6
    with contextlib.ExitStack() as st:
        wb = [P.sb([128, 16, 256], BF16, st) for _ in range(2)]
        psa = [P.ps([128, 512], F32, st) for _ in range(2)]
        psc = [P.ps([128, 512], F32, st) for _ in range(2)]
        psb = P.ps([128, 512], F32, st)
        pst = [P.ps([128, 4, 128], BF16, st) for _ in range(2)]
        Ub = [P.sb([128, PADW], BF16, st) for _ in range(2)]
        Dg = [P.sb([128, 5, 128], BF16, st) for _ in range(2)]
        accs = [P.sb([128, NT], F32, st) for _ in range(2)]
        sq = P.sb([128, NT], BF16, st)
        rr = P.sb([128, 512], F32, st)
        qo = [P.sb([128, NT], BF16, st) for _ in range(2)]
        tm = P.sb([128, 18, 128], BF16, st)
        for b in range(2):
            P.add("pool", lambda e, b=b: e.memset(Ub[b][:], 0.0), writes=[("Ub", b, ti) for ti in range(5)] + [("Ubz", b)])
        npa = 0
        npt = 0
        for blk in range(32):
            b = blk % 2
            P.add("pool", lambda e, b=b, blk=blk: e.dma_start(out=wb[b][:], in_=k.dn_w_in[i, blk]), writes=[("wb", b)], dma=True)
            for s in range(2):
                ch = blk * 2 + s
                ub = ch % 2
                acc = accs[ch % 2]
                ak = ("acc", ch % 2)
                qb = ch % 2
                kind = "q" if ch < 16 else ("k" if ch < 32 else "v")
                for kk in range(5):
                    w = k.vecs[:, dcw + kk * 64 + ch:dcw + kk * 64 + ch + 1]
                    P.add("dve", lambda e, ub=ub, kk=kk, w=w: e.tensor_scalar(out=Dg[ub][:, kk, :], in0=k.identb[:], scalar1=w, scalar2=None, op0=ALU.mult),
                          writes=[("Dg", ub, kk)])
                for ti, (t0, n) in enumerate(TILES):
                    pb = npa % 2; npa += 1
                    for kc in range(16):
                        P.add("pe", lambda e, b=b, s=s, pb=pb, kc=kc, t0=t0, n=n: e.matmul(
                            psa[pb][:, :n], wb[b][:, kc, s * 128:(s + 1) * 128], nx[:, kc, t0:t0 + n], start=(kc == 0), stop=(kc == 15)),
                            reads=[("wb", b)], writes=[("psa", pb)])
                    p0 = pos(t0)
                    P.add("act", lambda e, ub=ub, pb=pb, p0=p0, n=n: e.activation(out=Ub[ub][:, p0:p0 + n], in_=psa[pb][:, :n], func=AF.Copy),
                          reads=[("psa", pb), ("Ubz", ub)], writes=[("Ub", ub, ti)])
                Uk = [("Ub", ub, ti) for ti in range(5)] + [("Ubz", ub)]
                for ti, (t0, n) in enumerate(TILES):
                    pc = ti % 2
                    p0 = pos(t0)
                    for kk in range(5):
                        P.add("pe", lambda e, ub=ub, pc=pc, kk=kk, p0=p0, n=n: e.matmul(
                            psc[pc][:, :n], Dg[ub][:, kk, :], Ub[ub][:, p0 + kk - 2:p0 + kk - 2 + n], start=(kk == 0), stop=(kk == 4)),
                            reads=Uk + [("Dg", ub, kk)], writes=[("psc", pc)])
                    if kind == "v":
                        P.add("act", lambda e, qb=qb, pc=pc, t0=t0, n=n: e.activation(out=qo[qb][:, t0:t0 + n], in_=psc[pc][:, :n], func=AF.Silu),
                              reads=[("psc", pc)], writes=[("qo", qb)])
                    else:
                        P.add("act", lambda e, acc=acc, pc=pc, t0=t0, n=n: e.activation(out=acc[:, t0:t0 + n], in_=psc[pc][:, :n], func=AF.Silu),
                              reads=[("psc", pc)], writes=[ak])
                if kind != "v":
                    P.add("act", lambda e, acc=acc: e.activation(out=sq[:], in_=acc[:], func=AF.Square), reads=[ak], writes=["sq"])
                    for ti, (t0, n) in enumerate(TILES):
                        P.add("pe", lambda e, t0=t0, n=n: e.matmul(psb[:, :n], k.ones1[:], sq[:, t0:t0 + n], start=True, stop=True),
                              reads=["sq"], writes=["psb"])
                        rsqrt_ops(P, rr[:, :n], psb[:, :n], ["psb"], "rr")
                        sc = (128 ** -0.5) if kind == "q" else 1.0
                        P.add("dve", lambda e, acc=acc, qb=qb, t0=t0, n=n, sc=sc: e.scalar_tensor_tensor(
                            out=qo[qb][:, t0:t0 + n], in0=acc[:, t0:t0 + n], scalar=sc, in1=rr[:, :n], op0=ALU.mult, op1=ALU.mult),
                            reads=[ak, "rr"], writes=[("qo", qb)])
                if kind == "q":
                    P.add("sp", lambda e, qb=qb, ch=ch: e.dma_start(out=k.DQT[ch], in_=qo[qb][:]), reads=[("qo", qb)], writes=[("DQT", ch)], dma=True)
                else:
                    if kind == "k":
                        P.add("sp", lambda e, qb=qb, ch=ch: e.dma_start(out=k.DKT[ch - 16], in_=qo[qb][:]), reads=[("qo", qb)],
                              writes=[("DKT", ch)], dma=True)
                    for c4 in range(0, 18, 4):
                        nn = min(4, 18 - c4)
                        tb = npt % 2; npt += 1
                        for cc in range(nn):
                            c = c4 + cc
                            P.add("pe", lambda e, tb=tb, cc=cc, c=c, qb=qb: e.transpose(
                                out=pst[tb][:, cc, :], in_=qo[qb][:, c * 128:(c + 1) * 128], identity=k.identb[:]),
                                reads=[("qo", qb)], writes=[("pst", tb)])
                        P.add("act", lambda e, tb=tb, c4=c4, nn=nn: e.activation(out=tm[:, c4:c4 + nn, :], in_=pst[tb][:, :nn, :], func=AF.Copy),
                              reads=[("pst", tb)], writes=[("tm", 0)])
                    if kind == "k":
                        dst = k.DKTM[:, :, (ch - 16) * 128:(ch - 15) * 128]
                    else:
                        dst = k.DVTM[:, :, (ch - 32) * 128:(ch - 31) * 128]
                    P.add("sp", lambda e, dst=dst: e.dma_start(out=dst.rearrange("c p d -> p c d"), in_=tm[:]),
                          reads=[("tm", 0)], writes=[("DTM", ch)], dma=True)


def dn_proj_z(k, l, nx):
    P = k.P
    i = l // 2
    with contextlib.ExitStack() as st:
        wb = [P.sb([128, 16, 512], BF16, st) for _ in range(2)]
        psa = [P.ps([128, 512], F32, st) for _ in range(4)]
        zs = [P.sb([128, 512], BF16, st) for _ in range(4)]
        npa = 0
        for blk in range(8):
            b = blk % 2
            for q in range(2):
                P.add("pool", lambda e, b=b, blk=blk, q=q: e.dma_start(
                    out=wb[b][:, :, q * 256:(q + 1) * 256], in_=k.dn_w_in[i, 32 + blk * 2 + q]),
                    writes=[("wb", b, q)], dma=True)
            for c in range(18):
                pb = npa % 4; npa += 1
                for kc in range(16):
                    P.add("pe", lambda e, b=b, pb=pb, kc=kc, c=c: e.matmul(
                        psa[pb][:], nx[:, kc, c * 128:(c + 1) * 128], wb[b][:, kc, :], start=(kc == 0), stop=(kc == 15)),
                        reads=[("wb", b, 0), ("wb", b, 1)], writes=[("psa", pb)])
                P.add("act", lambda e, pb=pb: e.activation(out=zs[pb][:], in_=psa[pb][:], func=AF.Silu),
                      reads=[("psa", pb)], writes=[("zs", pb)])
                P.add("sp", lambda e, pb=pb, c=c, blk=blk: e.dma_start(out=k.DZ[c, :, blk * 512:(blk + 1) * 512], in_=zs[pb][:]),
                      reads=[("zs", pb)], writes=[("DZ", c, blk)], dma=True)


def dn_proj_ba(k, l, nx):
    P = k.P
    i = l // 2
    with contextlib.ExitStack() as st:
        wb = P.sb([128, 16, 128], BF16, st)
        BA = P.sb([128, 18, 128], F32, st)
        psa = [P.ps([128, 512], F32, st) for _ in range(2)]
        P.add("pool", lambda e: e.dma_start(out=wb[:], in_=k.dn_w_ba[i]), writes=["wb"], dma=True)
        for c in range(18):
            pb = c % 2
            for kc in range(16):
                P.add("pe", lambda e, pb=pb, kc=kc, c=c: e.matmul(
                    psa[pb][:, 0:128], nx[:, kc, c * 128:(c + 1) * 128], wb[:, kc, :], start=(kc == 0), stop=(kc == 15)),
                    reads=["wb"], writes=[("psa", pb)])
            P.add("act", lambda e, pb=pb, c=c: e.activation(out=BA[:, c, :], in_=psa[pb][:, 0:128], func=AF.Copy),
                  reads=[("psa", pb)], writes=[("BA", c)])
        BAk = [("BA", c) for c in range(18)]
        P.add("act", lambda e: e.activation(out=k.BETA[:], in_=BA[:, :, 0:64], func=AF.Sigmoid), reads=BAk, writes=["BETA"])
        P.add("dve", lambda e: e.tensor_copy(k.GG[:], BA[:, :, 64:128]), reads=BAk, writes=["GG"])
        P.barrier()


def dn_gates(k, l):
    P = k.P
    i = l // 2
    al = VOFF["alog"][0] + i * 64
    db = VOFF["dtb"][0] + i * 64
    with contextlib.ExitStack() as st:
        x = P.sb([128, 18, 64], F32, st)
        ax = P.sb([128, 18, 64], F32, st)
        nA = P.sb([128, 64], F32, st)
        onesf = P.sb([128, 128], F32, st)
        ps = [P.ps([128, 3, 64], F32, st) for _ in range(2)]
        P.add("pool", lambda e: e.memset(onesf[:], 1.0), writes=["onesf"])
        P.add("dve", lambda e: e.tensor_tensor(out=x[:], in0=k.GG[:], in1=bc(k.vecs[:, db:db + 64].unsqueeze(1), [128, 18, 64]), op=ALU.add),
              writes=["x"])
        P.add("act", lambda e: e.activation(out=ax[:], in_=x[:], func=AF.Abs), reads=["x"], writes=["ax"])
        P.add("act", lambda e: e.activation(out=ax[:], in_=ax[:], func=AF.Exp, scale=-1.0), reads=["ax"], writes=["ax"])
        P.add("act", lambda e: e.activation(out=ax[:], in_=ax[:], func=AF.Ln, bias=1.0), reads=["ax"], writes=["ax"])
        P.add("dve", lambda e: e.tensor_scalar_max(out=x[:], in0=x[:], scalar1=0.0), reads=["x"], writes=["x"])
        P.add("dve", lambda e: e.tensor_tensor(out=x[:], in0=x[:], in1=ax[:], op=ALU.add), reads=["x", "ax"], writes=["x"])
        P.add("act", lambda e: e.activation(out=nA[:], in_=k.vecs[:, al:al + 64], func=AF.Exp), writes=["nA"])
        P.add("dve", lambda e: e.scalar_tensor_tensor(out=k.GG[:], in0=x[:], scalar=-1.0, in1=bc(nA[:].unsqueeze(1), [128, 18, 64]),
                                                      op0=ALU.mult, op1=ALU.mult), reads=["x", "nA"], writes=["GG"])
        mo = {n: COFF[n][0] for n in ("m_le", "m_ge", "m_lt", "m_gt")}
        for c in range(18):
            pb = c % 2
            for d in range(2):
                m_gc = mo["m_le"] if d == 0 else mo["m_ge"]
                m_rm = mo["m_gt"] if d == 0 else mo["m_lt"]
                rhs = k.GG[:, c, d * 32:(d + 1) * 32]
                P.add("pe", lambda e, pb=pb, d=d, m=m_gc, rhs=rhs: e.matmul(ps[pb][:, 0, d * 32:(d + 1) * 32], k.consts[:, m:m + 128], rhs, start=True, stop=True),
                      reads=["GG"], writes=[("ps", pb)])
                P.add("pe", lambda e, pb=pb, d=d, m=m_rm, rhs=rhs: e.matmul(ps[pb][:, 1, d * 32:(d + 1) * 32], k.consts[:, m:m + 128], rhs, start=True, stop=True),
                      reads=["GG"], writes=[("ps", pb)])
                P.add("pe", lambda e, pb=pb, d=d, rhs=rhs: e.matmul(ps[pb][:, 2, d * 32:(d + 1) * 32], onesf[:], rhs, start=True, stop=True),
                      reads=["GG", "onesf"], writes=[("ps", pb)])
            P.add("act", lambda e, pb=pb, c=c: e.activation(out=k.EG[:, c, :], in_=ps[pb][:, 0, :], func=AF.Exp), reads=[("ps", pb)], writes=[("EG", c)])
            P.add("act", lambda e, pb=pb, c=c: e.activation(out=k.ER[:, c, :], in_=ps[pb][:, 1, :], func=AF.Exp), reads=[("ps", pb)], writes=[("ER", c)])
            P.add("act", lambda e, pb=pb, c=c: e.activation(out=k.EL[:, c, :], in_=ps[pb][:, 2, :], func=AF.Exp), reads=[("ps", pb)], writes=[("EL", c)])
            P.add("dve", lambda e, c=c: e.tensor_tensor(out=k.BEG[:, c, :], in0=k.BETA[:, c, :], in1=k.EG[:, c, :], op=ALU.mult),
                  reads=[("EG", c)], writes=[("BEG", c)])


def dn_prep(k, l):
    P = k.P
    mo = {n: COFF[n][0] for n in ("m_le", "m_ge", "m_lt", "m_gt", "ident")}
    NL = 2
    with contextlib.ExitStack() as st:
        kTc = [P.sb([128, 16, 128], BF16, st) for _ in range(2)]
        qTc = [P.sb([128, 16, 128], BF16, st) for _ in range(2)]
        ktm = [P.sb([128, 16, 128], BF16, st) for _ in range(2)]
        vtm = [P.sb([128, 32, 128], BF16, st) for _ in range(2)]
        lanes = []
        for ln in range(NL):
            B_ = K()
            B_.Gm = P.sb([128, 2, 128], F32, st); B_.QKm = P.sb([128, 2, 128], F32, st)
            B_.rhs2 = P.sb([128, 4, 128], F32, st); B_.eD = P.sb([128, 4, 128], F32, st)
            B_.L0 = P.sb([128, 4, 128], BF16, st); B_.QKD = P.sb([128, 4, 128], BF16, st)
            B_.LtQ = P.sb([128, 8, 128], BF16, st)
            B_.Xb = [P.sb([128, 4, 128], BF16, st) for _ in range(2)]
            B_.Yb = P.sb([128, 4, 128], BF16, st)
            B_.Zb = [P.sb([128, 4, 128], BF16, st) for _ in range(2)]
            B_.Ab = [P.sb([128, 4, 128], BF16, st) for _ in range(2)]
            B_.Bb = [P.sb([128, 4, 128], BF16, st) for _ in range(2)]
            B_.Lm = [P.sb([128, 4, 128], BF16, st) for _ in range(5)]
            B_.Ltm = [P.sb([128, 4, 128], BF16, st) for _ in range(4)]
            B_.T1a = P.sb([128, 4, 128], BF16, st); B_.T1b = P.sb([128, 4, 128], BF16, st)
            B_.vb = P.sb([128, 4, 128], BF16, st); B_.kbg = P.sb([128, 4, 128], BF16, st)
            B_.PH = [P.sb([128, 4, 4, 128], BF16, st) for _ in range(2)]
            B_.cnt = 0
            B_.psT = P.ps([128, 8, 128], BF16, st)
            B_.psX = P.ps([128, 4, 128], F32, st); B_.psY = P.ps([128, 4, 128], F32, st); B_.psZ = P.ps([128, 4, 128], F32, st)
            lanes.append(B_)

        def unit(ln, c, d, hg, cb):
            B_ = lanes[ln]
            K_ = lambda name, *a: (name, ln) + a
            Gm, QKm, rhs2, eD, L0, QKD, LtQ = B_.Gm, B_.QKm, B_.rhs2, B_.eD, B_.L0, B_.QKD, B_.LtQ
            Xb, Yb, Zb, Ab, Bb, Lm, Ltm, T1a, T1b = B_.Xb, B_.Yb, B_.Zb, B_.Ab, B_.Bb, B_.Lm, B_.Ltm, B_.T1a, B_.T1b
            vb, kbg = B_.vb, B_.kbg
            pi = B_.cnt % 2
            B_.cnt += 1
            PH = B_.PH[pi]
            Uo, Wo, Qo, kd = PH[:, 0], PH[:, 1], PH[:, 2], PH[:, 3]
            K2 = lambda name, *a: (name, ln, pi) + a
            psT, psX, psY, psZ = B_.psT, B_.psX, B_.psY, B_.psZ
            m_strict = mo["m_gt"] if d == 0 else mo["m_lt"]
            m_incl = mo["m_ge"] if d == 0 else mo["m_le"]
            m_l = mo["m_le"] if d == 0 else mo["m_ge"]
            m_r = mo["m_gt"] if d == 0 else mo["m_lt"]
            h0 = d * 32 + hg * 4
            kh0 = hg * 2
            P.add("pool", lambda e: e.tensor_tensor(
                out=rhs2[:], in0=bc(k.consts[:, m_r:m_r + 128].unsqueeze(1), [128, 4, 128]),
                in1=bc(k.GG[:, c, h0:h0 + 4].unsqueeze(2), [128, 4, 128]), op=ALU.mult), writes=[K_("rhs2")])
            for a in range(2):
                P.add("pe", lambda e, a=a: e.matmul(psX[:, a, :], kTc[cb][:, kh0 + a, :], kTc[cb][:, kh0 + a, :], start=True, stop=True),
                      reads=[("kTc", cb)], writes=[K_("psX")])
                P.add("pe", lambda e, a=a: e.matmul(psX[:, 2 + a, :], qTc[cb][:, kh0 + a, :], kTc[cb][:, kh0 + a, :], start=True, stop=True),
                      reads=[("kTc", cb), ("qTc", cb)], writes=[K_("psX")])
            P.add("pe", lambda e: e.matmul(psY[:], k.consts[:, m_l:m_l + 128], rhs2[:], start=True, stop=True),
                  reads=[K_("rhs2")], writes=[K_("psY")])
            P.add("dve", lambda e: e.tensor_tensor(out=Gm[:], in0=psX[:, 0:2, :], in1=bc(k.consts[:, m_strict:m_strict + 128].unsqueeze(1), [128, 2, 128]), op=ALU.mult),
                  reads=[K_("psX")], writes=[K_("Gm")])
            P.add("dve", lambda e: e.tensor_tensor(out=QKm[:], in0=psX[:, 2:4, :], in1=bc(k.consts[:, m_incl:m_incl + 128].unsqueeze(1), [128, 2, 128]), op=ALU.mult),
                  reads=[K_("psX")], writes=[K_("QKm")])
            P.add("act", lambda e: e.activation(out=eD[:], in_=psY[:], func=AF.Exp), reads=[K_("psY")], writes=[K_("eD")])
            yield
            for hl in range(4):
                P.add("dve", lambda e, hl=hl: e.scalar_tensor_tensor(
                    out=L0[:, hl, :], in0=Gm[:, hl // 2, :], scalar=k.BETA[:, c, h0 + hl:h0 + hl + 1], in1=eD[:, hl, :],
                    op0=ALU.mult, op1=ALU.mult), reads=[K_("Gm"), K_("eD")], writes=[K_("L0", hl)])
            P.add("dve", lambda e: e.tensor_tensor(
                out=QKD[:].rearrange("p (a b) j -> p a b j", b=2), in0=bc(QKm[:].unsqueeze(2), [128, 2, 2, 128]),
                in1=eD[:].rearrange("p (a b) j -> p a b j", b=2), op=ALU.mult), reads=[K_("QKm"), K_("eD")], writes=[K_("QKD")])
            L0k = [K_("L0", hl) for hl in range(4)]
            for hl in range(4):
                P.add("pe", lambda e, hl=hl: e.transpose(out=psT[:, hl, :], in_=L0[:, hl, :], identity=k.identb[:]),
                      reads=L0k, writes=[K_("psT")])
            for hl in range(4):
                P.add("pe", lambda e, hl=hl: e.transpose(out=psT[:, 4 + hl, :], in_=QKD[:, hl, :], identity=k.identb[:]),
                      reads=[K_("QKD")], writes=[K_("psT")])
            P.add("act", lambda e: e.activation(out=LtQ[:, 0:4, :], in_=psT[:, 0:4, :], func=AF.Copy), reads=[K_("psT")], writes=[K_("LtQ")])
            P.add("act", lambda e: e.activation(out=Qo, in_=psT[:, 4:8, :], func=AF.Copy), reads=[K_("psT")], writes=[K2("Qo")])
            yield
            Lt = LtQ[:, 0:4, :]
            mk = lambda q: bc(k.maskb[:, q, :].unsqueeze(1), [128, 4, 128])
            P.add("dve", lambda e: e.tensor_tensor(out=Lm[0][:], in0=L0[:], in1=mk(0), op=ALU.mult), reads=L0k, writes=[K_("Lm", 0)])
            P.add("dve", lambda e: e.tensor_tensor(out=Ltm[0][:], in0=Lt, in1=mk(0), op=ALU.mult), reads=[K_("LtQ")], writes=[K_("Ltm", 0)])
            for q in range(1, 5):
                eng_m = "pool"
                P.add(eng_m, lambda e, q=q: e.tensor_tensor(out=Lm[q][:], in0=L0[:], in1=mk(q), op=ALU.mult), reads=L0k, writes=[K_("Lm", q)])
                if q < 4:
                    P.add(eng_m, lambda e, q=q: e.tensor_tensor(out=Ltm[q][:], in0=Lt, in1=mk(q), op=ALU.mult), reads=[K_("LtQ")], writes=[K_("Ltm", q)])
            for hl in range(4):
                P.add("act", lambda e, hl=hl: e.activation(out=vb[:, hl, :], in_=vtm[cb][:, hg * 4 + hl, :], func=AF.Identity,
                                                           scale=k.BETA[:, c, h0 + hl:h0 + hl + 1]), reads=[("vtm", cb)], writes=[K_("vb", hl)])
                P.add("act", lambda e, hl=hl: e.activation(out=kbg[:, hl, :], in_=ktm[cb][:, kh0 + hl // 2, :], func=AF.Identity,
                                                           scale=k.BEG[:, c, h0 + hl:h0 + hl + 1]), reads=[("ktm", cb)], writes=[K_("kbg", hl)])
                P.add("act", lambda e, hl=hl: e.activation(out=kd[:, hl, :], in_=ktm[cb][:, kh0 + hl // 2, :], func=AF.Identity,
                                                           scale=k.ER[:, c, h0 + hl:h0 + hl + 1]), reads=[("ktm", cb)], writes=[K2("kd", hl)])
            for hl in range(4):
                P.add("pe", lambda e, hl=hl: e.matmul(psX[:, hl, :], Ltm[0][:, hl, :], Lm[0][:, hl, :], start=True, stop=True),
                      reads=[K_("Lm", 0), K_("Ltm", 0)], writes=[K_("psX")])
            for hl in range(4):
                P.add("pe", lambda e, hl=hl: e.matmul(psY[:, hl, :], Lm[0][:, hl, :], Ltm[0][:, hl, :], start=True, stop=True),
                      reads=[K_("Lm", 0), K_("Ltm", 0)], writes=[K_("psY")])
            P.add("act", lambda e: e.activation(out=Xb[0][:], in_=psX[:], func=AF.Copy), reads=[K_("psX")], writes=[K_("X", 0)])
            P.add("act", lambda e: e.activation(out=Yb[:], in_=psY[:], func=AF.Copy), reads=[K_("psY")], writes=[K_("Y")])
            P.add("dve", lambda e: e.scalar_tensor_tensor(
                out=Zb[0][:], in0=Ltm[0][:], scalar=-1.0, in1=bc(k.identb[:].unsqueeze(1), [128, 4, 128]),
                op0=ALU.mult, op1=ALU.add), reads=[K_("Ltm", 0)], writes=[K_("Z", 0)])
            yield
            for hl in range(4):
                P.add("pe", lambda e, hl=hl: e.matmul(psX[:, hl, :], Yb[:, hl, :], Xb[0][:, hl, :], start=True, stop=True),
                      reads=[K_("X", 0), K_("Y")], writes=[K_("psX")])
            for hl in range(4):
                P.add("pe", lambda e, hl=hl: e.matmul(psZ[:, hl, :], Xb[0][:, hl, :], Zb[0][:, hl, :], start=True, stop=True),
                      reads=[K_("X", 0), K_("Z", 0)], writes=[K_("psZ")])
            P.add("act", lambda e: e.activation(out=Xb[1][:], in_=psX[:], func=AF.Copy), reads=[K_("psX")], writes=[K_("X", 1)])
            P.add("dve", lambda e: e.tensor_tensor(out=Zb[1][:], in0=psZ[:], in1=Zb[0][:], op=ALU.add),
                  reads=[K_("psZ"), K_("Z", 0)], writes=[K_("Z", 1)])
            yield
            for hl in range(4):
                P.add("pe", lambda e, hl=hl: e.matmul(psZ[:, hl, :], Xb[1][:, hl, :], Zb[1][:, hl, :], start=True, stop=True),
                      reads=[K_("X", 1), K_("Z", 1)], writes=[K_("psZ")])
            P.add("dve", lambda e: e.tensor_tensor(out=Zb[0][:], in0=psZ[:], in1=Zb[1][:], op=ALU.add),
                  reads=[K_("psZ"), K_("Z", 1)], writes=[K_("Z", 0)])
            yield
            for hl in range(4):
                P.add("pe", lambda e, hl=hl: e.transpose(out=psT[:, hl, :], in_=Zb[0][:, hl, :], identity=k.identb[:]),
                      reads=[K_("Z", 0)], writes=[K_("psT")])
            P.add("act", lambda e: e.activation(out=Ab[0][:], in_=psT[:, 0:4, :], func=AF.Copy), reads=[K_("psT")], writes=[K_("A", 0)])
            yield
            Bcur, Bkey = Zb[0], K_("Z", 0)
            Acur, Akey = Ab[0], K_("A", 0)
            for q in range(1, 5):
                last = (q == 4)
                nb_ = q % 2
                if not last:
                    for hl in range(4):
                        P.add("pe", lambda e, hl=hl, q=q, Acur=Acur: e.matmul(psX[:, hl, :], Ltm[q][:, hl, :], Acur[:, hl, :], start=True, stop=True),
                              reads=[K_("Ltm", q), Akey], writes=[K_("psX")])
                for hl in range(4):
                    P.add("pe", lambda e, hl=hl, q=q, Bcur=Bcur: e.matmul(psY[:, hl, :], Lm[q][:, hl, :], Bcur[:, hl, :], start=True, stop=True),
                          reads=[K_("Lm", q), Bkey], writes=[K_("psY")])
                if not last:
                    P.add("act", lambda e: e.activation(out=T1a[:], in_=psX[:], func=AF.Copy), reads=[K_("psX")], writes=[K_("T1a")])
                P.add("act", lambda e: e.activation(out=T1b[:], in_=psY[:], func=AF.Copy), reads=[K_("psY")], writes=[K_("T1b")])
                yield
                if not last:
                    for hl in range(4):
                        P.add("pe", lambda e, hl=hl, Bcur=Bcur: e.matmul(psX[:, hl, :], Bcur[:, hl, :], T1a[:, hl, :], start=True, stop=True),
                              reads=[Bkey, K_("T1a")], writes=[K_("psX")])
                for hl in range(4):
                    P.add("pe", lambda e, hl=hl, Acur=Acur: e.matmul(psZ[:, hl, :], Acur[:, hl, :], T1b[:, hl, :], start=True, stop=True),
                          reads=[Akey, K_("T1b")], writes=[K_("psZ")])
                if not last:
                    P.add("dve", lambda e, nb_=nb_, Acur=Acur: e.tensor_tensor(out=Ab[nb_][:], in0=Acur[:], in1=psX[:], op=ALU.subtract),
                          reads=[K_("psX"), Akey], writes=[K_("A", nb_)])
                Bnew = Bb[nb_] if not last else Zb[0]
                Bnk = K_("B", nb_) if not last else K_("Z", 0)
                P.add("dve", lambda e, Bnew=Bnew, Bcur=Bcur: e.tensor_tensor(out=Bnew[:], in0=Bcur[:], in1=psZ[:], op=ALU.subtract),
                      reads=[K_("psZ"), Bkey], writes=[Bnk])
                Bcur, Bkey = Bnew, Bnk
                if not last:
                    Acur, Akey = Ab[nb_], K_("A", nb_)
                yield
            ZF = Zb[0]
            for hl in range(4):
                P.add("pe", lambda e, hl=hl: e.matmul(psX[:, hl, :], ZF[:, hl, :], vb[:, hl, :], start=True, stop=True),
                      reads=[K_("Z", 0)] + [K_("vb", q_) for q_ in range(4)], writes=[K_("psX")])
            for hl in range(4):
                P.add("pe", lambda e, hl=hl: e.matmul(psY[:, hl, :], kbg[:, hl, :], ZF[:, hl, :], start=True, stop=True),
                      reads=[K_("Z", 0)] + [K_("kbg", q_) for q_ in range(4)], writes=[K_("psY")])
            P.add("act", lambda e: e.activation(out=Uo, in_=psX[:], func=AF.Copy), reads=[K_("psX")], writes=[K2("Uo")])
            P.add("act", lambda e: e.activation(out=Wo, in_=psY[:], func=AF.Copy), reads=[K_("psY")], writes=[K2("Wo")])
            u = (c * 2 + d) * 8 + hg
            P.add("sp", lambda e: e.dma_start(out=k.DPH[u], in_=PH[:]),
                  reads=[K2("Uo"), K2("Wo"), K2("Qo")] + [K2("kd", q_) for q_ in range(4)], writes=[("DPH", u)], dma=True)
            yield

        def load_chunk(c):
            cb = c % 2
            tsl = slice(c * 128, (c + 1) * 128)
            P.add("sp", lambda e: e.dma_start(out=kTc[cb][:], in_=k.DKT.rearrange("h p t -> p h t")[:, :, tsl]), writes=[("kTc", cb)], dma=True)
            P.add("sp", lambda e: e.dma_start(out=qTc[cb][:], in_=k.DQT.rearrange("h p t -> p h t")[:, :, tsl]), writes=[("qTc", cb)], dma=True)
            P.add("sp", lambda e: e.dma_start(out=ktm[cb][:], in_=k.DKTM[c].rearrange("p (h d) -> p h d", d=128)), writes=[("ktm", cb)], dma=True)
            P.add("sp", lambda e: e.dma_start(out=vtm[cb][:], in_=k.DVTM[c].rearrange("p (h d) -> p h d", d=128)), writes=[("vtm", cb)], dma=True)

        for c in range(18):
            load_chunk(c)
            units = [(d, hg) for d in range(2) for hg in range(8)]
            for p0 in range(0, len(units), NL):
                gens = [unit(ln, c, units[p0 + ln][0], units[p0 + ln][1], c % 2) for ln in range(NL)]
                alive = list(gens)
                while alive:
                    nxt = []
                    for g in alive:
                        try:
                            next(g)
                            nxt.append(g)
                        except StopIteration:
                            pass
                    alive = nxt


def dn_scan(k, l):
    P = k.P
    order = [list(range(18)), [1, 0] + list(range(17, 1, -1))]
    with contextlib.ExitStack() as st:
        Sf = P.sb([128, 64, 128], F32, st)
        Sb = P.sb([128, 64, 128], BF16, st)
        NB = 3
        ph = [P.sb([128, 4, 4, 128], BF16, st) for _ in range(NB)]
        qT = [P.sb([128, 2, 128], BF16, st) for _ in range(NB)]
        vnew = [P.sb([128, 4, 128], BF16, st) for _ in range(2)]
        o1 = [P.sb([128, 4, 128], F32, st) for _ in range(NB)]
        ob_ = [P.sb([128, 4, 128], F32, st) for _ in range(NB)]
        psP = [P.ps([128, 4, 128], F32, st) for _ in range(2)]
        psO1 = [P.ps([128, 4, 128], F32, st) for _ in range(2)]
        psO2 = [P.ps([128, 4, 128], F32, st) for _ in range(2)]
        psS = [P.ps([128, 4, 128], F32, st) for _ in range(2)]
        P.add("pool", lambda e: e.memset(Sf[:], 0.0), writes=[("Sf", hh) for hh in range(16)])
        P.add("pool", lambda e: e.memset(Sb[:], 0.0), writes=[("Sb", hh) for hh in range(16)])
        P.barrier()
        units = [(s, d, hg) for s in range(18) for d in range(2) for hg in range(8)]

        def stage1(idx):
            s, d, hg = units[idx]
            c = order[d][s]
            u = (c * 2 + d) * 8 + hg
            nb = idx % NB; pb = idx % 2
            g16 = d * 8 + hg
            P.add("sp", lambda e: e.dma_start(out=ph[nb][:], in_=k.DPH[u]), writes=[("ph", nb)], dma=True)
            P.add("sp", lambda e: e.dma_start(out=qT[nb][:], in_=k.DQT[hg * 2:hg * 2 + 2, :, c * 128:(c + 1) * 128].rearrange("h p t -> p h t")),
                  writes=[("qT", nb)], dma=True)
            for hl in range(4):
                hs = d * 32 + hg * 4 + hl
                P.add("pe", lambda e, hl=hl, hs=hs: e.matmul(psP[pb][:, hl, :], ph[nb][:, 1, hl, :], Sb[:, hs, :], start=True, stop=True),
                      reads=[("ph", nb), ("Sb", g16)], writes=[("psP", pb)])
            for hl in range(4):
                hs = d * 32 + hg * 4 + hl
                P.add("pe", lambda e, hl=hl, hs=hs: e.matmul(psO1[pb][:, hl, :], qT[nb][:, hl // 2, :], Sb[:, hs, :], start=True, stop=True),
                      reads=[("qT", nb), ("Sb", g16)], writes=[("psO1", pb)])
            P.add("dve", lambda e: e.tensor_tensor(out=vnew[pb][:], in0=ph[nb][:, 0, :, :], in1=psP[pb][:], op=ALU.subtract),
                  reads=[("ph", nb), ("psP", pb)], writes=[("vnew", pb)])
            P.add("act", lambda e: e.activation(out=o1[nb][:], in_=psO1[pb][:], func=AF.Copy), reads=[("psO1", pb)], writes=[("o1", nb)])

        def stage2(idx):
            s, d, hg = units[idx]
            c = order[d][s]
            nb = idx % NB; pb = idx % 2
            g16 = d * 8 + hg
            h0 = d * 32 + hg * 4
            for hl in range(4):
                P.add("pe", lambda e, hl=hl: e.matmul(psO2[pb][:, hl, :], ph[nb][:, 2, hl, :], vnew[pb][:, hl, :], start=True, stop=True),
                      reads=[("ph", nb), ("vnew", pb)], writes=[("psO2", pb)])
            for hl in range(4):
                P.add("pe", lambda e, hl=hl: e.matmul(psS[pb][:, hl, :], ph[nb][:, 3, hl, :], vnew[pb][:, hl, :], start=True, stop=True),
                      reads=[("ph", nb), ("vnew", pb)], writes=[("psS", pb)])
            for hl in range(4):
                P.add("dve", lambda e, hl=hl: e.scalar_tensor_tensor(
                    out=ob_[nb][:, hl, :], in0=o1[nb][:, hl, :], scalar=k.EG[:, c, h0 + hl:h0 + hl + 1], in1=psO2[pb][:, hl, :],
                    op0=ALU.mult, op1=ALU.add), reads=[("psO2", pb), ("o1", nb)], writes=[("ob", nb, hl)])
            for hl in range(4):
                P.add("dve", lambda e, hl=hl: e.scalar_tensor_tensor(
                    out=Sf[:, h0 + hl, :], in0=Sf[:, h0 + hl, :], scalar=k.EL[:, c, h0 + hl:h0 + hl + 1], in1=psS[pb][:, hl, :],
                    op0=ALU.mult, op1=ALU.add), reads=[("psS", pb), ("Sf", g16, hl)], writes=[("Sf", g16, hl)])
            P.add("act", lambda e: e.activation(out=Sb[:, h0:h0 + 4, :], in_=Sf[:, h0:h0 + 4, :], func=AF.Copy),
                  reads=[("Sf", g16, hl) for hl in range(4)], writes=[("Sb", g16)])
            dst = k.DO[d, c, :, hg * 512:(hg + 1) * 512]
            P.add("sp", lambda e: e.dma_start(out=dst, in_=ob_[nb][:].rearrange("p h j -> p (h j)")),
                  reads=[("ob", nb, hl) for hl in range(4)], writes=[("DO", d, c, hg)], dma=True)

        for idx in range(len(units) + 1):
            if idx < len(units):
                stage1(idx)
            if idx >= 1:
                stage2(idx - 1)


def dn_outnorm(k, l):
    P = k.P
    i = l // 2
    ng = VOFF["dngrep"][0] + i * 128
    with contextlib.ExitStack() as st:
        of = [P.sb([128, 32, 128], F32, st) for _ in range(2)]
        obk = [P.sb([128, 32, 128], F32, st) for _ in range(2)]
        zz = [P.sb([128, 32, 128], BF16, st) for _ in range(2)]
        sq = P.sb([128, 32, 128], F32, st)
        ss = P.sb([128, 32], F32, st)
        y = [P.sb([128, 32, 128], BF16, st) for _ in range(2)]
        yT = [P.sb([128, 32, 128], BF16, st) for _ in range(2)]
        pst = [P.ps([128, 8, 128], BF16, st) for _ in range(2)]
        npt = 0
        for c in range(18):
            b = c % 2
            P.add("sp", lambda e, b=b, c=c: e.dma_start(out=of[b][:], in_=k.DO[0, c].rearrange("p (h j) -> p h j", j=128)), writes=[("of", b)], dma=True)
            P.add("sp", lambda e, b=b, c=c: e.dma_start(out=obk[b][:], in_=k.DO[1, c].rearrange("p (h j) -> p h j", j=128)), writes=[("obk", b)], dma=True)
            P.add("sp", lambda e, b=b, c=c: e.dma_start(out=zz[b][:], in_=k.DZ[c].rearrange("p (h j) -> p h j", j=128)), writes=[("zz", b)], dma=True)
            P.add("dve", lambda e, b=b: e.tensor_tensor(out=of[b][:], in0=of[b][:], in1=obk[b][:], op=ALU.add),
                  reads=[("of", b), ("obk", b)], writes=[("of", b)])
            P.add("act", lambda e, b=b: e.activation(out=sq[:], in_=of[b][:], func=AF.Square), reads=[("of", b)], writes=["sq"])
            P.add("dve", lambda e: e.tensor_reduce(out=ss[:], in_=sq[:], axis=AX.X, op=ALU.add), reads=["sq"], writes=["ss"])
            P.add("act", lambda e: e.activation(out=ss[:], in_=ss[:], func=AF.Sqrt, bias=EPS, scale=1.0 / 128), reads=["ss"], writes=["ss"])
            P.add("dve", lambda e: e.reciprocal(ss[:], ss[:]), reads=["ss"], writes=["ss"])
            P.add("dve", lambda e, b=b: e.tensor_tensor(out=of[b][:], in0=of[b][:], in1=bc(ss[:].unsqueeze(2), [128, 32, 128]), op=ALU.mult),
                  reads=[("of", b), "ss"], writes=[("of", b)])
            P.add("dve", lambda e, b=b: e.tensor_tensor(out=of[b][:], in0=of[b][:], in1=bc(k.vecs[:, ng:ng + 128].unsqueeze(1), [128, 32, 128]), op=ALU.mult),
                  reads=[("of", b)], writes=[("of", b)])
            P.add("dve", lambda e, b=b: e.tensor_tensor(out=y[b][:], in0=of[b][:], in1=zz[b][:], op=ALU.mult),
                  reads=[("of", b), ("zz", b)], writes=[("y", b)])
            for h8 in range(4):
                tb = npt % 2; npt += 1
                for hh in range(8):
                    P.add("pe", lambda e, tb=tb, hh=hh, h8=h8, b=b: e.transpose(out=pst[tb][:, hh, :], in_=y[b][:, h8 * 8 + hh, :], identity=k.identb[:]),
                          reads=[("y", b)], writes=[("pst", tb)])
                P.add("act", lambda e, tb=tb, h8=h8, b=b: e.activation(out=yT[b][:, h8 * 8:(h8 + 1) * 8, :], in_=pst[tb][:], func=AF.Copy),
                      reads=[("pst", tb)], writes=[("yT", b, h8)])
            P.add("sp", lambda e, b=b, c=c: e.dma_start(out=k.DYT[:, :, c * 128:(c + 1) * 128].rearrange("h p t -> p h t"), in_=yT[b][:]),
                  reads=[("yT", b, h8) for h8 in range(4)], writes=[("DYT", c)], dma=True)


def dn_outproj(k, l):
    P = k.P
    i = l // 2
    hsrc = k.hT.rearrange("(c p) t -> p c t", p=128)
    HALF = NT // 2
    SUB = 384
    with contextlib.ExitStack() as st:
        cat = P.sb([128, 32, HALF], BF16, st)
        wb = [P.sb([128, 32, 256], BF16, st) for _ in range(2)]
        hrow = [P.sb([128, HALF], F32, st) for _ in range(2)]
        pso = [P.ps([128, 512], F32, st) for _ in range(4)]
        ysrc = k.DYT.rearrange("h p t -> p h t")
        nps = 0
        for half in range(2):
            h0 = half * HALF
            for q in range(4):
                P.add("sp", lambda e, q=q, h0=h0: e.dma_start(out=cat[:, q * 8:(q + 1) * 8, :], in_=ysrc[:, q * 8:(q + 1) * 8, h0:h0 + HALF]),
                      writes=[("cat", q)], dma=True)
            for blk in range(8):
                b = blk % 2
                for q in range(2):
                    P.add("pool", lambda e, b=b, blk=blk, q=q: e.dma_start(
                        out=wb[b][:, q * 16:(q + 1) * 16, :], in_=k.dn_w_out[i, blk][:, q * 16:(q + 1) * 16, :]),
                        writes=[("wb", b, q)], dma=True)
                for s in range(2):
                    c = blk * 2 + s
                    hb = c % 2
                    P.add("sp", lambda e, hb=hb, c=c, h0=h0: e.dma_start(out=hrow[hb][:], in_=hsrc[:, c, h0:h0 + HALF]),
                          reads=[("hT", c)], writes=[("hrow", hb)], dma=True)
                    for sub in range(HALF // SUB):
                        u0 = sub * SUB
                        pb = nps % 4; nps += 1
                        for kc in range(32):
                            P.add("pe", lambda e, b=b, s=s, pb=pb, kc=kc, u0=u0: e.matmul(
                                pso[pb][:, :SUB], wb[b][:, kc, s * 128:(s + 1) * 128], cat[:, kc, u0:u0 + SUB],
                                start=(kc == 0), stop=(kc == 31)),
                                reads=[("wb", b, kc // 16), ("cat", kc // 8)], writes=[("pso", pb)])
                        for (g0, gn, sidx) in segs(h0 + u0, SUB):
                            r0 = g0 - h0
                            P.add("dve", lambda e, hb=hb, pb=pb, r0=r0, gn=gn, u0=u0, c=c, sidx=sidx: e.scalar_tensor_tensor(
                                out=hrow[hb][:, r0:r0 + gn], in0=pso[pb][:, r0 - u0:r0 - u0 + gn], scalar=modv(k, 2, c, sidx),
                                in1=hrow[hb][:, r0:r0 + gn], op0=ALU.mult, op1=ALU.add),
                                reads=[("pso", pb), ("hrow", hb), "mod"], writes=[("hrow", hb)])
                    P.add("sp", lambda e, hb=hb, c=c, h0=h0: e.dma_start(out=hsrc[:, c, h0:h0 + HALF], in_=hrow[hb][:]),
                          reads=[("hrow", hb)], writes=[("hT", c)], dma=True)


FULL_PLAN = [(l, ("mix", "ffn")) for l in range(DEPTH)]


def tile_w(W, cols=None, width=256):
    if cols is not None:
        W = W[:, cols]
    K_, N_ = W.shape
    return np.ascontiguousarray(W.reshape(K_ // 128, 128, N_ // width, width).transpose(2, 1, 0, 3))


def tile_weights(inp):
    out = {}
    out["w_mod"] = np.stack([tile_w(inp["w_mod"][l]) for l in range(DEPTH)])
    up_cols = np.concatenate([np.concatenate([np.arange(j * 128, (j + 1) * 128), DFF + np.arange(j * 128, (j + 1) * 128)]) for j in range(NFF)])
    out["ffn_w_up"] = np.stack([tile_w(inp["ffn_w_up"][l], up_cols) for l in range(DEPTH)])
    out["ffn_w_down"] = np.stack([tile_w(inp["ffn_w_down"][l]) for l in range(DEPTH)])
    ev_cols = np.concatenate([np.arange(0, 1536)] + [np.concatenate([1536 + np.arange(j * 128, (j + 1) * 128), 2560 + np.arange(j * 128, (j + 1) * 128)])
                                                      for j in range(8)])
    out["even_w_in"] = np.stack([tile_w(inp["even_w_in"][i], ev_cols) for i in range(2)])
    out["even_w_out"] = np.stack([tile_w(inp["even_w_out"][i]) for i in range(2)])
    out["dn_w_in"] = np.stack([tile_w(inp["dn_w_in"][i][:, :12288]) for i in range(2)])
    out["dn_w_ba"] = np.stack([tile_w(inp["dn_w_in"][i][:, 12288:12416], width=128)[0] for i in range(2)])
    out["dn_w_out"] = np.stack([tile_w(inp["dn_w_out"][i]) for i in range(2)])
    return out


def make_core_inputs(inp, b, consts, tiled=None):
    h_in = np.concatenate([inp["ctx"][b].T, inp["x"][b].T], axis=1)
    return {
        "h_in": np.ascontiguousarray(h_in, dtype=np.float32),
        "vecs": pack_vecs(inp, b),
        "consts": consts[0], "rope": consts[1],
        **(tiled if tiled is not None else {}),
    }


def kernel(**inputs):
    inp = {k_: np.asarray(v) for k_, v in inputs.items()}
    consts = make_consts()
    nc = build_program(FULL_PLAN)
    tiled = tile_weights(inp)
    in_maps = [make_core_inputs(inp, c % 4, consts, tiled) for c in range(8)]
    res = run_bass_kernel_spmd(nc, in_maps, core_ids=list(range(8)))
    out = np.stack([res.results[b]["hT"][:, LCTX:].T for b in range(4)], axis=0)
    return np.ascontiguousarray(out.astype(np.float32))
```

```python
import contextlib
import numpy as np
import concourse.bass as bass
import concourse.mybir as mybir
from concourse.bass_utils import run_bass_kernel_spmd

F32 = mybir.dt.float32
BF16 = mybir.dt.bfloat16
AF = mybir.ActivationFunctionType
ALU = mybir.AluOpType
AX = mybir.AxisListType

D = 2048
NCH = 16
LCTX = 256
TLAT = 2048
NT = LCTX + TLAT
DEPTH = 4
DFF = 5632
NFF = DFF // 128
EPS = 1e-6
TILES = [(0, 256), (256, 512), (768, 512), (1280, 512), (1792, 512)]
EVEN_IN_W = 3584
DN_IN_W = 12416

ENGS = ("pe", "act", "dve", "pool", "sp")
NDSEM = 6


class Op:
    __slots__ = ("eng", "fn", "idx", "dma", "slot", "val", "deps", "needed", "cnt", "waits", "clock")


class Prog:
    def __init__(self, nc, same_engine_sync=True):
        self.nc = nc
        self.ops = {e: [] for e in ENGS}
        self.order = []
        self.last_w = {}
        self.readers = {}
        self.same_engine_sync = same_engine_sync
        self.dma_rr = {e: 0 for e in ENGS}
        self.dma_val = {}
        self.stack = contextlib.ExitStack()
        self.sems = {}
        self.dsems = {}
        self._n = 0

    def sb(self, shape, dtype, stack=None):
        self._n += 1
        return (stack or self.stack).enter_context(self.nc.sbuf_tensor(f"sb{self._n}", list(shape), dtype))

    def ps(self, shape, dtype, stack=None):
        self._n += 1
        return (stack or self.stack).enter_context(self.nc.psum_tensor(f"ps{self._n}", list(shape), dtype))

    def add(self, eng, fn, reads=(), writes=(), dma=False):
        op = Op()
        op.eng = eng; op.fn = fn; op.dma = dma; op.needed = False; op.waits = None
        op.idx = len(self.ops[eng]); op.slot = None; op.val = None
        deps = set()
        writes = list(writes)
        if dma:
            s = self.dma_rr[eng] % NDSEM
            self.dma_rr[eng] += 1
            op.slot = (eng, s)
            writes.append(("dsem", eng, s))
            self.dma_val[op.slot] = self.dma_val.get(op.slot, 0) + 16
            op.val = self.dma_val[op.slot]
        for r in reads:
            w = self.last_w.get(r)
            if w is not None:
                deps.add(w)
        for r in writes:
            w = self.last_w.get(r)
            if w is not None:
                deps.add(w)
            for rd in self.readers.get(r, ()):
                deps.add(rd)
        for r in writes:
            self.last_w[r] = op
            self.readers[r] = []
        for r in reads:
            self.readers.setdefault(r, []).append(op)
        deps.discard(op)
        op.deps = deps
        self.ops[eng].append(op)
        self.order.append(op)
        return op

    def barrier(self):
        lasts = []
        for e in ENGS:
            for o in reversed(self.ops[e]):
                if o.fn is not None and not o.dma:
                    lasts.append(o)
                    break
        latest = {}
        for e in ENGS:
            for o in reversed(self.ops[e]):
                if o.dma and o.slot not in latest:
                    latest[o.slot] = o
        for e in ENGS:
            op = Op()
            op.eng = e; op.fn = None; op.dma = False; op.needed = False; op.waits = None
            op.idx = len(self.ops[e]); op.slot = None; op.val = None
            op.deps = set(o for o in lasts if o.eng != e and o.fn is not None and not o.dma) | set(latest.values())
            self.ops[e].append(op)
            self.order.append(op)
        self.last_w.clear(); self.readers.clear()

    def emit(self):
        nc = self.nc
        seen = {e: {} for e in ENGS}
        for op in self.order:
            s = seen[op.eng]
            waits = []
            for d in sorted(op.deps, key=lambda o: (o.eng, o.idx)):
                if d.dma:
                    key = ("d",) + d.slot
                    if s.get(key, 0) >= d.val:
                        continue
                    waits.append(d)
                    s[key] = d.val
                else:
                    if d.fn is None:
                        continue
                    if d.eng == op.eng and (d.eng == "pe" or not self.same_engine_sync or op.fn is None):
                        continue
                    if s.get(d.eng, -1) >= d.idx:
                        continue
                    waits.append(d)
                    s[d.eng] = d.idx
                for k, v in d.clock.items():
                    if k == op.eng:
                        continue
                    if s.get(k, -1) < v:
                        s[k] = v
            for d in waits:
                d.needed = True
            op.waits = waits
            op.clock = dict(s)
        for e in ENGS:
            c = 0
            for op in self.ops[e]:
                if op.needed and not op.dma:
                    c += 1
                op.cnt = c
        st = self.stack
        for e in ENGS:
            self.sems[e] = st.enter_context(nc.semaphore(f"s_{e}"))
        for key in self.dma_val:
            self.dsems[key] = st.enter_context(nc.semaphore(f"d_{key[0]}{key[1]}"))
        block = st.enter_context(nc.Block())
        handles = {"pe": block.tensor, "act": block.scalar, "dve": block.vector, "pool": block.gpsimd, "sp": block.sync}

        def mk(e):
            def body(h):
                for op in self.ops[e]:
                    for d in op.waits:
                        if d.dma:
                            h.wait_ge(self.dsems[d.slot], d.val)
                        else:
                            h.wait_ge(self.sems[d.eng], d.cnt)
                    if op.fn is None:
                        continue
                    ins = op.fn(h)
                    if op.dma:
                        ins.then_inc(self.dsems[op.slot], 16)
                    elif op.needed:
                        ins.then_inc(self.sems[e], 1)
            return body
        for e in ENGS:
            handles[e](mk(e))

    def finish(self):
        self.barrier()
        self.emit()
        self.stack.close()


def _layout(items):
    off = {}
    o = 0
    for name, w in items:
        off[name] = (o, w)
        o += w
    return off, o


VEC_ITEMS = [
    ("bmod", 4 * 96), ("n1g", 64), ("n2g", 64), ("fcw", 4 * 3 * NFF), ("fcb", 4 * NFF),
    ("qg", 2), ("kg", 2), ("sink", 16), ("cdw", 2 * 31 * 8), ("cdb", 16), ("lng", 16), ("lnb", 16),
    ("dcw", 2 * 5 * 64), ("alog", 128), ("dtb", 128), ("dngrep", 256), ("c", 16), ("cctx", 16),
]
VOFF, NV = _layout(VEC_ITEMS)

CONST_ITEMS = [
    ("ident", 128), ("ropeT", 128), ("band", 384),
    ("m_le", 128), ("m_ge", 128), ("m_lt", 128), ("m_gt", 128),
    ("mblk", 5 * 128),
]
COFF, NCONST = _layout(CONST_ITEMS)


def fm(v):
    return np.ascontiguousarray(v.reshape(-1, 128).T)


def pack_vecs(inp, b):
    V = np.zeros((128, NV), np.float32)

    def put(name, arr):
        o, w = VOFF[name]
        assert arr.shape == (128, w), (name, arr.shape, w)
        V[:, o:o + w] = arr
    put("bmod", np.concatenate([fm(inp["b_mod"][l]) for l in range(4)], axis=1))
    put("n1g", np.concatenate([fm(inp["norm1_g"][l]) for l in range(4)], axis=1))
    put("n2g", np.concatenate([fm(inp["norm2_g"][l]) for l in range(4)], axis=1))
    put("fcw", np.concatenate([fm(inp["ffn_conv_w"][l, k]) for l in range(4) for k in range(3)], axis=1))
    put("fcb", np.concatenate([fm(inp["ffn_conv_b"][l]) for l in range(4)], axis=1))
    put("qg", inp["attn_q_norm_g"].T.copy())
    put("kg", inp["attn_k_norm_g"].T.copy())
    put("sink", np.broadcast_to(inp["attn_sink"].reshape(1, 16), (128, 16)).copy())
    put("cdw", np.concatenate([fm(inp["conv_dw_w"][i, k]) for i in range(2) for k in range(31)], axis=1))
    put("cdb", np.concatenate([fm(inp["conv_dw_b"][i]) for i in range(2)], axis=1))
    put("lng", np.concatenate([fm(inp["conv_ln_g"][i]) for i in range(2)], axis=1))
    put("lnb", np.concatenate([fm(inp["conv_ln_b"][i]) for i in range(2)], axis=1))
    put("dcw", np.concatenate([fm(inp["dn_conv_w"][i, k]) for i in range(2) for k in range(5)], axis=1))
    put("alog", np.broadcast_to(inp["dn_a_log"].reshape(1, 128), (128, 128)).copy())
    put("dtb", np.broadcast_to(inp["dn_dt_bias"].reshape(1, 128), (128, 128)).copy())
    put("dngrep", np.broadcast_to(inp["dn_norm_g"].reshape(1, 256), (128, 256)).copy())
    put("c", fm(inp["c"][b]))
    put("cctx", fm(inp["c_ctx"]))
    return V


def make_consts():
    C = np.zeros((128, NCONST), np.float32)

    def put(name, arr):
        o, w = COFF[name]
        assert arr.shape == (128, w), (name, arr.shape)
        C[:, o:o + w] = arr
    idx = np.arange(128)
    put("ident", np.eye(128, dtype=np.float32))
    R = np.zeros((128, 128), np.float32)
    for i in range(32):
        R[i, 32 + i] = -1.0
        R[32 + i, i] = 1.0
        R[64 + i, 96 + i] = -1.0
        R[96 + i, 64 + i] = 1.0
    put("ropeT", R.T.copy())
    kk = idx[:, None]; qq = idx[None, :]
    band = np.concatenate([(kk <= qq), np.ones((128, 128), bool), (qq <= kk)], axis=1).astype(np.float32)
    put("band", band)
    put("m_le", (kk <= qq).astype(np.float32))
    put("m_ge", (kk >= qq).astype(np.float32))
    put("m_lt", (kk < qq).astype(np.float32))
    put("m_gt", (kk > qq).astype(np.float32))
    mb = [(kk // 8 == qq // 8)]
    for bsz in (8, 16, 32, 64):
        mb.append((kk // (2 * bsz) == qq // (2 * bsz)) & (kk // bsz != qq // bsz))
    put("mblk", np.concatenate(mb, axis=1).astype(np.float32))
    GRID_W = 64
    rows = TLAT // GRID_W
    row = np.repeat(np.arange(rows, dtype=np.float32), GRID_W)
    col = np.tile(np.arange(GRID_W, dtype=np.float32), rows)
    half = 64
    inv_freq = (10000.0 ** (-np.arange(0, half, 2, dtype=np.float32) / half)).astype(np.float32)
    ang_r = row[:, None] * inv_freq
    ang_c = col[:, None] * inv_freq
    ang = np.concatenate([ang_r, ang_r, ang_c, ang_c], axis=-1)
    rope = np.concatenate([np.cos(ang).T, np.sin(ang).T], axis=1).astype(np.float32)
    return C, np.ascontiguousarray(rope)


class K:
    pass


def segs(t0, n):
    out = []
    if t0 < LCTX:
        e = min(t0 + n, LCTX)
        out.append((t0, e - t0, 1))
        if t0 + n > LCTX:
            out.append((LCTX, t0 + n - LCTX, 0))
    else:
        out.append((t0, n, 0))
    return out


def bc(ap, shape):
    return ap.broadcast_to(list(shape))


def stage_adaln(k, l):
    P = k.P
    with contextlib.ExitStack() as st:
        wb = [P.sb([128, 16, 256], BF16, st) for _ in range(2)]
        pm = P.ps([128, 96, 2], F32, st)
        for blk in range(48):
            b = blk % 2
            P.add("pool", lambda e, b=b, blk=blk: e.dma_start(out=wb[b][:], in_=k.w_mod[l, blk]),
                  writes=[("wb", b)], dma=True)
            for s in range(2):
                j = blk * 2 + s
                for kc in range(16):
                    P.add("pe", lambda e, b=b, s=s, j=j, kc=kc: e.matmul(
                        pm[:, j, :], wb[b][:, kc, s * 128:(s + 1) * 128], k.siluc[:, kc, :],
                        start=(kc == 0), stop=(kc == 15)),
                        reads=[("wb", b), "siluc"], writes=["pm"])
        bo = VOFF["bmod"][0] + l * 96
        P.add("dve", lambda e: e.tensor_tensor(out=k.mod[:], in0=pm[:], in1=bc(k.vecs[:, bo:bo + 96].unsqueeze(2), [128, 96, 2]),
                                               op=ALU.add), reads=["pm"], writes=["mod"])
        for (dst, m, gname) in ((k.A1, 1, "n1g"), (k.A2, 4, "n2g")):
            go = VOFF[gname][0] + l * 16
            P.add("dve", lambda e, dst=dst, m=m: e.tensor_scalar_add(out=dst[:], in0=k.mod[:, m * 16:(m + 1) * 16, :], scalar1=1.0),
                  reads=["mod"], writes=[("A", m)])
            P.add("dve", lambda e, dst=dst, go=go: e.tensor_tensor(
                out=dst[:], in0=dst[:], in1=bc(k.vecs[:, go:go + 16].unsqueeze(2), [128, 16, 2]), op=ALU.mult),
                reads=[("A", m)], writes=[("A", m)])
    P.barrier()


def modv(k, m, c, s):
    return k.mod[:, m * 16 + c, s:s + 1]


def stage_norm(k, which, nx, nbuf=2):
    P = k.P
    A = k.A1 if which == 1 else k.A2
    msh = 0 if which == 1 else 3
    hsrc = k.hT.rearrange("(c p) t -> p c t", p=128)
    with contextlib.ExitStack() as st:
        hs = [P.sb([128, 16, 512], F32, st) for _ in range(nbuf)]
        sq = P.sb([128, 16, 512], BF16, st)
        rstd = P.sb([128, 512], F32, st)
        pss = P.ps([128, 512], F32, st)
        for ti, (t0, n) in enumerate(TILES):
            b = ti % nbuf
            P.add("sp", lambda e, b=b, t0=t0, n=n: e.dma_start(out=hs[b][:, :, :n], in_=hsrc[:, :, t0:t0 + n]),
                  reads=["hT"], writes=[("hs", b)], dma=True)
            P.add("act", lambda e, b=b, n=n: e.activation(out=sq[:, :, :n], in_=hs[b][:, :, :n], func=AF.Square),
                  reads=[("hs", b)], writes=["sq"])
            for c in range(16):
                P.add("pe", lambda e, c=c, n=n: e.matmul(pss[:, :n], k.onesD[:], sq[:, c, :n], start=(c == 0), stop=(c == 15)),
                      reads=["sq"], writes=["pss"])
            P.add("act", lambda e, n=n: e.activation(out=rstd[:, :n], in_=pss[:, :n], func=AF.Sqrt, bias=EPS, scale=1.0),
                  reads=["pss"], writes=["rstd"])
            P.add("dve", lambda e, n=n: e.reciprocal(rstd[:, :n], rstd[:, :n]), reads=["rstd"], writes=["rstd"])
            P.add("dve", lambda e, b=b, n=n: e.tensor_tensor(out=hs[b][:, :, :n], in0=hs[b][:, :, :n],
                                                             in1=bc(rstd[:, :n].unsqueeze(1), [128, 16, n]), op=ALU.mult),
                  reads=["rstd", ("hs", b)], writes=[("hs", b)])
            s = 1 if t0 < LCTX else 0
            for c in range(16):
                P.add("act", lambda e, b=b, c=c, t0=t0, n=n, s=s: e.activation(
                    out=nx[:, c, t0:t0 + n], in_=hs[b][:, c, :n], func=AF.Identity,
                    bias=modv(k, msh, c, s), scale=A[:, c, s:s + 1]),
                    reads=[("hs", b), "mod", ("A", 1), ("A", 4)], writes=[("nx", ti, c)])


def stage_ffn(k, l):
    P = k.P
    with contextlib.ExitStack() as st0:
        nx = P.sb([128, 16, NT], BF16, st0)
        stage_norm(k, 2, nx)
        P.barrier()
        stage_ffn_up(k, l, nx)
        P.barrier()
    stage_ffn_down(k, l)


def stage_ffn_up(k, l, nx):
    P = k.P
    with contextlib.ExitStack() as st:
        wb = [P.sb([128, 16, 256], BF16, st) for _ in range(3)]
        G = [P.sb([128, NT], F32, st) for _ in range(2)]
        Vv = [P.sb([128, NT], BF16, st) for _ in range(2)]
        acc = [P.sb([128, NT], F32, st) for _ in range(2)]
        sg = [P.sb([128, NT], BF16, st) for _ in range(2)]
        hh = [P.sb([128, NT], BF16, st) for _ in range(2)]
        psg = [P.ps([128, 512], F32, st) for _ in range(2)]
        psv = [P.ps([128, 512], F32, st) for _ in range(2)]
        cw = VOFF["fcw"][0] + l * 3 * NFF
        cb = VOFF["fcb"][0] + l * NFF

        def make_post(j):
            b = j % 2
            Gk = [("G", b, ti) for ti in range(5)]
            Vk = [("V", b, ti) for ti in range(5)]
            w0 = k.vecs[:, cw + 0 * NFF + j:cw + 0 * NFF + j + 1]
            w1 = k.vecs[:, cw + 1 * NFF + j:cw + 1 * NFF + j + 1]
            w2 = k.vecs[:, cw + 2 * NFF + j:cw + 2 * NFF + j + 1]
            bb = k.vecs[:, cb + j:cb + j + 1]
            conv = []
            conv.append(lambda: P.add("dve", lambda e: e.tensor_scalar(out=acc[b][:], in0=G[b][:], scalar1=w1, scalar2=bb,
                                                                       op0=ALU.mult, op1=ALU.add), reads=Gk, writes=[("acc", b)]))
            for (s0, sn) in ((0, LCTX), (LCTX, TLAT)):
                conv.append(lambda s0=s0, sn=sn: P.add("dve", lambda e: e.scalar_tensor_tensor(
                    out=acc[b][:, s0 + 1:s0 + sn], in0=G[b][:, s0:s0 + sn - 1], scalar=w0, in1=acc[b][:, s0 + 1:s0 + sn],
                    op0=ALU.mult, op1=ALU.add), reads=Gk + [("acc", b)], writes=[("acc", b)]))
                conv.append(lambda s0=s0, sn=sn: P.add("dve", lambda e: e.scalar_tensor_tensor(
                    out=acc[b][:, s0:s0 + sn - 1], in0=G[b][:, s0 + 1:s0 + sn], scalar=w2, in1=acc[b][:, s0:s0 + sn - 1],
                    op0=ALU.mult, op1=ALU.add), reads=Gk + [("acc", b)], writes=[("acc", b)]))

            def tail():
                P.add("act", lambda e: e.activation(out=sg[b][:], in_=acc[b][:], func=AF.Silu), reads=[("acc", b)], writes=[("sg", b)])
                P.add("pool", lambda e: e.tensor_tensor(out=hh[b][:], in0=sg[b][:], in1=Vv[b][:], op=ALU.mult),
                      reads=[("sg", b)] + Vk, writes=[("hh", b)])
                P.add("sp", lambda e: e.dma_start(out=k.HF[j * 128:(j + 1) * 128, :], in_=hh[b][:]),
                      reads=[("hh", b)], writes=["HF"], dma=True)
            return conv, tail

        pending = None
        for j in range(NFF):
            b = j % 2
            wi = j % 3
            P.add("pool", lambda e, wi=wi, j=j: e.dma_start(out=wb[wi][:], in_=k.ffn_w_up[l, j]),
                  writes=[("wb", wi)], dma=True)
            conv_prev = list(pending[0]) if pending else []
            for ti, (t0, n) in enumerate(TILES):
                pb = ti % 2
                for kc in range(16):
                    P.add("pe", lambda e, wi=wi, pb=pb, kc=kc, t0=t0, n=n: e.matmul(
                        psg[pb][:, :n], wb[wi][:, kc, 0:128], nx[:, kc, t0:t0 + n], start=(kc == 0), stop=(kc == 15)),
                        reads=[("wb", wi)], writes=[("psg", pb)])
                for kc in range(16):
                    P.add("pe", lambda e, wi=wi, pb=pb, kc=kc, t0=t0, n=n: e.matmul(
                        psv[pb][:, :n], wb[wi][:, kc, 128:256], nx[:, kc, t0:t0 + n], start=(kc == 0), stop=(kc == 15)),
                        reads=[("wb", wi)], writes=[("psv", pb)])
                P.add("act", lambda e, b=b, pb=pb, t0=t0, n=n: e.activation(out=G[b][:, t0:t0 + n], in_=psg[pb][:, :n], func=AF.Copy),
                      reads=[("psg", pb)], writes=[("G", b, ti)])
                P.add("dve", lambda e, b=b, pb=pb, t0=t0, n=n: e.tensor_copy(Vv[b][:, t0:t0 + n], psv[pb][:, :n]),
                      reads=[("psv", pb)], writes=[("V", b, ti)])
                if conv_prev:
                    conv_prev.pop(0)()
            for c_ in conv_prev:
                c_()
            if pending:
                pending[1]()
            pending = make_post(j)
        for c_ in pending[0]:
            c_()
        pending[1]()


def stage_ffn_down(k, l):
    P = k.P
    HALF = NT // 2
    SUB = 384
    with contextlib.ExitStack() as st:
        hres = P.sb([128, NFF, HALF], BF16, st)
        wb = [P.sb([128, NFF, 256], BF16, st) for _ in range(2)]
        hrow = [P.sb([128, HALF], F32, st) for _ in range(2)]
        pso = [P.ps([128, 512], F32, st) for _ in range(4)]
        hfsrc = k.HF.rearrange("(j p) t -> p j t", p=128)
        hsrc = k.hT.rearrange("(c p) t -> p c t", p=128)
        nps = 0
        for half in range(2):
            h0 = half * HALF
            for q in range(4):
                P.add("sp", lambda e, q=q, h0=h0: e.dma_start(out=hres[:, q * 11:(q + 1) * 11, :], in_=hfsrc[:, q * 11:(q + 1) * 11, h0:h0 + HALF]),
                      reads=["HF"], writes=[("hres", q)], dma=True)
            for blk in range(8):
                b = blk % 2
                for q in range(2):
                    P.add("pool", lambda e, b=b, blk=blk, q=q: e.dma_start(
                        out=wb[b][:, q * 22:(q + 1) * 22, :], in_=k.ffn_w_down[l, blk][:, q * 22:(q + 1) * 22, :]),
                        writes=[("wb", b, q)], dma=True)
                for s in range(2):
                    c = blk * 2 + s
                    hb = c % 2
                    P.add("sp", lambda e, hb=hb, c=c, h0=h0: e.dma_start(out=hrow[hb][:], in_=hsrc[:, c, h0:h0 + HALF]),
                          reads=[("hT", c)], writes=[("hrow", hb)], dma=True)
                    for sub in range(HALF // SUB):
                        u0 = sub * SUB
                        pb = nps % 4
                        nps += 1
                        for kc in range(NFF):
                            P.add("pe", lambda e, b=b, s=s, pb=pb, kc=kc, u0=u0: e.matmul(
                                pso[pb][:, :SUB], wb[b][:, kc, s * 128:(s + 1) * 128], hres[:, kc, u0:u0 + SUB],
                                start=(kc == 0), stop=(kc == NFF - 1)),
                                reads=[("wb", b, kc // 22), ("hres", kc // 11)], writes=[("pso", pb)])
                        for (g0, gn, sidx) in segs(h0 + u0, SUB):
                            r0 = g0 - h0
                            P.add("dve", lambda e, hb=hb, pb=pb, r0=r0, gn=gn, u0=u0, c=c, sidx=sidx: e.scalar_tensor_tensor(
                                out=hrow[hb][:, r0:r0 + gn], in0=pso[pb][:, r0 - u0:r0 - u0 + gn], scalar=modv(k, 5, c, sidx),
                                in1=hrow[hb][:, r0:r0 + gn], op0=ALU.mult, op1=ALU.add),
                                reads=[("pso", pb), ("hrow", hb), "mod"], writes=[("hrow", hb)])
                    P.add("sp", lambda e, hb=hb, c=c, h0=h0: e.dma_start(out=hsrc[:, c, h0:h0 + HALF], in_=hrow[hb][:]),
                          reads=[("hrow", hb)], writes=[("hT", c)], dma=True)
    P.barrier()


DEBUG_OUT = ()


def build_program(plan, test_out_ctx=False):
    nc = bass.Bass("TRN2", target_bir_lowering=False)
    k = K()
    k.nc = nc
    P = Prog(nc)
    k.P = P
    def dt(name, shape, dtype, kind="Internal"):
        if name in DEBUG_OUT:
            kind = "ExternalOutput"
        return nc.dram_tensor(name, shape, dtype, kind=kind)
    k.h_in = dt("h_in", [D, NT], F32, kind="ExternalInput").ap()
    k.vecs_d = dt("vecs", [128, NV], F32, kind="ExternalInput").ap()
    k.consts_d = dt("consts", [128, NCONST], F32, kind="ExternalInput").ap()
    k.rope_d = dt("rope", [128, 2 * TLAT], F32, kind="ExternalInput").ap()
    k.w_mod = dt("w_mod", [DEPTH, 48, 128, 16, 256], F32, kind="ExternalInput").ap()
    k.ffn_w_up = dt("ffn_w_up", [DEPTH, NFF, 128, 16, 256], F32, kind="ExternalInput").ap()
    k.ffn_w_down = dt("ffn_w_down", [DEPTH, 8, 128, NFF, 256], F32, kind="ExternalInput").ap()
    k.even_w_in = dt("even_w_in", [2, 14, 128, 16, 256], F32, kind="ExternalInput").ap()
    k.even_w_out = dt("even_w_out", [2, 8, 128, 16, 256], F32, kind="ExternalInput").ap()
    k.dn_w_in = dt("dn_w_in", [2, 48, 128, 16, 256], F32, kind="ExternalInput").ap()
    k.dn_w_ba = dt("dn_w_ba", [2, 128, 16, 128], F32, kind="ExternalInput").ap()
    k.dn_w_out = dt("dn_w_out", [2, 8, 128, 32, 256], F32, kind="ExternalInput").ap()
    k.hT = dt("hT", [D, NT], F32, kind="ExternalOutput").ap()
    k.HF = dt("HF", [DFF, NT], BF16, kind="Internal").ap()
    k.QT = dt("QT", [8, 128, NT], BF16, kind="Internal").ap()
    k.KT = dt("KT", [2, 128, NT], BF16, kind="Internal").ap()
    k.VTM = dt("VTM", [18, 128, 256], BF16, kind="Internal").ap()
    k.CAT = dt("CAT", [16, 128, NT], BF16, kind="Internal").ap()
    k.DQT = dt("DQT", [16, 128, NT], BF16, kind="Internal").ap()
    k.DKT = dt("DKT", [16, 128, NT], BF16, kind="Internal").ap()
    k.DKTM = dt("DKTM", [18, 128, 2048], BF16, kind="Internal").ap()
    k.DVTM = dt("DVTM", [18, 128, 4096], BF16, kind="Internal").ap()
    k.DZ = dt("DZ", [18, 128, 4096], BF16, kind="Internal").ap()
    k.DPH = dt("DPH", [288, 128, 4, 4, 128], BF16, kind="Internal").ap()
    k.DO = dt("DO", [2, 18, 128, 4096], F32, kind="Internal").ap()
    k.DYT = dt("DYT", [32, 128, NT], BF16, kind="Internal").ap()
    k.DBG_G = dt("DBG_G", [5, 128, 18, 64], F32, kind="Internal").ap()

    k.vecs = P.sb([128, NV], F32)
    k.consts = P.sb([128, NCONST], F32)
    k.siluc = P.sb([128, 16, 2], BF16)
    k.mod = P.sb([128, 96, 2], F32)
    k.A1 = P.sb([128, 16, 2], F32)
    k.A2 = P.sb([128, 16, 2], F32)
    k.onesD = P.sb([128, 128], BF16)
    k.ones128 = P.sb([128, 128], BF16)
    k.ones1 = P.sb([128, 128], BF16)
    k.onesf128 = P.sb([128, 128], F32)
    k.bandb = P.sb([128, 384], BF16)
    k.identb = P.sb([128, 128], BF16)
    k.maskb = P.sb([128, 5, 128], BF16)

    P.add("sp", lambda e: e.dma_start(out=k.vecs[:], in_=k.vecs_d[:, :]), writes=["vecs"], dma=True)
    P.add("sp", lambda e: e.dma_start(out=k.consts[:], in_=k.consts_d[:, :]), writes=["consts"], dma=True)
    P.add("pool", lambda e: e.memset(k.onesD[:], 1.0 / D), writes=["onesD"])
    P.add("pool", lambda e: e.memset(k.ones128[:], 1.0 / 128), writes=["ones128"])
    P.add("pool", lambda e: e.memset(k.ones1[:], 1.0), writes=["ones1"])
    P.add("pool", lambda e: e.memset(k.onesf128[:], 1.0 / 128), writes=["onesf128"])
    P.add("dve", lambda e: e.tensor_copy(k.bandb[:], k.consts[:, COFF["band"][0]:COFF["band"][0] + 384]), reads=["consts"], writes=["bandb"])
    P.add("dve", lambda e: e.tensor_copy(k.maskb[:].rearrange("p q j -> p (q j)"), k.consts[:, COFF["mblk"][0]:COFF["mblk"][0] + 640]), reads=["consts"], writes=["maskb"])
    P.add("dve", lambda e: e.tensor_copy(k.identb[:], k.consts[:, COFF["ident"][0]:COFF["ident"][0] + 128]), reads=["consts"], writes=["identb"])
    co = VOFF["c"][0]
    P.add("act", lambda e: e.activation(out=k.siluc[:, :, 0], in_=k.vecs[:, co:co + 16], func=AF.Silu), reads=["vecs"], writes=["siluc"])
    co2 = VOFF["cctx"][0]
    P.add("act", lambda e: e.activation(out=k.siluc[:, :, 1], in_=k.vecs[:, co2:co2 + 16], func=AF.Silu), reads=["vecs", "siluc"], writes=["siluc"])
    with contextlib.ExitStack() as st:
        tmp = [P.sb([128, 4, NT], F32, st) for _ in range(2)]
        src = k.h_in.rearrange("(c p) t -> p c t", p=128)
        dst = k.hT.rearrange("(c p) t -> p c t", p=128)
        for q in range(4):
            b = q % 2
            P.add("sp", lambda e, b=b, q=q: e.dma_start(out=tmp[b][:], in_=src[:, q * 4:(q + 1) * 4, :]), writes=[("tmp", b)], dma=True)
            P.add("sp", lambda e, b=b, q=q: e.dma_start(out=dst[:, q * 4:(q + 1) * 4, :], in_=tmp[b][:]), reads=[("tmp", b)], writes=["hT"], dma=True)
    P.barrier()

    for (l, parts) in plan:
        stage_adaln(k, l)
        if "mix" in parts:
            if l % 2 == 0:
                stage_even(k, l)
            else:
                stage_dn(k, l)
        if "ffn" in parts:
            stage_ffn(k, l)
    P.finish()
    return nc


def stage_even(k, l):
    P = k.P
    with contextlib.ExitStack() as st0:
        nx = P.sb([128, 16, NT], BF16, st0)
        stage_norm(k, 1, nx)
        P.barrier()
        even_qkv(k, l, nx)
        P.barrier()
        even_glu(k, l, nx)
        P.barrier()
    even_attn(k, l)
    P.barrier()
    even_out(k, l)
    P.barrier()


def rsqrt_ops(P, out_ap, in_ap, reads, wkey):
    P.add("act", lambda e: e.activation(out=out_ap, in_=in_ap, func=AF.Sqrt, bias=EPS, scale=1.0), reads=reads, writes=[wkey])
    P.add("dve", lambda e: e.reciprocal(out_ap, out_ap), reads=[wkey], writes=[wkey])


def even_qkv(k, l, nx):
    P = k.P
    i = l // 2
    cos0 = 0; sin0 = TLAT; rp0 = COFF["ropeT"][0]
    with contextlib.ExitStack() as st:
        rope = P.sb([128, 2 * TLAT], F32, st)
        P.add("sp", lambda e: e.dma_start(out=rope[:], in_=k.rope_d[:, :]), writes=["rope"], dma=True)
        wb = [P.sb([128, 16, 256], BF16, st) for _ in range(2)]
        psa = [P.ps([128, 512], F32, st) for _ in range(2)]
        psb = P.ps([128, 512], F32, st)
        psc = P.ps([128, 512], F32, st)
        x0 = [P.sb([128, 512], F32, st) for _ in range(2)]
        sqb = P.sb([128, 512], BF16, st)
        rr = P.sb([128, 512], F32, st)
        qn = P.sb([128, 512], F32, st)
        t1 = P.sb([128, 512], F32, st)
        t2 = P.sb([128, 512], F32, st)
        qo = [P.sb([128, NT], BF16, st) for _ in range(2)]
        vt = P.sb([128, 18, 256], BF16, st)
        gs = P.sb([128, 2], F32, st)
        qg0 = VOFF["qg"][0] + i; kg0 = VOFF["kg"][0] + i
        P.add("act", lambda e: e.mul(gs[:, 0:1], k.vecs[:, qg0:qg0 + 1], 128 ** -0.5), writes=["gs"])
        P.add("act", lambda e: e.copy(gs[:, 1:2], k.vecs[:, kg0:kg0 + 1]), reads=["gs"], writes=["gs"])
        nblk = 0
        nchunk = 0
        npa = 0
        for blk in range(5):
            b = nblk % 2; nblk += 1
            P.add("pool", lambda e, b=b, blk=blk: e.dma_start(out=wb[b][:], in_=k.even_w_in[i, blk]),
                  writes=[("wb", b)], dma=True)
            for s in range(2):
                ch = blk * 2 + s
                isq = ch < 8
                qb = nchunk % 2; nchunk += 1
                gcol = gs[:, 0:1] if isq else gs[:, 1:2]
                for ti, (t0, n) in enumerate(TILES):
                    pb = npa % 2; npa += 1
                    for kc in range(16):
                        P.add("pe", lambda e, b=b, s=s, pb=pb, kc=kc, t0=t0, n=n: e.matmul(
                            psa[pb][:, :n], wb[b][:, kc, s * 128:(s + 1) * 128], nx[:, kc, t0:t0 + n], start=(kc == 0), stop=(kc == 15)),
                            reads=[("wb", b)], writes=[("psa", pb)])
                    P.add("act", lambda e, pb=pb, n=n: e.activation(out=x0[pb][:, :n], in_=psa[pb][:, :n], func=AF.Copy),
                          reads=[("psa", pb)], writes=[("x0", pb)])
                    P.add("act", lambda e, pb=pb, n=n: e.activation(out=sqb[:, :n], in_=psa[pb][:, :n], func=AF.Square),
                          reads=[("psa", pb)], writes=["sqb"])
                    P.add("pe", lambda e, n=n: e.matmul(psb[:, :n], k.ones128[:], sqb[:, :n], start=True, stop=True),
                          reads=["sqb"], writes=["psb"])
                    rsqrt_ops(P, rr[:, :n], psb[:, :n], ["psb"], "rr")
                    if t0 < LCTX:
                        P.add("dve", lambda e, pb=pb, qb=qb, n=n, t0=t0, gcol=gcol: e.scalar_tensor_tensor(
                            out=qo[qb][:, t0:t0 + n], in0=x0[pb][:, :n], scalar=gcol, in1=rr[:, :n], op0=ALU.mult, op1=ALU.mult),
                            reads=[("x0", pb), "rr", "gs"], writes=[("qo", qb, ti)])
                    else:
                        P.add("dve", lambda e, pb=pb, n=n, gcol=gcol: e.scalar_tensor_tensor(
                            out=qn[:, :n], in0=x0[pb][:, :n], scalar=gcol, in1=rr[:, :n], op0=ALU.mult, op1=ALU.mult),
                            reads=[("x0", pb), "rr", "gs"], writes=["qn"])
                        P.add("pe", lambda e, n=n: e.matmul(psc[:, :n], k.consts[:, rp0:rp0 + 128], qn[:, :n], start=True, stop=True),
                              reads=["qn"], writes=["psc"])
                        c0 = cos0 + t0 - LCTX; s0 = sin0 + t0 - LCTX
                        P.add("dve", lambda e, n=n, c0=c0: e.tensor_tensor(out=t1[:, :n], in0=qn[:, :n], in1=rope[:, c0:c0 + n], op=ALU.mult),
                              reads=["qn", "rope"], writes=["t1"])
                        P.add("dve", lambda e, n=n, s0=s0: e.tensor_tensor(out=t2[:, :n], in0=psc[:, :n], in1=rope[:, s0:s0 + n], op=ALU.mult),
                              reads=["psc", "rope"], writes=["t2"])
                        P.add("pool", lambda e, qb=qb, n=n, t0=t0: e.tensor_tensor(out=qo[qb][:, t0:t0 + n], in0=t1[:, :n], in1=t2[:, :n], op=ALU.add),
                              reads=["t1", "t2"], writes=[("qo", qb, ti)])
                dst = k.QT[ch] if isq else k.KT[ch - 8]
                P.add("sp", lambda e, qb=qb, dst=dst: e.dma_start(out=dst, in_=qo[qb][:]),
                      reads=[("qo", qb, ti) for ti in range(5)], writes=[("qkT", ch)], dma=True)
        b = nblk % 2; nblk += 1
        P.add("pool", lambda e, b=b: e.dma_start(out=wb[b][:], in_=k.even_w_in[i, 5]), writes=[("wb", b)], dma=True)
        for tt in range(18):
            pb = npa % 2; npa += 1
            for kc in range(16):
                P.add("pe", lambda e, b=b, pb=pb, kc=kc, tt=tt: e.matmul(
                    psa[pb][:, :256], nx[:, kc, tt * 128:(tt + 1) * 128], wb[b][:, kc, :], start=(kc == 0), stop=(kc == 15)),
                    reads=[("wb", b)], writes=[("psa", pb)])
            P.add("act", lambda e, pb=pb, tt=tt: e.activation(out=vt[:, tt, :], in_=psa[pb][:, :256], func=AF.Copy),
                  reads=[("psa", pb)], writes=[("vt", tt)])
        P.add("sp", lambda e: e.dma_start(out=k.VTM.rearrange("t p c -> p t c"), in_=vt[:]),
              reads=[("vt", tt) for tt in range(18)], writes=["VTM"], dma=True)


def even_glu(k, l, nx):
    P = k.P
    i = l // 2
    cdw = VOFF["cdw"][0] + i * 31 * 8
    cdb = VOFF["cdb"][0] + i * 8
    lng = VOFF["lng"][0] + i * 8
    lnb = VOFF["lnb"][0] + i * 8
    PADW = NT + 60

    def pos(t):
        return t + 15 if t < LCTX else t + 45
    with contextlib.ExitStack() as st:
        wb = [P.sb([128, 16, 256], BF16, st) for _ in range(2)]
        psa = [P.ps([128, 512], F32, st) for _ in range(2)]
        psb = [P.ps([128, 512], F32, st) for _ in range(2)]
        psc = [P.ps([128, 512], F32, st) for _ in range(2)]
        psm = P.ps([128, 512], F32, st)
        psq = P.ps([128, 512], F32, st)
        sgm = [P.sb([128, 512], F32, st) for _ in range(2)]
        Ub = [P.sb([128, PADW], BF16, st) for _ in range(2)]
        Dg = [P.sb([128, 31, 128], BF16, st) for _ in range(2)]
        accA = [P.sb([128, NT], F32, st) for _ in range(2)]
        hsq = P.sb([128, NT], F32, st)
        co = [P.sb([128, NT], BF16, st) for _ in range(2)]
        msb = P.sb([128, 512], F32, st)
        m2 = P.sb([128, 512], F32, st)
        var = P.sb([128, 512], F32, st)
        dd = P.sb([128, 512], F32, st)
        for b in range(2):
            P.add("pool", lambda e, b=b: e.memset(Ub[b][:], 0.0), writes=[("Ub", b, ti) for ti in range(5)] + [("Ubz", b)])

        def proj(j):
            b = j % 2
            P.add("pool", lambda e: e.dma_start(out=wb[b][:], in_=k.even_w_in[i, 6 + j]), writes=[("wb", b)], dma=True)
            for kk in range(31):
                w = k.vecs[:, cdw + kk * 8 + j:cdw + kk * 8 + j + 1]
                P.add("dve", lambda e, kk=kk, w=w: e.tensor_scalar(out=Dg[b][:, kk, :], in0=k.identb[:], scalar1=w, scalar2=None, op0=ALU.mult),
                      writes=[("Dg", b, kk)])
            for ti, (t0, n) in enumerate(TILES):
                pb = ti % 2
                for kc in range(16):
                    P.add("pe", lambda e, pb=pb, kc=kc, t0=t0, n=n: e.matmul(
                        psa[pb][:, :n], wb[b][:, kc, 0:128], nx[:, kc, t0:t0 + n], start=(kc == 0), stop=(kc == 15)),
                        reads=[("wb", b)], writes=[("psa", pb)])
                for kc in range(16):
                    P.add("pe", lambda e, pb=pb, kc=kc, t0=t0, n=n: e.matmul(
                        psb[pb][:, :n], wb[b][:, kc, 128:256], nx[:, kc, t0:t0 + n], start=(kc == 0), stop=(kc == 15)),
                        reads=[("wb", b)], writes=[("psb", pb)])
                P.add("act", lambda e, pb=pb, n=n: e.activation(out=sgm[pb][:, :n], in_=psb[pb][:, :n], func=AF.Sigmoid),
                      reads=[("psb", pb)], writes=[("sgm", pb)])
                p0 = pos(t0)
                P.add("dve", lambda e, pb=pb, p0=p0, n=n: e.tensor_tensor(out=Ub[b][:, p0:p0 + n], in0=psa[pb][:, :n], in1=sgm[pb][:, :n], op=ALU.mult),
                      reads=[("psa", pb), ("sgm", pb), ("Ubz", b)], writes=[("Ub", b, ti)])

        def conv(j):
            b = j % 2
            Uk = [("Ub", b, ti) for ti in range(5)] + [("Ubz", b)]
            bias = k.vecs[:, cdb + j:cdb + j + 1]
            for ti, (t0, n) in enumerate(TILES):
                pb = ti % 2
                p0 = pos(t0)
                for kk in range(31):
                    P.add("pe", lambda e, pb=pb, kk=kk, p0=p0, n=n: e.matmul(
                        psc[pb][:, :n], Dg[b][:, kk, :], Ub[b][:, p0 + kk - 15:p0 + kk - 15 + n], start=(kk == 0), stop=(kk == 30)),
                        reads=Uk + [("Dg", b, kk)], writes=[("psc", pb)])
                P.add("act", lambda e, pb=pb, t0=t0, n=n: e.activation(out=accA[b][:, t0:t0 + n], in_=psc[pb][:, :n], func=AF.Identity,
                                                                      bias=bias, scale=1.0),
                      reads=[("psc", pb)], writes=[("accA", b, ti)])

        def lnorm(j):
            b = j % 2
            Ak = [("accA", b, ti) for ti in range(5)]
            P.add("act", lambda e: e.activation(out=hsq[:], in_=accA[b][:], func=AF.Square), reads=Ak, writes=["hsq"])
            for ti, (t0, n) in enumerate(TILES):
                P.add("pe", lambda e, t0=t0, n=n: e.matmul(psm[:, :n], k.onesf128[:], accA[b][:, t0:t0 + n], start=True, stop=True),
                      reads=Ak, writes=["psm"])
                P.add("pe", lambda e, t0=t0, n=n: e.matmul(psq[:, :n], k.onesf128[:], hsq[:, t0:t0 + n], start=True, stop=True),
                      reads=["hsq"], writes=["psq"])
                P.add("act", lambda e, n=n: e.activation(out=msb[:, :n], in_=psm[:, :n], func=AF.Copy), reads=["psm"], writes=["msb"])
                P.add("act", lambda e, n=n: e.activation(out=m2[:, :n], in_=psm[:, :n], func=AF.Square), reads=["psm"], writes=["m2"])
                P.add("dve", lambda e, n=n: e.tensor_tensor(out=var[:, :n], in0=psq[:, :n], in1=m2[:, :n], op=ALU.subtract),
                      reads=["psq", "m2"], writes=["var"])
                P.add("dve", lambda e, n=n: e.tensor_scalar_max(out=var[:, :n], in0=var[:, :n], scalar1=0.0), reads=["var"], writes=["var"])
                rsqrt_ops(P, var[:, :n], var[:, :n], ["var"], "var")
                P.add("dve", lambda e, t0=t0, n=n: e.tensor_tensor(out=dd[:, :n], in0=accA[b][:, t0:t0 + n], in1=msb[:, :n], op=ALU.subtract),
                      reads=Ak + ["msb"], writes=["dd"])
                P.add("dve", lambda e, n=n: e.tensor_tensor(out=dd[:, :n], in0=dd[:, :n], in1=var[:, :n], op=ALU.mult),
                      reads=["dd", "var"], writes=["dd"])
                P.add("act", lambda e, t0=t0, n=n: e.activation(
                    out=co[b][:, t0:t0 + n], in_=dd[:, :n], func=AF.Silu,
                    bias=k.vecs[:, lnb + j:lnb + j + 1], scale=k.vecs[:, lng + j:lng + j + 1]),
                    reads=["dd"], writes=[("co", b, ti)])
            P.add("sp", lambda e: e.dma_start(out=k.CAT[8 + j], in_=co[b][:]),
                  reads=[("co", b, ti) for ti in range(5)], writes=[("CAT", 8 + j)], dma=True)

        for j in range(8):
            proj(j)
            if j >= 1:
                lnorm(j - 1)
            conv(j)
        lnorm(7)


def even_attn(k, l):
    P = k.P
    i = l // 2
    band0 = COFF["band"][0]
    with contextlib.ExitStack() as st:
        kT = P.sb([128, 2, NT], BF16, st)
        V = P.sb([128, 18, 256], BF16, st)
        qh = [P.sb([128, NT], BF16, st) for _ in range(2)]
        PC = [[P.sb([128, NT], BF16, st) for _ in range(2)] for _ in range(2)]
        PB = [P.sb([128, 16, 384], BF16, st) for _ in range(2)]
        tmpE = [P.sb([128, 384], BF16, st) for _ in range(2)]
        ao = [P.sb([128, NT], BF16, st) for _ in range(2)]
        den = P.sb([128, 512], F32, st)
        esink = P.sb([128, 8], F32, st)
        pss = [P.ps([128, 512], F32, st) for _ in range(2)]
        pso = [P.ps([128, 512], F32, st) for _ in range(2)]
        psd = [P.ps([128, 512], F32, st) for _ in range(2)]
        so = VOFF["sink"][0] + i * 8
        P.add("act", lambda e: e.activation(out=esink[:], in_=k.vecs[:, so:so + 8], func=AF.Exp), writes=["esink"])
        for g in range(2):
            P.add("sp", lambda e, g=g: e.dma_start(out=kT[:, g, :], in_=k.KT[g]), writes=[("kT", g)], dma=True)
        P.add("sp", lambda e: e.dma_start(out=V[:], in_=k.VTM.rearrange("t p c -> p t c")), writes=["V"], dma=True)
        nps = 0
        npo = 0
        for h in range(8):
            g = h // 4
            hb = h % 2
            P.add("sp", lambda e, hb=hb, h=h: e.dma_start(out=qh[hb][:], in_=k.QT[h]), writes=[("qh", hb)], dma=True)
            for cb in range(2):
                for ti, (t0, n) in enumerate(TILES):
                    pb = nps % 2; nps += 1
                    P.add("pe", lambda e, pb=pb, g=g, cb=cb, hb=hb, t0=t0, n=n: e.matmul(
                        pss[pb][:, :n], kT[:, g, cb * 128:(cb + 1) * 128], qh[hb][:, t0:t0 + n], start=True, stop=True),
                        reads=[("kT", g), ("qh", hb)], writes=[("pss", pb)])
                    P.add("act", lambda e, pb=pb, hb=hb, cb=cb, t0=t0, n=n: e.activation(
                        out=PC[hb][cb][:, t0:t0 + n], in_=pss[pb][:, :n], func=AF.Exp),
                        reads=[("pss", pb)], writes=[("PC", hb, cb, ti)])
            for jb in range(16):
                lo = max(jb - 1, 0); hi = min(jb + 1, 15)
                n = (hi - lo + 1) * 128
                off = (lo - (jb - 1)) * 128
                pb = nps % 2; nps += 1
                eb = jb % 2
                P.add("pe", lambda e, pb=pb, g=g, jb=jb, hb=hb, lo=lo, n=n: e.matmul(
                    pss[pb][:, :n], kT[:, g, LCTX + jb * 128:LCTX + (jb + 1) * 128], qh[hb][:, LCTX + lo * 128:LCTX + lo * 128 + n],
                    start=True, stop=True), reads=[("kT", g), ("qh", hb)], writes=[("pss", pb)])
                P.add("act", lambda e, pb=pb, eb=eb, n=n: e.activation(out=tmpE[eb][:, :n], in_=pss[pb][:, :n], func=AF.Exp),
                      reads=[("pss", pb)], writes=[("tmpE", eb)])
                P.add("pool", lambda e, eb=eb, hb=hb, jb=jb, off=off, n=n: e.tensor_tensor(
                    out=PB[hb][:, jb, off:off + n], in0=tmpE[eb][:, :n], in1=k.bandb[:, off:off + n], op=ALU.mult),
                    reads=[("tmpE", eb)], writes=[("PB", hb, jb)])
            for ti, (t0, n) in enumerate(TILES):
                ob = npo % 2; npo += 1
                mms = []
                for cb in range(2):
                    mms.append((V[:, cb, g * 128:(g + 1) * 128], PC[hb][cb][:, t0:t0 + n], 0, n, [("PC", hb, cb, ti)]))
                if t0 >= LCTX:
                    qt = (t0 - LCTX) // 512
                    for nb in range(4 * qt, 4 * qt + 4):
                        for jb in (nb - 1, nb, nb + 1):
                            if 0 <= jb <= 15:
                                c0 = (nb - (jb - 1)) * 128
                                mms.append((V[:, 2 + jb, g * 128:(g + 1) * 128], PB[hb][:, jb, c0:c0 + 128], (nb - 4 * qt) * 128, 128,
                                            [("PB", hb, jb)]))
                for mi, (lhs, rhs, o0, on, rk) in enumerate(mms):
                    P.add("pe", lambda e, ob=ob, lhs=lhs, rhs=rhs, o0=o0, on=on, mi=mi, last=(mi == len(mms) - 1): e.matmul(
                        pso[ob][:, o0:o0 + on], lhs, rhs, start=(mi == 0), stop=last, skip_group_check=True),
                        reads=rk + ["V"], writes=[("pso", ob)])
                for mi, (lhs, rhs, o0, on, rk) in enumerate(mms):
                    P.add("pe", lambda e, ob=ob, rhs=rhs, o0=o0, on=on, mi=mi, last=(mi == len(mms) - 1): e.matmul(
                        psd[ob][:, o0:o0 + on], k.ones1[:], rhs, start=(mi == 0), stop=last, skip_group_check=True),
                        reads=rk, writes=[("psd", ob)])
                P.add("dve", lambda e, ob=ob, n=n, h=h: e.tensor_scalar_add(out=den[:, :n], in0=psd[ob][:, :n], scalar1=esink[:, h:h + 1]),
                      reads=[("psd", ob), "esink"], writes=["den"])
                P.add("dve", lambda e, n=n: e.reciprocal(den[:, :n], den[:, :n]), reads=["den"], writes=["den"])
                P.add("dve", lambda e, ob=ob, hb=hb, t0=t0, n=n: e.tensor_tensor(out=ao[hb][:, t0:t0 + n], in0=pso[ob][:, :n], in1=den[:, :n], op=ALU.mult),
                      reads=[("pso", ob), "den"], writes=[("ao", hb, ti)])
            P.add("sp", lambda e, hb=hb, h=h: e.dma_start(out=k.CAT[h], in_=ao[hb][:]),
                  reads=[("ao", hb, ti) for ti in range(5)], writes=[("CAT", h)], dma=True)


def proj_out_residual(k, wsrc, nkc, cat, gate_m, wkeys_extra=()):
    P = k.P
    hsrc = k.hT.rearrange("(c p) t -> p c t", p=128)
    with contextlib.ExitStack() as st:
        wb = [P.sb([128, nkc, 256], BF16, st) for _ in range(2)]
        hrow = [P.sb([128, NT], F32, st) for _ in range(2)]
        pso = [P.ps([128, 512], F32, st) for _ in range(4)]
        nps = 0
        for blk in range(8):
            b = blk % 2
            P.add("pool", lambda e, b=b, blk=blk: e.dma_start(out=wb[b][:], in_=wsrc(blk)),
                  writes=[("wb", b)], dma=True)
            for s in range(2):
                c = blk * 2 + s
                hb = c % 2
                P.add("sp", lambda e, hb=hb, c=c: e.dma_start(out=hrow[hb][:], in_=hsrc[:, c, :]),
                      reads=[("hT", c)], writes=[("hrow", hb)], dma=True)
                for ti, (t0, n) in enumerate(TILES):
                    pb = nps % 4; nps += 1
                    for kc in range(nkc):
                        P.add("pe", lambda e, b=b, s=s, pb=pb, kc=kc, t0=t0, n=n: e.matmul(
                            pso[pb][:, :n], wb[b][:, kc, s * 128:(s + 1) * 128], cat[:, kc, t0:t0 + n],
                            start=(kc == 0), stop=(kc == nkc - 1)),
                            reads=[("wb", b), ("cat", kc)], writes=[("pso", pb)])
                    sidx = 1 if t0 < LCTX else 0
                    P.add("dve", lambda e, hb=hb, pb=pb, t0=t0, n=n, c=c, sidx=sidx: e.scalar_tensor_tensor(
                        out=hrow[hb][:, t0:t0 + n], in0=pso[pb][:, :n], scalar=modv(k, gate_m, c, sidx),
                        in1=hrow[hb][:, t0:t0 + n], op0=ALU.mult, op1=ALU.add),
                        reads=[("pso", pb), ("hrow", hb), "mod"], writes=[("hrow", hb)])
                P.add("sp", lambda e, hb=hb, c=c: e.dma_start(out=hsrc[:, c, :], in_=hrow[hb][:]),
                      reads=[("hrow", hb)], writes=[("hT", c)], dma=True)


def even_out(k, l):
    P = k.P
    i = l // 2
    wsrc = lambda blk: k.even_w_out[i, blk]
    with contextlib.ExitStack() as st:
        cat = P.sb([128, 16, NT], BF16, st)
        for c in range(16):
            P.add("sp", lambda e, c=c: e.dma_start(out=cat[:, c, :], in_=k.CAT[c]), writes=[("cat", c)], dma=True)
        proj_out_residual(k, wsrc, 16, cat, 2)


DN_STOP = 99


def stage_dn(k, l):
    P = k.P
    with contextlib.ExitStack() as st1:
        k.BETA = P.sb([128, 18, 64], F32, st1)
        k.GG = P.sb([128, 18, 64], F32, st1)
        k.EG = P.sb([128, 18, 64], F32, st1)
        k.EL = P.sb([128, 18, 64], F32, st1)
        k.BEG = P.sb([128, 18, 64], F32, st1)
        k.ER = P.sb([128, 18, 64], F32, st1)
        with contextlib.ExitStack() as st0:
            nx = P.sb([128, 16, NT], BF16, st0)
            stage_norm(k, 1, nx, nbuf=1)
            P.barrier()
            if DN_STOP >= 1:
                dn_proj_qkv(k, l, nx)
                P.barrier()
            if DN_STOP >= 2:
                dn_proj_z(k, l, nx)
                P.barrier()
            if DN_STOP >= 3:
                dn_proj_ba(k, l, nx)
                P.barrier()
        if DN_STOP >= 4:
            dn_gates(k, l)
            P.barrier()
            if "DBG_G" in DEBUG_OUT:
                for ii, t in enumerate((k.BETA, k.GG, k.EG, k.EL, k.ER)):
                    P.add("sp", lambda e, ii=ii, t=t: e.dma_start(out=k.DBG_G[ii], in_=t[:]), dma=True)
                P.barrier()
        if DN_STOP >= 5:
            dn_prep(k, l)
            P.barrier()
        if DN_STOP >= 6:
            dn_scan(k, l)
            P.barrier()
    if DN_STOP >= 7:
        dn_outnorm(k, l)
        P.barrier()
    if DN_STOP >= 8:
        dn_outproj(k, l)
        P.barrier()


def dn_proj_qkv(k, l, nx):
    P = k.P
    i = l // 2
    dcw = VOFF["dcw"][0] + i * 5 * 64
    PADW = NT + 8

    def pos(t):
        return t + 2 if t < LCTX else t + 6
    with contextlib.ExitStack() as st:
        wb = [P.sb([128, 16, 256], BF16, st) for _ in range(2)]
        psa = [P.ps([128, 512], F32, st) for _ in range(2)]
        psc = [P.ps([128, 512], F32, st) for _ in range(2)]
        psb = P.ps([128, 512], F32, st)
        pst = [P.ps([128, 4, 128], BF16, st) for _ in range(2)]
        Ub = [P.sb([128, PADW], BF16, st) for _ in range(2)]
        Dg = [P.sb([128, 5, 128], BF16, st) for _ in range(2)]
        accs = [P.sb([128, NT], F32, st) for _ in range(2)]
        sq = P.sb([128, NT], BF16, st)
        rr = P.sb([128, 512], F32, st)
        qo = [P.sb([128, NT], BF16, st) for _ in range(2)]
        tm = P.sb([128, 18, 128], BF16, st)
        for b in range(2):
            P.add("pool", lambda e, b=b: e.memset(Ub[b][:], 0.0), writes=[("Ub", b, ti) for ti in range(5)] + [("Ubz", b)])
        npa = 0
        npt = 0
        for blk in range(32):
            b = blk % 2
            P.add("pool", lambda e, b=b, blk=blk: e.dma_start(out=wb[b][:], in_=k.dn_w_in[i, blk]), writes=[("wb", b)], dma=True)
            for s in range(2):
                ch = blk * 2 + s
                ub = ch % 2
                acc = accs[ch % 2]
                ak = ("acc", ch % 2)
                qb = ch % 2
                kind = "q" if ch < 16 else ("k" if ch < 32 else "v")
                for kk in range(5):
                    w = k.vecs[:, dcw + kk * 64 + ch:dcw + kk * 64 + ch + 1]
                    P.add("dve", lambda e, ub=ub, kk=kk, w=w: e.tensor_scalar(out=Dg[ub][:, kk, :], in0=k.identb[:], scalar1=w, scalar2=None, op0=ALU.mult),
                          writes=[("Dg", ub, kk)])
                for ti, (t0, n) in enumerate(TILES):
                    pb = npa % 2; npa += 1
                    for kc in range(16):
                        P.add("pe", lambda e, b=b, s=s, pb=pb, kc=kc, t0=t0, n=n: e.matmul(
                            psa[pb][:, :n], wb[b][:, kc, s * 128:(s + 1) * 128], nx[:, kc, t0:t0 + n], start=(kc == 0), stop=(kc == 15)),
                            reads=[("wb", b)], writes=[("psa", pb)])
                    p0 = pos(t0)
                    P.add("act", lambda e, ub=ub, pb=pb, p0=p0, n=n: e.activation(out=Ub[ub][:, p0:p0 + n], in_=psa[pb][:, :n], func=AF.Copy),
                          reads=[("psa", pb), ("Ubz", ub)], writes=[("Ub", ub, ti)])
                Uk = [("Ub", ub, ti) for ti in range(5)] + [("Ubz", ub)]
                for ti, (t0, n) in enumerate(TILES):
                    pc = ti % 2
                    p0 = pos(t0)
                    for kk in range(5):
                        P.add("pe", lambda e, ub=ub, pc=pc, kk=kk, p0=p0, n=n: e.matmul(
                            psc[pc][:, :n], Dg[ub][:, kk, :], Ub[ub][:, p0 + kk - 2:p0 + kk - 2 + n], start=(kk == 0), stop=(kk == 4)),
                            reads=Uk + [("Dg", ub, kk)], writes=[("psc", pc)])
                    if kind == "v":
                        P.add("act", lambda e, qb=qb, pc=pc, t0=t0, n=n: e.activation(out=qo[qb][:, t0:t0 + n], in_=psc[pc][:, :n], func=AF.Silu),
                              reads=[("psc", pc)], writes=[("qo", qb)])
                    else:
                        P.add("act", lambda e, acc=acc, pc=pc, t0=t0, n=n: e.activation(out=acc[:, t0:t0 + n], in_=psc[pc][:, :n], func=AF.Silu),
                              reads=[("psc", pc)], writes=[ak])
                if kind != "v":
                    P.add("act", lambda e, acc=acc: e.activation(out=sq[:], in_=acc[:], func=AF.Square), reads=[ak], writes=["sq"])
                    for ti, (t0, n) in enumerate(TILES):
                        P.add("pe", lambda e, t0=t0, n=n: e.matmul(psb[:, :n], k.ones1[:], sq[:, t0:t0 + n], start=True, stop=True),
                              reads=["sq"], writes=["psb"])
                        rsqrt_ops(P, rr[:, :n], psb[:, :n], ["psb"], "rr")
                        sc = (128 ** -0.5) if kind == "q" else 1.0
                        P.add("dve", lambda e, acc=acc, qb=qb, t0=t0, n=n, sc=sc: e.scalar_tensor_tensor(
                            out=qo[qb][:, t0:t0 + n], in0=acc[:, t0:t0 + n], scalar=sc, in1=rr[:, :n], op0=ALU.mult, op1=ALU.mult),
                            reads=[ak, "rr"], writes=[("qo", qb)])
                if kind == "q":
                    P.add("sp", lambda e, qb=qb, ch=ch: e.dma_start(out=k.DQT[ch], in_=qo[qb][:]), reads=[("qo", qb)], writes=[("DQT", ch)], dma=True)
                else:
                    if kind == "k":
                        P.add("sp", lambda e, qb=qb, ch=ch: e.dma_start(out=k.DKT[ch - 16], in_=qo[qb][:]), reads=[("qo", qb)],
                              writes=[("DKT", ch)], dma=True)
                    for c4 in range(0, 18, 4):
                        nn = min(4, 18 - c4)
                        tb = npt % 2; npt += 1
                        for cc in range(nn):
                            c = c4 + cc
                            P.add("pe", lambda e, tb=tb, cc=cc, c=c, qb=qb: e.transpose(
                                out=pst[tb][:, cc, :], in_=qo[qb][:, c * 128:(c + 1) * 128], identity=k.identb[:]),
                                reads=[("qo", qb)], writes=[("pst", tb)])
                        P.add("act", lambda e, tb=tb, c4=c4, nn=nn: e.activation(out=tm[:, c4:c4 + nn, :], in_=pst[tb][:, :nn, :], func=AF.Copy),
                              reads=[("pst", tb)], writes=[("tm", 0)])
                    if kind == "k":
                        dst = k.DKTM[:, :, (ch - 16) * 128:(ch - 15) * 128]
                    else:
                        dst = k.DVTM[:, :, (ch - 32) * 128:(ch - 31) * 128]
                    P.add("sp", lambda e, dst=dst: e.dma_start(out=dst.rearrange("c p d -> p c d"), in_=tm[:]),
                          reads=[("tm", 0)], writes=[("DTM", ch)], dma=True)


def dn_proj_z(k, l, nx):
    P = k.P
    i = l // 2
    with contextlib.ExitStack() as st:
        wb = [P.sb([128, 16, 512], BF16, st) for _ in range(2)]
        psa = [P.ps([128, 512], F32, st) for _ in range(4)]
        zs = [P.sb([128, 512], BF16, st) for _ in range(4)]
        npa = 0
        for blk in range(8):
            b = blk % 2
            for q in range(2):
                P.add("pool", lambda e, b=b, blk=blk, q=q: e.dma_start(
                    out=wb[b][:, :, q * 256:(q + 1) * 256], in_=k.dn_w_in[i, 32 + blk * 2 + q]),
                    writes=[("wb", b, q)], dma=True)
            for c in range(18):
                pb = npa % 4; npa += 1
                for kc in range(16):
                    P.add("pe", lambda e, b=b, pb=pb, kc=kc, c=c: e.matmul(
                        psa[pb][:], nx[:, kc, c * 128:(c + 1) * 128], wb[b][:, kc, :], start=(kc == 0), stop=(kc == 15)),
                        reads=[("wb", b, 0), ("wb", b, 1)], writes=[("psa", pb)])
                P.add("act", lambda e, pb=pb: e.activation(out=zs[pb][:], in_=psa[pb][:], func=AF.Silu),
                      reads=[("psa", pb)], writes=[("zs", pb)])
                P.add("sp", lambda e, pb=pb, c=c, blk=blk: e.dma_start(out=k.DZ[c, :, blk * 512:(blk + 1) * 512], in_=zs[pb][:]),
                      reads=[("zs", pb)], writes=[("DZ", c, blk)], dma=True)


def dn_proj_ba(k, l, nx):
    P = k.P
    i = l // 2
    with contextlib.ExitStack() as st:
        wb = P.sb([128, 16, 128], BF16, st)
        BA = P.sb([128, 18, 128], F32, st)
        psa = [P.ps([128, 512], F32, st) for _ in range(2)]
        P.add("pool", lambda e: e.dma_start(out=wb[:], in_=k.dn_w_ba[i]), writes=["wb"], dma=True)
        for c in range(18):
            pb = c % 2
            for kc in range(16):
                P.add("pe", lambda e, pb=pb, kc=kc, c=c: e.matmul(
                    psa[pb][:, 0:128], nx[:, kc, c * 128:(c + 1) * 128], wb[:, kc, :], start=(kc == 0), stop=(kc == 15)),
                    reads=["wb"], writes=[("psa", pb)])
            P.add("act", lambda e, pb=pb, c=c: e.activation(out=BA[:, c, :], in_=psa[pb][:, 0:128], func=AF.Copy),
                  reads=[("psa", pb)], writes=[("BA", c)])
        BAk = [("BA", c) for c in range(18)]
        P.add("act", lambda e: e.activation(out=k.BETA[:], in_=BA[:, :, 0:64], func=AF.Sigmoid), reads=BAk, writes=["BETA"])
        P.add("dve", lambda e: e.tensor_copy(k.GG[:], BA[:, :, 64:128]), reads=BAk, writes=["GG"])
        P.barrier()


def dn_gates(k, l):
    P = k.P
    i = l // 2
    al = VOFF["alog"][0] + i * 64
    db = VOFF["dtb"][0] + i * 64
    with contextlib.ExitStack() as st:
        x = P.sb([128, 18, 64], F32, st)
        ax = P.sb([128, 18, 64], F32, st)
        nA = P.sb([128, 64], F32, st)
        onesf = P.sb([128, 128], F32, st)
        ps = [P.ps([128, 3, 64], F32, st) for _ in range(2)]
        P.add("pool", lambda e: e.memset(onesf[:], 1.0), writes=["onesf"])
        P.add("dve", lambda e: e.tensor_tensor(out=x[:], in0=k.GG[:], in1=bc(k.vecs[:, db:db + 64].unsqueeze(1), [128, 18, 64]), op=ALU.add),
              writes=["x"])
        P.add("act", lambda e: e.activation(out=ax[:], in_=x[:], func=AF.Abs), reads=["x"], writes=["ax"])
        P.add("act", lambda e: e.activation(out=ax[:], in_=ax[:], func=AF.Exp, scale=-1.0), reads=["ax"], writes=["ax"])
        P.add("act", lambda e: e.activation(out=ax[:], in_=ax[:], func=AF.Ln, bias=1.0), reads=["ax"], writes=["ax"])
        P.add("dve", lambda e: e.tensor_scalar_max(out=x[:], in0=x[:], scalar1=0.0), reads=["x"], writes=["x"])
        P.add("dve", lambda e: e.tensor_tensor(out=x[:], in0=x[:], in1=ax[:], op=ALU.add), reads=["x", "ax"], writes=["x"])
        P.add("act", lambda e: e.activation(out=nA[:], in_=k.vecs[:, al:al + 64], func=AF.Exp), writes=["nA"])
        P.add("dve", lambda e: e.scalar_tensor_tensor(out=k.GG[:], in0=x[:], scalar=-1.0, in1=bc(nA[:].unsqueeze(1), [128, 18, 64]),
                                                      op0=ALU.mult, op1=ALU.mult), reads=["x", "nA"], writes=["GG"])
        mo = {n: COFF[n][0] for n in ("m_le", "m_ge", "m_lt", "m_gt")}
        for c in range(18):
            pb = c % 2
            for d in range(2):
                m_gc = mo["m_le"] if d == 0 else mo["m_ge"]
                m_rm = mo["m_gt"] if d == 0 else mo["m_lt"]
                rhs = k.GG[:, c, d * 32:(d + 1) * 32]
                P.add("pe", lambda e, pb=pb, d=d, m=m_gc, rhs=rhs: e.matmul(ps[pb][:, 0, d * 32:(d + 1) * 32], k.consts[:, m:m + 128], rhs, start=True, stop=True),
                      reads=["GG"], writes=[("ps", pb)])
                P.add("pe", lambda e, pb=pb, d=d, m=m_rm, rhs=rhs: e.matmul(ps[pb][:, 1, d * 32:(d + 1) * 32], k.consts[:, m:m + 128], rhs, start=True, stop=True),
                      reads=["GG"], writes=[("ps", pb)])
                P.add("pe", lambda e, pb=pb, d=d, rhs=rhs: e.matmul(ps[pb][:, 2, d * 32:(d + 1) * 32], onesf[:], rhs, start=True, stop=True),
                      reads=["GG", "onesf"], writes=[("ps", pb)])
            P.add("act", lambda e, pb=pb, c=c: e.activation(out=k.EG[:, c, :], in_=ps[pb][:, 0, :], func=AF.Exp), reads=[("ps", pb)], writes=[("EG", c)])
            P.add("act", lambda e, pb=pb, c=c: e.activation(out=k.ER[:, c, :], in_=ps[pb][:, 1, :], func=AF.Exp), reads=[("ps", pb)], writes=[("ER", c)])
            P.add("act", lambda e, pb=pb, c=c: e.activation(out=k.EL[:, c, :], in_=ps[pb][:, 2, :], func=AF.Exp), reads=[("ps", pb)], writes=[("EL", c)])
            P.add("dve", lambda e, c=c: e.tensor_tensor(out=k.BEG[:, c, :], in0=k.BETA[:, c, :], in1=k.EG[:, c, :], op=ALU.mult),
                  reads=[("EG", c)], writes=[("BEG", c)])


def dn_prep(k, l):
    P = k.P
    mo = {n: COFF[n][0] for n in ("m_le", "m_ge", "m_lt", "m_gt", "ident")}
    NL = 2
    with contextlib.ExitStack() as st:
        kTc = [P.sb([128, 16, 128], BF16, st) for _ in range(2)]
        qTc = [P.sb([128, 16, 128], BF16, st) for _ in range(2)]
        ktm = [P.sb([128, 16, 128], BF16, st) for _ in range(2)]
        vtm = [P.sb([128, 32, 128], BF16, st) for _ in range(2)]
        lanes = []
        for ln in range(NL):
            B_ = K()
            B_.Gm = P.sb([128, 2, 128], F32, st); B_.QKm = P.sb([128, 2, 128], F32, st)
            B_.rhs2 = P.sb([128, 4, 128], F32, st); B_.eD = P.sb([128, 4, 128], F32, st)
            B_.L0 = P.sb([128, 4, 128], BF16, st); B_.QKD = P.sb([128, 4, 128], BF16, st)
            B_.LtQ = P.sb([128, 8, 128], BF16, st)
            B_.Xb = [P.sb([128, 4, 128], BF16, st) for _ in range(2)]
            B_.Yb = P.sb([128, 4, 128], BF16, st)
            B_.Zb = [P.sb([128, 4, 128], BF16, st) for _ in range(2)]
            B_.Ab = [P.sb([128, 4, 128], BF16, st) for _ in range(2)]
            B_.Bb = [P.sb([128, 4, 128], BF16, st) for _ in range(2)]
            B_.Lm = [P.sb([128, 4, 128], BF16, st) for _ in range(5)]
            B_.Ltm = [P.sb([128, 4, 128], BF16, st) for _ in range(4)]
            B_.T1a = P.sb([128, 4, 128], BF16, st); B_.T1b = P.sb([128, 4, 128], BF16, st)
            B_.vb = P.sb([128, 4, 128], BF16, st); B_.kbg = P.sb([128, 4, 128], BF16, st)
            B_.PH = [P.sb([128, 4, 4, 128], BF16, st) for _ in range(2)]
            B_.cnt = 0
            B_.psT = P.ps([128, 8, 128], BF16, st)
            B_.psX = P.ps([128, 4, 128], F32, st); B_.psY = P.ps([128, 4, 128], F32, st); B_.psZ = P.ps([128, 4, 128], F32, st)
            lanes.append(B_)

        def unit(ln, c, d, hg, cb):
            B_ = lanes[ln]
            K_ = lambda name, *a: (name, ln) + a
            Gm, QKm, rhs2, eD, L0, QKD, LtQ = B_.Gm, B_.QKm, B_.rhs2, B_.eD, B_.L0, B_.QKD, B_.LtQ
            Xb, Yb, Zb, Ab, Bb, Lm, Ltm, T1a, T1b = B_.Xb, B_.Yb, B_.Zb, B_.Ab, B_.Bb, B_.Lm, B_.Ltm, B_.T1a, B_.T1b
            vb, kbg = B_.vb, B_.kbg
            pi = B_.cnt % 2
            B_.cnt += 1
            PH = B_.PH[pi]
            Uo, Wo, Qo, kd = PH[:, 0], PH[:, 1], PH[:, 2], PH[:, 3]
            K2 = lambda name, *a: (name, ln, pi) + a
            psT, psX, psY, psZ = B_.psT, B_.psX, B_.psY, B_.psZ
            m_strict = mo["m_gt"] if d == 0 else mo["m_lt"]
            m_incl = mo["m_ge"] if d == 0 else mo["m_le"]
            m_l = mo["m_le"] if d == 0 else mo["m_ge"]
            m_r = mo["m_gt"] if d == 0 else mo["m_lt"]
            h0 = d * 32 + hg * 4
            kh0 = hg * 2
            P.add("pool", lambda e: e.tensor_tensor(
                out=rhs2[:], in0=bc(k.consts[:, m_r:m_r + 128].unsqueeze(1), [128, 4, 128]),
                in1=bc(k.GG[:, c, h0:h0 + 4].unsqueeze(2), [128, 4, 128]), op=ALU.mult), writes=[K_("rhs2")])
            for a in range(2):
                P.add("pe", lambda e, a=a: e.matmul(psX[:, a, :], kTc[cb][:, kh0 + a, :], kTc[cb][:, kh0 + a, :], start=True, stop=True),
                      reads=[("kTc", cb)], writes=[K_("psX")])
                P.add("pe", lambda e, a=a: e.matmul(psX[:, 2 + a, :], qTc[cb][:, kh0 + a, :], kTc[cb][:, kh0 + a, :], start=True, stop=True),
                      reads=[("kTc", cb), ("qTc", cb)], writes=[K_("psX")])
            P.add("pe", lambda e: e.matmul(psY[:], k.consts[:, m_l:m_l + 128], rhs2[:], start=True, stop=True),
                  reads=[K_("rhs2")], writes=[K_("psY")])
            P.add("dve", lambda e: e.tensor_tensor(out=Gm[:], in0=psX[:, 0:2, :], in1=bc(k.consts[:, m_strict:m_strict + 128].unsqueeze(1), [128, 2, 128]), op=ALU.mult),
                  reads=[K_("psX")], writes=[K_("Gm")])
            P.add("dve", lambda e: e.tensor_tensor(out=QKm[:], in0=psX[:, 2:4, :], in1=bc(k.consts[:, m_incl:m_incl + 128].unsqueeze(1), [128, 2, 128]), op=ALU.mult),
                  reads=[K_("psX")], writes=[K_("QKm")])
            P.add("act", lambda e: e.activation(out=eD[:], in_=psY[:], func=AF.Exp), reads=[K_("psY")], writes=[K_("eD")])
            yield
            for hl in range(4):
                P.add("dve", lambda e, hl=hl: e.scalar_tensor_tensor(
                    out=L0[:, hl, :], in0=Gm[:, hl // 2, :], scalar=k.BETA[:, c, h0 + hl:h0 + hl + 1], in1=eD[:, hl, :],
                    op0=ALU.mult, op1=ALU.mult), reads=[K_("Gm"), K_("eD")], writes=[K_("L0", hl)])
            P.add("dve", lambda e: e.tensor_tensor(
                out=QKD[:].rearrange("p (a b) j -> p a b j", b=2), in0=bc(QKm[:].unsqueeze(2), [128, 2, 2, 128]),
                in1=eD[:].rearrange("p (a b) j -> p a b j", b=2), op=ALU.mult), reads=[K_("QKm"), K_("eD")], writes=[K_("QKD")])
            L0k = [K_("L0", hl) for hl in range(4)]
            for hl in range(4):
                P.add("pe", lambda e, hl=hl: e.transpose(out=psT[:, hl, :], in_=L0[:, hl, :], identity=k.identb[:]),
                      reads=L0k, writes=[K_("psT")])
            for hl in range(4):
                P.add("pe", lambda e, hl=hl: e.transpose(out=psT[:, 4 + hl, :], in_=QKD[:, hl, :], identity=k.identb[:]),
                      reads=[K_("QKD")], writes=[K_("psT")])
            P.add("act", lambda e: e.activation(out=LtQ[:, 0:4, :], in_=psT[:, 0:4, :], func=AF.Copy), reads=[K_("psT")], writes=[K_("LtQ")])
            P.add("act", lambda e: e.activation(out=Qo, in_=psT[:, 4:8, :], func=AF.Copy), reads=[K_("psT")], writes=[K2("Qo")])
            yield
            Lt = LtQ[:, 0:4, :]
            mk = lambda q: bc(k.maskb[:, q, :].unsqueeze(1), [128, 4, 128])
            P.add("dve", lambda e: e.tensor_tensor(out=Lm[0][:], in0=L0[:], in1=mk(0), op=ALU.mult), reads=L0k, writes=[K_("Lm", 0)])
            P.add("dve", lambda e: e.tensor_tensor(out=Ltm[0][:], in0=Lt, in1=mk(0), op=ALU.mult), reads=[K_("LtQ")], writes=[K_("Ltm", 0)])
            for q in range(1, 5):
                eng_m = "pool"
                P.add(eng_m, lambda e, q=q: e.tensor_tensor(out=Lm[q][:], in0=L0[:], in1=mk(q), op=ALU.mult), reads=L0k, writes=[K_("Lm", q)])
                if q < 4:
                    P.add(eng_m, lambda e, q=q: e.tensor_tensor(out=Ltm[q][:], in0=Lt, in1=mk(q), op=ALU.mult), reads=[K_("LtQ")], writes=[K_("Ltm", q)])
            for hl in range(4):
                P.add("act", lambda e, hl=hl: e.activation(out=vb[:, hl, :], in_=vtm[cb][:, hg * 4 + hl, :], func=AF.Identity,
                                                           scale=k.BETA[:, c, h0 + hl:h0 + hl + 1]), reads=[("vtm", cb)], writes=[K_("vb", hl)])
                P.add("act", lambda e, hl=hl: e.activation(out=kbg[:, hl, :], in_=ktm[cb][:, kh0 + hl // 2, :], func=AF.Identity,
                                                           scale=k.BEG[:, c, h0 + hl:h0 + hl + 1]), reads=[("ktm", cb)], writes=[K_("kbg", hl)])
                P.add("act", lambda e, hl=hl: e.activation(out=kd[:, hl, :], in_=ktm[cb][:, kh0 + hl // 2, :], func=AF.Identity,
                                                           scale=k.ER[:, c, h0 + hl:h0 + hl + 1]), reads=[("ktm", cb)], writes=[K2("kd", hl)])
            for hl in range(4):
                P.add("pe", lambda e, hl=hl: e.matmul(psX[:, hl, :], Ltm[0][:, hl, :], Lm[0][:, hl, :], start=True, stop=True),
                      reads=[K_("Lm", 0), K_("Ltm", 0)], writes=[K_("psX")])
            for hl in range(4):
                P.add("pe", lambda e, hl=hl: e.matmul(psY[:, hl, :], Lm[0][:, hl, :], Ltm[0][:, hl, :], start=True, stop=True),
                      reads=[K_("Lm", 0), K_("Ltm", 0)], writes=[K_("psY")])
            P.add("act", lambda e: e.activation(out=Xb[0][:], in_=psX[:], func=AF.Copy), reads=[K_("psX")], writes=[K_("X", 0)])
            P.add("act", lambda e: e.activation(out=Yb[:], in_=psY[:], func=AF.Copy), reads=[K_("psY")], writes=[K_("Y")])
            P.add("dve", lambda e: e.scalar_tensor_tensor(
                out=Zb[0][:], in0=Ltm[0][:], scalar=-1.0, in1=bc(k.identb[:].unsqueeze(1), [128, 4, 128]),
                op0=ALU.mult, op1=ALU.add), reads=[K_("Ltm", 0)], writes=[K_("Z", 0)])
            yield
            for hl in range(4):
                P.add("pe", lambda e, hl=hl: e.matmul(psX[:, hl, :], Yb[:, hl, :], Xb[0][:, hl, :], start=True, stop=True),
                      reads=[K_("X", 0), K_("Y")], writes=[K_("psX")])
            for hl in range(4):
                P.add("pe", lambda e, hl=hl: e.matmul(psZ[:, hl, :], Xb[0][:, hl, :], Zb[0][:, hl, :], start=True, stop=True),
                      reads=[K_("X", 0), K_("Z", 0)], writes=[K_("psZ")])
            P.add("act", lambda e: e.activation(out=Xb[1][:], in_=psX[:], func=AF.Copy), reads=[K_("psX")], writes=[K_("X", 1)])
            P.add("dve", lambda e: e.tensor_tensor(out=Zb[1][:], in0=psZ[:], in1=Zb[0][:], op=ALU.add),
                  reads=[K_("psZ"), K_("Z", 0)], writes=[K_("Z", 1)])
            yield
            for hl in range(4):
                P.add("pe", lambda e, hl=hl: e.matmul(psZ[:, hl, :], Xb[1][:, hl, :], Zb[1][:, hl, :], start=True, stop=True),
                      reads=[K_("X", 1), K_("Z", 1)], writes=[K_("psZ")])
            P.add("dve", lambda e: e.tensor_tensor(out=Zb[0][:], in0=psZ[:], in1=Zb[1][:], op=ALU.add),
                  reads=[K_("psZ"), K_("Z", 1)], writes=[K_("Z", 0)])
            yield
            for hl in range(4):
                P.add("pe", lambda e, hl=hl: e.transpose(out=psT[:, hl, :], in_=Zb[0][:, hl, :], identity=k.identb[:]),
                      reads=[K_("Z", 0)], writes=[K_("psT")])
            P.add("act", lambda e: e.activation(out=Ab[0][:], in_=psT[:, 0:4, :], func=AF.Copy), reads=[K_("psT")], writes=[K_("A", 0)])
            yield
            Bcur, Bkey = Zb[0], K_("Z", 0)
            Acur, Akey = Ab[0], K_("A", 0)
            for q in range(1, 5):
                last = (q == 4)
                nb_ = q % 2
                if not last:
                    for hl in range(4):
                        P.add("pe", lambda e, hl=hl, q=q, Acur=Acur: e.matmul(psX[:, hl, :], Ltm[q][:, hl, :], Acur[:, hl, :], start=True, stop=True),
                              reads=[K_("Ltm", q), Akey], writes=[K_("psX")])
                for hl in range(4):
                    P.add("pe", lambda e, hl=hl, q=q, Bcur=Bcur: e.matmul(psY[:, hl, :], Lm[q][:, hl, :], Bcur[:, hl, :], start=True, stop=True),
                          reads=[K_("Lm", q), Bkey], writes=[K_("psY")])
                if not last:
                    P.add("act", lambda e: e.activation(out=T1a[:], in_=psX[:], func=AF.Copy), reads=[K_("psX")], writes=[K_("T1a")])
                P.add("act", lambda e: e.activation(out=T1b[:], in_=psY[:], func=AF.Copy), reads=[K_("psY")], writes=[K_("T1b")])
                yield
                if not last:
                    for hl in range(4):
                        P.add("pe", lambda e, hl=hl, Bcur=Bcur: e.matmul(psX[:, hl, :], Bcur[:, hl, :], T1a[:, hl, :], start=True, stop=True),
                              reads=[Bkey, K_("T1a")], writes=[K_("psX")])
                for hl in range(4):
                    P.add("pe", lambda e, hl=hl, Acur=Acur: e.matmul(psZ[:, hl, :], Acur[:, hl, :], T1b[:, hl, :], start=True, stop=True),
                          reads=[Akey, K_("T1b")], writes=[K_("psZ")])
                if not last:
                    P.add("dve", lambda e, nb_=nb_, Acur=Acur: e.tensor_tensor(out=Ab[nb_][:], in0=Acur[:], in1=psX[:], op=ALU.subtract),
                          reads=[K_("psX"), Akey], writes=[K_("A", nb_)])
                Bnew = Bb[nb_] if not last else Zb[0]
                Bnk = K_("B", nb_) if not last else K_("Z", 0)
                P.add("dve", lambda e, Bnew=Bnew, Bcur=Bcur: e.tensor_tensor(out=Bnew[:], in0=Bcur[:], in1=psZ[:], op=ALU.subtract),
                      reads=[K_("psZ"), Bkey], writes=[Bnk])
                Bcur, Bkey = Bnew, Bnk
                if not last:
                    Acur, Akey = Ab[nb_], K_("A", nb_)
                yield
            ZF = Zb[0]
            for hl in range(4):
                P.add("pe", lambda e, hl=hl: e.matmul(psX[:, hl, :], ZF[:, hl, :], vb[:, hl, :], start=True, stop=True),
                      reads=[K_("Z", 0)] + [K_("vb", q_) for q_ in range(4)], writes=[K_("psX")])
            for hl in range(4):
                P.add("pe", lambda e, hl=hl: e.matmul(psY[:, hl, :], kbg[:, hl, :], ZF[:, hl, :], start=True, stop=True),
                      reads=[K_("Z", 0)] + [K_("kbg", q_) for q_ in range(4)], writes=[K_("psY")])
            P.add("act", lambda e: e.activation(out=Uo, in_=psX[:], func=AF.Copy), reads=[K_("psX")], writes=[K2("Uo")])
            P.add("act", lambda e: e.activation(out=Wo, in_=psY[:], func=AF.Copy), reads=[K_("psY")], writes=[K2("Wo")])
            u = (c * 2 + d) * 8 + hg
            P.add("sp", lambda e: e.dma_start(out=k.DPH[u], in_=PH[:]),
                  reads=[K2("Uo"), K2("Wo"), K2("Qo")] + [K2("kd", q_) for q_ in range(4)], writes=[("DPH", u)], dma=True)
            yield

        def load_chunk(c):
            cb = c % 2
            tsl = slice(c * 128, (c + 1) * 128)
            P.add("sp", lambda e: e.dma_start(out=kTc[cb][:], in_=k.DKT.rearrange("h p t -> p h t")[:, :, tsl]), writes=[("kTc", cb)], dma=True)
            P.add("sp", lambda e: e.dma_start(out=qTc[cb][:], in_=k.DQT.rearrange("h p t -> p h t")[:, :, tsl]), writes=[("qTc", cb)], dma=True)
            P.add("sp", lambda e: e.dma_start(out=ktm[cb][:], in_=k.DKTM[c].rearrange("p (h d) -> p h d", d=128)), writes=[("ktm", cb)], dma=True)
            P.add("sp", lambda e: e.dma_start(out=vtm[cb][:], in_=k.DVTM[c].rearrange("p (h d) -> p h d", d=128)), writes=[("vtm", cb)], dma=True)

        for c in range(18):
            load_chunk(c)
            units = [(d, hg) for d in range(2) for hg in range(8)]
            for p0 in range(0, len(units), NL):
                gens = [unit(ln, c, units[p0 + ln][0], units[p0 + ln][1], c % 2) for ln in range(NL)]
                alive = list(gens)
                while alive:
                    nxt = []
                    for g in alive:
                        try:
                            next(g)
                            nxt.append(g)
                        except StopIteration:
                            pass
                    alive = nxt


def dn_scan(k, l):
    P = k.P
    order = [list(range(18)), [1, 0] + list(range(17, 1, -1))]
    with contextlib.ExitStack() as st:
        Sf = P.sb([128, 64, 128], F32, st)
        Sb = P.sb([128, 64, 128], BF16, st)
        NB = 3
        ph = [P.sb([128, 4, 4, 128], BF16, st) for _ in range(NB)]
        qT = [P.sb([128, 2, 128], BF16, st) for _ in range(NB)]
        vnew = [P.sb([128, 4, 128], BF16, st) for _ in range(2)]
        o1 = [P.sb([128, 4, 128], F32, st) for _ in range(NB)]
        ob_ = [P.sb([128, 4, 128], F32, st) for _ in range(NB)]
        psP = [P.ps([128, 4, 128], F32, st) for _ in range(2)]
        psO1 = [P.ps([128, 4, 128], F32, st) for _ in range(2)]
        psO2 = [P.ps([128, 4, 128], F32, st) for _ in range(2)]
        psS = [P.ps([128, 4, 128], F32, st) for _ in range(2)]
        P.add("pool", lambda e: e.memset(Sf[:], 0.0), writes=[("Sf", hh) for hh in range(16)])
        P.add("pool", lambda e: e.memset(Sb[:], 0.0), writes=[("Sb", hh) for hh in range(16)])
        P.barrier()
        units = [(s, d, hg) for s in range(18) for d in range(2) for hg in range(8)]

        def stage1(idx):
            s, d, hg = units[idx]
            c = order[d][s]
            u = (c * 2 + d) * 8 + hg
            nb = idx % NB; pb = idx % 2
            g16 = d * 8 + hg
            P.add("sp", lambda e: e.dma_start(out=ph[nb][:], in_=k.DPH[u]), writes=[("ph", nb)], dma=True)
            P.add("sp", lambda e: e.dma_start(out=qT[nb][:], in_=k.DQT[hg * 2:hg * 2 + 2, :, c * 128:(c + 1) * 128].rearrange("h p t -> p h t")),
                  writes=[("qT", nb)], dma=True)
            for hl in range(4):
                hs = d * 32 + hg * 4 + hl
                P.add("pe", lambda e, hl=hl, hs=hs: e.matmul(psP[pb][:, hl, :], ph[nb][:, 1, hl, :], Sb[:, hs, :], start=True, stop=True),
                      reads=[("ph", nb), ("Sb", g16)], writes=[("psP", pb)])
            for hl in range(4):
                hs = d * 32 + hg * 4 + hl
                P.add("pe", lambda e, hl=hl, hs=hs: e.matmul(psO1[pb][:, hl, :], qT[nb][:, hl // 2, :], Sb[:, hs, :], start=True, stop=True),
                      reads=[("qT", nb), ("Sb", g16)], writes=[("psO1", pb)])
            P.add("dve", lambda e: e.tensor_tensor(out=vnew[pb][:], in0=ph[nb][:, 0, :, :], in1=psP[pb][:], op=ALU.subtract),
                  reads=[("ph", nb), ("psP", pb)], writes=[("vnew", pb)])
            P.add("act", lambda e: e.activation(out=o1[nb][:], in_=psO1[pb][:], func=AF.Copy), reads=[("psO1", pb)], writes=[("o1", nb)])

        def stage2(idx):
            s, d, hg = units[idx]
            c = order[d][s]
            nb = idx % NB; pb = idx % 2
            g16 = d * 8 + hg
            h0 = d * 32 + hg * 4
            for hl in range(4):
                P.add("pe", lambda e, hl=hl: e.matmul(psO2[pb][:, hl, :], ph[nb][:, 2, hl, :], vnew[pb][:, hl, :], start=True, stop=True),
                      reads=[("ph", nb), ("vnew", pb)], writes=[("psO2", pb)])
            for hl in range(4):
                P.add("pe", lambda e, hl=hl: e.matmul(psS[pb][:, hl, :], ph[nb][:, 3, hl, :], vnew[pb][:, hl, :], start=True, stop=True),
                      reads=[("ph", nb), ("vnew", pb)], writes=[("psS", pb)])
            for hl in range(4):
                P.add("dve", lambda e, hl=hl: e.scalar_tensor_tensor(
                    out=ob_[nb][:, hl, :], in0=o1[nb][:, hl, :], scalar=k.EG[:, c, h0 + hl:h0 + hl + 1], in1=psO2[pb][:, hl, :],
                    op0=ALU.mult, op1=ALU.add), reads=[("psO2", pb), ("o1", nb)], writes=[("ob", nb, hl)])
            for hl in range(4):
                P.add("dve", lambda e, hl=hl: e.scalar_tensor_tensor(
                    out=Sf[:, h0 + hl, :], in0=Sf[:, h0 + hl, :], scalar=k.EL[:, c, h0 + hl:h0 + hl + 1], in1=psS[pb][:, hl, :],
                    op0=ALU.mult, op1=ALU.add), reads=[("psS", pb), ("Sf", g16, hl)], writes=[("Sf", g16, hl)])
            P.add("act", lambda e: e.activation(out=Sb[:, h0:h0 + 4, :], in_=Sf[:, h0:h0 + 4, :], func=AF.Copy),
                  reads=[("Sf", g16, hl) for hl in range(4)], writes=[("Sb", g16)])
            dst = k.DO[d, c, :, hg * 512:(hg + 1) * 512]
            P.add("pool", lambda e: e.dma_start(out=dst, in_=ob_[nb][:].rearrange("p h j -> p (h j)")),
                  reads=[("ob", nb, hl) for hl in range(4)], writes=[("DO", d, c, hg)], dma=True)

        for idx in range(len(units) + 1):
            if idx < len(units):
                stage1(idx)
            if idx >= 1:
                stage2(idx - 1)


def dn_outnorm(k, l):
    P = k.P
    i = l // 2
    ng = VOFF["dngrep"][0] + i * 128
    with contextlib.ExitStack() as st:
        of = [P.sb([128, 32, 128], F32, st) for _ in range(2)]
        obk = [P.sb([128, 32, 128], F32, st) for _ in range(2)]
        zz = [P.sb([128, 32, 128], BF16, st) for _ in range(2)]
        sq = P.sb([128, 32, 128], F32, st)
        ss = P.sb([128, 32], F32, st)
        y = [P.sb([128, 32, 128], BF16, st) for _ in range(2)]
        yT = [P.sb([128, 32, 128], BF16, st) for _ in range(2)]
        pst = [P.ps([128, 8, 128], BF16, st) for _ in range(2)]
        npt = 0
        for c in range(18):
            b = c % 2
            P.add("sp", lambda e, b=b, c=c: e.dma_start(out=of[b][:], in_=k.DO[0, c].rearrange("p (h j) -> p h j", j=128)), writes=[("of", b)], dma=True)
            P.add("sp", lambda e, b=b, c=c: e.dma_start(out=obk[b][:], in_=k.DO[1, c].rearrange("p (h j) -> p h j", j=128)), writes=[("obk", b)], dma=True)
            P.add("sp", lambda e, b=b, c=c: e.dma_start(out=zz[b][:], in_=k.DZ[c].rearrange("p (h j) -> p h j", j=128)), writes=[("zz", b)], dma=True)
            P.add("dve", lambda e, b=b: e.tensor_tensor(out=of[b][:], in0=of[b][:], in1=obk[b][:], op=ALU.add),
                  reads=[("of", b), ("obk", b)], writes=[("of", b)])
            P.add("act", lambda e, b=b: e.activation(out=sq[:], in_=of[b][:], func=AF.Square), reads=[("of", b)], writes=["sq"])
            P.add("dve", lambda e: e.tensor_reduce(out=ss[:], in_=sq[:], axis=AX.X, op=ALU.add), reads=["sq"], writes=["ss"])
            P.add("act", lambda e: e.activation(out=ss[:], in_=ss[:], func=AF.Sqrt, bias=EPS, scale=1.0 / 128), reads=["ss"], writes=["ss"])
            P.add("dve", lambda e: e.reciprocal(ss[:], ss[:]), reads=["ss"], writes=["ss"])
            P.add("dve", lambda e, b=b: e.tensor_tensor(out=of[b][:], in0=of[b][:], in1=bc(ss[:].unsqueeze(2), [128, 32, 128]), op=ALU.mult),
                  reads=[("of", b), "ss"], writes=[("of", b)])
            P.add("dve", lambda e, b=b: e.tensor_tensor(out=of[b][:], in0=of[b][:], in1=bc(k.vecs[:, ng:ng + 128].unsqueeze(1), [128, 32, 128]), op=ALU.mult),
                  reads=[("of", b)], writes=[("of", b)])
            P.add("dve", lambda e, b=b: e.tensor_tensor(out=y[b][:], in0=of[b][:], in1=zz[b][:], op=ALU.mult),
                  reads=[("of", b), ("zz", b)], writes=[("y", b)])
            for h8 in range(4):
                tb = npt % 2; npt += 1
                for hh in range(8):
                    P.add("pe", lambda e, tb=tb, hh=hh, h8=h8, b=b: e.transpose(out=pst[tb][:, hh, :], in_=y[b][:, h8 * 8 + hh, :], identity=k.identb[:]),
                          reads=[("y", b)], writes=[("pst", tb)])
                P.add("act", lambda e, tb=tb, h8=h8, b=b: e.activation(out=yT[b][:, h8 * 8:(h8 + 1) * 8, :], in_=pst[tb][:], func=AF.Copy),
                      reads=[("pst", tb)], writes=[("yT", b, h8)])
            P.add("pool", lambda e, b=b, c=c: e.dma_start(out=k.DYT[:, :, c * 128:(c + 1) * 128].rearrange("h p t -> p h t"), in_=yT[b][:]),
                  reads=[("yT", b, h8) for h8 in range(4)], writes=[("DYT", c)], dma=True)


def dn_outproj(k, l):
    P = k.P
    i = l // 2
    hsrc = k.hT.rearrange("(c p) t -> p c t", p=128)
    HALF = NT // 2
    SUB = 384
    with contextlib.ExitStack() as st:
        cat = P.sb([128, 32, HALF], BF16, st)
        wb = [P.sb([128, 32, 256], BF16, st) for _ in range(2)]
        hrow = [P.sb([128, HALF], F32, st) for _ in range(2)]
        pso = [P.ps([128, 512], F32, st) for _ in range(4)]
        ysrc = k.DYT.rearrange("h p t -> p h t")
        nps = 0
        for half in range(2):
            h0 = half * HALF
            for q in range(4):
                P.add("sp", lambda e, q=q, h0=h0: e.dma_start(out=cat[:, q * 8:(q + 1) * 8, :], in_=ysrc[:, q * 8:(q + 1) * 8, h0:h0 + HALF]),
                      writes=[("cat", q)], dma=True)
            for blk in range(8):
                b = blk % 2
                for q in range(2):
                    P.add("pool", lambda e, b=b, blk=blk, q=q: e.dma_start(
                        out=wb[b][:, q * 16:(q + 1) * 16, :], in_=k.dn_w_out[i, blk][:, q * 16:(q + 1) * 16, :]),
                        writes=[("wb", b, q)], dma=True)
                for s in range(2):
                    c = blk * 2 + s
                    hb = c % 2
                    P.add("sp", lambda e, hb=hb, c=c, h0=h0: e.dma_start(out=hrow[hb][:], in_=hsrc[:, c, h0:h0 + HALF]),
                          reads=[("hT", c)], writes=[("hrow", hb)], dma=True)
                    for sub in range(HALF // SUB):
                        u0 = sub * SUB
                        pb = nps % 4; nps += 1
                        for kc in range(32):
                            P.add("pe", lambda e, b=b, s=s, pb=pb, kc=kc, u0=u0: e.matmul(
                                pso[pb][:, :SUB], wb[b][:, kc, s * 128:(s + 1) * 128], cat[:, kc, u0:u0 + SUB],
                                start=(kc == 0), stop=(kc == 31)),
                                reads=[("wb", b, kc // 16), ("cat", kc // 8)], writes=[("pso", pb)])
                        for (g0, gn, sidx) in segs(h0 + u0, SUB):
                            r0 = g0 - h0
                            P.add("dve", lambda e, hb=hb, pb=pb, r0=r0, gn=gn, u0=u0, c=c, sidx=sidx: e.scalar_tensor_tensor(
                                out=hrow[hb][:, r0:r0 + gn], in0=pso[pb][:, r0 - u0:r0 - u0 + gn], scalar=modv(k, 2, c, sidx),
                                in1=hrow[hb][:, r0:r0 + gn], op0=ALU.mult, op1=ALU.add),
                                reads=[("pso", pb), ("hrow", hb), "mod"], writes=[("hrow", hb)])
                    P.add("sp", lambda e, hb=hb, c=c, h0=h0: e.dma_start(out=hsrc[:, c, h0:h0 + HALF], in_=hrow[hb][:]),
                          reads=[("hrow", hb)], writes=[("hT", c)], dma=True)


FULL_PLAN = [(l, ("mix", "ffn")) for l in range(DEPTH)]


def tile_w(W, cols=None, width=256):
    if cols is not None:
        W = W[:, cols]
    K_, N_ = W.shape
    return np.ascontiguousarray(W.reshape(K_ // 128, 128, N_ // width, width).transpose(2, 1, 0, 3))


def tile_weights(inp):
    out = {}
    out["w_mod"] = np.stack([tile_w(inp["w_mod"][l]) for l in range(DEPTH)])
    up_cols = np.concatenate([np.concatenate([np.arange(j * 128, (j + 1) * 128), DFF + np.arange(j * 128, (j + 1) * 128)]) for j in range(NFF)])
    out["ffn_w_up"] = np.stack([tile_w(inp["ffn_w_up"][l], up_cols) for l in range(DEPTH)])
    out["ffn_w_down"] = np.stack([tile_w(inp["ffn_w_down"][l]) for l in range(DEPTH)])
    ev_cols = np.concatenate([np.arange(0, 1536)] + [np.concatenate([1536 + np.arange(j * 128, (j + 1) * 128), 2560 + np.arange(j * 128, (j + 1) * 128)])
                                                      for j in range(8)])
    out["even_w_in"] = np.stack([tile_w(inp["even_w_in"][i], ev_cols) for i in range(2)])
    out["even_w_out"] = np.stack([tile_w(inp["even_w_out"][i]) for i in range(2)])
    out["dn_w_in"] = np.stack([tile_w(inp["dn_w_in"][i][:, :12288]) for i in range(2)])
    out["dn_w_ba"] = np.stack([tile_w(inp["dn_w_in"][i][:, 12288:12416], width=128)[0] for i in range(2)])
    out["dn_w_out"] = np.stack([tile_w(inp["dn_w_out"][i]) for i in range(2)])
    return out


def make_core_inputs(inp, b, consts, tiled=None):
    h_in = np.concatenate([inp["ctx"][b].T, inp["x"][b].T], axis=1)
    return {
        "h_in": np.ascontiguousarray(h_in, dtype=np.float32),
        "vecs": pack_vecs(inp, b),
        "consts": consts[0], "rope": consts[1],
        **(tiled if tiled is not None else {}),
    }


def kernel(**inputs):
    inp = {k_: np.asarray(v) for k_, v in inputs.items()}
    consts = make_consts()
    nc = build_program(FULL_PLAN)
    tiled = tile_weights(inp)
    in_maps = [make_core_inputs(inp, c % 4, consts, tiled) for c in range(8)]
    res = run_bass_kernel_spmd(nc, in_maps, core_ids=list(range(8)))
    out = np.stack([res.results[b]["hT"][:, LCTX:].T for b in range(4)], axis=0)
    return np.ascontiguousarray(out.astype(np.float32))
```

```python
import contextlib
import numpy as np
import concourse.bass as bass
import concourse.mybir as mybir
from concourse.bass_utils import run_bass_kernel_spmd

F32 = mybir.dt.float32
BF16 = mybir.dt.bfloat16
AF = mybir.ActivationFunctionType
ALU = mybir.AluOpType
AX = mybir.AxisListType

D = 2048
NCH = 16
LCTX = 256
TLAT = 2048
NT = LCTX + TLAT
DEPTH = 4
DFF = 5632
NFF = DFF // 128
EPS = 1e-6
TILES = [(0, 256), (256, 512), (768, 512), (1280, 512), (1792, 512)]
EVEN_IN_W = 3584
DN_IN_W = 12416

ENGS = ("pe", "act", "dve", "pool", "sp")
NDSEM = 6


class Op:
    __slots__ = ("eng", "fn", "idx", "dma", "slot", "val", "deps", "needed", "cnt", "waits", "clock")


class Prog:
    def __init__(self, nc, same_engine_sync=True):
        self.nc = nc
        self.ops = {e: [] for e in ENGS}
        self.order = []
        self.last_w = {}
        self.readers = {}
        self.same_engine_sync = same_engine_sync
        self.dma_rr = {e: 0 for e in ENGS}
        self.dma_val = {}
        self.stack = contextlib.ExitStack()
        self.sems = {}
        self.dsems = {}
        self._n = 0

    def sb(self, shape, dtype, stack=None):
        self._n += 1
        return (stack or self.stack).enter_context(self.nc.sbuf_tensor(f"sb{self._n}", list(shape), dtype))

    def ps(self, shape, dtype, stack=None):
        self._n += 1
        return (stack or self.stack).enter_context(self.nc.psum_tensor(f"ps{self._n}", list(shape), dtype))

    def add(self, eng, fn, reads=(), writes=(), dma=False):
        op = Op()
        op.eng = eng; op.fn = fn; op.dma = dma; op.needed = False; op.waits = None
        op.idx = len(self.ops[eng]); op.slot = None; op.val = None
        deps = set()
        writes = list(writes)
        if dma:
            s = self.dma_rr[eng] % NDSEM
            self.dma_rr[eng] += 1
            op.slot = (eng, s)
            writes.append(("dsem", eng, s))
            self.dma_val[op.slot] = self.dma_val.get(op.slot, 0) + 16
            op.val = self.dma_val[op.slot]
        for r in reads:
            w = self.last_w.get(r)
            if w is not None:
                deps.add(w)
        for r in writes:
            w = self.last_w.get(r)
            if w is not None:
                deps.add(w)
            for rd in self.readers.get(r, ()):
                deps.add(rd)
        for r in writes:
            self.last_w[r] = op
            self.readers[r] = []
        for r in reads:
            self.readers.setdefault(r, []).append(op)
        deps.discard(op)
        op.deps = deps
        self.ops[eng].append(op)
        self.order.append(op)
        return op

    def barrier(self):
        lasts = []
        for e in ENGS:
            for o in reversed(self.ops[e]):
                if o.fn is not None and not o.dma:
                    lasts.append(o)
                    break
        latest = {}
        for e in ENGS:
            for o in reversed(self.ops[e]):
                if o.dma and o.slot not in latest:
                    latest[o.slot] = o
        for e in ENGS:
            op = Op()
            op.eng = e; op.fn = None; op.dma = False; op.needed = False; op.waits = None
            op.idx = len(self.ops[e]); op.slot = None; op.val = None
            op.deps = set(o for o in lasts if o.eng != e and o.fn is not None and not o.dma) | set(latest.values())
            self.ops[e].append(op)
            self.order.append(op)
        self.last_w.clear(); self.readers.clear()

    def emit(self):
        nc = self.nc
        seen = {e: {} for e in ENGS}
        for op in self.order:
            s = seen[op.eng]
            waits = []
            for d in sorted(op.deps, key=lambda o: (o.eng, o.idx)):
                if d.dma:
                    key = ("d",) + d.slot
                    if s.get(key, 0) >= d.val:
                        continue
                    waits.append(d)
                    s[key] = d.val
                else:
                    if d.fn is None:
                        continue
                    if d.eng == op.eng and (d.eng == "pe" or not self.same_engine_sync or op.fn is None):
                        continue
                    if s.get(d.eng, -1) >= d.idx:
                        continue
                    waits.append(d)
                    s[d.eng] = d.idx
                for k, v in d.clock.items():
                    if k == op.eng:
                        continue
                    if s.get(k, -1) < v:
                        s[k] = v
            for d in waits:
                d.needed = True
            op.waits = waits
            op.clock = dict(s)
        for e in ENGS:
            c = 0
            for op in self.ops[e]:
                if op.needed and not op.dma:
                    c += 1
                op.cnt = c
        st = self.stack
        for e in ENGS:
            self.sems[e] = st.enter_context(nc.semaphore(f"s_{e}"))
        for key in self.dma_val:
            self.dsems[key] = st.enter_context(nc.semaphore(f"d_{key[0]}{key[1]}"))
        block = st.enter_context(nc.Block())
        handles = {"pe": block.tensor, "act": block.scalar, "dve": block.vector, "pool": block.gpsimd, "sp": block.sync}

        def mk(e):
            def body(h):
                for op in self.ops[e]:
                    for d in op.waits:
                        if d.dma:
                            h.wait_ge(self.dsems[d.slot], d.val)
                        else:
                            h.wait_ge(self.sems[d.eng], d.cnt)
                    if op.fn is None:
                        continue
                    ins = op.fn(h)
                    if op.dma:
                        ins.then_inc(self.dsems[op.slot], 16)
                    elif op.needed:
                        ins.then_inc(self.sems[e], 1)
            return body
        for e in ENGS:
            handles[e](mk(e))

    def finish(self):
        self.barrier()
        self.emit()
        self.stack.close()


def _layout(items):
    off = {}
    o = 0
    for name, w in items:
        off[name] = (o, w)
        o += w
    return off, o


VEC_ITEMS = [
    ("bmod", 4 * 96), ("n1g", 64), ("n2g", 64), ("fcw", 4 * 3 * NFF), ("fcb", 4 * NFF),
    ("qg", 2), ("kg", 2), ("sink", 16), ("cdw", 2 * 31 * 8), ("cdb", 16), ("lng", 16), ("lnb", 16),
    ("dcw", 2 * 5 * 64), ("alog", 128), ("dtb", 128), ("dngrep", 256), ("c", 16), ("cctx", 16),
]
VOFF, NV = _layout(VEC_ITEMS)

CONST_ITEMS = [
    ("ident", 128), ("ropeT", 128), ("band", 384),
    ("m_le", 128), ("m_ge", 128), ("m_lt", 128), ("m_gt", 128),
    ("mblk", 5 * 128),
]
COFF, NCONST = _layout(CONST_ITEMS)


def fm(v):
    return np.ascontiguousarray(v.reshape(-1, 128).T)


def pack_vecs(inp, b):
    V = np.zeros((128, NV), np.float32)

    def put(name, arr):
        o, w = VOFF[name]
        assert arr.shape == (128, w), (name, arr.shape, w)
        V[:, o:o + w] = arr
    put("bmod", np.concatenate([fm(inp["b_mod"][l]) for l in range(4)], axis=1))
    put("n1g", np.concatenate([fm(inp["norm1_g"][l]) for l in range(4)], axis=1))
    put("n2g", np.concatenate([fm(inp["norm2_g"][l]) for l in range(4)], axis=1))
    put("fcw", np.concatenate([fm(inp["ffn_conv_w"][l, k]) for l in range(4) for k in range(3)], axis=1))
    put("fcb", np.concatenate([fm(inp["ffn_conv_b"][l]) for l in range(4)], axis=1))
    put("qg", inp["attn_q_norm_g"].T.copy())
    put("kg", inp["attn_k_norm_g"].T.copy())
    put("sink", np.broadcast_to(inp["attn_sink"].reshape(1, 16), (128, 16)).copy())
    put("cdw", np.concatenate([fm(inp["conv_dw_w"][i, k]) for i in range(2) for k in range(31)], axis=1))
    put("cdb", np.concatenate([fm(inp["conv_dw_b"][i]) for i in range(2)], axis=1))
    put("lng", np.concatenate([fm(inp["conv_ln_g"][i]) for i in range(2)], axis=1))
    put("lnb", np.concatenate([fm(inp["conv_ln_b"][i]) for i in range(2)], axis=1))
    put("dcw", np.concatenate([fm(inp["dn_conv_w"][i, k]) for i in range(2) for k in range(5)], axis=1))
    put("alog", np.broadcast_to(inp["dn_a_log"].reshape(1, 128), (128, 128)).copy())
    put("dtb", np.broadcast_to(inp["dn_dt_bias"].reshape(1, 128), (128, 128)).copy())
    put("dngrep", np.broadcast_to(inp["dn_norm_g"].reshape(1, 256), (128, 256)).copy())
    put("c", fm(inp["c"][b]))
    put("cctx", fm(inp["c_ctx"]))
    return V


def make_consts():
    C = np.zeros((128, NCONST), np.float32)

    def put(name, arr):
        o, w = COFF[name]
        assert arr.shape == (128, w), (name, arr.shape)
        C[:, o:o + w] = arr
    idx = np.arange(128)
    put("ident", np.eye(128, dtype=np.float32))
    R = np.zeros((128, 128), np.float32)
    for i in range(32):
        R[i, 32 + i] = -1.0
        R[32 + i, i] = 1.0
        R[64 + i, 96 + i] = -1.0
        R[96 + i, 64 + i] = 1.0
    put("ropeT", R.T.copy())
    kk = idx[:, None]; qq = idx[None, :]
    band = np.concatenate([(kk <= qq), np.ones((128, 128), bool), (qq <= kk)], axis=1).astype(np.float32)
    put("band", band)
    put("m_le", (kk <= qq).astype(np.float32))
    put("m_ge", (kk >= qq).astype(np.float32))
    put("m_lt", (kk < qq).astype(np.float32))
    put("m_gt", (kk > qq).astype(np.float32))
    mb = [(kk // 8 == qq // 8)]
    for bsz in (8, 16, 32, 64):
        mb.append((kk // (2 * bsz) == qq // (2 * bsz)) & (kk // bsz != qq // bsz))
    put("mblk", np.concatenate(mb, axis=1).astype(np.float32))
    GRID_W = 64
    rows = TLAT // GRID_W
    row = np.repeat(np.arange(rows, dtype=np.float32), GRID_W)
    col = np.tile(np.arange(GRID_W, dtype=np.float32), rows)
    half = 64
    inv_freq = (10000.0 ** (-np.arange(0, half, 2, dtype=np.float32) / half)).astype(np.float32)
    ang_r = row[:, None] * inv_freq
    ang_c = col[:, None] * inv_freq
    ang = np.concatenate([ang_r, ang_r, ang_c, ang_c], axis=-1)
    rope = np.concatenate([np.cos(ang).T, np.sin(ang).T], axis=1).astype(np.float32)
    return C, np.ascontiguousarray(rope)


class K:
    pass


def segs(t0, n):
    out = []
    if t0 < LCTX:
        e = min(t0 + n, LCTX)
        out.append((t0, e - t0, 1))
        if t0 + n > LCTX:
            out.append((LCTX, t0 + n - LCTX, 0))
    else:
        out.append((t0, n, 0))
    return out


def bc(ap, shape):
    return ap.broadcast_to(list(shape))


def stage_adaln(k, l):
    P = k.P
    with contextlib.ExitStack() as st:
        wb = [P.sb([128, 16, 256], BF16, st) for _ in range(2)]
        pm = P.ps([128, 96, 2], F32, st)
        for blk in range(48):
            b = blk % 2
            P.add("pool", lambda e, b=b, blk=blk: e.dma_start(out=wb[b][:], in_=k.w_mod[l, blk]),
                  writes=[("wb", b)], dma=True)
            for s in range(2):
                j = blk * 2 + s
                for kc in range(16):
                    P.add("pe", lambda e, b=b, s=s, j=j, kc=kc: e.matmul(
                        pm[:, j, :], wb[b][:, kc, s * 128:(s + 1) * 128], k.siluc[:, kc, :],
                        start=(kc == 0), stop=(kc == 15)),
                        reads=[("wb", b), "siluc"], writes=["pm"])
        bo = VOFF["bmod"][0] + l * 96
        P.add("dve", lambda e: e.tensor_tensor(out=k.mod[:], in0=pm[:], in1=bc(k.vecs[:, bo:bo + 96].unsqueeze(2), [128, 96, 2]),
                                               op=ALU.add), reads=["pm"], writes=["mod"])
        for (dst, m, gname) in ((k.A1, 1, "n1g"), (k.A2, 4, "n2g")):
            go = VOFF[gname][0] + l * 16
            P.add("dve", lambda e, dst=dst, m=m: e.tensor_scalar_add(out=dst[:], in0=k.mod[:, m * 16:(m + 1) * 16, :], scalar1=1.0),
                  reads=["mod"], writes=[("A", m)])
            P.add("dve", lambda e, dst=dst, go=go: e.tensor_tensor(
                out=dst[:], in0=dst[:], in1=bc(k.vecs[:, go:go + 16].unsqueeze(2), [128, 16, 2]), op=ALU.mult),
                reads=[("A", m)], writes=[("A", m)])
    P.barrier()


def modv(k, m, c, s):
    return k.mod[:, m * 16 + c, s:s + 1]


def stage_norm(k, which, nx, nbuf=2):
    P = k.P
    A = k.A1 if which == 1 else k.A2
    msh = 0 if which == 1 else 3
    hsrc = k.hT.rearrange("(c p) t -> p c t", p=128)
    with contextlib.ExitStack() as st:
        hs = [P.sb([128, 16, 512], F32, st) for _ in range(nbuf)]
        sq = P.sb([128, 16, 512], BF16, st)
        rstd = P.sb([128, 512], F32, st)
        pss = P.ps([128, 512], F32, st)
        for ti, (t0, n) in enumerate(TILES):
            b = ti % nbuf
            P.add("sp", lambda e, b=b, t0=t0, n=n: e.dma_start(out=hs[b][:, :, :n], in_=hsrc[:, :, t0:t0 + n]),
                  reads=["hT"], writes=[("hs", b)], dma=True)
            P.add("act", lambda e, b=b, n=n: e.activation(out=sq[:, :, :n], in_=hs[b][:, :, :n], func=AF.Square),
                  reads=[("hs", b)], writes=["sq"])
            for c in range(16):
                P.add("pe", lambda e, c=c, n=n: e.matmul(pss[:, :n], k.onesD[:], sq[:, c, :n], start=(c == 0), stop=(c == 15)),
                      reads=["sq"], writes=["pss"])
            P.add("act", lambda e, n=n: e.activation(out=rstd[:, :n], in_=pss[:, :n], func=AF.Sqrt, bias=EPS, scale=1.0),
                  reads=["pss"], writes=["rstd"])
            P.add("dve", lambda e, n=n: e.reciprocal(rstd[:, :n], rstd[:, :n]), reads=["rstd"], writes=["rstd"])
            P.add("dve", lambda e, b=b, n=n: e.tensor_tensor(out=hs[b][:, :, :n], in0=hs[b][:, :, :n],
                                                             in1=bc(rstd[:, :n].unsqueeze(1), [128, 16, n]), op=ALU.mult),
                  reads=["rstd", ("hs", b)], writes=[("hs", b)])
            s = 1 if t0 < LCTX else 0
            for c in range(16):
                P.add("act", lambda e, b=b, c=c, t0=t0, n=n, s=s: e.activation(
                    out=nx[:, c, t0:t0 + n], in_=hs[b][:, c, :n], func=AF.Identity,
                    bias=modv(k, msh, c, s), scale=A[:, c, s:s + 1]),
                    reads=[("hs", b), "mod", ("A", 1), ("A", 4)], writes=[("nx", ti, c)])


def stage_ffn(k, l):
    P = k.P
    with contextlib.ExitStack() as st0:
        nx = P.sb([128, 16, NT], BF16, st0)
        stage_norm(k, 2, nx)
        P.barrier()
        stage_ffn_up(k, l, nx)
        P.barrier()
    stage_ffn_down(k, l)


def stage_ffn_up(k, l, nx):
    P = k.P
    with contextlib.ExitStack() as st:
        wb = [P.sb([128, 16, 256], BF16, st) for _ in range(3)]
        G = [P.sb([128, NT], F32, st) for _ in range(2)]
        Vv = [P.sb([128, NT], BF16, st) for _ in range(2)]
        acc = [P.sb([128, NT], F32, st) for _ in range(2)]
        sg = [P.sb([128, NT], BF16, st) for _ in range(2)]
        hh = [P.sb([128, NT], BF16, st) for _ in range(2)]
        psg = [P.ps([128, 512], F32, st) for _ in range(2)]
        psv = [P.ps([128, 512], F32, st) for _ in range(2)]
        cw = VOFF["fcw"][0] + l * 3 * NFF
        cb = VOFF["fcb"][0] + l * NFF

        def make_post(j):
            b = j % 2
            Gk = [("G", b, ti) for ti in range(5)]
            Vk = [("V", b, ti) for ti in range(5)]
            w0 = k.vecs[:, cw + 0 * NFF + j:cw + 0 * NFF + j + 1]
            w1 = k.vecs[:, cw + 1 * NFF + j:cw + 1 * NFF + j + 1]
            w2 = k.vecs[:, cw + 2 * NFF + j:cw + 2 * NFF + j + 1]
            bb = k.vecs[:, cb + j:cb + j + 1]
            conv = []
            conv.append(lambda: P.add("dve", lambda e: e.tensor_scalar(out=acc[b][:], in0=G[b][:], scalar1=w1, scalar2=bb,
                                                                       op0=ALU.mult, op1=ALU.add), reads=Gk, writes=[("acc", b)]))
            for (s0, sn) in ((0, LCTX), (LCTX, TLAT)):
                conv.append(lambda s0=s0, sn=sn: P.add("dve", lambda e: e.scalar_tensor_tensor(
                    out=acc[b][:, s0 + 1:s0 + sn], in0=G[b][:, s0:s0 + sn - 1], scalar=w0, in1=acc[b][:, s0 + 1:s0 + sn],
                    op0=ALU.mult, op1=ALU.add), reads=Gk + [("acc", b)], writes=[("acc", b)]))
                conv.append(lambda s0=s0, sn=sn: P.add("dve", lambda e: e.scalar_tensor_tensor(
                    out=acc[b][:, s0:s0 + sn - 1], in0=G[b][:, s0 + 1:s0 + sn], scalar=w2, in1=acc[b][:, s0:s0 + sn - 1],
                    op0=ALU.mult, op1=ALU.add), reads=Gk + [("acc", b)], writes=[("acc", b)]))

            def tail():
                P.add("act", lambda e: e.activation(out=sg[b][:], in_=acc[b][:], func=AF.Silu), reads=[("acc", b)], writes=[("sg", b)])
                P.add("pool", lambda e: e.tensor_tensor(out=hh[b][:], in0=sg[b][:], in1=Vv[b][:], op=ALU.mult),
                      reads=[("sg", b)] + Vk, writes=[("hh", b)])
                P.add("sp", lambda e: e.dma_start(out=k.HF[j * 128:(j + 1) * 128, :], in_=hh[b][:]),
                      reads=[("hh", b)], writes=["HF"], dma=True)
            return conv, tail

        pending = None
        for j in range(NFF):
            b = j % 2
            wi = j % 3
            P.add("pool", lambda e, wi=wi, j=j: e.dma_start(out=wb[wi][:], in_=k.ffn_w_up[l, j]),
                  writes=[("wb", wi)], dma=True)
            conv_prev = list(pending[0]) if pending else []
            for ti, (t0, n) in enumerate(TILES):
                pb = ti % 2
                for kc in range(16):
                    P.add("pe", lambda e, wi=wi, pb=pb, kc=kc, t0=t0, n=n: e.matmul(
                        psg[pb][:, :n], wb[wi][:, kc, 0:128], nx[:, kc, t0:t0 + n], start=(kc == 0), stop=(kc == 15)),
                        reads=[("wb", wi)], writes=[("psg", pb)])
                for kc in range(16):
                    P.add("pe", lambda e, wi=wi, pb=pb, kc=kc, t0=t0, n=n: e.matmul(
                        psv[pb][:, :n], wb[wi][:, kc, 128:256], nx[:, kc, t0:t0 + n], start=(kc == 0), stop=(kc == 15)),
                        reads=[("wb", wi)], writes=[("psv", pb)])
                P.add("act", lambda e, b=b, pb=pb, t0=t0, n=n: e.activation(out=G[b][:, t0:t0 + n], in_=psg[pb][:, :n], func=AF.Copy),
                      reads=[("psg", pb)], writes=[("G", b, ti)])
                P.add("dve", lambda e, b=b, pb=pb, t0=t0, n=n: e.tensor_copy(Vv[b][:, t0:t0 + n], psv[pb][:, :n]),
                      reads=[("psv", pb)], writes=[("V", b, ti)])
                if conv_prev:
                    conv_prev.pop(0)()
            for c_ in conv_prev:
                c_()
            if pending:
                pending[1]()
            pending = make_post(j)
        for c_ in pending[0]:
            c_()
        pending[1]()


def stage_ffn_down(k, l):
    P = k.P
    HALF = NT // 2
    SUB = 384
    with contextlib.ExitStack() as st:
        hres = P.sb([128, NFF, HALF], BF16, st)
        wb = [P.sb([128, NFF, 256], BF16, st) for _ in range(2)]
        hrow = [P.sb([128, HALF], F32, st) for _ in range(2)]
        pso = [P.ps([128, 512], F32, st) for _ in range(4)]
        hfsrc = k.HF.rearrange("(j p) t -> p j t", p=128)
        hsrc = k.hT.rearrange("(c p) t -> p c t", p=128)
        nps = 0
        for half in range(2):
            h0 = half * HALF
            for q in range(4):
                P.add("sp", lambda e, q=q, h0=h0: e.dma_start(out=hres[:, q * 11:(q + 1) * 11, :], in_=hfsrc[:, q * 11:(q + 1) * 11, h0:h0 + HALF]),
                      reads=["HF"], writes=[("hres", q)], dma=True)
            for blk in range(8):
                b = blk % 2
                for q in range(2):
                    P.add("pool", lambda e, b=b, blk=blk, q=q: e.dma_start(
                        out=wb[b][:, q * 22:(q + 1) * 22, :], in_=k.ffn_w_down[l, blk][:, q * 22:(q + 1) * 22, :]),
                        writes=[("wb", b, q)], dma=True)
                for s in range(2):
                    c = blk * 2 + s
                    hb = c % 2
                    P.add("sp", lambda e, hb=hb, c=c, h0=h0: e.dma_start(out=hrow[hb][:], in_=hsrc[:, c, h0:h0 + HALF]),
                          reads=[("hT", c)], writes=[("hrow", hb)], dma=True)
                    for sub in range(HALF // SUB):
                        u0 = sub * SUB
                        pb = nps % 4
                        nps += 1
                        for kc in range(NFF):
                            P.add("pe", lambda e, b=b, s=s, pb=pb, kc=kc, u0=u0: e.matmul(
                                pso[pb][:, :SUB], wb[b][:, kc, s * 128:(s + 1) * 128], hres[:, kc, u0:u0 + SUB],
                                start=(kc == 0), stop=(kc == NFF - 1)),
                                reads=[("wb", b, kc // 22), ("hres", kc // 11)], writes=[("pso", pb)])
                        for (g0, gn, sidx) in segs(h0 + u0, SUB):
                            r0 = g0 - h0
                            P.add("dve", lambda e, hb=hb, pb=pb, r0=r0, gn=gn, u0=u0, c=c, sidx=sidx: e.scalar_tensor_tensor(
                                out=hrow[hb][:, r0:r0 + gn], in0=pso[pb][:, r0 - u0:r0 - u0 + gn], scalar=modv(k, 5, c, sidx),
                                in1=hrow[hb][:, r0:r0 + gn], op0=ALU.mult, op1=ALU.add),
                                reads=[("pso", pb), ("hrow", hb), "mod"], writes=[("hrow", hb)])
                    P.add("sp", lambda e, hb=hb, c=c, h0=h0: e.dma_start(out=hsrc[:, c, h0:h0 + HALF], in_=hrow[hb][:]),
                          reads=[("hrow", hb)], writes=[("hT", c)], dma=True)
    P.barrier()


DEBUG_OUT = ()


def build_program(plan, test_out_ctx=False):
    nc = bass.Bass("TRN2", target_bir_lowering=False)
    k = K()
    k.nc = nc
    P = Prog(nc)
    k.P = P
    def dt(name, shape, dtype, kind="Internal"):
        if name in DEBUG_OUT:
            kind = "ExternalOutput"
        return nc.dram_tensor(name, shape, dtype, kind=kind)
    k.h_in = dt("h_in", [D, NT], F32, kind="ExternalInput").ap()
    k.vecs_d = dt("vecs", [128, NV], F32, kind="ExternalInput").ap()
    k.consts_d = dt("consts", [128, NCONST], F32, kind="ExternalInput").ap()
    k.rope_d = dt("rope", [128, 2 * TLAT], F32, kind="ExternalInput").ap()
    k.w_mod = dt("w_mod", [DEPTH, 48, 128, 16, 256], F32, kind="ExternalInput").ap()
    k.ffn_w_up = dt("ffn_w_up", [DEPTH, NFF, 128, 16, 256], F32, kind="ExternalInput").ap()
    k.ffn_w_down = dt("ffn_w_down", [DEPTH, 8, 128, NFF, 256], F32, kind="ExternalInput").ap()
    k.even_w_in = dt("even_w_in", [2, 14, 128, 16, 256], F32, kind="ExternalInput").ap()
    k.even_w_out = dt("even_w_out", [2, 8, 128, 16, 256], F32, kind="ExternalInput").ap()
    k.dn_w_in = dt("dn_w_in", [2, 48, 128, 16, 256], F32, kind="ExternalInput").ap()
    k.dn_w_ba = dt("dn_w_ba", [2, 128, 16, 128], F32, kind="ExternalInput").ap()
    k.dn_w_out = dt("dn_w_out", [2, 8, 128, 32, 256], F32, kind="ExternalInput").ap()
    k.hT = dt("hT", [D, NT], F32, kind="ExternalOutput").ap()
    k.HF = dt("HF", [DFF, NT], BF16, kind="Internal").ap()
    k.QT = dt("QT", [8, 128, NT], BF16, kind="Internal").ap()
    k.KT = dt("KT", [2, 128, NT], BF16, kind="Internal").ap()
    k.VTM = dt("VTM", [18, 128, 256], BF16, kind="Internal").ap()
    k.CAT = dt("CAT", [16, 128, NT], BF16, kind="Internal").ap()
    k.DQT = dt("DQT", [16, 128, NT], BF16, kind="Internal").ap()
    k.DKT = dt("DKT", [16, 128, NT], BF16, kind="Internal").ap()
    k.DKTM = dt("DKTM", [18, 128, 2048], BF16, kind="Internal").ap()
    k.DVTM = dt("DVTM", [18, 128, 4096], BF16, kind="Internal").ap()
    k.DZ = dt("DZ", [18, 128, 4096], BF16, kind="Internal").ap()
    k.DPH = dt("DPH", [288, 128, 4, 4, 128], BF16, kind="Internal").ap()
    k.DO = dt("DO", [2, 18, 128, 4096], F32, kind="Internal").ap()
    k.DYT = dt("DYT", [32, 128, NT], BF16, kind="Internal").ap()
    k.DBG_G = dt("DBG_G", [5, 128, 18, 64], F32, kind="Internal").ap()

    k.vecs = P.sb([128, NV], F32)
    k.consts = P.sb([128, NCONST], F32)
    k.siluc = P.sb([128, 16, 2], BF16)
    k.mod = P.sb([128, 96, 2], F32)
    k.A1 = P.sb([128, 16, 2], F32)
    k.A2 = P.sb([128, 16, 2], F32)
    k.onesD = P.sb([128, 128], BF16)
    k.ones128 = P.sb([128, 128], BF16)
    k.ones1 = P.sb([128, 128], BF16)
    k.onesf128 = P.sb([128, 128], F32)
    k.bandb = P.sb([128, 384], BF16)
    k.identb = P.sb([128, 128], BF16)
    k.maskb = P.sb([128, 5, 128], BF16)

    P.add("sp", lambda e: e.dma_start(out=k.vecs[:], in_=k.vecs_d[:, :]), writes=["vecs"], dma=True)
    P.add("sp", lambda e: e.dma_start(out=k.consts[:], in_=k.consts_d[:, :]), writes=["consts"], dma=True)
    P.add("pool", lambda e: e.memset(k.onesD[:], 1.0 / D), writes=["onesD"])
    P.add("pool", lambda e: e.memset(k.ones128[:], 1.0 / 128), writes=["ones128"])
    P.add("pool", lambda e: e.memset(k.ones1[:], 1.0), writes=["ones1"])
    P.add("pool", lambda e: e.memset(k.onesf128[:], 1.0 / 128), writes=["onesf128"])
    P.add("dve", lambda e: e.tensor_copy(k.bandb[:], k.consts[:, COFF["band"][0]:COFF["band"][0] + 384]), reads=["consts"], writes=["bandb"])
    P.add("dve", lambda e: e.tensor_copy(k.maskb[:].rearrange("p q j -> p (q j)"), k.consts[:, COFF["mblk"][0]:COFF["mblk"][0] + 640]), reads=["consts"], writes=["maskb"])
    P.add("dve", lambda e: e.tensor_copy(k.identb[:], k.consts[:, COFF["ident"][0]:COFF["ident"][0] + 128]), reads=["consts"], writes=["identb"])
    co = VOFF["c"][0]
    P.add("act", lambda e: e.activation(out=k.siluc[:, :, 0], in_=k.vecs[:, co:co + 16], func=AF.Silu), reads=["vecs"], writes=["siluc"])
    co2 = VOFF["cctx"][0]
    P.add("act", lambda e: e.activation(out=k.siluc[:, :, 1], in_=k.vecs[:, co2:co2 + 16], func=AF.Silu), reads=["vecs", "siluc"], writes=["siluc"])
    with contextlib.ExitStack() as st:
        tmp = [P.sb([128, 4, NT], F32, st) for _ in range(2)]
        src = k.h_in.rearrange("(c p) t -> p c t", p=128)
        dst = k.hT.rearrange("(c p) t -> p c t", p=128)
        for q in range(4):
            b = q % 2
            P.add("sp", lambda e, b=b, q=q: e.dma_start(out=tmp[b][:], in_=src[:, q * 4:(q + 1) * 4, :]), writes=[("tmp", b)], dma=True)
            P.add("sp", lambda e, b=b, q=q: e.dma_start(out=dst[:, q * 4:(q + 1) * 4, :], in_=tmp[b][:]), reads=[("tmp", b)], writes=["hT"], dma=True)
    P.barrier()

    for (l, parts) in plan:
        stage_adaln(k, l)
        if "mix" in parts:
            if l % 2 == 0:
                stage_even(k, l)
            else:
                stage_dn(k, l)
        if "ffn" in parts:
            stage_ffn(k, l)
    P.finish()
    return nc


def stage_even(k, l):
    P = k.P
    with contextlib.ExitStack() as st0:
        nx = P.sb([128, 16, NT], BF16, st0)
        stage_norm(k, 1, nx)
        P.barrier()
        even_qkv(k, l, nx)
        P.barrier()
        even_glu(k, l, nx)
        P.barrier()
    even_attn(k, l)
    P.barrier()
    even_out(k, l)
    P.barrier()


def rsqrt_ops(P, out_ap, in_ap, reads, wkey):
    P.add("act", lambda e: e.activation(out=out_ap, in_=in_ap, func=AF.Sqrt, bias=EPS, scale=1.0), reads=reads, writes=[wkey])
    P.add("dve", lambda e: e.reciprocal(out_ap, out_ap), reads=[wkey], writes=[wkey])


def even_qkv(k, l, nx):
    P = k.P
    i = l // 2
    cos0 = 0; sin0 = TLAT; rp0 = COFF["ropeT"][0]
    with contextlib.ExitStack() as st:
        rope = P.sb([128, 2 * TLAT], F32, st)
        P.add("sp", lambda e: e.dma_start(out=rope[:], in_=k.rope_d[:, :]), writes=["rope"], dma=True)
        wb = [P.sb([128, 16, 256], BF16, st) for _ in range(2)]
        psa = [P.ps([128, 512], F32, st) for _ in range(2)]
        psb = P.ps([128, 512], F32, st)
        psc = P.ps([128, 512], F32, st)
        x0 = [P.sb([128, 512], F32, st) for _ in range(2)]
        sqb = P.sb([128, 512], BF16, st)
        rr = P.sb([128, 512], F32, st)
        qn = P.sb([128, 512], F32, st)
        t1 = P.sb([128, 512], F32, st)
        t2 = P.sb([128, 512], F32, st)
        qo = [P.sb([128, NT], BF16, st) for _ in range(2)]
        vt = P.sb([128, 18, 256], BF16, st)
        gs = P.sb([128, 2], F32, st)
        qg0 = VOFF["qg"][0] + i; kg0 = VOFF["kg"][0] + i
        P.add("act", lambda e: e.mul(gs[:, 0:1], k.vecs[:, qg0:qg0 + 1], 128 ** -0.5), writes=["gs"])
        P.add("act", lambda e: e.copy(gs[:, 1:2], k.vecs[:, kg0:kg0 + 1]), reads=["gs"], writes=["gs"])
        nblk = 0
        nchunk = 0
        npa = 0
        for blk in range(5):
            b = nblk % 2; nblk += 1
            P.add("pool", lambda e, b=b, blk=blk: e.dma_start(out=wb[b][:], in_=k.even_w_in[i, blk]),
                  writes=[("wb", b)], dma=True)
            for s in range(2):
                ch = blk * 2 + s
                isq = ch < 8
                qb = nchunk % 2; nchunk += 1
                gcol = gs[:, 0:1] if isq else gs[:, 1:2]
                for ti, (t0, n) in enumerate(TILES):
                    pb = npa % 2; npa += 1
                    for kc in range(16):
                        P.add("pe", lambda e, b=b, s=s, pb=pb, kc=kc, t0=t0, n=n: e.matmul(
                            psa[pb][:, :n], wb[b][:, kc, s * 128:(s + 1) * 128], nx[:, kc, t0:t0 + n], start=(kc == 0), stop=(kc == 15)),
                            reads=[("wb", b)], writes=[("psa", pb)])
                    P.add("act", lambda e, pb=pb, n=n: e.activation(out=x0[pb][:, :n], in_=psa[pb][:, :n], func=AF.Copy),
                          reads=[("psa", pb)], writes=[("x0", pb)])
                    P.add("act", lambda e, pb=pb, n=n: e.activation(out=sqb[:, :n], in_=psa[pb][:, :n], func=AF.Square),
                          reads=[("psa", pb)], writes=["sqb"])
                    P.add("pe", lambda e, n=n: e.matmul(psb[:, :n], k.ones128[:], sqb[:, :n], start=True, stop=True),
                          reads=["sqb"], writes=["psb"])
                    rsqrt_ops(P, rr[:, :n], psb[:, :n], ["psb"], "rr")
                    if t0 < LCTX:
                        P.add("dve", lambda e, pb=pb, qb=qb, n=n, t0=t0, gcol=gcol: e.scalar_tensor_tensor(
                            out=qo[qb][:, t0:t0 + n], in0=x0[pb][:, :n], scalar=gcol, in1=rr[:, :n], op0=ALU.mult, op1=ALU.mult),
                            reads=[("x0", pb), "rr", "gs"], writes=[("qo", qb, ti)])
                    else:
                        P.add("dve", lambda e, pb=pb, n=n, gcol=gcol: e.scalar_tensor_tensor(
                            out=qn[:, :n], in0=x0[pb][:, :n], scalar=gcol, in1=rr[:, :n], op0=ALU.mult, op1=ALU.mult),
                            reads=[("x0", pb), "rr", "gs"], writes=["qn"])
                        P.add("pe", lambda e, n=n: e.matmul(psc[:, :n], k.consts[:, rp0:rp0 + 128], qn[:, :n], start=True, stop=True),
                              reads=["qn"], writes=["psc"])
                        c0 = cos0 + t0 - LCTX; s0 = sin0 + t0 - LCTX
                        P.add("dve", lambda e, n=n, c0=c0: e.tensor_tensor(out=t1[:, :n], in0=qn[:, :n], in1=rope[:, c0:c0 + n], op=ALU.mult),
                              reads=["qn", "rope"], writes=["t1"])
                        P.add("dve", lambda e, n=n, s0=s0: e.tensor_tensor(out=t2[:, :n], in0=psc[:, :n], in1=rope[:, s0:s0 + n], op=ALU.mult),
                              reads=["psc", "rope"], writes=["t2"])
                        P.add("pool", lambda e, qb=qb, n=n, t0=t0: e.tensor_tensor(out=qo[qb][:, t0:t0 + n], in0=t1[:, :n], in1=t2[:, :n], op=ALU.add),
                              reads=["t1", "t2"], writes=[("qo", qb, ti)])
                dst = k.QT[ch] if isq else k.KT[ch - 8]
                P.add("sp", lambda e, qb=qb, dst=dst: e.dma_start(out=dst, in_=qo[qb][:]),
                      reads=[("qo", qb, ti) for ti in range(5)], writes=[("qkT", ch)], dma=True)
        b = nblk % 2; nblk += 1
        P.add("pool", lambda e, b=b: e.dma_start(out=wb[b][:], in_=k.even_w_in[i, 5]), writes=[("wb", b)], dma=True)
        for tt in range(18):
            pb = npa % 2; npa += 1
            for kc in range(16):
                P.add("pe", lambda e, b=b, pb=pb, kc=kc, tt=tt: e.matmul(
                    psa[pb][:, :256], nx[:, kc, tt * 128:(tt + 1) * 128], wb[b][:, kc, :], start=(kc == 0), stop=(kc == 15)),
                    reads=[("wb", b)], writes=[("psa", pb)])
            P.add("act", lambda e, pb=pb, tt=tt: e.activation(out=vt[:, tt, :], in_=psa[pb][:, :256], func=AF.Copy),
                  reads=[("psa", pb)], writes=[("vt", tt)])
        P.add("sp", lambda e: e.dma_start(out=k.VTM.rearrange("t p c -> p t c"), in_=vt[:]),
              reads=[("vt", tt) for tt in range(18)], writes=["VTM"], dma=True)


def even_glu(k, l, nx):
    P = k.P
    i = l // 2
    cdw = VOFF["cdw"][0] + i * 31 * 8
    cdb = VOFF["cdb"][0] + i * 8
    lng = VOFF["lng"][0] + i * 8
    lnb = VOFF["lnb"][0] + i * 8
    PADW = NT + 60

    def pos(t):
        return t + 15 if t < LCTX else t + 45
    with contextlib.ExitStack() as st:
        wb = [P.sb([128, 16, 256], BF16, st) for _ in range(2)]
        psa = [P.ps([128, 512], F32, st) for _ in range(2)]
        psb = [P.ps([128, 512], F32, st) for _ in range(2)]
        psc = [P.ps([128, 512], F32, st) for _ in range(2)]
        psm = P.ps([128, 512], F32, st)
        psq = P.ps([128, 512], F32, st)
        sgm = [P.sb([128, 512], F32, st) for _ in range(2)]
        Ub = [P.sb([128, PADW], BF16, st) for _ in range(2)]
        Dg = [P.sb([128, 31, 128], BF16, st) for _ in range(2)]
        accA = [P.sb([128, NT], F32, st) for _ in range(2)]
        hsq = P.sb([128, NT], F32, st)
        co = [P.sb([128, NT], BF16, st) for _ in range(2)]
        msb = P.sb([128, 512], F32, st)
        m2 = P.sb([128, 512], F32, st)
        var = P.sb([128, 512], F32, st)
        dd = P.sb([128, 512], F32, st)
        for b in range(2):
            P.add("pool", lambda e, b=b: e.memset(Ub[b][:], 0.0), writes=[("Ub", b, ti) for ti in range(5)] + [("Ubz", b)])

        def proj(j):
            b = j % 2
            P.add("pool", lambda e: e.dma_start(out=wb[b][:], in_=k.even_w_in[i, 6 + j]), writes=[("wb", b)], dma=True)
            for kk in range(31):
                w = k.vecs[:, cdw + kk * 8 + j:cdw + kk * 8 + j + 1]
                P.add("dve", lambda e, kk=kk, w=w: e.tensor_scalar(out=Dg[b][:, kk, :], in0=k.identb[:], scalar1=w, scalar2=None, op0=ALU.mult),
                      writes=[("Dg", b, kk)])
            for ti, (t0, n) in enumerate(TILES):
                pb = ti % 2
                for kc in range(16):
                    P.add("pe", lambda e, pb=pb, kc=kc, t0=t0, n=n: e.matmul(
                        psa[pb][:, :n], wb[b][:, kc, 0:128], nx[:, kc, t0:t0 + n], start=(kc == 0), stop=(kc == 15)),
                        reads=[("wb", b)], writes=[("psa", pb)])
                for kc in range(16):
                    P.add("pe", lambda e, pb=pb, kc=kc, t0=t0, n=n: e.matmul(
                        psb[pb][:, :n], wb[b][:, kc, 128:256], nx[:, kc, t0:t0 + n], start=(kc == 0), stop=(kc == 15)),
                        reads=[("wb", b)], writes=[("psb", pb)])
                P.add("act", lambda e, pb=pb, n=n: e.activation(out=sgm[pb][:, :n], in_=psb[pb][:, :n], func=AF.Sigmoid),
                      reads=[("psb", pb)], writes=[("sgm", pb)])
                p0 = pos(t0)
                P.add("dve", lambda e, pb=pb, p0=p0, n=n: e.tensor_tensor(out=Ub[b][:, p0:p0 + n], in0=psa[pb][:, :n], in1=sgm[pb][:, :n], op=ALU.mult),
                      reads=[("psa", pb), ("sgm", pb), ("Ubz", b)], writes=[("Ub", b, ti)])

        def conv(j):
            b = j % 2
            Uk = [("Ub", b, ti) for ti in range(5)] + [("Ubz", b)]
            bias = k.vecs[:, cdb + j:cdb + j + 1]
            for ti, (t0, n) in enumerate(TILES):
                pb = ti % 2
                p0 = pos(t0)
                for kk in range(31):
                    P.add("pe", lambda e, pb=pb, kk=kk, p0=p0, n=n: e.matmul(
                        psc[pb][:, :n], Dg[b][:, kk, :], Ub[b][:, p0 + kk - 15:p0 + kk - 15 + n], start=(kk == 0), stop=(kk == 30)),
                        reads=Uk + [("Dg", b, kk)], writes=[("psc", pb)])
                P.add("act", lambda e, pb=pb, t0=t0, n=n: e.activation(out=accA[b][:, t0:t0 + n], in_=psc[pb][:, :n], func=AF.Identity,
                                                                      bias=bias, scale=1.0),
                      reads=[("psc", pb)], writes=[("accA", b, ti)])

        def lnorm(j):
            b = j % 2
            Ak = [("accA", b, ti) for ti in range(5)]
            P.add("act", lambda e: e.activation(out=hsq[:], in_=accA[b][:], func=AF.Square), reads=Ak, writes=["hsq"])
            for ti, (t0, n) in enumerate(TILES):
                P.add("pe", lambda e, t0=t0, n=n: e.matmul(psm[:, :n], k.onesf128[:], accA[b][:, t0:t0 + n], start=True, stop=True),
                      reads=Ak, writes=["psm"])
                P.add("pe", lambda e, t0=t0, n=n: e.matmul(psq[:, :n], k.onesf128[:], hsq[:, t0:t0 + n], start=True, stop=True),
                      reads=["hsq"], writes=["psq"])
                P.add("act", lambda e, n=n: e.activation(out=msb[:, :n], in_=psm[:, :n], func=AF.Copy), reads=["psm"], writes=["msb"])
                P.add("act", lambda e, n=n: e.activation(out=m2[:, :n], in_=psm[:, :n], func=AF.Square), reads=["psm"], writes=["m2"])
                P.add("dve", lambda e, n=n: e.tensor_tensor(out=var[:, :n], in0=psq[:, :n], in1=m2[:, :n], op=ALU.subtract),
                      reads=["psq", "m2"], writes=["var"])
                P.add("dve", lambda e, n=n: e.tensor_scalar_max(out=var[:, :n], in0=var[:, :n], scalar1=0.0), reads=["var"], writes=["var"])
                rsqrt_ops(P, var[:, :n], var[:, :n], ["var"], "var")
                P.add("dve", lambda e, t0=t0, n=n: e.tensor_tensor(out=dd[:, :n], in0=accA[b][:, t0:t0 + n], in1=msb[:, :n], op=ALU.subtract),
                      reads=Ak + ["msb"], writes=["dd"])
                P.add("dve", lambda e, n=n: e.tensor_tensor(out=dd[:, :n], in0=dd[:, :n], in1=var[:, :n], op=ALU.mult),
                      reads=["dd", "var"], writes=["dd"])
                P.add("act", lambda e, t0=t0, n=n: e.activation(
                    out=co[b][:, t0:t0 + n], in_=dd[:, :n], func=AF.Silu,
                    bias=k.vecs[:, lnb + j:lnb + j + 1], scale=k.vecs[:, lng + j:lng + j + 1]),
                    reads=["dd"], writes=[("co", b, ti)])
            P.add("sp", lambda e: e.dma_start(out=k.CAT[8 + j], in_=co[b][:]),
                  reads=[("co", b, ti) for ti in range(5)], writes=[("CAT", 8 + j)], dma=True)

        for j in range(8):
            proj(j)
            if j >= 1:
                lnorm(j - 1)
            conv(j)
        lnorm(7)


def even_attn(k, l):
    P = k.P
    i = l // 2
    band0 = COFF["band"][0]
    with contextlib.ExitStack() as st:
        kT = P.sb([128, 2, NT], BF16, st)
        V = P.sb([128, 18, 256], BF16, st)
        qh = [P.sb([128, NT], BF16, st) for _ in range(2)]
        PC = [[P.sb([128, NT], BF16, st) for _ in range(2)] for _ in range(2)]
        PB = [P.sb([128, 16, 384], BF16, st) for _ in range(2)]
        tmpE = [P.sb([128, 384], BF16, st) for _ in range(2)]
        ao = [P.sb([128, NT], BF16, st) for _ in range(2)]
        den = P.sb([128, 512], F32, st)
        esink = P.sb([128, 8], F32, st)
        pss = [P.ps([128, 512], F32, st) for _ in range(2)]
        pso = [P.ps([128, 512], F32, st) for _ in range(2)]
        psd = [P.ps([128, 512], F32, st) for _ in range(2)]
        so = VOFF["sink"][0] + i * 8
        P.add("act", lambda e: e.activation(out=esink[:], in_=k.vecs[:, so:so + 8], func=AF.Exp), writes=["esink"])
        for g in range(2):
            P.add("sp", lambda e, g=g: e.dma_start(out=kT[:, g, :], in_=k.KT[g]), writes=[("kT", g)], dma=True)
        P.add("sp", lambda e: e.dma_start(out=V[:], in_=k.VTM.rearrange("t p c -> p t c")), writes=["V"], dma=True)
        nps = 0
        npo = 0
        for h in range(8):
            g = h // 4
            hb = h % 2
            P.add("sp", lambda e, hb=hb, h=h: e.dma_start(out=qh[hb][:], in_=k.QT[h]), writes=[("qh", hb)], dma=True)
            for cb in range(2):
                for ti, (t0, n) in enumerate(TILES):
                    pb = nps % 2; nps += 1
                    P.add("pe", lambda e, pb=pb, g=g, cb=cb, hb=hb, t0=t0, n=n: e.matmul(
                        pss[pb][:, :n], kT[:, g, cb * 128:(cb + 1) * 128], qh[hb][:, t0:t0 + n], start=True, stop=True),
                        reads=[("kT", g), ("qh", hb)], writes=[("pss", pb)])
                    P.add("act", lambda e, pb=pb, hb=hb, cb=cb, t0=t0, n=n: e.activation(
                        out=PC[hb][cb][:, t0:t0 + n], in_=pss[pb][:, :n], func=AF.Exp),
                        reads=[("pss", pb)], writes=[("PC", hb, cb, ti)])
            for jb in range(16):
                lo = max(jb - 1, 0); hi = min(jb + 1, 15)
                n = (hi - lo + 1) * 128
                off = (lo - (jb - 1)) * 128
                pb = nps % 2; nps += 1
                eb = jb % 2
                P.add("pe", lambda e, pb=pb, g=g, jb=jb, hb=hb, lo=lo, n=n: e.matmul(
                    pss[pb][:, :n], kT[:, g, LCTX + jb * 128:LCTX + (jb + 1) * 128], qh[hb][:, LCTX + lo * 128:LCTX + lo * 128 + n],
                    start=True, stop=True), reads=[("kT", g), ("qh", hb)], writes=[("pss", pb)])
                P.add("act", lambda e, pb=pb, eb=eb, n=n: e.activation(out=tmpE[eb][:, :n], in_=pss[pb][:, :n], func=AF.Exp),
                      reads=[("pss", pb)], writes=[("tmpE", eb)])
                P.add("pool", lambda e, eb=eb, hb=hb, jb=jb, off=off, n=n: e.tensor_tensor(
                    out=PB[hb][:, jb, off:off + n], in0=tmpE[eb][:, :n], in1=k.bandb[:, off:off + n], op=ALU.mult),
                    reads=[("tmpE", eb)], writes=[("PB", hb, jb)])
            for ti, (t0, n) in enumerate(TILES):
                ob = npo % 2; npo += 1
                mms = []
                for cb in range(2):
                    mms.append((V[:, cb, g * 128:(g + 1) * 128], PC[hb][cb][:, t0:t0 + n], 0, n, [("PC", hb, cb, ti)]))
                if t0 >= LCTX:
                    qt = (t0 - LCTX) // 512
                    for nb in range(4 * qt, 4 * qt + 4):
                        for jb in (nb - 1, nb, nb + 1):
                            if 0 <= jb <= 15:
                                c0 = (nb - (jb - 1)) * 128
                                mms.append((V[:, 2 + jb, g * 128:(g + 1) * 128], PB[hb][:, jb, c0:c0 + 128], (nb - 4 * qt) * 128, 128,
                                            [("PB", hb, jb)]))
                for mi, (lhs, rhs, o0, on, rk) in enumerate(mms):
                    P.add("pe", lambda e, ob=ob, lhs=lhs, rhs=rhs, o0=o0, on=on, mi=mi, last=(mi == len(mms) - 1): e.matmul(
                        pso[ob][:, o0:o0 + on], lhs, rhs, start=(mi == 0), stop=last, skip_group_check=True),
                        reads=rk + ["V"], writes=[("pso", ob)])
                for mi, (lhs, rhs, o0, on, rk) in enumerate(mms):
                    P.add("pe", lambda e, ob=ob, rhs=rhs, o0=o0, on=on, mi=mi, last=(mi == len(mms) - 1): e.matmul(
                        psd[ob][:, o0:o0 + on], k.ones1[:], rhs, start=(mi == 0), stop=last, skip_group_check=True),
                        reads=rk, writes=[("psd", ob)])
                P.add("dve", lambda e, ob=ob, n=n, h=h: e.tensor_scalar_add(out=den[:, :n], in0=psd[ob][:, :n], scalar1=esink[:, h:h + 1]),
                      reads=[("psd", ob), "esink"], writes=["den"])
                P.add("dve", lambda e, n=n: e.reciprocal(den[:, :n], den[:, :n]), reads=["den"], writes=["den"])
                P.add("dve", lambda e, ob=ob, hb=hb, t0=t0, n=n: e.tensor_tensor(out=ao[hb][:, t0:t0 + n], in0=pso[ob][:, :n], in1=den[:, :n], op=ALU.mult),
                      reads=[("pso", ob), "den"], writes=[("ao", hb, ti)])
            P.add("sp", lambda e, hb=hb, h=h: e.dma_start(out=k.CAT[h], in_=ao[hb][:]),
                  reads=[("ao", hb, ti) for ti in range(5)], writes=[("CAT", h)], dma=True)


def proj_out_residual(k, wsrc, nkc, cat, gate_m, wkeys_extra=()):
    P = k.P
    hsrc = k.hT.rearrange("(c p) t -> p c t", p=128)
    with contextlib.ExitStack() as st:
        wb = [P.sb([128, nkc, 256], BF16, st) for _ in range(2)]
        hrow = [P.sb([128, NT], F32, st) for _ in range(2)]
        pso = [P.ps([128, 512], F32, st) for _ in range(4)]
        nps = 0
        for blk in range(8):
            b = blk % 2
            P.add("pool", lambda e, b=b, blk=blk: e.dma_start(out=wb[b][:], in_=wsrc(blk)),
                  writes=[("wb", b)], dma=True)
            for s in range(2):
                c = blk * 2 + s
                hb = c % 2
                P.add("sp", lambda e, hb=hb, c=c: e.dma_start(out=hrow[hb][:], in_=hsrc[:, c, :]),
                      reads=[("hT", c)], writes=[("hrow", hb)], dma=True)
                for ti, (t0, n) in enumerate(TILES):
                    pb = nps % 4; nps += 1
                    for kc in range(nkc):
                        P.add("pe", lambda e, b=b, s=s, pb=pb, kc=kc, t0=t0, n=n: e.matmul(
                            pso[pb][:, :n], wb[b][:, kc, s * 128:(s + 1) * 128], cat[:, kc, t0:t0 + n],
                            start=(kc == 0), stop=(kc == nkc - 1)),
                            reads=[("wb", b), ("cat", kc)], writes=[("pso", pb)])
                    sidx = 1 if t0 < LCTX else 0
                    P.add("dve", lambda e, hb=hb, pb=pb, t0=t0, n=n, c=c, sidx=sidx: e.scalar_tensor_tensor(
                        out=hrow[hb][:, t0:t0 + n], in0=pso[pb][:, :n], scalar=modv(k, gate_m, c, sidx),
                        in1=hrow[hb][:, t0:t0 + n], op0=ALU.mult, op1=ALU.add),
                        reads=[("pso", pb), ("hrow", hb), "mod"], writes=[("hrow", hb)])
                P.add("sp", lambda e, hb=hb, c=c: e.dma_start(out=hsrc[:, c, :], in_=hrow[hb][:]),
                      reads=[("hrow", hb)], writes=[("hT", c)], dma=True)


def even_out(k, l):
    P = k.P
    i = l // 2
    wsrc = lambda blk: k.even_w_out[i, blk]
    with contextlib.ExitStack() as st:
        cat = P.sb([128, 16, NT], BF16, st)
        for c in range(16):
            P.add("sp", lambda e, c=c: e.dma_start(out=cat[:, c, :], in_=k.CAT[c]), writes=[("cat", c)], dma=True)
        proj_out_residual(k, wsrc, 16, cat, 2)


DN_STOP = 99


def stage_dn(k, l):
    P = k.P
    with contextlib.ExitStack() as st1:
        k.BETA = P.sb([128, 18, 64], F32, st1)
        k.GG = P.sb([128, 18, 64], F32, st1)
        k.EG = P.sb([128, 18, 64], F32, st1)
        k.EL = P.sb([128, 18, 64], F32, st1)
        k.BEG = P.sb([128, 18, 64], F32, st1)
        k.ER = P.sb([128, 18, 64], F32, st1)
        with contextlib.ExitStack() as st0:
            nx = P.sb([128, 16, NT], BF16, st0)
            stage_norm(k, 1, nx, nbuf=1)
            P.barrier()
            if DN_STOP >= 1:
                dn_proj_qkv(k, l, nx)
                P.barrier()
            if DN_STOP >= 2:
                dn_proj_z(k, l, nx)
                P.barrier()
            if DN_STOP >= 3:
                dn_proj_ba(k, l, nx)
                P.barrier()
        if DN_STOP >= 4:
            dn_gates(k, l)
            P.barrier()
            if "DBG_G" in DEBUG_OUT:
                for ii, t in enumerate((k.BETA, k.GG, k.EG, k.EL, k.ER)):
                    P.add("sp", lambda e, ii=ii, t=t: e.dma_start(out=k.DBG_G[ii], in_=t[:]), dma=True)
                P.barrier()
        if DN_STOP >= 5:
            dn_prep(k, l)
            P.barrier()
        if DN_STOP >= 6:
            dn_scan(k, l)
            P.barrier()
    if DN_STOP >= 7:
        dn_outnorm(k, l)
        P.barrier()
    if DN_STOP >= 8:
        dn_outproj(k, l)
        P.barrier()


def dn_proj_qkv(k, l, nx):
    P = k.P
    i = l // 2
    dcw = VOFF["dcw"][0] + i * 5 * 64
    PADW = NT + 8

    def pos(t):
        return t + 2 if t < LCTX else t + 6
    with contextlib.ExitStack() as st:
        wb = [P.sb([128, 16, 256], BF16, st) for _ in range(2)]
        psa = [P.ps([128, 512], F32, st) for _ in range(2)]
        psc = [P.ps([128, 512], F32, st) for _ in range(2)]
        psb = P.ps([128, 512], F32, st)
        pst = [P.ps([128, 4, 128], BF16, st) for _ in range(2)]
        Ub = [P.sb([128, PADW], BF16, st) for _ in range(2)]
        Dg = [P.sb([128, 5, 128], BF16, st) for _ in range(2)]
        accs = [P.sb([128, NT], F32, st) for _ in range(2)]
        sq = P.sb([128, NT], BF16, st)
        rr = P.sb([128, 512], F32, st)
        qo = [P.sb([128, NT], BF16, st) for _ in range(2)]
        tm = P.sb([128, 18, 128], BF16, st)
        for b in range(2):
            P.add("pool", lambda e, b=b: e.memset(Ub[b][:], 0.0), writes=[("Ub", b, ti) for ti in range(5)] + [("Ubz", b)])
        npa = 0
        npt = 0
        for blk in range(32):
            b = blk % 2
            P.add("pool", lambda e, b=b, blk=blk: e.dma_start(out=wb[b][:], in_=k.dn_w_in[i, blk]), writes=[("wb", b)], dma=True)
            for s in range(2):
                ch = blk * 2 + s
                ub = ch % 2
                acc = accs[ch % 2]
                ak = ("acc", ch % 2)
                qb = ch % 2
                kind = "q" if ch < 16 else ("k" if ch < 32 else "v")
                for kk in range(5):
                    w = k.vecs[:, dcw + kk * 64 + ch:dcw + kk * 64 + ch + 1]
                    P.add("dve", lambda e, ub=ub, kk=kk, w=w: e.tensor_scalar(out=Dg[ub][:, kk, :], in0=k.identb[:], scalar1=w, scalar2=None, op0=ALU.mult),
                          writes=[("Dg", ub, kk)])
                for ti, (t0, n) in enumerate(TILES):
                    pb = npa % 2; npa += 1
                    for kc in range(16):
                        P.add("pe", lambda e, b=b, s=s, pb=pb, kc=kc, t0=t0, n=n: e.matmul(
                            psa[pb][:, :n], wb[b][:, kc, s * 128:(s + 1) * 128], nx[:, kc, t0:t0 + n], start=(kc == 0), stop=(kc == 15)),
                            reads=[("wb", b)], writes=[("psa", pb)])
                    p0 = pos(t0)
                    P.add("act", lambda e, ub=ub, pb=pb, p0=p0, n=n: e.activation(out=Ub[ub][:, p0:p0 + n], in_=psa[pb][:, :n], func=AF.Copy),
                          reads=[("psa", pb), ("Ubz", ub)], writes=[("Ub", ub, ti)])
                Uk = [("Ub", ub, ti) for ti in range(5)] + [("Ubz", ub)]
                for ti, (t0, n) in enumerate(TILES):
                    pc = ti % 2
                    p0 = pos(t0)
                    for kk in range(5):
                        P.add("pe", lambda e, ub=ub, pc=pc, kk=kk, p0=p0, n=n: e.matmul(
                            psc[pc][:, :n], Dg[ub][:, kk, :], Ub[ub][:, p0 + kk - 2:p0 + kk - 2 + n], start=(kk == 0), stop=(kk == 4)),
                            reads=Uk + [("Dg", ub, kk)], writes=[("psc", pc)])
                    if kind == "v":
                        P.add("act", lambda e, qb=qb, pc=pc, t0=t0, n=n: e.activation(out=qo[qb][:, t0:t0 + n], in_=psc[pc][:, :n], func=AF.Silu),
                              reads=[("psc", pc)], writes=[("qo", qb)])
                    else:
                        P.add("act", lambda e, acc=acc, pc=pc, t0=t0, n=n: e.activation(out=acc[:, t0:t0 + n], in_=psc[pc][:, :n], func=AF.Silu),
                              reads=[("psc", pc)], writes=[ak])
                if kind != "v":
                    P.add("act", lambda e, acc=acc: e.activation(out=sq[:], in_=acc[:], func=AF.Square), reads=[ak], writes=["sq"])
                    for ti, (t0, n) in enumerate(TILES):
                        P.add("pe", lambda e, t0=t0, n=n: e.matmul(psb[:, :n], k.ones1[:], sq[:, t0:t0 + n], start=True, stop=True),
                              reads=["sq"], writes=["psb"])
                        rsqrt_ops(P, rr[:, :n], psb[:, :n], ["psb"], "rr")
                        sc = (128 ** -0.5) if kind == "q" else 1.0
                        P.add("dve", lambda e, acc=acc, qb=qb, t0=t0, n=n, sc=sc: e.scalar_tensor_tensor(
                            out=qo[qb][:, t0:t0 + n], in0=acc[:, t0:t0 + n], scalar=sc, in1=rr[:, :n], op0=ALU.mult, op1=ALU.mult),
                            reads=[ak, "rr"], writes=[("qo", qb)])
                if kind == "q":
                    P.add("sp", lambda e, qb=qb, ch=ch: e.dma_start(out=k.DQT[ch], in_=qo[qb][:]), reads=[("qo", qb)], writes=[("DQT", ch)], dma=True)
                else:
                    if kind == "k":
                        P.add("sp", lambda e, qb=qb, ch=ch: e.dma_start(out=k.DKT[ch - 16], in_=qo[qb][:]), reads=[("qo", qb)],
                              writes=[("DKT", ch)], dma=True)
                    for c4 in range(0, 18, 4):
                        nn = min(4, 18 - c4)
                        tb = npt % 2; npt += 1
                        for cc in range(nn):
                            c = c4 + cc
                            P.add("pe", lambda e, tb=tb, cc=cc, c=c, qb=qb: e.transpose(
                                out=pst[tb][:, cc, :], in_=qo[qb][:, c * 128:(c + 1) * 128], identity=k.identb[:]),
                                reads=[("qo", qb)], writes=[("pst", tb)])
                        P.add("act", lambda e, tb=tb, c4=c4, nn=nn: e.activation(out=tm[:, c4:c4 + nn, :], in_=pst[tb][:, :nn, :], func=AF.Copy),
                              reads=[("pst", tb)], writes=[("tm", 0)])
                    if kind == "k":
                        dst = k.DKTM[:, :, (ch - 16) * 128:(ch - 15) * 128]
                    else:
                        dst = k.DVTM[:, :, (ch - 32) * 128:(ch - 31) * 128]
                    P.add("sp", lambda e, dst=dst: e.dma_start(out=dst.rearrange("c p d -> p c d"), in_=tm[:]),
                          reads=[("tm", 0)], writes=[("DTM", ch)], dma=True)


def dn_proj_z(k, l, nx):
    P = k.P
    i = l // 2
    with contextlib.ExitStack() as st:
        wb = [P.sb([128, 16, 512], BF16, st) for _ in range(2)]
        psa = [P.ps([128, 512], F32, st) for _ in range(4)]
        zs = [P.sb([128, 512], BF16, st) for _ in range(4)]
        npa = 0
        for blk in range(8):
            b = blk % 2
            for q in range(2):
                P.add("pool", lambda e, b=b, blk=blk, q=q: e.dma_start(
                    out=wb[b][:, :, q * 256:(q + 1) * 256], in_=k.dn_w_in[i, 32 + blk * 2 + q]),
                    writes=[("wb", b, q)], dma=True)
            for c in range(18):
                pb = npa % 4; npa += 1
                for kc in range(16):
                    P.add("pe", lambda e, b=b, pb=pb, kc=kc, c=c: e.matmul(
                        psa[pb][:], nx[:, kc, c * 128:(c + 1) * 128], wb[b][:, kc, :], start=(kc == 0), stop=(kc == 15)),
                        reads=[("wb", b, 0), ("wb", b, 1)], writes=[("psa", pb)])
                P.add("act", lambda e, pb=pb: e.activation(out=zs[pb][:], in_=psa[pb][:], func=AF.Silu),
                      reads=[("psa", pb)], writes=[("zs", pb)])
                P.add("sp", lambda e, pb=pb, c=c, blk=blk: e.dma_start(out=k.DZ[c, :, blk * 512:(blk + 1) * 512], in_=zs[pb][:]),
                      reads=[("zs", pb)], writes=[("DZ", c, blk)], dma=True)


def dn_proj_ba(k, l, nx):
    P = k.P
    i = l // 2
    with contextlib.ExitStack() as st:
        wb = P.sb([128, 16, 128], BF16, st)
        BA = P.sb([128, 18, 128], F32, st)
        psa = [P.ps([128, 512], F32, st) for _ in range(2)]
        P.add("pool", lambda e: e.dma_start(out=wb[:], in_=k.dn_w_ba[i]), writes=["wb"], dma=True)
        for c in range(18):
            pb = c % 2
            for kc in range(16):
                P.add("pe", lambda e, pb=pb, kc=kc, c=c: e.matmul(
                    psa[pb][:, 0:128], nx[:, kc, c * 128:(c + 1) * 128], wb[:, kc, :], start=(kc == 0), stop=(kc == 15)),
                    reads=["wb"], writes=[("psa", pb)])
            P.add("act", lambda e, pb=pb, c=c: e.activation(out=BA[:, c, :], in_=psa[pb][:, 0:128], func=AF.Copy),
                  reads=[("psa", pb)], writes=[("BA", c)])
        BAk = [("BA", c) for c in range(18)]
        P.add("act", lambda e: e.activation(out=k.BETA[:], in_=BA[:, :, 0:64], func=AF.Sigmoid), reads=BAk, writes=["BETA"])
        P.add("dve", lambda e: e.tensor_copy(k.GG[:], BA[:, :, 64:128]), reads=BAk, writes=["GG"])
        P.barrier()


def dn_gates(k, l):
    P = k.P
    i = l // 2
    al = VOFF["alog"][0] + i * 64
    db = VOFF["dtb"][0] + i * 64
    with contextlib.ExitStack() as st:
        x = P.sb([128, 18, 64], F32, st)
        ax = P.sb([128, 18, 64], F32, st)
        nA = P.sb([128, 64], F32, st)
        onesf = P.sb([128, 128], F32, st)
        ps = [P.ps([128, 3, 64], F32, st) for _ in range(2)]
        P.add("pool", lambda e: e.memset(onesf[:], 1.0), writes=["onesf"])
        P.add("dve", lambda e: e.tensor_tensor(out=x[:], in0=k.GG[:], in1=bc(k.vecs[:, db:db + 64].unsqueeze(1), [128, 18, 64]), op=ALU.add),
              writes=["x"])
        P.add("act", lambda e: e.activation(out=ax[:], in_=x[:], func=AF.Abs), reads=["x"], writes=["ax"])
        P.add("act", lambda e: e.activation(out=ax[:], in_=ax[:], func=AF.Exp, scale=-1.0), reads=["ax"], writes=["ax"])
        P.add("act", lambda e: e.activation(out=ax[:], in_=ax[:], func=AF.Ln, bias=1.0), reads=["ax"], writes=["ax"])
        P.add("dve", lambda e: e.tensor_scalar_max(out=x[:], in0=x[:], scalar1=0.0), reads=["x"], writes=["x"])
        P.add("dve", lambda e: e.tensor_tensor(out=x[:], in0=x[:], in1=ax[:], op=ALU.add), reads=["x", "ax"], writes=["x"])
        P.add("act", lambda e: e.activation(out=nA[:], in_=k.vecs[:, al:al + 64], func=AF.Exp), writes=["nA"])
        P.add("dve", lambda e: e.scalar_tensor_tensor(out=k.GG[:], in0=x[:], scalar=-1.0, in1=bc(nA[:].unsqueeze(1), [128, 18, 64]),
                                                      op0=ALU.mult, op1=ALU.mult), reads=["x", "nA"], writes=["GG"])
        mo = {n: COFF[n][0] for n in ("m_le", "m_ge", "m_lt", "m_gt")}
        for c in range(18):
            pb = c % 2
            for d in range(2):
                m_gc = mo["m_le"] if d == 0 else mo["m_ge"]
                m_rm = mo["m_gt"] if d == 0 else mo["m_lt"]
                rhs = k.GG[:, c, d * 32:(d + 1) * 32]
                P.add("pe", lambda e, pb=pb, d=d, m=m_gc, rhs=rhs: e.matmul(ps[pb][:, 0, d * 32:(d + 1) * 32], k.consts[:, m:m + 128], rhs, start=True, stop=True),
                      reads=["GG"], writes=[("ps", pb)])
                P.add("pe", lambda e, pb=pb, d=d, m=m_rm, rhs=rhs: e.matmul(ps[pb][:, 1, d * 32:(d + 1) * 32], k.consts[:, m:m + 128], rhs, start=True, stop=True),
                      reads=["GG"], writes=[("ps", pb)])
                P.add("pe", lambda e, pb=pb, d=d, rhs=rhs: e.matmul(ps[pb][:, 2, d * 32:(d + 1) * 32], onesf[:], rhs, start=True, stop=True),
                      reads=["GG", "onesf"], writes=[("ps", pb)])
            P.add("act", lambda e, pb=pb, c=c: e.activation(out=k.EG[:, c, :], in_=ps[pb][:, 0, :], func=AF.Exp), reads=[("ps", pb)], writes=[("EG", c)])
            P.add("act", lambda e, pb=pb, c=c: e.activation(out=k.ER[:, c, :], in_=ps[pb][:, 1, :], func=AF.Exp), reads=[("ps", pb)], writes=[("ER", c)])
            P.add("act", lambda e, pb=pb, c=c: e.activation(out=k.EL[:, c, :], in_=ps[pb][:, 2, :], func=AF.Exp), reads=[("ps", pb)], writes=[("EL", c)])
            P.add("dve", lambda e, c=c: e.tensor_tensor(out=k.BEG[:, c, :], in0=k.BETA[:, c, :], in1=k.EG[:, c, :], op=ALU.mult),
                  reads=[("EG", c)], writes=[("BEG", c)])


def dn_prep(k, l):
    P = k.P
    mo = {n: COFF[n][0] for n in ("m_le", "m_ge", "m_lt", "m_gt", "ident")}
    NL = 2
    with contextlib.ExitStack() as st:
        kTc = [P.sb([128, 16, 128], BF16, st) for _ in range(2)]
        qTc = [P.sb([128, 16, 128], BF16, st) for _ in range(2)]
        ktm = [P.sb([128, 16, 128], BF16, st) for _ in range(2)]
        vtm = [P.sb([128, 32, 128], BF16, st) for _ in range(2)]
        lanes = []
        for ln in range(NL):
            B_ = K()
            B_.Gm = P.sb([128, 2, 128], F32, st); B_.QKm = P.sb([128, 2, 128], F32, st)
            B_.rhs2 = P.sb([128, 4, 128], F32, st); B_.eD = P.sb([128, 4, 128], F32, st)
            B_.L0 = P.sb([128, 4, 128], BF16, st); B_.QKD = P.sb([128, 4, 128], BF16, st)
            B_.LtQ = P.sb([128, 8, 128], BF16, st)
            B_.Xb = [P.sb([128, 4, 128], BF16, st) for _ in range(2)]
            B_.Yb = P.sb([128, 4, 128], BF16, st)
            B_.Zb = [P.sb([128, 4, 128], BF16, st) for _ in range(2)]
            B_.Ab = [P.sb([128, 4, 128], BF16, st) for _ in range(2)]
            B_.Bb = [P.sb([128, 4, 128], BF16, st) for _ in range(2)]
            B_.Lm = [P.sb([128, 4, 128], BF16, st) for _ in range(5)]
            B_.Ltm = [P.sb([128, 4, 128], BF16, st) for _ in range(4)]
            B_.T1a = P.sb([128, 4, 128], BF16, st); B_.T1b = P.sb([128, 4, 128], BF16, st)
            B_.vb = P.sb([128, 4, 128], BF16, st); B_.kbg = P.sb([128, 4, 128], BF16, st)
            B_.PH = [P.sb([128, 4, 4, 128], BF16, st) for _ in range(2)]
            B_.cnt = 0
            B_.psT = P.ps([128, 8, 128], BF16, st)
            B_.psX = P.ps([128, 4, 128], F32, st); B_.psY = P.ps([128, 4, 128], F32, st); B_.psZ = P.ps([128, 4, 128], F32, st)
            lanes.append(B_)

        def unit(ln, c, d, hg, cb):
            B_ = lanes[ln]
            K_ = lambda name, *a: (name, ln) + a
            Gm, QKm, rhs2, eD, L0, QKD, LtQ = B_.Gm, B_.QKm, B_.rhs2, B_.eD, B_.L0, B_.QKD, B_.LtQ
            Xb, Yb, Zb, Ab, Bb, Lm, Ltm, T1a, T1b = B_.Xb, B_.Yb, B_.Zb, B_.Ab, B_.Bb, B_.Lm, B_.Ltm, B_.T1a, B_.T1b
            vb, kbg = B_.vb, B_.kbg
            pi = B_.cnt % 2
            B_.cnt += 1
            PH = B_.PH[pi]
            Uo, Wo, Qo, kd = PH[:, 0], PH[:, 1], PH[:, 2], PH[:, 3]
            K2 = lambda name, *a: (name, ln, pi) + a
            psT, psX, psY, psZ = B_.psT, B_.psX, B_.psY, B_.psZ
            m_strict = mo["m_gt"] if d == 0 else mo["m_lt"]
            m_incl = mo["m_ge"] if d == 0 else mo["m_le"]
            m_l = mo["m_le"] if d == 0 else mo["m_ge"]
            m_r = mo["m_gt"] if d == 0 else mo["m_lt"]
            h0 = d * 32 + hg * 4
            kh0 = hg * 2
            P.add("pool", lambda e: e.tensor_tensor(
                out=rhs2[:], in0=bc(k.consts[:, m_r:m_r + 128].unsqueeze(1), [128, 4, 128]),
                in1=bc(k.GG[:, c, h0:h0 + 4].unsqueeze(2), [128, 4, 128]), op=ALU.mult), writes=[K_("rhs2")])
            for a in range(2):
                P.add("pe", lambda e, a=a: e.matmul(psX[:, a, :], kTc[cb][:, kh0 + a, :], kTc[cb][:, kh0 + a, :], start=True, stop=True),
                      reads=[("kTc", cb)], writes=[K_("psX")])
                P.add("pe", lambda e, a=a: e.matmul(psX[:, 2 + a, :], qTc[cb][:, kh0 + a, :], kTc[cb][:, kh0 + a, :], start=True, stop=True),
                      reads=[("kTc", cb), ("qTc", cb)], writes=[K_("psX")])
            P.add("pe", lambda e: e.matmul(psY[:], k.consts[:, m_l:m_l + 128], rhs2[:], start=True, stop=True),
                  reads=[K_("rhs2")], writes=[K_("psY")])
            P.add("dve", lambda e: e.tensor_tensor(out=Gm[:], in0=psX[:, 0:2, :], in1=bc(k.consts[:, m_strict:m_strict + 128].unsqueeze(1), [128, 2, 128]), op=ALU.mult),
                  reads=[K_("psX")], writes=[K_("Gm")])
            P.add("dve", lambda e: e.tensor_tensor(out=QKm[:], in0=psX[:, 2:4, :], in1=bc(k.consts[:, m_incl:m_incl + 128].unsqueeze(1), [128, 2, 128]), op=ALU.mult),
                  reads=[K_("psX")], writes=[K_("QKm")])
            P.add("act", lambda e: e.activation(out=eD[:], in_=psY[:], func=AF.Exp), reads=[K_("psY")], writes=[K_("eD")])
            yield
            for hl in range(4):
                P.add("dve", lambda e, hl=hl: e.scalar_tensor_tensor(
                    out=L0[:, hl, :], in0=Gm[:, hl // 2, :], scalar=k.BETA[:, c, h0 + hl:h0 + hl + 1], in1=eD[:, hl, :],
                    op0=ALU.mult, op1=ALU.mult), reads=[K_("Gm"), K_("eD")], writes=[K_("L0", hl)])
            P.add("dve", lambda e: e.tensor_tensor(
                out=QKD[:].rearrange("p (a b) j -> p a b j", b=2), in0=bc(QKm[:].unsqueeze(2), [128, 2, 2, 128]),
                in1=eD[:].rearrange("p (a b) j -> p a b j", b=2), op=ALU.mult), reads=[K_("QKm"), K_("eD")], writes=[K_("QKD")])
            L0k = [K_("L0", hl) for hl in range(4)]
            for hl in range(4):
                P.add("pe", lambda e, hl=hl: e.transpose(out=psT[:, hl, :], in_=L0[:, hl, :], identity=k.identb[:]),
                      reads=L0k, writes=[K_("psT")])
            for hl in range(4):
                P.add("pe", lambda e, hl=hl: e.transpose(out=psT[:, 4 + hl, :], in_=QKD[:, hl, :], identity=k.identb[:]),
                      reads=[K_("QKD")], writes=[K_("psT")])
            P.add("act", lambda e: e.activation(out=LtQ[:, 0:4, :], in_=psT[:, 0:4, :], func=AF.Copy), reads=[K_("psT")], writes=[K_("LtQ")])
            P.add("act", lambda e: e.activation(out=Qo, in_=psT[:, 4:8, :], func=AF.Copy), reads=[K_("psT")], writes=[K2("Qo")])
            yield
            Lt = LtQ[:, 0:4, :]
            mk = lambda q: bc(k.maskb[:, q, :].unsqueeze(1), [128, 4, 128])
            P.add("dve", lambda e: e.tensor_tensor(out=Lm[0][:], in0=L0[:], in1=mk(0), op=ALU.mult), reads=L0k, writes=[K_("Lm", 0)])
            P.add("dve", lambda e: e.tensor_tensor(out=Ltm[0][:], in0=Lt, in1=mk(0), op=ALU.mult), reads=[K_("LtQ")], writes=[K_("Ltm", 0)])
            for q in range(1, 5):
                eng_m = "pool"
                P.add(eng_m, lambda e, q=q: e.tensor_tensor(out=Lm[q][:], in0=L0[:], in1=mk(q), op=ALU.mult), reads=L0k, writes=[K_("Lm", q)])
                if q < 4:
                    P.add(eng_m, lambda e, q=q: e.tensor_tensor(out=Ltm[q][:], in0=Lt, in1=mk(q), op=ALU.mult), reads=[K_("LtQ")], writes=[K_("Ltm", q)])
            for hl in range(4):
                P.add("act", lambda e, hl=hl: e.activation(out=vb[:, hl, :], in_=vtm[cb][:, hg * 4 + hl, :], func=AF.Identity,
                                                           scale=k.BETA[:, c, h0 + hl:h0 + hl + 1]), reads=[("vtm", cb)], writes=[K_("vb", hl)])
                P.add("act", lambda e, hl=hl: e.activation(out=kbg[:, hl, :], in_=ktm[cb][:, kh0 + hl // 2, :], func=AF.Identity,
                                                           scale=k.BEG[:, c, h0 + hl:h0 + hl + 1]), reads=[("ktm", cb)], writes=[K_("kbg", hl)])
                P.add("act", lambda e, hl=hl: e.activation(out=kd[:, hl, :], in_=ktm[cb][:, kh0 + hl // 2, :], func=AF.Identity,
                                                           scale=k.ER[:, c, h0 + hl:h0 + hl + 1]), reads=[("ktm", cb)], writes=[K2("kd", hl)])
            for hl in range(4):
                P.add("pe", lambda e, hl=hl: e.matmul(psX[:, hl, :], Ltm[0][:, hl, :], Lm[0][:, hl, :], start=True, stop=True),
                      reads=[K_("Lm", 0), K_("Ltm", 0)], writes=[K_("psX")])
            for hl in range(4):
                P.add("pe", lambda e, hl=hl: e.matmul(psY[:, hl, :], Lm[0][:, hl, :], Ltm[0][:, hl, :], start=True, stop=True),
                      reads=[K_("Lm", 0), K_("Ltm", 0)], writes=[K_("psY")])
            P.add("act", lambda e: e.activation(out=Xb[0][:], in_=psX[:], func=AF.Copy), reads=[K_("psX")], writes=[K_("X", 0)])
            P.add("act", lambda e: e.activation(out=Yb[:], in_=psY[:], func=AF.Copy), reads=[K_("psY")], writes=[K_("Y")])
            P.add("dve", lambda e: e.scalar_tensor_tensor(
                out=Zb[0][:], in0=Ltm[0][:], scalar=-1.0, in1=bc(k.identb[:].unsqueeze(1), [128, 4, 128]),
                op0=ALU.mult, op1=ALU.add), reads=[K_("Ltm", 0)], writes=[K_("Z", 0)])
            yield
            for hl in range(4):
                P.add("pe", lambda e, hl=hl: e.matmul(psX[:, hl, :], Yb[:, hl, :], Xb[0][:, hl, :], start=True, stop=True),
                      reads=[K_("X", 0), K_("Y")], writes=[K_("psX")])
            for hl in range(4):
                P.add("pe", lambda e, hl=hl: e.matmul(psZ[:, hl, :], Xb[0][:, hl, :], Zb[0][:, hl, :], start=True, stop=True),
                      reads=[K_("X", 0), K_("Z", 0)], writes=[K_("psZ")])
            P.add("act", lambda e: e.activation(out=Xb[1][:], in_=psX[:], func=AF.Copy), reads=[K_("psX")], writes=[K_("X", 1)])
            P.add("dve", lambda e: e.tensor_tensor(out=Zb[1][:], in0=psZ[:], in1=Zb[0][:], op=ALU.add),
                  reads=[K_("psZ"), K_("Z", 0)], writes=[K_("Z", 1)])
            yield
            for hl in range(4):
                P.add("pe", lambda e, hl=hl: e.matmul(psZ[:, hl, :], Xb[1][:, hl, :], Zb[1][:, hl, :], start=True, stop=True),
                      reads=[K_("X", 1), K_("Z", 1)], writes=[K_("psZ")])
            P.add("dve", lambda e: e.tensor_tensor(out=Zb[0][:], in0=psZ[:], in1=Zb[1][:], op=ALU.add),
                  reads=[K_("psZ"), K_("Z", 1)], writes=[K_("Z", 0)])
            yield
            for hl in range(4):
                P.add("pe", lambda e, hl=hl: e.transpose(out=psT[:, hl, :], in_=Zb[0][:, hl, :], identity=k.identb[:]),
                      reads=[K_("Z", 0)], writes=[K_("psT")])
            P.add("act", lambda e: e.activation(out=Ab[0][:], in_=psT[:, 0:4, :], func=AF.Copy), reads=[K_("psT")], writes=[K_("A", 0)])
            yield
            Bcur, Bkey = Zb[0], K_("Z", 0)
            Acur, Akey = Ab[0], K_("A", 0)
            for q in range(1, 5):
                last = (q == 4)
                nb_ = q % 2
                if not last:
                    for hl in range(4):
                        P.add("pe", lambda e, hl=hl, q=q, Acur=Acur: e.matmul(psX[:, hl, :], Ltm[q][:, hl, :], Acur[:, hl, :], start=True, stop=True),
                              reads=[K_("Ltm", q), Akey], writes=[K_("psX")])
                for hl in range(4):
                    P.add("pe", lambda e, hl=hl, q=q, Bcur=Bcur: e.matmul(psY[:, hl, :], Lm[q][:, hl, :], Bcur[:, hl, :], start=True, stop=True),
                          reads=[K_("Lm", q), Bkey], writes=[K_("psY")])
                if not last:
                    P.add("act", lambda e: e.activation(out=T1a[:], in_=psX[:], func=AF.Copy), reads=[K_("psX")], writes=[K_("T1a")])
                P.add("act", lambda e: e.activation(out=T1b[:], in_=psY[:], func=AF.Copy), reads=[K_("psY")], writes=[K_("T1b")])
                yield
                if not last:
                    for hl in range(4):
                        P.add("pe", lambda e, hl=hl, Bcur=Bcur: e.matmul(psX[:, hl, :], Bcur[:, hl, :], T1a[:, hl, :], start=True, stop=True),
                              reads=[Bkey, K_("T1a")], writes=[K_("psX")])
                for hl in range(4):
                    P.add("pe", lambda e, hl=hl, Acur=Acur: e.matmul(psZ[:, hl, :], Acur[:, hl, :], T1b[:, hl, :], start=True, stop=True),
                          reads=[Akey, K_("T1b")], writes=[K_("psZ")])
                if not last:
                    P.add("dve", lambda e, nb_=nb_, Acur=Acur: e.tensor_tensor(out=Ab[nb_][:], in0=Acur[:], in1=psX[:], op=ALU.subtract),
                          reads=[K_("psX"), Akey], writes=[K_("A", nb_)])
                Bnew = Bb[nb_] if not last else Zb[0]
                Bnk = K_("B", nb_) if not last else K_("Z", 0)
                P.add("dve", lambda e, Bnew=Bnew, Bcur=Bcur: e.tensor_tensor(out=Bnew[:], in0=Bcur[:], in1=psZ[:], op=ALU.subtract),
                      reads=[K_("psZ"), Bkey], writes=[Bnk])
                Bcur, Bkey = Bnew, Bnk
                if not last:
                    Acur, Akey = Ab[nb_], K_("A", nb_)
                yield
            ZF = Zb[0]
            for hl in range(4):
                P.add("pe", lambda e, hl=hl: e.matmul(psX[:, hl, :], ZF[:, hl, :], vb[:, hl, :], start=True, stop=True),
                      reads=[K_("Z", 0)] + [K_("vb", q_) for q_ in range(4)], writes=[K_("psX")])
            for hl in range(4):
                P.add("pe", lambda e, hl=hl: e.matmul(psY[:, hl, :], kbg[:, hl, :], ZF[:, hl, :], start=True, stop=True),
                      reads=[K_("Z", 0)] + [K_("kbg", q_) for q_ in range(4)], writes=[K_("psY")])
            P.add("act", lambda e: e.activation(out=Uo, in_=psX[:], func=AF.Copy), reads=[K_("psX")], writes=[K2("Uo")])
            P.add("act", lambda e: e.activation(out=Wo, in_=psY[:], func=AF.Copy), reads=[K_("psY")], writes=[K2("Wo")])
            u = (c * 2 + d) * 8 + hg
            P.add("sp", lambda e: e.dma_start(out=k.DPH[u], in_=PH[:]),
                  reads=[K2("Uo"), K2("Wo"), K2("Qo")] + [K2("kd", q_) for q_ in range(4)], writes=[("DPH", u)], dma=True)
            yield

        def load_chunk(c):
            cb = c % 2
            tsl = slice(c * 128, (c + 1) * 128)
            P.add("sp", lambda e: e.dma_start(out=kTc[cb][:], in_=k.DKT.rearrange("h p t -> p h t")[:, :, tsl]), writes=[("kTc", cb)], dma=True)
            P.add("sp", lambda e: e.dma_start(out=qTc[cb][:], in_=k.DQT.rearrange("h p t -> p h t")[:, :, tsl]), writes=[("qTc", cb)], dma=True)
            P.add("sp", lambda e: e.dma_start(out=ktm[cb][:], in_=k.DKTM[c].rearrange("p (h d) -> p h d", d=128)), writes=[("ktm", cb)], dma=True)
            P.add("sp", lambda e: e.dma_start(out=vtm[cb][:], in_=k.DVTM[c].rearrange("p (h d) -> p h d", d=128)), writes=[("vtm", cb)], dma=True)

        load_chunk(0)
        for c in range(18):
            if c + 1 < 18:
                load_chunk(c + 1)
            units = [(d, hg) for d in range(2) for hg in range(8)]
            for p0 in range(0, len(units), NL):
                gens = [unit(ln, c, units[p0 + ln][0], units[p0 + ln][1], c % 2) for ln in range(NL)]
                alive = list(gens)
                while alive:
                    nxt = []
                    for g in alive:
                        try:
                            next(g)
                            nxt.append(g)
                        except StopIteration:
                            pass
                    alive = nxt


def dn_scan(k, l):
    P = k.P
    order = [list(range(18)), [1, 0] + list(range(17, 1, -1))]
    with contextlib.ExitStack() as st:
        Sf = P.sb([128, 64, 128], F32, st)
        Sb = P.sb([128, 64, 128], BF16, st)
        NB = 3
        ph = [P.sb([128, 4, 4, 128], BF16, st) for _ in range(NB)]
        qT = [P.sb([128, 2, 128], BF16, st) for _ in range(NB)]
        vnew = [P.sb([128, 4, 128], BF16, st) for _ in range(2)]
        o1 = [P.sb([128, 4, 128], F32, st) for _ in range(NB)]
        ob_ = [P.sb([128, 4, 128], F32, st) for _ in range(NB)]
        psP = [P.ps([128, 4, 128], F32, st) for _ in range(2)]
        psO1 = [P.ps([128, 4, 128], F32, st) for _ in range(2)]
        psO2 = [P.ps([128, 4, 128], F32, st) for _ in range(2)]
        psS = [P.ps([128, 4, 128], F32, st) for _ in range(2)]
        P.add("pool", lambda e: e.memset(Sf[:], 0.0), writes=[("Sf", hh) for hh in range(16)])
        P.add("pool", lambda e: e.memset(Sb[:], 0.0), writes=[("Sb", hh) for hh in range(16)])
        P.barrier()
        units = [(s, d, hg) for s in range(18) for d in range(2) for hg in range(8)]

        def stage1(idx):
            s, d, hg = units[idx]
            c = order[d][s]
            u = (c * 2 + d) * 8 + hg
            nb = idx % NB; pb = idx % 2
            g16 = d * 8 + hg
            P.add("sp", lambda e: e.dma_start(out=ph[nb][:], in_=k.DPH[u]), writes=[("ph", nb)], dma=True)
            P.add("sp", lambda e: e.dma_start(out=qT[nb][:], in_=k.DQT[hg * 2:hg * 2 + 2, :, c * 128:(c + 1) * 128].rearrange("h p t -> p h t")),
                  writes=[("qT", nb)], dma=True)
            for hl in range(4):
                hs = d * 32 + hg * 4 + hl
                P.add("pe", lambda e, hl=hl, hs=hs: e.matmul(psP[pb][:, hl, :], ph[nb][:, 1, hl, :], Sb[:, hs, :], start=True, stop=True),
                      reads=[("ph", nb), ("Sb", g16)], writes=[("psP", pb)])
            for hl in range(4):
                hs = d * 32 + hg * 4 + hl
                P.add("pe", lambda e, hl=hl, hs=hs: e.matmul(psO1[pb][:, hl, :], qT[nb][:, hl // 2, :], Sb[:, hs, :], start=True, stop=True),
                      reads=[("qT", nb), ("Sb", g16)], writes=[("psO1", pb)])
            P.add("dve", lambda e: e.tensor_tensor(out=vnew[pb][:], in0=ph[nb][:, 0, :, :], in1=psP[pb][:], op=ALU.subtract),
                  reads=[("ph", nb), ("psP", pb)], writes=[("vnew", pb)])
            P.add("act", lambda e: e.activation(out=o1[nb][:], in_=psO1[pb][:], func=AF.Copy), reads=[("psO1", pb)], writes=[("o1", nb)])

        def stage2(idx):
            s, d, hg = units[idx]
            c = order[d][s]
            nb = idx % NB; pb = idx % 2
            g16 = d * 8 + hg
            h0 = d * 32 + hg * 4
            for hl in range(4):
                P.add("pe", lambda e, hl=hl: e.matmul(psO2[pb][:, hl, :], ph[nb][:, 2, hl, :], vnew[pb][:, hl, :], start=True, stop=True),
                      reads=[("ph", nb), ("vnew", pb)], writes=[("psO2", pb)])
            for hl in range(4):
                P.add("pe", lambda e, hl=hl: e.matmul(psS[pb][:, hl, :], ph[nb][:, 3, hl, :], vnew[pb][:, hl, :], start=True, stop=True),
                      reads=[("ph", nb), ("vnew", pb)], writes=[("psS", pb)])
            for hl in range(4):
                P.add("dve", lambda e, hl=hl: e.scalar_tensor_tensor(
                    out=ob_[nb][:, hl, :], in0=o1[nb][:, hl, :], scalar=k.EG[:, c, h0 + hl:h0 + hl + 1], in1=psO2[pb][:, hl, :],
                    op0=ALU.mult, op1=ALU.add), reads=[("psO2", pb), ("o1", nb)], writes=[("ob", nb, hl)])
            for hl in range(4):
                P.add("dve", lambda e, hl=hl: e.scalar_tensor_tensor(
                    out=Sf[:, h0 + hl, :], in0=Sf[:, h0 + hl, :], scalar=k.EL[:, c, h0 + hl:h0 + hl + 1], in1=psS[pb][:, hl, :],
                    op0=ALU.mult, op1=ALU.add), reads=[("psS", pb), ("Sf", g16, hl)], writes=[("Sf", g16, hl)])
            P.add("act", lambda e: e.activation(out=Sb[:, h0:h0 + 4, :], in_=Sf[:, h0:h0 + 4, :], func=AF.Copy),
                  reads=[("Sf", g16, hl) for hl in range(4)], writes=[("Sb", g16)])
            dst = k.DO[d, c, :, hg * 512:(hg + 1) * 512]
            P.add("pool", lambda e: e.dma_start(out=dst, in_=ob_[nb][:].rearrange("p h j -> p (h j)")),
                  reads=[("ob", nb, hl) for hl in range(4)], writes=[("DO", d, c, hg)], dma=True)

        for idx in range(len(units) + 1):
            if idx < len(units):
                stage1(idx)
            if idx >= 1:
                stage2(idx - 1)


def dn_outnorm(k, l):
    P = k.P
    i = l // 2
    ng = VOFF["dngrep"][0] + i * 128
    with contextlib.ExitStack() as st:
        of = [P.sb([128, 32, 128], F32, st) for _ in range(2)]
        obk = [P.sb([128, 32, 128], F32, st) for _ in range(2)]
        zz = [P.sb([128, 32, 128], BF16, st) for _ in range(2)]
        sq = P.sb([128, 32, 128], F32, st)
        ss = P.sb([128, 32], F32, st)
        y = [P.sb([128, 32, 128], BF16, st) for _ in range(2)]
        yT = [P.sb([128, 32, 128], BF16, st) for _ in range(2)]
        pst = [P.ps([128, 8, 128], BF16, st) for _ in range(2)]
        npt = 0
        for c in range(18):
            b = c % 2
            P.add("sp", lambda e, b=b, c=c: e.dma_start(out=of[b][:], in_=k.DO[0, c].rearrange("p (h j) -> p h j", j=128)), writes=[("of", b)], dma=True)
            P.add("sp", lambda e, b=b, c=c: e.dma_start(out=obk[b][:], in_=k.DO[1, c].rearrange("p (h j) -> p h j", j=128)), writes=[("obk", b)], dma=True)
            P.add("sp", lambda e, b=b, c=c: e.dma_start(out=zz[b][:], in_=k.DZ[c].rearrange("p (h j) -> p h j", j=128)), writes=[("zz", b)], dma=True)
            P.add("dve", lambda e, b=b: e.tensor_tensor(out=of[b][:], in0=of[b][:], in1=obk[b][:], op=ALU.add),
                  reads=[("of", b), ("obk", b)], writes=[("of", b)])
            P.add("act", lambda e, b=b: e.activation(out=sq[:], in_=of[b][:], func=AF.Square), reads=[("of", b)], writes=["sq"])
            P.add("dve", lambda e: e.tensor_reduce(out=ss[:], in_=sq[:], axis=AX.X, op=ALU.add), reads=["sq"], writes=["ss"])
            P.add("act", lambda e: e.activation(out=ss[:], in_=ss[:], func=AF.Sqrt, bias=EPS, scale=1.0 / 128), reads=["ss"], writes=["ss"])
            P.add("dve", lambda e: e.reciprocal(ss[:], ss[:]), reads=["ss"], writes=["ss"])
            P.add("dve", lambda e, b=b: e.tensor_tensor(out=of[b][:], in0=of[b][:], in1=bc(ss[:].unsqueeze(2), [128, 32, 128]), op=ALU.mult),
                  reads=[("of", b), "ss"], writes=[("of", b)])
            P.add("dve", lambda e, b=b: e.tensor_tensor(out=of[b][:], in0=of[b][:], in1=bc(k.vecs[:, ng:ng + 128].unsqueeze(1), [128, 32, 128]), op=ALU.mult),
                  reads=[("of", b)], writes=[("of", b)])
            P.add("dve", lambda e, b=b: e.tensor_tensor(out=y[b][:], in0=of[b][:], in1=zz[b][:], op=ALU.mult),
                  reads=[("of", b), ("zz", b)], writes=[("y", b)])
            for h8 in range(4):
                tb = npt % 2; npt += 1
                for hh in range(8):
                    P.add("pe", lambda e, tb=tb, hh=hh, h8=h8, b=b: e.transpose(out=pst[tb][:, hh, :], in_=y[b][:, h8 * 8 + hh, :], identity=k.identb[:]),
                          reads=[("y", b)], writes=[("pst", tb)])
                P.add("act", lambda e, tb=tb, h8=h8, b=b: e.activation(out=yT[b][:, h8 * 8:(h8 + 1) * 8, :], in_=pst[tb][:], func=AF.Copy),
                      reads=[("pst", tb)], writes=[("yT", b, h8)])
            P.add("pool", lambda e, b=b, c=c: e.dma_start(out=k.DYT[:, :, c * 128:(c + 1) * 128].rearrange("h p t -> p h t"), in_=yT[b][:]),
                  reads=[("yT", b, h8) for h8 in range(4)], writes=[("DYT", c)], dma=True)


def dn_outproj(k, l):
    P = k.P
    i = l // 2
    hsrc = k.hT.rearrange("(c p) t -> p c t", p=128)
    HALF = NT // 2
    SUB = 384
    with contextlib.ExitStack() as st:
        cat = P.sb([128, 32, HALF], BF16, st)
        wb = [P.sb([128, 32, 256], BF16, st) for _ in range(2)]
        hrow = [P.sb([128, HALF], F32, st) for _ in range(2)]
        pso = [P.ps([128, 512], F32, st) for _ in range(4)]
        ysrc = k.DYT.rearrange("h p t -> p h t")
        nps = 0
        for half in range(2):
            h0 = half * HALF
            for q in range(4):
                P.add("sp", lambda e, q=q, h0=h0: e.dma_start(out=cat[:, q * 8:(q + 1) * 8, :], in_=ysrc[:, q * 8:(q + 1) * 8, h0:h0 + HALF]),
                      writes=[("cat", q)], dma=True)
            for blk in range(8):
                b = blk % 2
                for q in range(2):
                    P.add("pool", lambda e, b=b, blk=blk, q=q: e.dma_start(
                        out=wb[b][:, q * 16:(q + 1) * 16, :], in_=k.dn_w_out[i, blk][:, q * 16:(q + 1) * 16, :]),
                        writes=[("wb", b, q)], dma=True)
                for s in range(2):
                    c = blk * 2 + s
                    hb = c % 2
                    P.add("sp", lambda e, hb=hb, c=c, h0=h0: e.dma_start(out=hrow[hb][:], in_=hsrc[:, c, h0:h0 + HALF]),
                          reads=[("hT", c)], writes=[("hrow", hb)], dma=True)
                    for sub in range(HALF // SUB):
                        u0 = sub * SUB
                        pb = nps % 4; nps += 1
                        for kc in range(32):
                            P.add("pe", lambda e, b=b, s=s, pb=pb, kc=kc, u0=u0: e.matmul(
                                pso[pb][:, :SUB], wb[b][:, kc, s * 128:(s + 1) * 128], cat[:, kc, u0:u0 + SUB],
                                start=(kc == 0), stop=(kc == 31)),
                                reads=[("wb", b, kc // 16), ("cat", kc // 8)], writes=[("pso", pb)])
                        for (g0, gn, sidx) in segs(h0 + u0, SUB):
                            r0 = g0 - h0
                            P.add("dve", lambda e, hb=hb, pb=pb, r0=r0, gn=gn, u0=u0, c=c, sidx=sidx: e.scalar_tensor_tensor(
                                out=hrow[hb][:, r0:r0 + gn], in0=pso[pb][:, r0 - u0:r0 - u0 + gn], scalar=modv(k, 2, c, sidx),
                                in1=hrow[hb][:, r0:r0 + gn], op0=ALU.mult, op1=ALU.add),
                                reads=[("pso", pb), ("hrow", hb), "mod"], writes=[("hrow", hb)])
                    P.add("sp", lambda e, hb=hb, c=c, h0=h0: e.dma_start(out=hsrc[:, c, h0:h0 + HALF], in_=hrow[hb][:]),
                          reads=[("hrow", hb)], writes=[("hT", c)], dma=True)


FULL_PLAN = [(l, ("mix", "ffn")) for l in range(DEPTH)]


def tile_w(W, cols=None, width=256):
    if cols is not None:
        W = W[:, cols]
    K_, N_ = W.shape
    return np.ascontiguousarray(W.reshape(K_ // 128, 128, N_ // width, width).transpose(2, 1, 0, 3))


def tile_weights(inp):
    out = {}
    out["w_mod"] = np.stack([tile_w(inp["w_mod"][l]) for l in range(DEPTH)])
    up_cols = np.concatenate([np.concatenate([np.arange(j * 128, (j + 1) * 128), DFF + np.arange(j * 128, (j + 1) * 128)]) for j in range(NFF)])
    out["ffn_w_up"] = np.stack([tile_w(inp["ffn_w_up"][l], up_cols) for l in range(DEPTH)])
    out["ffn_w_down"] = np.stack([tile_w(inp["ffn_w_down"][l]) for l in range(DEPTH)])
    ev_cols = np.concatenate([np.arange(0, 1536)] + [np.concatenate([1536 + np.arange(j * 128, (j + 1) * 128), 2560 + np.arange(j * 128, (j + 1) * 128)])
                                                      for j in range(8)])
    out["even_w_in"] = np.stack([tile_w(inp["even_w_in"][i], ev_cols) for i in range(2)])
    out["even_w_out"] = np.stack([tile_w(inp["even_w_out"][i]) for i in range(2)])
    out["dn_w_in"] = np.stack([tile_w(inp["dn_w_in"][i][:, :12288]) for i in range(2)])
    out["dn_w_ba"] = np.stack([tile_w(inp["dn_w_in"][i][:, 12288:12416], width=128)[0] for i in range(2)])
    out["dn_w_out"] = np.stack([tile_w(inp["dn_w_out"][i]) for i in range(2)])
    return out


def make_core_inputs(inp, b, consts, tiled=None):
    h_in = np.concatenate([inp["ctx"][b].T, inp["x"][b].T], axis=1)
    return {
        "h_in": np.ascontiguousarray(h_in, dtype=np.float32),
        "vecs": pack_vecs(inp, b),
        "consts": consts[0], "rope": consts[1],
        **(tiled if tiled is not None else {}),
    }


def kernel(**inputs):
    inp = {k_: np.asarray(v) for k_, v in inputs.items()}
    consts = make_consts()
    nc = build_program(FULL_PLAN)
    tiled = tile_weights(inp)
    in_maps = [make_core_inputs(inp, c % 4, consts, tiled) for c in range(8)]
    res = run_bass_kernel_spmd(nc, in_maps, core_ids=list(range(8)))
    out = np.stack([res.results[b]["hT"][:, LCTX:].T for b in range(4)], axis=0)
    return np.ascontiguousarray(out.astype(np.float32))
```
